# Optimizing a Trainium2 kernel written in Bass

```python
import math
import jax
import jax.numpy as jnp
from jax import lax
import numpy as np

D_MODEL = 1024
BATCH = 8
SEQ = 2048
DEPTH = 2
DEC_BATCH = 128
DEC_SEQ = 1
PAST_LEN = 16384
PAGE_SIZE = 128

N_AB = (DEPTH + 1) // 2
N_RET = DEPTH // 2
GDN_HEADS = 8
GDN_DK = 128
GDN_DV = 128
SSM_HEADS = 16
SSM_P = 64
SSM_N = 128
SSM_G = 2
CONV_W = 4
RET_HEADS = 4
RET_DK = 256
RET_DV = 512
ROPE_BASE = 10000.0
PEER_HEADS = 8
PEER_NKEYS = 128
PEER_EXPERTS = PEER_NKEYS * PEER_NKEYS
PEER_TOPK = 16
PEER_DQ = 256
PEER_BLOCK = 256
CHUNK = 64
EPS = 1e-6

GDN_QK = GDN_HEADS * GDN_DK
GDN_V = GDN_HEADS * GDN_DV
SSM_INNER = SSM_HEADS * SSM_P
SSM_BC = SSM_G * SSM_N
CONV_CH = 2 * GDN_QK + GDN_V + SSM_INNER + 2 * SSM_BC
AB_SPLIT = [CONV_CH, GDN_V, SSM_INNER, GDN_HEADS, GDN_HEADS, SSM_HEADS]
AB_IN = sum(AB_SPLIT)
CONV_SPLIT = [GDN_QK, GDN_QK, GDN_V, SSM_INNER, SSM_BC, SSM_BC]
AB_OUT = GDN_V + SSM_INNER
RET_QK = RET_HEADS * RET_DK
RET_V = RET_HEADS * RET_DV
RET_SPLIT = [RET_QK, RET_QK, RET_V, RET_V]
RET_IN = sum(RET_SPLIT)

kernel_name = 'hybrid_gdn_ssd_retention_peer_step'


def _cuts(sizes):
    return np.cumsum(sizes)[:-1].tolist()


def rmsnorm(x, g):
    xf = x.astype(jnp.float32)
    y = xf * lax.rsqrt(jnp.mean(xf * xf, axis=-1, keepdims=True) + EPS)
    return (y * g.astype(jnp.float32)).astype(x.dtype)


def l2norm(x):
    return x * lax.rsqrt(jnp.sum(x * x, axis=-1, keepdims=True) + EPS)


def causal_conv_silu(u, buf, w, b):
    L = u.shape[1]
    up = jnp.concatenate([buf.astype(u.dtype), u], axis=1)
    y = b + sum(up[:, i:i + L] * w[i] for i in range(CONV_W))
    return jax.nn.silu(y), up[:, L:]


def rope(x, pos0):
    L, d = x.shape[1], x.shape[-1]
    inv = ROPE_BASE ** (-jnp.arange(0, d, 2, dtype=jnp.float32) / d)
    ang = (pos0 + jnp.arange(L, dtype=jnp.float32))[:, None] * inv[None, :]
    cos, sin = jnp.cos(ang)[None, :, None, :], jnp.sin(ang)[None, :, None, :]
    x1, x2 = x[..., :d // 2], x[..., d // 2:]
    return jnp.concatenate([x1 * cos - x2 * sin, x1 * sin + x2 * cos], axis=-1)


def _chunking(L):
    c = min(CHUNK, L)
    n = -(-L // c)
    return c, n, n * c - L


def _to_chunks(a, c, n, pad):
    a = jnp.pad(a, [(0, 0), (0, pad)] + [(0, 0)] * (a.ndim - 2))
    a = a.reshape(a.shape[0], n, c, *a.shape[2:])
    return jnp.moveaxis(a, 1, 0)


def _from_chunks(a, L):
    a = jnp.moveaxis(a, 0, 1)
    a = a.reshape(a.shape[0], -1, *a.shape[3:])
    return a[:, :L]


def decay_linear_attn(q, k, v, log_a, s0):
    f32 = jnp.float32
    L = q.shape[1]
    c, n, pad = _chunking(L)
    qc, kc, vc, gc = (_to_chunks(t.astype(f32), c, n, pad) for t in (q, k, v, log_a))
    mask = jnp.tril(jnp.ones((c, c), dtype=bool))

    def step(s, inp):
        qi, ki, vi, gi = inp
        cum = jnp.cumsum(gi, axis=1)
        diff = cum[:, :, None, :] - cum[:, None, :, :]
        dec = jnp.exp(jnp.where(mask[None, :, :, None], diff, -jnp.inf))
        scores = jnp.einsum('bihd,bjhd->bijh', qi, ki) * dec
        o = (jnp.einsum('bijh,bjhe->bihe', scores, vi)
             + jnp.einsum('bihd,bhde->bihe', qi * jnp.exp(cum)[..., None], s))
        last = cum[:, -1]
        kd = ki * jnp.exp(last[:, None, :] - cum)[..., None]
        s = s * jnp.exp(last)[:, :, None, None] + jnp.einsum('bjhd,bjhe->bhde', kd, vi)
        return s, o

    s, o = lax.scan(step, s0.astype(f32), (qc, kc, vc, gc))
    return _from_chunks(o, L), s


def gated_delta_rule(q, k, v, beta, g, s0):
    f32 = jnp.float32
    L, dv = q.shape[1], v.shape[-1]
    c, n, pad = _chunking(L)
    heads_first = lambda t: jnp.moveaxis(_to_chunks(t.astype(f32), c, n, pad), 3, 2)
    qh, kh, vh, bh, gh = (heads_first(t) for t in (q, k, v, beta, g))
    gh = jnp.cumsum(gh, axis=-1)
    incl = jnp.tril(jnp.ones((c, c), dtype=bool))
    strict = jnp.tril(jnp.ones((c, c), dtype=bool), -1)
    diff = gh[..., :, None] - gh[..., None, :]
    dec_strict = jnp.exp(jnp.where(strict, diff, -jnp.inf))
    dec_incl = jnp.exp(jnp.where(incl, diff, -jnp.inf))
    lower = (bh[..., :, None] * jnp.einsum('nbhid,nbhjd->nbhij', kh, kh) * dec_strict
             + jnp.eye(c, dtype=f32))
    rhs = jnp.concatenate([vh * bh[..., None], kh * (bh * jnp.exp(gh))[..., None]], axis=-1)
    sol = lax.linalg.triangular_solve(lower, rhs, left_side=True, lower=True, unit_diagonal=True)
    u0, w = sol[..., :dv], sol[..., dv:]
    qk = jnp.einsum('nbhid,nbhjd->nbhij', qh, kh) * dec_incl

    def step(s, inp):
        qi, ki, u0i, wi, qki, gi = inp
        u = u0i - jnp.einsum('bhid,bhde->bhie', wi, s)
        o = (jnp.einsum('bhid,bhde->bhie', qi * jnp.exp(gi)[..., None], s)
             + jnp.einsum('bhij,bhje->bhie', qki, u))
        last = gi[..., -1]
        kd = ki * jnp.exp(last[..., None] - gi)[..., None]
        s = s * jnp.exp(last)[..., None, None] + jnp.einsum('bhjd,bhje->bhde', kd, u)
        return s, o

    s, o = lax.scan(step, s0.astype(f32), (qh, kh, u0, w, qk, gh))
    return _from_chunks(jnp.moveaxis(o, 2, 3), L), s


def mixer_ab(h, conv_buf, gdn_s, ssm_s, w_in, conv_w, conv_b, gdn_a_log, gdn_dt_bias, gdn_norm_g,
             ssm_a_log, ssm_dt_bias, ssm_d, ssm_norm_g, w_out):
    f32 = jnp.float32
    B_, L, _ = h.shape
    proj = (h @ w_in).astype(f32)
    xbc, gate, z, b_raw, a_raw, dt_raw = jnp.split(proj, _cuts(AB_SPLIT), axis=-1)
    xbc, conv_new = causal_conv_silu(xbc, conv_buf, conv_w.astype(f32), conv_b.astype(f32))
    q, k, v, xs, bm, cm = jnp.split(xbc, _cuts(CONV_SPLIT), axis=-1)
    q = l2norm(q.reshape(B_, L, GDN_HEADS, GDN_DK)) * GDN_DK ** -0.5
    k = l2norm(k.reshape(B_, L, GDN_HEADS, GDN_DK))
    v = v.reshape(B_, L, GDN_HEADS, GDN_DV)
    beta = jax.nn.sigmoid(b_raw)
    g = -jnp.exp(gdn_a_log.astype(f32)) * jax.nn.softplus(a_raw + gdn_dt_bias.astype(f32))
    o_a, gdn_new = gated_delta_rule(q, k, v, beta, g, gdn_s)
    o_a = rmsnorm(o_a, gdn_norm_g) * jax.nn.silu(gate.reshape(B_, L, GDN_HEADS, GDN_DV))
    dt = jax.nn.softplus(dt_raw + ssm_dt_bias.astype(f32))
    xs = xs.reshape(B_, L, SSM_HEADS, SSM_P)
    rep = SSM_HEADS // SSM_G
    bm = jnp.repeat(bm.reshape(B_, L, SSM_G, SSM_N), rep, axis=2)
    cm = jnp.repeat(cm.reshape(B_, L, SSM_G, SSM_N), rep, axis=2)
    log_a = -dt * jnp.exp(ssm_a_log.astype(f32))
    o_b, ssm_new = decay_linear_attn(cm, bm * dt[..., None], xs, log_a, ssm_s)
    o_b = (o_b + ssm_d.astype(f32)[:, None] * xs).reshape(B_, L, SSM_INNER)
    o_b = rmsnorm(o_b * jax.nn.silu(z), ssm_norm_g)
    out = jnp.concatenate([o_a.reshape(B_, L, GDN_V), o_b], axis=-1) @ w_out
    return out.astype(h.dtype), conv_new, gdn_new, ssm_new


def mixer_ret(h, ret_s, pos0, w_in, norm_g, w_out):
    f32 = jnp.float32
    B_, L, _ = h.shape
    proj = (h @ w_in).astype(f32)
    q, k, v, gate = jnp.split(proj, _cuts(RET_SPLIT), axis=-1)
    q = rope(q.reshape(B_, L, RET_HEADS, RET_DK), pos0)
    k = rope(k.reshape(B_, L, RET_HEADS, RET_DK), pos0) * RET_DK ** -0.5
    v = v.reshape(B_, L, RET_HEADS, RET_DV)
    log_gamma = jnp.log(1.0 - 2.0 ** (-5.0 - jnp.arange(RET_HEADS, dtype=f32)))
    log_a = jnp.broadcast_to(log_gamma, (B_, L, RET_HEADS))
    o, s = decay_linear_attn(q, k, v, log_a, ret_s)
    mu = jnp.mean(o, axis=-1, keepdims=True)
    var = jnp.mean(jnp.square(o - mu), axis=-1, keepdims=True)
    o = (o - mu) * lax.rsqrt(var + EPS) * norm_g.astype(f32)
    o = jax.nn.silu(gate) * o.reshape(B_, L, RET_V)
    return (o @ w_out).astype(h.dtype), s


def peer(h, w_q, keys, u, v):
    B_, L, D = h.shape
    T = B_ * L
    blk = min(PEER_BLOCK, T)
    nblk = -(-T // blk)
    xt = jnp.pad(h.reshape(T, D), ((0, nblk * blk - T), (0, 0))).reshape(nblk, blk, D)

    def block(xb):
        q = (xb @ w_q).astype(jnp.float32).reshape(blk, PEER_HEADS, 2, PEER_DQ // 2)
        s = jnp.einsum('thpd,pnd->thpn', q, keys.astype(jnp.float32))
        sv, si = lax.top_k(s, PEER_TOPK)
        cand = (sv[:, :, 0, :, None] + sv[:, :, 1, None, :]).reshape(blk, PEER_HEADS, PEER_TOPK * PEER_TOPK)
        cv, ci = lax.top_k(cand, PEER_TOPK)
        idx = (jnp.take_along_axis(si[:, :, 0], ci // PEER_TOPK, axis=-1) * PEER_NKEYS
               + jnp.take_along_axis(si[:, :, 1], ci % PEER_TOPK, axis=-1))
        gate = jax.nn.softmax(cv, axis=-1)
        act = jax.nn.gelu(jnp.einsum('td,thkd->thk', xb, u[idx]).astype(jnp.float32), approximate=False) * gate
        return jnp.einsum('thk,thkd->td', act.astype(xb.dtype), v[idx])

    y = lax.map(block, xt)
    return y.reshape(nblk * blk, D)[:T].reshape(B_, L, D)


def trunk(x, c, pos0, conv0, gdn0, ssm0, ret0, weights):
    (ada_w, ada_b, norm1_g, norm2_g, ab_w_in, ab_conv_w, ab_conv_b, gdn_a_log, gdn_dt_bias,
     gdn_norm_g, ssm_a_log, ssm_dt_bias, ssm_d, ssm_norm_g, ab_w_out, ret_w_in, ret_norm_g,
     ret_w_out, peer_w_q, peer_keys, peer_u, peer_v, final_g) = weights
    convs, gdns, ssms, rets = [], [], [], []
    for layer in range(DEPTH):
        mod = jax.nn.silu(c) @ ada_w[layer] + ada_b[layer]
        sh1, sc1, g1, sh2, sc2, g2 = jnp.split(mod[:, None, :], 6, axis=-1)
        hm = rmsnorm(x, norm1_g[layer]) * (1 + sc1) + sh1
        i = layer // 2
        if layer % 2 == 0:
            out, cs, gs, ss = mixer_ab(hm, conv0[i], gdn0[i], ssm0[i], ab_w_in[i], ab_conv_w[i],
                                       ab_conv_b[i], gdn_a_log[i], gdn_dt_bias[i], gdn_norm_g[i],
                                       ssm_a_log[i], ssm_dt_bias[i], ssm_d[i], ssm_norm_g[i],
                                       ab_w_out[i])
            convs.append(cs)
            gdns.append(gs)
            ssms.append(ss)
        else:
            out, rs = mixer_ret(hm, ret0[i], pos0, ret_w_in[i], ret_norm_g[i], ret_w_out[i])
            rets.append(rs)
        x = x + g1 * out
        hm = rmsnorm(x, norm2_g[layer]) * (1 + sc2) + sh2
        x = x + g2 * peer(hm, peer_w_q[layer], peer_keys[layer], peer_u[layer], peer_v[layer])
    return rmsnorm(x, final_g), jnp.stack(convs), jnp.stack(gdns), jnp.stack(ssms), jnp.stack(rets)


def setup_inputs(seed: int = 0) -> dict:
    key = jax.random.key(seed)
    ks = iter(jax.random.split(key, 40))
    f32 = jnp.float32

    def nrm(shape, scale):
        return jax.random.normal(next(ks), shape, f32) * scale

    def dt_bias(shape):
        dt = jnp.exp(jax.random.uniform(next(ks), shape, f32, math.log(1e-3), math.log(1e-1)))
        return dt + jnp.log(-jnp.expm1(-dt))

    def a_log(shape):
        return jnp.log(jax.random.uniform(next(ks), shape, f32, 1.0, 16.0))

    return {
        'x_prompt': nrm((BATCH, SEQ, D_MODEL), 1.0),
        'x_sample': nrm((DEC_BATCH, DEC_SEQ, D_MODEL), 1.0),
        'c_prompt': nrm((BATCH, D_MODEL), 1.0),
        'c_sample': nrm((DEC_BATCH, D_MODEL), 1.0),
        'state_conv': nrm((N_AB, DEC_BATCH, CONV_W - 1, CONV_CH), 1.0),
        'state_gdn': nrm((N_AB, DEC_BATCH, GDN_HEADS, GDN_DK, GDN_DV), 0.1),
        'state_ssm': nrm((N_AB, DEC_BATCH, SSM_HEADS, SSM_N, SSM_P), 0.1),
        'state_ret': nrm((N_RET, DEC_BATCH, RET_HEADS, RET_DK, RET_DV), 0.1),
        'ada_w': nrm((DEPTH, D_MODEL, 6 * D_MODEL), 0.5 * D_MODEL ** -0.5),
        'ada_b': nrm((DEPTH, 6 * D_MODEL), 0.02),
        'norm1_g': 1.0 + nrm((DEPTH, D_MODEL), 0.02),
        'norm2_g': 1.0 + nrm((DEPTH, D_MODEL), 0.02),
        'ab_w_in': nrm((N_AB, D_MODEL, AB_IN), D_MODEL ** -0.5),
        'ab_conv_w': nrm((N_AB, CONV_W, CONV_CH), CONV_W ** -0.5),
        'ab_conv_b': nrm((N_AB, CONV_CH), 0.02),
        'gdn_a_log': a_log((N_AB, GDN_HEADS)),
        'gdn_dt_bias': dt_bias((N_AB, GDN_HEADS)),
        'gdn_norm_g': 1.0 + nrm((N_AB, GDN_DV), 0.02),
        'ssm_a_log': a_log((N_AB, SSM_HEADS)),
        'ssm_dt_bias': dt_bias((N_AB, SSM_HEADS)),
        'ssm_d': 1.0 + nrm((N_AB, SSM_HEADS), 0.1),
        'ssm_norm_g': 1.0 + nrm((N_AB, SSM_INNER), 0.02),
        'ab_w_out': nrm((N_AB, AB_OUT, D_MODEL), AB_OUT ** -0.5),
        'ret_w_in': nrm((N_RET, D_MODEL, RET_IN), D_MODEL ** -0.5),
        'ret_norm_g': 1.0 + nrm((N_RET, RET_HEADS, RET_DV), 0.02),
        'ret_w_out': nrm((N_RET, RET_V, D_MODEL), RET_V ** -0.5),
        'peer_w_q': nrm((DEPTH, D_MODEL, PEER_HEADS * PEER_DQ), D_MODEL ** -0.5),
        'peer_keys': nrm((DEPTH, 2, PEER_NKEYS, PEER_DQ // 2), (PEER_DQ // 2) ** -0.5),
        'peer_u': nrm((DEPTH, PEER_EXPERTS, D_MODEL), D_MODEL ** -0.5),
        'peer_v': nrm((DEPTH, PEER_EXPERTS, D_MODEL), 0.3),
        'final_g': 1.0 + nrm((D_MODEL,), 0.02),
    }


def reference(x_prompt, x_sample, c_prompt, c_sample, state_conv, state_gdn, state_ssm, state_ret,
              ada_w, ada_b, norm1_g, norm2_g, ab_w_in, ab_conv_w, ab_conv_b, gdn_a_log, gdn_dt_bias,
              gdn_norm_g, ssm_a_log, ssm_dt_bias, ssm_d, ssm_norm_g, ab_w_out, ret_w_in, ret_norm_g,
              ret_w_out, peer_w_q, peer_keys, peer_u, peer_v, final_g):
    weights = (ada_w, ada_b, norm1_g, norm2_g, ab_w_in, ab_conv_w, ab_conv_b, gdn_a_log, gdn_dt_bias,
               gdn_norm_g, ssm_a_log, ssm_dt_bias, ssm_d, ssm_norm_g, ab_w_out, ret_w_in, ret_norm_g,
               ret_w_out, peer_w_q, peer_keys, peer_u, peer_v, final_g)
    nb = x_prompt.shape[0]
    zeros = lambda s: jnp.zeros((s.shape[0], nb) + s.shape[2:], s.dtype)
    y_prompt, p_conv, p_gdn, p_ssm, p_ret = trunk(
        x_prompt, c_prompt, 0, zeros(state_conv), zeros(state_gdn), zeros(state_ssm), zeros(state_ret), weights)
    y_sample, s_conv, s_gdn, s_ssm, s_ret = trunk(
        x_sample, c_sample, PAST_LEN, state_conv, state_gdn, state_ssm, state_ret, weights)
    return (y_prompt, y_sample, p_conv, p_gdn, p_ssm, p_ret, s_conv, s_gdn, s_ssm, s_ret)
```

```python
import contextlib
import os
import numpy as np
import concourse.bass as bass
import concourse.mybir as mybir
from concourse.bass_utils import run_bass_kernel_spmd

F32 = mybir.dt.float32
BF16 = mybir.dt.bfloat16
I32 = mybir.dt.int32
U32 = mybir.dt.uint32
AF = mybir.ActivationFunctionType
ALU = mybir.AluOpType
AX = mybir.AxisListType

NT = 2064
NP = 2048
NS = 16
EPS = 1e-6
BIG = 30000.0


class Tk:
    __slots__ = ("name", "w", "rs")

    def __init__(self, name=""):
        self.name = name
        self.w = []
        self.rs = []


class Lane:
    __slots__ = ("sem", "cum", "key")

    def __init__(self, sem, key):
        self.sem = sem
        self.cum = 0
        self.key = key


class _Rec:
    __slots__ = ("call",)

    def __init__(self):
        self.call = None

    def __getattr__(self, name):
        def f(*a, **kw):
            self.call = (name, a, kw)
            return self
        return f


def _eager(fn):
    r = _Rec()
    fn(r)
    name, a, kw = r.call
    return lambda e: getattr(e, name)(*a, **kw)


class Prog:
    ENGS = ("pe", "dve", "act", "pool", "sp")

    def __init__(self, nc):
        self.nc = nc
        self.ops = {e: [] for e in self.ENGS}
        self.cnt = {e: 0 for e in self.ENGS}
        self.sems = {}
        self.waited = {e: {} for e in self.ENGS}
        self._ctx = []
        for e in ("pe", "dve", "act", "pool"):
            self.sems[e] = self._sem("p_" + e)
        self.lanes = []

    def _sem(self, name):
        g = self.nc.semaphore(name)
        s = g.__enter__()
        self._ctx.append(g)
        return s

    def lane(self):
        key = "L%d" % len(self.lanes)
        self.sems[key] = self._sem(key)
        ln = Lane(self.sems[key], key)
        self.lanes.append(ln)
        return ln

    def _need(self, eng, deps):
        wd = self.waited[eng]
        best = {}
        for k, v in deps:
            if eng == "pe" and k == "pe":
                continue
            if wd.get(k, 0) >= v:
                continue
            if best.get(k, 0) < v:
                best[k] = v
        for k, v in best.items():
            wd[k] = v
            sem = self.sems[k]
            self.ops[eng].append(lambda e, sem=sem, v=v: e.wait_ge(sem, v))

    @staticmethod
    def _deps_for(r, w):
        deps = []
        for t in r:
            deps.extend(t.w)
        for t in w:
            deps.extend(t.w)
            deps.extend(t.rs)
        return deps

    def op(self, eng, fn, r=(), w=()):
        fn = _eager(fn)
        self._need(eng, self._deps_for(r, w))
        self.cnt[eng] += 1
        n = self.cnt[eng]
        sem = self.sems[eng]
        self.ops[eng].append(lambda e, fn=fn, sem=sem: fn(e).then_inc(sem, 1))
        tag = (eng, n)
        for t in r:
            t.rs.append(tag)
        for t in w:
            t.w = [tag]
            t.rs = []

    def dma(self, q, lane, fn, r=(), w=(), group=False):
        fn = _eager(fn)
        deps = self._deps_for(r, w)
        if group:
            deps = [d for d in deps if d[0] != lane.key]
        elif lane.cum > 0:
            deps.append((lane.key, lane.cum))
        self._need(q, deps)
        lane.cum += 16
        sem = lane.sem
        self.ops[q].append(lambda e, fn=fn, sem=sem: fn(e).then_inc(sem, 16))
        tag = (lane.key, lane.cum)
        for t in r:
            t.rs.append(tag)
        for t in w:
            if group:
                t.w = [x for x in t.w if x[0] != lane.key] + [tag]
            else:
                t.w = [tag]
            t.rs = []

    def group_finish(self, lane, tks):
        tag = (lane.key, lane.cum)
        for t in tks:
            t.w = [tag if x[0] == lane.key else x for x in t.w]
            t.rs = [tag if x[0] == lane.key else x for x in t.rs]

    def wait_all(self, eng):
        deps = [(e, self.cnt[e]) for e in ("pe", "dve", "act", "pool") if self.cnt[e] > 0]
        deps += [(ln.key, ln.cum) for ln in self.lanes if ln.cum > 0]
        self._need(eng, deps)

    def barrier(self):
        for e in self.ENGS:
            self.wait_all(e)

    def emit(self):
        with self.nc.Block() as block:
            def mk(name):
                lst = self.ops[name]

                def body(e):
                    for f in lst:
                        f(e)
                return body
            block.tensor(mk("pe"))
            block.vector(mk("dve"))
            block.scalar(mk("act"))
            block.gpsimd(mk("pool"))
            block.sync(mk("sp"))

    def close(self):
        for g in reversed(self._ctx):
            g.__exit__(None, None, None)
        self._ctx = []


class B:
    __slots__ = ("t", "k")

    def __init__(self, t, name=""):
        self.t = t
        self.k = Tk(name)


def interleave(gens, width):
    gens = list(gens)
    active = []
    while gens or active:
        while gens and len(active) < width:
            active.append(gens.pop(0))
        nxt = []
        for g in active:
            try:
                next(g)
                nxt.append(g)
            except StopIteration:
                pass
        active = nxt


class Builder:
    def __init__(self, dbg=(), stages=("l0", "p0", "l1", "p1")):
        self.nc = bass.Bass("TRN2", target_bir_lowering=False)
        self.P = Prog(self.nc)
        self.es = contextlib.ExitStack()
        self.dbg = set(dbg)
        self.stages = stages
        self.in_names = []
        self.out_names = []
        self.rr = 0

    def din(self, name, shape, dt=F32):
        self.in_names.append(name)
        return self.nc.dram_tensor(name, list(shape), dt, kind="ExternalInput").ap()

    def dout(self, name, shape, dt=F32):
        self.out_names.append(name)
        return self.nc.dram_tensor(name, list(shape), dt, kind="ExternalOutput").ap()

    def dscratch(self, name, shape, dt=F32):
        return self.nc.dram_tensor(name, list(shape), dt, kind="Internal").ap()

    def sb(self, name, shape, dt=F32, es=None):
        t = (es or self.es).enter_context(self.nc.sbuf_tensor(name, list(shape), dt))
        return B(t, name)

    def psb(self, name, shape, dt=F32):
        t = self.es.enter_context(self.nc.psum_tensor(name, list(shape), dt))
        return B(t, name)

    def ew(self):
        self.rr += 1
        return ("dve", "pool")[self.rr % 2]

    def build(self):
        nc, P = self.nc, self.P
        op, dma = P.op, P.dma
        xT_d = self.din("xT", [128, 8, NT])
        cT_d = self.din("cT", [128, 8, 17])
        ada_w = self.din("ada_w", [2, 1024, 6144])
        ada_b = self.din("ada_b", [128, 2, 48])
        n1g = self.din("n1g", [128, 2, 8])
        n2g = self.din("n2g", [128, 2, 8])
        fg = self.din("fg", [128, 8])
        consts_d = self.din("consts", [128, 6, 128])
        w_in = self.din("ab_w_in", [1024, 6688])
        cw_d = self.din("cw", [128, 36, 4])
        cb_d = self.din("cb", [128, 36])
        auxp_d = self.din("auxp", [1, 5, 32])
        gng_d = self.din("gng", [128, 1])
        sng_d = self.din("sng", [128, 8])
        w_out = self.din("ab_w_out", [2048, 1024])
        D = dict(w_in=w_in, w_out=w_out, cw=cw_d, cb=cb_d, auxp=auxp_d, gng=gng_d, sng=sng_d)
        D["st_conv"] = self.din("st_conv", [128, 36, 16, 3])
        D["st_gdn"] = self.din("st_gdn", [16, 8, 128, 128])
        D["st_ssm"] = self.din("st_ssm", [16, 16, 128, 64])
        D["ret_w_in"] = self.din("ret_w_in", [1024, 6144])
        D["ret_w_out"] = self.din("ret_w_out", [2048, 1024])
        D["rng"] = self.din("rng", [1, 4, 512])
        D["retc"] = self.din("retc", [128, 4, 132])
        D["cosT"] = self.din("cosT", [128, NT])
        D["sinT"] = self.din("sinT", [128, NT])
        D["st_ret"] = self.din("st_ret", [16, 4, 256, 512])
        D["w_q"] = self.din("w_q", [2, 1024, 2048])
        D["keysT"] = self.din("keysT", [2, 2, 128, 128])
        D["uT"] = self.din("uT", [2, 1024, 16384])
        D["pv"] = self.din("pv", [2, 16384, 1024])
        yT_d = self.dout("yT", [128, 8, NT])
        D["ret_p"] = self.dout("ret_p", [4, 256, 512])
        D["ret_s"] = self.dout("ret_s", [16, 4, 256, 512])
        D["conv_p"] = self.dout("conv_p", [128, 36, 3])
        D["gdn_p"] = self.dout("gdn_p", [8, 128, 128])
        D["ssm_p"] = self.dout("ssm_p", [16, 128, 64])
        D["conv_s"] = self.dout("conv_s", [128, 36, 16, 3])
        D["gdn_s"] = self.dout("gdn_s", [16, 8, 128, 128])
        D["ssm_s"] = self.dout("ssm_s", [16, 16, 128, 64])
        dbg_d = {}
        for nm in self.dbg:
            dbg_d[nm] = self.dout("dbg_" + nm, [128, 8, NT])

        xT = self.sb("xT_sb", [128, 8, NT])
        xk = [[Tk("x%d_%d" % (m, g)) for g in range(5)] for m in range(8)]
        cT = self.sb("cT_sb", [128, 8, 17])
        modT = self.sb("modT", [128, 2, 48, 17])
        consts = self.sb("consts_sb", [128, 6, 128])
        n1 = self.sb("n1_sb", [128, 2, 8])
        n2 = self.sb("n2_sb", [128, 2, 8])
        fgs = self.sb("fg_sb", [128, 8])
        adab = self.sb("adab_sb", [128, 2, 48])
        ident = consts.t[:, 0, :]
        triU = consts.t[:, 1, :]
        ones = consts.t[:, 2, :]
        maskA = consts.t[:, 3, :]
        maskB = consts.t[:, 4, :]
        iota = consts.t[:, 5, :]
        CK = consts.k

        banks = [self.psb("ps%d" % i, [128, 512]) for i in range(8)]
        self.banks = banks

        lc = P.lane()
        first = [True]

        def cload(dst, src_ap, q="sp"):
            dma(q, lc, lambda e: e.dma_start(out=dst.t[:], in_=src_ap), w=[dst.k], group=not first[0])
            first[0] = False

        cload(consts, consts_d[:, :, :])
        cload(cT, cT_d[:, :, :])
        cload(n1, n1g[:, :, :])
        cload(n2, n2g[:, :, :])
        cload(fgs, fg[:, :])
        cload(adab, ada_b[:, :, :])
        P.group_finish(lc, [consts.k, cT.k, n1.k, n2.k, fgs.k, adab.k])
        lx = P.lane()
        for m in range(8):
            dma("sp", lx, lambda e, m=m: e.dma_start(out=xT.t[:, m, :], in_=xT_d[:, m, :]),
                w=xk[m], group=(m > 0))
        P.group_finish(lx, [t for row in xk for t in row])

        op("act", lambda e: e.activation(out=cT.t[:], in_=cT.t[:], func=AF.Silu), r=[cT.k], w=[cT.k])
        with contextlib.ExitStack() as es:
            NAB = 4
            abuf = [self.sb("adaw%d" % i, [128, 8, 128], es=es) for i in range(NAB)]
            la = [P.lane() for _ in range(NAB)]
            it = 0
            for l in range(2):
                for half in range(2):
                    pst = banks[half]
                    for mm in range(24):
                        bi = it % NAB
                        it += 1
                        buf = abuf[bi]
                        col0 = (half * 24 + mm) * 128
                        dma("sp", la[bi], lambda e, l=l, col0=col0, buf=buf: e.dma_start(
                            out=buf.t[:], in_=ada_w[l, :, col0:col0 + 128].rearrange("(k p) c -> p k c", p=128)),
                            w=[buf.k])
                        for kk in range(8):
                            op("pe", lambda e, mm=mm, kk=kk, buf=buf, pst=pst: e.matmul(
                                pst.t[:, mm * 17:(mm + 1) * 17], buf.t[:, kk, :], cT.t[:, kk, :],
                                start=(kk == 0), stop=(kk == 7)), r=[buf.k, cT.k], w=[pst.k])
                    op("dve", lambda e, l=l, half=half, pst=pst: e.tensor_tensor(
                        modT.t[:, l, half * 24:(half + 1) * 24, :],
                        pst.t[:, 0:408].rearrange("p (m j) -> p m j", j=17),
                        adab.t[:, l, half * 24:(half + 1) * 24].unsqueeze(2).broadcast_to([128, 24, 17]), ALU.add),
                        r=[pst.k, adab.k], w=[modT.k])
            P.barrier()

        modA = self.sb("modA", [128, 2, 2, 8, 17])
        for l in range(2):
            for wh in range(2):
                sc0 = 8 + 24 * wh
                gsrc = (n1, n2)[wh]
                op("dve", lambda e, l=l, wh=wh, sc0=sc0: e.tensor_scalar(
                    modA.t[:, l, wh, :, :], modT.t[:, l, sc0:sc0 + 8, :], 1.0, None, ALU.add),
                    r=[modT.k], w=[modA.k])
                op("dve", lambda e, l=l, wh=wh, gsrc=gsrc: e.tensor_tensor(
                    modA.t[:, l, wh, :, :], modA.t[:, l, wh, :, :],
                    gsrc.t[:, l, :].unsqueeze(2).broadcast_to([128, 8, 17]), ALU.mult),
                    r=[modA.k, gsrc.k], w=[modA.k])

        self.groups = [(g * 512, 512) for g in range(4)] + [(2048, 16)]

        nsq = self.sb("nsq", [128, 512])
        nrs = self.sb("nrs", [128, 512])
        ntmp = self.sb("ntmp", [128, 512])

        def norm_mod(g, hm, A_ap_fn, B_ap_fn, t0, T, samp_A=None, samp_B=None, out_dt_tile=None):
            pst = banks[7]
            for k in range(8):
                op("act", lambda e, k=k: e.activation(out=nsq.t[:, :T], in_=xT.t[:, k, t0:t0 + T], func=AF.Square),
                   r=[xk[k][g]], w=[nsq.k])
                op("pe", lambda e, k=k: e.matmul(pst.t[:, :T], ones, nsq.t[:, :T], start=(k == 0), stop=(k == 7)),
                   r=[nsq.k, CK], w=[pst.k])
            op("act", lambda e: e.activation(out=nrs.t[:, :T], in_=pst.t[:, :T], func=AF.Sqrt, bias=EPS, scale=1.0 / 1024),
               r=[pst.k], w=[nrs.k])
            op("dve", lambda e: e.reciprocal(nrs.t[:, :T], nrs.t[:, :T]), r=[nrs.k], w=[nrs.k])
            for k in range(8):
                if samp_A is None:
                    op("dve", lambda e, k=k: e.scalar_tensor_tensor(
                        ntmp.t[:, :T], xT.t[:, k, t0:t0 + T], A_ap_fn(k), nrs.t[:, :T], ALU.mult, ALU.mult),
                        r=[xk[k][g], nrs.k, modA.k, fgs.k], w=[ntmp.k])
                    if B_ap_fn is None:
                        op("act", lambda e, k=k: e.activation(out=hm.t[:, k, :T], in_=ntmp.t[:, :T], func=AF.Copy),
                           r=[ntmp.k], w=[hm.k])
                    else:
                        op("act", lambda e, k=k: e.activation(out=hm.t[:, k, :T], in_=ntmp.t[:, :T], func=AF.Identity,
                                                             bias=B_ap_fn(k), scale=1.0),
                           r=[ntmp.k, modT.k], w=[hm.k])
                else:
                    op("dve", lambda e, k=k: e.tensor_tensor(ntmp.t[:, :T], xT.t[:, k, t0:t0 + T], nrs.t[:, :T], ALU.mult),
                       r=[xk[k][g], nrs.k], w=[ntmp.k])
                    op("dve", lambda e, k=k: e.tensor_tensor(ntmp.t[:, :T], ntmp.t[:, :T], samp_A(k), ALU.mult),
                       r=[ntmp.k, modA.k], w=[ntmp.k])
                    op("dve", lambda e, k=k: e.tensor_tensor(hm.t[:, k, :T], ntmp.t[:, :T], samp_B(k), ALU.add),
                       r=[ntmp.k, modT.k], w=[hm.k])

        def norm_mod_layer(l, wh, g, hm):
            t0, T = self.groups[g]
            sh0 = 24 * wh
            if g < 4:
                norm_mod(g, hm, lambda k: modA.t[:, l, wh, k, 0:1], lambda k: modT.t[:, l, sh0 + k, 0:1], t0, T)
            else:
                norm_mod(g, hm, None, None, t0, T,
                         samp_A=lambda k: modA.t[:, l, wh, k, 1:17], samp_B=lambda k: modT.t[:, l, sh0 + k, 1:17])

        self.norm_mod = norm_mod
        self.nsq, self.nrs = nsq, nrs
        self.norm_mod_layer = norm_mod_layer
        self.xT, self.xk, self.modT, self.modA = xT, xk, modT, modA
        self.c = dict(ident=ident, triU=triU, ones=ones, maskA=maskA, maskB=maskB, iota=iota, CK=CK)

        def dump(nm):
            if nm in self.dbg:
                P.barrier()
                ld = P.lane()
                dma("sp", ld, lambda e: e.dma_start(out=dbg_d[nm][:, :, :], in_=xT.t[:]),
                    r=[t for row in xk for t in row])
                P.barrier()

        if "l0" in self.stages:
            self.layer0(D)
        dump("xa0")
        if "p0" in self.stages:
            self.peer(0, D)
        dump("xb0")
        if "l1" in self.stages:
            self.layer1(D)
        dump("xa1")
        if "p1" in self.stages:
            self.peer(1, D)
        dump("xb1")

        yb = self.sb("ybuf_fin", [128, 8, 512])
        ly = P.lane()
        for g in range(5):
            t0, T = self.groups[g]
            norm_mod(g, yb, lambda k: fgs.t[:, k:k + 1], None, t0, T)
            dma("sp", ly, lambda e, t0=t0, T=T: e.dma_start(out=yT_d[:, :, t0:t0 + T], in_=yb.t[:, :, :T]), r=[yb.k])
        P.wait_all("sp")
        P.emit()
        P.close()
        self.es.close()

    def layer0(self, D):
        nc, P = self.nc, self.P
        op, dma = P.op, P.dma
        c = self.c
        ident, triU, ones, maskA, maskB, CK = c["ident"], c["triU"], c["ones"], c["maskA"], c["maskB"], c["CK"]
        banks = self.banks
        xT, xk, modT, modA = self.xT, self.xk, self.modT, self.modA
        w_in, w_out = D["w_in"], D["w_out"]
        es = contextlib.ExitStack()
        sb = lambda name, shape, dt=F32: self.sb(name, shape, dt, es=es)
        KG = int(os.environ.get('KG', '5'))
        KH = int(os.environ.get('KH', '8'))
        KSS = int(os.environ.get('KSS', '1'))
        KSMP = int(os.environ.get('KSMP', '16'))

        lp = P.lane()
        cw = sb("cw_sb", [128, 36, 4])
        cb = sb("cb_sb", [128, 36])
        auxp = sb("auxp_sb", [128, 5, 32])
        gng = sb("gng_sb", [128, 1])
        sng = sb("sng_sb", [128, 8])
        dma("sp", lp, lambda e: e.dma_start(out=cw.t[:], in_=D["cw"][:, :, :]), w=[cw.k])
        dma("sp", lp, lambda e: e.dma_start(out=cb.t[:], in_=D["cb"][:, :]), w=[cb.k], group=True)
        dma("sp", lp, lambda e: e.dma_start(out=auxp.t[:], in_=D["auxp"][0:1, :, :].broadcast_to([128, 5, 32])), w=[auxp.k], group=True)
        dma("sp", lp, lambda e: e.dma_start(out=gng.t[:], in_=D["gng"][:, :]), w=[gng.k], group=True)
        dma("sp", lp, lambda e: e.dma_start(out=sng.t[:], in_=D["sng"][:, :]), w=[sng.k], group=True)
        P.group_finish(lp, [cw.k, cb.k, auxp.k, gng.k, sng.k])
        op("act", lambda e: e.activation(out=auxp.t[:, 1, 8:32], in_=auxp.t[:, 1, 8:32], func=AF.Exp), r=[auxp.k], w=[auxp.k])
        op("dve", lambda e: e.tensor_scalar(auxp.t[:, 1, 8:32], auxp.t[:, 1, 8:32], -1.0, None, ALU.mult), r=[auxp.k], w=[auxp.k])

        Sg = sb("Sg", [128, 8, 128])
        Sgk = [Tk("Sg%d" % h) for h in range(8)]
        Ss = sb("Ss", [128, 16, 64])
        Ssk = [Tk("Ss%d" % g) for g in range(2)]
        halo = sb("halo", [128, 36, 3])
        halok = [Tk("halo%d" % i) for i in range(36)]
        op("pool", lambda e: e.memset(Sg.t[:], 0.0), w=Sgk)
        op("pool", lambda e: e.memset(Ss.t[:], 0.0), w=Ssk)
        op("pool", lambda e: e.memset(halo.t[:], 0.0), w=halok)

        hm = sb("hm", [128, 8, 512], BF16)
        NWB = 4
        wbuf = [sb("wb%d" % i, [128, 8, 128], BF16) for i in range(NWB)]
        wl = [P.lane() for _ in range(NWB)]
        wi = [0]

        def load_w(col0, src, ncol=128, row0=0):
            i = wi[0] % NWB
            wi[0] += 1
            buf = wbuf[i]
            dma("pool", wl[i], lambda e: e.dma_start(
                out=buf.t[:, :, :ncol], in_=src[row0:row0 + 1024, col0:col0 + ncol].rearrange("(k p) c -> p k c", p=128)), w=[buf.k])
            return buf

        pring = [0]

        def proj_ps():
            b = banks[pring[0] % 2]
            pring[0] += 1
            return b

        def project(col0, T, ncol=128):
            buf = load_w(col0, w_in, ncol=ncol)
            pst = proj_ps()
            for k in range(8):
                op("pe", lambda e, k=k: e.matmul(pst.t[:ncol, :T], buf.t[:, k, :ncol], hm.t[:, k, :T],
                                                 start=(k == 0), stop=(k == 7)), r=[buf.k, hm.k], w=[pst.k])
            return pst

        NPRE = 2
        prehk = [Tk("preh%d" % i) for i in range(NPRE)]
        prei = [0]
        acc_t = [sb("cacc%d" % i, [128, 512]) for i in range(2)]
        acci = [0]
        pres = [sb("pres%d" % i, [128, 16, 4]) for i in range(2)]
        preshk = [Tk() for i in range(2)]
        presi = [0]
        lcs_in = [P.lane() for _ in range(2)]
        lcs_out = P.lane()

        def conv_chunk(ch, T, dst_ap, dst_k, col0):
            pst = project(col0, T)
            p = pre[prei[0] % NPRE]
            ph = prehk[prei[0] % NPRE]
            prei[0] += 1
            op("act", lambda e: e.activation(out=p.t[:, 3:3 + T], in_=pst.t[:, :T], func=AF.Copy), r=[pst.k], w=[p.k])
            op("pool", lambda e: e.tensor_copy(p.t[:, 0:3], halo.t[:, ch, :]), r=[halok[ch]], w=[ph])
            a = acc_t[acci[0] % 2]
            acci[0] += 1
            op("dve", lambda e: e.tensor_scalar(a.t[:, :T], p.t[:, 0:T], cw.t[:, ch, 0:1], cb.t[:, ch:ch + 1], ALU.mult, ALU.add),
               r=[p.k, ph, cw.k, cb.k], w=[a.k])
            for i in range(1, 4):
                op("dve", lambda e, i=i: e.scalar_tensor_tensor(a.t[:, :T], p.t[:, i:i + T], cw.t[:, ch, i:i + 1], a.t[:, :T],
                                                                ALU.mult, ALU.add), r=[p.k, ph, a.k], w=[a.k])
            op("pool", lambda e: e.tensor_copy(halo.t[:, ch, :], p.t[:, T:T + 3]), r=[p.k, ph], w=[halok[ch]])
            op("act", lambda e: e.activation(out=dst_ap, in_=a.t[:, :T], func=AF.Silu), r=[a.k], w=[dst_k])

        def conv_chunk_s(ch, dst_ap, dst_k, col0):
            pst = project(col0, 16)
            j = presi[0] % 2
            presi[0] += 1
            p, ph = pres[j], preshk[j]
            dma("sp", lcs_in[j], lambda e: e.dma_start(out=p.t[:, :, 0:3], in_=D["st_conv"][:, ch, :, :]), w=[ph])
            op("act", lambda e: e.activation(out=p.t[:, :, 3], in_=pst.t[:, :16], func=AF.Copy), r=[pst.k], w=[p.k])
            a = acc_t[acci[0] % 2]
            acci[0] += 1
            op("dve", lambda e: e.tensor_scalar(a.t[:, :16], p.t[:, :, 0], cw.t[:, ch, 0:1], cb.t[:, ch:ch + 1], ALU.mult, ALU.add),
               r=[p.k, ph, cw.k, cb.k], w=[a.k])
            for i in range(1, 4):
                op("dve", lambda e, i=i: e.scalar_tensor_tensor(a.t[:, :16], p.t[:, :, i], cw.t[:, ch, i:i + 1], a.t[:, :16],
                                                                ALU.mult, ALU.add), r=[p.k, ph, a.k], w=[a.k])
            dma("sp", lcs_out, lambda e: e.dma_start(out=D["conv_s"][:, ch, :, :], in_=p.t[:, :, 1:4]), r=[p.k, ph])
            op("act", lambda e: e.activation(out=dst_ap, in_=a.t[:, :16], func=AF.Silu), r=[a.k], w=[dst_k])

        qT = sb("qT", [128, 512])
        kT = sb("kT", [128, 512])
        vT = sb("vT", [128, 512])
        sgT = sb("sgT", [128, 512])
        sq = self.nsq
        rs = self.nrs
        oT = sb("oT", [128, 16, 512], BF16)
        oTk = [Tk("oT%d" % j) for j in range(16)]
        auxT = sb("auxT", [32, 512])
        aux = [sb("aux%d" % cc, [128, 32]) for cc in range(4)]
        beta = [sb("beta%d" % cc, [128, 8]) for cc in range(4)]
        gg = [sb("gg%d" % cc, [128, 24]) for cc in range(4)]
        dtt = [sb("dt%d" % cc, [128, 16]) for cc in range(4)]
        gc = [sb("gc%d" % cc, [128, 24]) for cc in range(4)]
        ngc = [sb("ngc%d" % cc, [128, 24]) for cc in range(4)]
        egc = [sb("egc%d" % cc, [128, 24]) for cc in range(4)]
        elast = [sb("elast%d" % cc, [128, 24]) for cc in range(4)]
        edl = [sb("edl%d" % cc, [128, 24]) for cc in range(4)]
        begc = [sb("begc%d" % cc, [128, 8]) for cc in range(4)]
        dte = [sb("dte%d" % cc, [128, 16]) for cc in range(4)]
        sptmp = sb("sptmp", [128, 32])

        def aux_compute(cc, src_ap, src_k, sample=False):
            pa = banks[7]
            op("pe", lambda e: e.matmul(pa.t[:, 0:32], src_ap, ident[:32, :32], start=True, stop=True), r=[src_k, CK], w=[pa.k])
            op("dve", lambda e: e.tensor_copy(aux[cc].t[:], pa.t[:, 0:32]), r=[pa.k], w=[aux[cc].k])
            op("act", lambda e: e.activation(out=beta[cc].t[:], in_=aux[cc].t[:, 0:8], func=AF.Sigmoid), r=[aux[cc].k], w=[beta[cc].k])
            op("dve", lambda e: e.tensor_tensor(sptmp.t[:, 8:32], aux[cc].t[:, 8:32], auxp.t[:, 0, 8:32], ALU.add),
               r=[aux[cc].k, auxp.k], w=[sptmp.k])
            op("act", lambda e: e.activation(out=sptmp.t[:, 8:32], in_=sptmp.t[:, 8:32], func=AF.Exp), r=[sptmp.k], w=[sptmp.k])
            op("act", lambda e: e.activation(out=sptmp.t[:, 8:32], in_=sptmp.t[:, 8:32], func=AF.Ln, bias=1.0), r=[sptmp.k], w=[sptmp.k])
            op("dve", lambda e: e.tensor_copy(dtt[cc].t[:], sptmp.t[:, 16:32]), r=[sptmp.k], w=[dtt[cc].k])
            op("dve", lambda e: e.tensor_tensor(gg[cc].t[:], sptmp.t[:, 8:32], auxp.t[:, 1, 8:32], ALU.mult),
               r=[sptmp.k, auxp.k], w=[gg[cc].k])
            if sample:
                op("dve", lambda e: e.tensor_scalar(gg[cc].t[:], gg[cc].t[:], ident[:, 0:1], None, ALU.mult), r=[gg[cc].k, CK], w=[gg[cc].k])
            op("pe", lambda e: e.matmul(pa.t[:, 32:56], triU, gg[cc].t[:], start=True, stop=True), r=[gg[cc].k, CK], w=[pa.k])
            op("pe", lambda e: e.matmul(pa.t[:, 64:88], ones, gg[cc].t[:], start=True, stop=True), r=[gg[cc].k, CK], w=[pa.k])
            op("dve", lambda e: e.tensor_copy(gc[cc].t[:], pa.t[:, 32:56]), r=[pa.k], w=[gc[cc].k])
            op("dve", lambda e: e.tensor_scalar(ngc[cc].t[:], pa.t[:, 32:56], -1.0, None, ALU.mult), r=[pa.k], w=[ngc[cc].k])
            op("act", lambda e: e.activation(out=egc[cc].t[:], in_=pa.t[:, 32:56], func=AF.Exp), r=[pa.k], w=[egc[cc].k])
            op("act", lambda e: e.activation(out=elast[cc].t[:], in_=pa.t[:, 64:88], func=AF.Exp), r=[pa.k], w=[elast[cc].k])
            op("dve", lambda e: e.tensor_tensor(edl[cc].t[:], pa.t[:, 64:88], gc[cc].t[:], ALU.subtract), r=[pa.k, gc[cc].k], w=[edl[cc].k])
            op("act", lambda e: e.activation(out=edl[cc].t[:], in_=edl[cc].t[:], func=AF.Exp), r=[edl[cc].k], w=[edl[cc].k])
            op("dve", lambda e: e.tensor_tensor(begc[cc].t[:], beta[cc].t[:], egc[cc].t[:, 0:8], ALU.mult),
               r=[beta[cc].k, egc[cc].k], w=[begc[cc].k])
            op("dve", lambda e: e.tensor_tensor(dte[cc].t[:], dtt[cc].t[:], edl[cc].t[:, 8:24], ALU.mult),
               r=[dtt[cc].k, edl[cc].k], w=[dte[cc].k])

        def l2norm(buf, T, scale):
            pst = banks[7]
            op("act", lambda e: e.activation(out=sq.t[:, :T], in_=buf.t[:, :T], func=AF.Square), r=[buf.k], w=[sq.k])
            op("pe", lambda e: e.matmul(pst.t[:, :T], ones, sq.t[:, :T], start=True, stop=True), r=[sq.k, CK], w=[pst.k])
            op("act", lambda e: e.activation(out=rs.t[:, :T], in_=pst.t[:, :T], func=AF.Sqrt, bias=EPS, scale=1.0), r=[pst.k], w=[rs.k])
            op("dve", lambda e: e.reciprocal(rs.t[:, :T], rs.t[:, :T]), r=[rs.k], w=[rs.k])
            op("dve", lambda e: e.scalar_tensor_tensor(buf.t[:, :T], buf.t[:, :T], scale, rs.t[:, :T], ALU.mult, ALU.mult),
               r=[buf.k, rs.k], w=[buf.k])

        class S:
            pass
        s = S()
        for nm in ("tmp", "dec", "decT", "A", "Bm", "A2", "B2", "Pm", "P2", "kbg", "kd", "vb", "nwT", "u", "qk", "t1", "o", "on",
                   "GT", "Btok", "MT", "Bd"):
            setattr(s, nm, sb("u_%s" % nm, [128, 128]))
        s.ss = sb("u_ss", [128, 2])
        s.ps = [banks[2], banks[3]]
        s.pi = 0
        s.xtok = sb("u_xtok", [128, 512])
        s.t512 = sb("u_t512", [128, 512])
        s.t512b = sb("u_t512b", [128, 512])
        s.otok = sb("u_otok", [128, 512])
        fring = [0]

        def fullbank():
            b = banks[4 + fring[0] % 3]
            fring[0] += 1
            return b

        def pst_of(s):
            j = s.pi % 8
            s.pi += 1
            b = s.ps[j % 2]
            r = j // 2
            return b.t[:, r * 128:(r + 1) * 128], b.k

        def gdn_unit(h, cc, q_ap, k_ap, v_ap, sg_ap, rk, S_ap, S_k, out_ap, out_k, single=False):
            g_b = gg[cc].t[:, h:h + 1].broadcast_to([128, 128])
            p2, k2 = pst_of(s)
            op("pe", lambda e: e.matmul(p2, ident, maskB, start=True, stop=False), r=[CK], w=[k2])
            op("pe", lambda e: e.matmul(p2, g_b, triU, start=False, stop=True), r=[gg[cc].k, CK], w=[k2])
            op("act", lambda e: e.activation(out=s.decT.t[:], in_=p2, func=AF.Exp, bias=ngc[cc].t[:, h:h + 1], scale=1.0),
               r=[k2, ngc[cc].k], w=[s.decT.k])
            if not single:
                p1, k1 = pst_of(s)
                op("pe", lambda e: e.matmul(p1, ident, maskA, start=True, stop=False), r=[CK], w=[k1])
                op("pe", lambda e: e.matmul(p1, g_b, triU, start=False, stop=True), r=[gg[cc].k, CK], w=[k1])
                op("act", lambda e: e.activation(out=s.dec.t[:], in_=p1, func=AF.Exp, bias=gc[cc].t[:, h:h + 1], scale=-1.0),
                   r=[k1, gc[cc].k], w=[s.dec.k])
                p3, k3 = pst_of(s)
                op("pe", lambda e: e.matmul(p3, k_ap, k_ap, start=True, stop=True), r=rk, w=[k3])
                op("dve", lambda e: e.scalar_tensor_tensor(s.A.t[:], p3, beta[cc].t[:, h:h + 1], s.dec.t[:], ALU.mult, ALU.mult),
                   r=[k3, beta[cc].k, s.dec.k], w=[s.A.k])
                p4, k4 = pst_of(s)
                op("pe", lambda e: e.matmul(p4, s.A.t[:], ident, start=True, stop=True), r=[s.A.k, CK], w=[k4])
                op("act", lambda e: e.activation(out=s.Bm.t[:], in_=p4, func=AF.Copy), r=[k4], w=[s.Bm.k])
                op("dve", lambda e: e.tensor_tensor(s.Pm.t[:], ident, s.Bm.t[:], ALU.subtract), r=[s.Bm.k, CK], w=[s.Pm.k])
                Ac, Bc, An, Bn = s.A, s.Bm, s.A2, s.B2
                Pc, Pn = s.Pm, s.P2
                for lvl in range(6):
                    pa_, ka = pst_of(s)
                    op("pe", lambda e, Ac=Ac, Bc=Bc, pa_=pa_: e.matmul(pa_, Bc.t[:], Ac.t[:], start=True, stop=True),
                       r=[Ac.k, Bc.k], w=[ka])
                    op("act", lambda e, An=An, pa_=pa_: e.activation(out=An.t[:], in_=pa_, func=AF.Copy), r=[ka], w=[An.k])
                    if lvl < 5:
                        pb, kb = pst_of(s)
                        op("pe", lambda e, Ac=Ac, Bc=Bc, pb=pb: e.matmul(pb, Ac.t[:], Bc.t[:], start=True, stop=True),
                           r=[Ac.k, Bc.k], w=[kb])
                        op("dve", lambda e, Bn=Bn, pb=pb: e.tensor_copy(Bn.t[:], pb), r=[kb], w=[Bn.k])
                    pc, kc = pst_of(s)
                    op("pe", lambda e, Pc=Pc, pc=pc: e.matmul(pc, ident, Pc.t[:], start=True, stop=False), r=[Pc.k, CK], w=[kc])
                    op("pe", lambda e, Pc=Pc, An=An, pc=pc: e.matmul(pc, An.t[:], Pc.t[:], start=False, stop=True),
                       r=[Pc.k, An.k], w=[kc])
                    op("dve", lambda e, Pn=Pn, pc=pc: e.tensor_copy(Pn.t[:], pc), r=[kc], w=[Pn.k])
                    Ac, An = An, Ac
                    Bc, Bn = Bn, Bc
                    Pc, Pn = Pn, Pc
                TT_ap, TT_k = Pc.t[:], Pc.k
            else:
                TT_ap, TT_k = ident, CK
            p5, k5 = pst_of(s)
            op("pe", lambda e: e.matmul(p5, k_ap, ident, start=True, stop=True), r=rk + [CK], w=[k5])
            op("dve", lambda e: e.tensor_scalar(s.kbg.t[:], p5, begc[cc].t[:, h:h + 1], None, ALU.mult), r=[k5, begc[cc].k], w=[s.kbg.k])
            op("dve", lambda e: e.tensor_scalar(s.kd.t[:], p5, edl[cc].t[:, h:h + 1], None, ALU.mult), r=[k5, edl[cc].k], w=[s.kd.k])
            p6, k6 = pst_of(s)
            op("pe", lambda e: e.matmul(p6, v_ap, ident, start=True, stop=True), r=rk + [CK], w=[k6])
            op("dve", lambda e: e.tensor_scalar(s.vb.t[:], p6, beta[cc].t[:, h:h + 1], None, ALU.mult), r=[k6, beta[cc].k], w=[s.vb.k])
            p7, k7 = pst_of(s)
            op("pe", lambda e: e.matmul(p7, s.kbg.t[:], TT_ap, start=True, stop=True), r=[s.kbg.k, TT_k], w=[k7])
            op("act", lambda e: e.activation(out=s.nwT.t[:], in_=p7, func=AF.Identity, scale=-1.0), r=[k7], w=[s.nwT.k])
            p8, k8 = pst_of(s)
            op("pe", lambda e: e.matmul(p8, TT_ap, s.vb.t[:], start=True, stop=False), r=[TT_k, s.vb.k], w=[k8])
            op("pe", lambda e: e.matmul(p8, s.nwT.t[:], S_ap, start=False, stop=True), r=[s.nwT.k, S_k], w=[k8])
            op("act", lambda e: e.activation(out=s.u.t[:], in_=p8, func=AF.Copy), r=[k8], w=[s.u.k])
            p9, k9 = pst_of(s)
            op("pe", lambda e: e.matmul(p9, k_ap, q_ap, start=True, stop=True), r=rk, w=[k9])
            op("dve", lambda e: e.tensor_tensor(s.qk.t[:], p9, s.decT.t[:], ALU.mult), r=[k9, s.decT.k], w=[s.qk.k])
            p10, k10 = pst_of(s)
            op("pe", lambda e: e.matmul(p10, q_ap, S_ap, start=True, stop=True), r=rk + [S_k], w=[k10])
            op("dve", lambda e: e.tensor_scalar(s.t1.t[:], p10, egc[cc].t[:, h:h + 1], None, ALU.mult), r=[k10, egc[cc].k], w=[s.t1.k])
            p11, k11 = pst_of(s)
            op("pe", lambda e: e.matmul(p11, s.qk.t[:], s.u.t[:], start=True, stop=True), r=[s.qk.k, s.u.k], w=[k11])
            op("dve", lambda e: e.tensor_tensor(s.o.t[:], p11, s.t1.t[:], ALU.add), r=[k11, s.t1.k], w=[s.o.k])
            p12, k12 = pst_of(s)
            op("pe", lambda e: e.matmul(p12, s.kd.t[:], s.u.t[:], start=True, stop=True), r=[s.kd.k, s.u.k], w=[k12])
            op("dve", lambda e: e.tensor_scalar(s.tmp.t[:], S_ap, elast[cc].t[:, h:h + 1], None, ALU.mult),
               r=[elast[cc].k, S_k], w=[s.tmp.k])
            op("dve", lambda e: e.tensor_tensor(S_ap, p12, s.tmp.t[:], ALU.add), r=[k12, s.tmp.k], w=[S_k])
            op("act", lambda e: e.activation(out=s.on.t[:], in_=s.o.t[:], func=AF.Square), r=[s.o.k], w=[s.on.k])
            op("dve", lambda e: e.reduce_sum(s.ss.t[:, 0:1], s.on.t[:], AX.X), r=[s.on.k], w=[s.ss.k])
            op("act", lambda e: e.activation(out=s.ss.t[:, 1:2], in_=s.ss.t[:, 0:1], func=AF.Sqrt, bias=EPS, scale=1.0 / 128), r=[s.ss.k], w=[s.ss.k])
            op("dve", lambda e: e.reciprocal(s.ss.t[:, 1:2], s.ss.t[:, 1:2]), r=[s.ss.k], w=[s.ss.k])
            op("dve", lambda e: e.tensor_scalar(s.on.t[:], s.o.t[:], s.ss.t[:, 1:2], None, ALU.mult), r=[s.o.k, s.ss.k], w=[s.on.k])
            p13, k13 = pst_of(s)
            op("pe", lambda e: e.matmul(p13, s.on.t[:], ident, start=True, stop=True), r=[s.on.k, CK], w=[k13])
            op("dve", lambda e: e.scalar_tensor_tensor(out_ap, p13, gng.t[:, 0:1], sg_ap, ALU.mult, ALU.mult),
               r=[k13, gng.k] + rk, w=[out_k])


        def ssd_unit(grp, cc, B_ap, C_ap, xs_aps, sz_aps, rk, S_ap, S_k, y_aps, y_ks):
            h0 = 8 * grp
            pg, kg_ = pst_of(s)
            op("pe", lambda e: e.matmul(pg, B_ap, C_ap, start=True, stop=True), r=rk, w=[kg_])
            op("act", lambda e: e.activation(out=s.GT.t[:], in_=pg, func=AF.Copy), r=[kg_], w=[s.GT.k])
            px = fullbank()
            for j in range(4):
                op("pe", lambda e, j=j: e.matmul(px.t[:, j * 128:(j + 1) * 128], xs_aps[j], ident, start=True, stop=True),
                   r=rk + [CK], w=[px.k])
            op("act", lambda e: e.activation(out=s.xtok.t[:], in_=px.t[:], func=AF.Copy), r=[px.k], w=[s.xtok.k])
            pb_, kb_ = pst_of(s)
            op("pe", lambda e: e.matmul(pb_, B_ap, ident, start=True, stop=True), r=rk + [CK], w=[kb_])
            op("dve", lambda e: e.tensor_copy(s.Btok.t[:], pb_), r=[kb_], w=[s.Btok.k])
            pcs = fullbank()
            op("pe", lambda e: e.matmul(pcs.t[:], C_ap, S_ap.rearrange("p h q -> p (h q)"), start=True, stop=True), r=rk + [S_k], w=[pcs.k])
            op("dve", lambda e: e.tensor_tensor(s.t512.t[:].rearrange("p (h q) -> p h q", q=64),
                                                pcs.t[:].rearrange("p (h q) -> p h q", q=64),
                                                egc[cc].t[:, 8 + h0:16 + h0].unsqueeze(2).broadcast_to([128, 8, 64]), ALU.mult),
               r=[pcs.k, egc[cc].k], w=[s.t512.k])
            po = fullbank()
            pS = fullbank()
            for hh in range(8):
                col = 8 + h0 + hh
                g_b = gg[cc].t[:, col:col + 1].broadcast_to([128, 128])
                pd, kd_ = pst_of(s)
                op("pe", lambda e: e.matmul(pd, ident, maskB, start=True, stop=False), r=[CK], w=[kd_])
                op("pe", lambda e: e.matmul(pd, g_b, triU, start=False, stop=True), r=[gg[cc].k, CK], w=[kd_])
                op("act", lambda e: e.activation(out=s.decT.t[:], in_=pd, func=AF.Exp, bias=ngc[cc].t[:, col:col + 1], scale=1.0),
                   r=[kd_, ngc[cc].k], w=[s.decT.k])
                op("dve", lambda e: e.scalar_tensor_tensor(s.MT.t[:], s.GT.t[:], dtt[cc].t[:, h0 + hh:h0 + hh + 1], s.decT.t[:],
                                                           ALU.mult, ALU.mult), r=[s.GT.k, dtt[cc].k, s.decT.k], w=[s.MT.k])
                op("pe", lambda e: e.matmul(po.t[:, hh * 64:(hh + 1) * 64], s.MT.t[:], s.xtok.t[:, hh * 64:(hh + 1) * 64],
                                            start=True, stop=True), r=[s.MT.k, s.xtok.k], w=[po.k])
                op("dve", lambda e: e.tensor_scalar(s.Bd.t[:], s.Btok.t[:], dte[cc].t[:, h0 + hh:h0 + hh + 1], None, ALU.mult),
                   r=[s.Btok.k, dte[cc].k], w=[s.Bd.k])
                op("pe", lambda e: e.matmul(pS.t[:, hh * 64:(hh + 1) * 64], s.Bd.t[:], s.xtok.t[:, hh * 64:(hh + 1) * 64],
                                            start=True, stop=True), r=[s.Bd.k, s.xtok.k], w=[pS.k])
            op("dve", lambda e: e.tensor_tensor(s.otok.t[:], po.t[:], s.t512.t[:], ALU.add), r=[po.k, s.t512.k], w=[s.otok.k])
            op("dve", lambda e: e.tensor_tensor(s.t512b.t[:].rearrange("p (h q) -> p h q", q=64),
                                                s.xtok.t[:].rearrange("p (h q) -> p h q", q=64),
                                                auxp.t[:, 2, 16 + h0:24 + h0].unsqueeze(2).broadcast_to([128, 8, 64]), ALU.mult),
               r=[s.xtok.k, auxp.k], w=[s.t512b.k])
            op("dve", lambda e: e.tensor_tensor(s.otok.t[:], s.otok.t[:], s.t512b.t[:], ALU.add), r=[s.otok.k, s.t512b.k], w=[s.otok.k])
            op("dve", lambda e: e.tensor_tensor(s.t512.t[:].rearrange("p (h q) -> p h q", q=64), S_ap,
                                                elast[cc].t[:, 8 + h0:16 + h0].unsqueeze(2).broadcast_to([128, 8, 64]), ALU.mult),
               r=[S_k, elast[cc].k, s.t512.k], w=[s.t512.k])
            op("dve", lambda e: e.tensor_tensor(S_ap, pS.t[:].rearrange("p (h q) -> p h q", q=64),
                                                s.t512.t[:].rearrange("p (h q) -> p h q", q=64), ALU.add),
               r=[pS.k, s.t512.k], w=[S_k])
            pt = fullbank()
            for j in range(4):
                op("pe", lambda e, j=j: e.matmul(pt.t[:, j * 128:(j + 1) * 128], s.otok.t[:, j * 128:(j + 1) * 128], ident,
                                                 start=True, stop=True), r=[s.otok.k, CK], w=[pt.k])
            for j in range(4):
                op("dve", lambda e, j=j: e.tensor_tensor(y_aps[j], pt.t[:, j * 128:(j + 1) * 128], sz_aps[j], ALU.mult),
                   r=[pt.k] + rk, w=[y_ks[j]])

        def ssd_norm_out(T):
            pst = banks[7]
            for j in range(8):
                op("act", lambda e, j=j: e.activation(out=sq.t[:, :T], in_=oT.t[:, 8 + j, :T], func=AF.Square), r=[oTk[8 + j]], w=[sq.k])
                op("pe", lambda e, j=j: e.matmul(pst.t[:, :T], ones, sq.t[:, :T], start=(j == 0), stop=(j == 7)), r=[sq.k, CK], w=[pst.k])
            op("act", lambda e: e.activation(out=rs.t[:, :T], in_=pst.t[:, :T], func=AF.Sqrt, bias=EPS, scale=1.0 / 1024), r=[pst.k], w=[rs.k])
            op("dve", lambda e: e.reciprocal(rs.t[:, :T], rs.t[:, :T]), r=[rs.k], w=[rs.k])
            for j in range(8):
                op("dve", lambda e, j=j: e.scalar_tensor_tensor(oT.t[:, 8 + j, :T], oT.t[:, 8 + j, :T], sng.t[:, j:j + 1], rs.t[:, :T],
                                                                ALU.mult, ALU.mult), r=[oTk[8 + j], sng.k, rs.k], w=[oTk[8 + j]])

        def out_proj(g, T, t0):
            for m in range(8):
                pst = proj_ps()
                for half in range(2):
                    buf = load_w(m * 128, w_out, row0=half * 1024)
                    for k in range(8):
                        kk = half * 8 + k
                        op("pe", lambda e, k=k, kk=kk: e.matmul(pst.t[:, :T], buf.t[:, k, :], oT.t[:, kk, :T],
                                                                start=(kk == 0), stop=(kk == 15)), r=[buf.k, oTk[kk]], w=[pst.k])
                if g < 4:
                    op("dve", lambda e, m=m: e.scalar_tensor_tensor(
                        xT.t[:, m, t0:t0 + T], pst.t[:, :T], modT.t[:, 0, 16 + m, 0:1], xT.t[:, m, t0:t0 + T], ALU.mult, ALU.add),
                        r=[pst.k, modT.k, xk[m][g]], w=[xk[m][g]])
                else:
                    op("dve", lambda e, m=m: e.tensor_tensor(sq.t[:, :T], pst.t[:, :T], modT.t[:, 0, 16 + m, 1:17], ALU.mult),
                       r=[pst.k, modT.k], w=[sq.k])
                    op("dve", lambda e, m=m: e.tensor_tensor(xT.t[:, m, t0:t0 + T], xT.t[:, m, t0:t0 + T], sq.t[:, :T], ALU.add),
                       r=[sq.k, xk[m][g]], w=[xk[m][g]])

        es_p = contextlib.ExitStack()
        sbp = lambda name, shape, dt=F32: self.sb(name, shape, dt, es=es_p)
        pre = [sbp("pre%d" % i, [128, 515]) for i in range(NPRE)]
        BT = [sbp("BT%d" % i, [128, 512]) for i in range(2)]
        CT = [sbp("CT%d" % i, [128, 512]) for i in range(2)]
        xsT = [sbp("xsT%d" % i, [128, 512]) for i in range(4)]
        szT = [sbp("szT%d" % i, [128, 512]) for i in range(4)]
        for g in range(min(KG, 4)):
            t0, T = self.groups[g]
            nch = T // 128
            self.norm_mod_layer(0, 0, g, hm)
            pst = project(6656, T, ncol=32)
            op("act", lambda e: e.activation(out=auxT.t[:, :T], in_=pst.t[:32, :T], func=AF.Copy), r=[pst.k], w=[auxT.k])
            for cc in range(nch):
                aux_compute(cc, auxT.t[:, cc * 128:(cc + 1) * 128], auxT.k)
            for h in range(KH):
                conv_chunk(h, T, qT.t[:, :T], qT.k, h * 128)
                conv_chunk(8 + h, T, kT.t[:, :T], kT.k, 1024 + h * 128)
                conv_chunk(16 + h, T, vT.t[:, :T], vT.k, 2048 + h * 128)
                pst = project(4608 + h * 128, T)
                op("act", lambda e: e.activation(out=sgT.t[:, :T], in_=pst.t[:, :T], func=AF.Silu), r=[pst.k], w=[sgT.k])
                l2norm(qT, T, 128.0 ** -0.5)
                l2norm(kT, T, 1.0)
                for cc in range(nch):
                    cs = slice(cc * 128, (cc + 1) * 128)
                    gdn_unit(h, cc, qT.t[:, cs], kT.t[:, cs], vT.t[:, cs], sgT.t[:, cs], [qT.k, kT.k, vT.k, sgT.k],
                             Sg.t[:, h, :], Sgk[h], oT.t[:, h, cs], oTk[h])
            for j in range(KH, 8):
                op("pool", lambda e, j=j: e.memset(oT.t[:, j, :T], 0.0), w=[oTk[j]])
            if KSS:
                for grp in range(2):
                    conv_chunk(32 + grp, T, BT[grp].t[:, :T], BT[grp].k, 4096 + grp * 128)
                    conv_chunk(34 + grp, T, CT[grp].t[:, :T], CT[grp].k, 4352 + grp * 128)
                    for j in range(4):
                        jj = 4 * grp + j
                        conv_chunk(24 + jj, T, xsT[j].t[:, :T], xsT[j].k, 3072 + jj * 128)
                        pst = project(5632 + jj * 128, T)
                        op("act", lambda e, j=j: e.activation(out=szT[j].t[:, :T], in_=pst.t[:, :T], func=AF.Silu), r=[pst.k], w=[szT[j].k])
                    rk = [BT[grp].k, CT[grp].k] + [xsT[j].k for j in range(4)] + [szT[j].k for j in range(4)]
                    for cc in range(nch):
                        cs = slice(cc * 128, (cc + 1) * 128)
                        ssd_unit(grp, cc, BT[grp].t[:, cs], CT[grp].t[:, cs], [xsT[j].t[:, cs] for j in range(4)],
                                 [szT[j].t[:, cs] for j in range(4)], rk, Ss.t[:, 8 * grp:8 * grp + 8, :], Ssk[grp],
                                 [oT.t[:, 8 + 4 * grp + j, cs] for j in range(4)], [oTk[8 + 4 * grp + j] for j in range(4)])
                ssd_norm_out(T)
            else:
                for j in range(8, 16):
                    op("pool", lambda e, j=j: e.memset(oT.t[:, j, :T], 0.0), w=[oTk[j]])
            out_proj(g, T, t0)

        op_l = P.lane()
        dma("sp", op_l, lambda e: e.dma_start(out=D["gdn_p"].rearrange("h k v -> k h v"), in_=Sg.t[:]), r=Sgk)
        dma("sp", op_l, lambda e: e.dma_start(out=D["ssm_p"].rearrange("h n p -> n h p"), in_=Ss.t[:]), r=Ssk, group=True)
        dma("sp", op_l, lambda e: e.dma_start(out=D["conv_p"][:, :, :], in_=halo.t[:]), r=halok, group=True)
        P.group_finish(op_l, Sgk + Ssk + halok)

        P.barrier()
        es_p.close()
        if KG >= 5:
            g, (t0, T) = 4, self.groups[4]
            self.norm_mod_layer(0, 0, 4, hm)
            pst = project(6656, T, ncol=32)
            op("act", lambda e: e.activation(out=auxT.t[:, :T], in_=pst.t[:32, :T], func=AF.Copy), r=[pst.k], w=[auxT.k])
            qs = sb("qs", [128, 8, 16])
            ks_ = sb("ks", [128, 8, 16])
            vs = sb("vs", [128, 8, 16])
            sgs = sb("sgs", [128, 8, 16])
            Bs = sb("Bs", [128, 2, 16])
            Cs = sb("Cs", [128, 2, 16])
            xss = sb("xss", [128, 8, 16])
            szs = sb("szs", [128, 8, 16])
            for h in range(8):
                conv_chunk_s(h, qT.t[:, :16], qT.k, h * 128)
                conv_chunk_s(8 + h, kT.t[:, :16], kT.k, 1024 + h * 128)
                conv_chunk_s(16 + h, vs.t[:, h, :], vs.k, 2048 + h * 128)
                pst = project(4608 + h * 128, 16)
                op("act", lambda e, h=h: e.activation(out=sgs.t[:, h, :], in_=pst.t[:, :16], func=AF.Silu), r=[pst.k], w=[sgs.k])
                l2norm(qT, 16, 128.0 ** -0.5)
                l2norm(kT, 16, 1.0)
                op("pool", lambda e, h=h: e.tensor_copy(qs.t[:, h, :], qT.t[:, :16]), r=[qT.k], w=[qs.k])
                op("pool", lambda e, h=h: e.tensor_copy(ks_.t[:, h, :], kT.t[:, :16]), r=[kT.k], w=[ks_.k])
            for grp in range(2):
                conv_chunk_s(32 + grp, Bs.t[:, grp, :], Bs.k, 4096 + grp * 128)
                conv_chunk_s(34 + grp, Cs.t[:, grp, :], Cs.k, 4352 + grp * 128)
            for jj in range(8):
                conv_chunk_s(24 + jj, xss.t[:, jj, :], xss.k, 3072 + jj * 128)
                pst = project(5632 + jj * 128, 16)
                op("act", lambda e, jj=jj: e.activation(out=szs.t[:, jj, :], in_=pst.t[:, :16], func=AF.Silu), r=[pst.k], w=[szs.k])
            pads = {}
            for nm in ["q", "k", "v", "sg", "B", "C"] + ["xs%d" % j for j in range(4)] + ["sz%d" % j for j in range(4)]:
                pads[nm] = sb("pad_" + nm, [128, 128])
                op("pool", lambda e, nm=nm: e.memset(pads[nm].t[:], 0.0), w=[pads[nm].k])
            auxpad = sb("auxpad", [32, 128])
            op("pool", lambda e: e.memset(auxpad.t[:], 0.0), w=[auxpad.k])
            otmp = sb("otmp", [128, 128])
            ytmp = sb("ytmp", [128, 4, 128])
            ytk = [Tk() for _ in range(4)]
            NSB = 2
            Sgs = [sb("Sgs%d" % i, [128, 128]) for i in range(NSB)]
            Sss = [sb("Sss%d" % i, [128, 8, 64]) for i in range(NSB)]
            lsi = [P.lane() for _ in range(NSB)]
            lso = [P.lane() for _ in range(NSB)]
            lsi2 = [P.lane() for _ in range(NSB)]
            lso2 = [P.lane() for _ in range(NSB)]
            cnt = 0
            for smp in range(KSMP):
                op("pool", lambda e, smp=smp: e.tensor_copy(auxpad.t[:, 0:1], auxT.t[:, smp:smp + 1]), r=[auxT.k], w=[auxpad.k])
                aux_compute(0, auxpad.t[:], auxpad.k, sample=True)
                for h in range(KH):
                    i = cnt % NSB
                    cnt += 1
                    St = Sgs[i]
                    dma("sp", lsi[i], lambda e: e.dma_start(out=St.t[:], in_=D["st_gdn"][smp, h, :, :]), w=[St.k])
                    for nm, src in (("q", qs), ("k", ks_), ("v", vs), ("sg", sgs)):
                        op("pool", lambda e, nm=nm, src=src: e.tensor_copy(pads[nm].t[:, 0:1], src.t[:, h, smp:smp + 1]),
                           r=[src.k], w=[pads[nm].k])
                    gdn_unit(h, 0, pads["q"].t[:], pads["k"].t[:], pads["v"].t[:], pads["sg"].t[:],
                             [pads["q"].k, pads["k"].k, pads["v"].k, pads["sg"].k], St.t[:], St.k, otmp.t[:], otmp.k, single=True)
                    dma("sp", lso[i], lambda e: e.dma_start(out=D["gdn_s"][smp, h, :, :], in_=St.t[:]), r=[St.k])
                    op("pool", lambda e, h=h, smp=smp: e.tensor_copy(oT.t[:, h, smp:smp + 1], otmp.t[:, 0:1]), r=[otmp.k], w=[oTk[h]])
                if KSS:
                    for grp in range(2):
                        i = cnt % NSB
                        cnt += 1
                        St = Sss[i]
                        dma("sp", lsi2[i], lambda e: e.dma_start(
                            out=St.t[:], in_=D["st_ssm"][smp, 8 * grp:8 * grp + 8, :, :].rearrange("h n p -> n h p")), w=[St.k])
                        op("pool", lambda e: e.tensor_copy(pads["B"].t[:, 0:1], Bs.t[:, grp, smp:smp + 1]), r=[Bs.k], w=[pads["B"].k])
                        op("pool", lambda e: e.tensor_copy(pads["C"].t[:, 0:1], Cs.t[:, grp, smp:smp + 1]), r=[Cs.k], w=[pads["C"].k])
                        for j in range(4):
                            op("pool", lambda e, j=j: e.tensor_copy(pads["xs%d" % j].t[:, 0:1], xss.t[:, 4 * grp + j, smp:smp + 1]),
                               r=[xss.k], w=[pads["xs%d" % j].k])
                            op("pool", lambda e, j=j: e.tensor_copy(pads["sz%d" % j].t[:, 0:1], szs.t[:, 4 * grp + j, smp:smp + 1]),
                               r=[szs.k], w=[pads["sz%d" % j].k])
                        rk = [pads[n].k for n in ["B", "C"] + ["xs%d" % j for j in range(4)] + ["sz%d" % j for j in range(4)]]
                        ssd_unit(grp, 0, pads["B"].t[:], pads["C"].t[:], [pads["xs%d" % j].t[:] for j in range(4)],
                                 [pads["sz%d" % j].t[:] for j in range(4)], rk, St.t[:], St.k,
                                 [ytmp.t[:, j, :] for j in range(4)], ytk)
                        dma("sp", lso2[i], lambda e: e.dma_start(
                            out=D["ssm_s"][smp, 8 * grp:8 * grp + 8, :, :].rearrange("h n p -> n h p"), in_=St.t[:]), r=[St.k])
                        for j in range(4):
                            op("pool", lambda e, j=j: e.tensor_copy(oT.t[:, 8 + 4 * grp + j, smp:smp + 1], ytmp.t[:, j, 0:1]),
                               r=[ytk[j]], w=[oTk[8 + 4 * grp + j]])
            for j in range(KH, 8):
                op("pool", lambda e, j=j: e.memset(oT.t[:, j, :T], 0.0), w=[oTk[j]])
            if KSS:
                ssd_norm_out(T)
            else:
                for j in range(8, 16):
                    op("pool", lambda e, j=j: e.memset(oT.t[:, j, :T], 0.0), w=[oTk[j]])
            out_proj(4, T, t0)
        P.barrier()
        es.close()


    def layer1(self, D):
        nc, P = self.nc, self.P
        op, dma = P.op, P.dma
        c = self.c
        ident, ones, CK = c["ident"], c["ones"], c["CK"]
        banks = self.banks
        xT, xk, modT = self.xT, self.xk, self.modT
        w_in, w_out = D["ret_w_in"], D["ret_w_out"]
        es = contextlib.ExitStack()
        sb = lambda name, shape, dt=F32: self.sb(name, shape, dt, es=es)
        KG = int(os.environ.get('KG', '5'))
        KSMP = int(os.environ.get('KSMP', '16'))
        gam = [1.0 - 2.0 ** (-5.0 - h) for h in range(4)]

        lp = P.lane()
        retc = sb("retc_sb", [128, 4, 132])
        rngb = sb("rngb", [128, 4, 512])
        dma("sp", lp, lambda e: e.dma_start(out=retc.t[:], in_=D["retc"][:, :, :]), w=[retc.k])
        dma("sp", lp, lambda e: e.dma_start(out=rngb.t[:], in_=D["rng"][0:1, :, :].broadcast_to([128, 4, 512])), w=[rngb.k], group=True)
        P.group_finish(lp, [retc.k, rngb.k])
        Sr = sb("Sr", [128, 4, 2, 512])
        Srk = [Tk("Sr%d" % h) for h in range(4)]
        op("pool", lambda e: e.memset(Sr.t[:], 0.0), w=Srk)

        hm = sb("hm1", [128, 8, 512], BF16)
        NWB = 4
        wbuf = [sb("w1b%d" % i, [128, 8, 128], BF16) for i in range(NWB)]
        wl = [P.lane() for _ in range(NWB)]
        wi = [0]

        def load_w(col0, src, row0=0):
            i = wi[0] % NWB
            wi[0] += 1
            buf = wbuf[i]
            dma("pool", wl[i], lambda e: e.dma_start(
                out=buf.t[:], in_=src[row0:row0 + 1024, col0:col0 + 128].rearrange("(k p) c -> p k c", p=128)), w=[buf.k])
            return buf

        pring = [0]

        def proj_ps():
            b = banks[pring[0] % 2]
            pring[0] += 1
            return b

        def project(col0, T):
            buf = load_w(col0, w_in)
            pst = proj_ps()
            for k in range(8):
                op("pe", lambda e, k=k: e.matmul(pst.t[:, :T], buf.t[:, k, :], hm.t[:, k, :T], start=(k == 0), stop=(k == 7)),
                   r=[buf.k, hm.k], w=[pst.k])
            return pst

        cosb = sb("cosb", [128, 512])
        sinb = sb("sinb", [128, 512])
        lcs = P.lane()
        raw = [sb("rraw%d" % i, [128, 512]) for i in range(2)]
        ta = sb("rta", [128, 512])
        tb = sb("rtb", [128, 512])
        qT = [sb("rq%d" % i, [128, 512]) for i in range(2)]
        kT = [sb("rk%d" % i, [128, 512]) for i in range(2)]
        vT = [sb("rv%d" % i, [128, 512]) for i in range(4)]
        sgT = [sb("rsg%d" % i, [128, 512]) for i in range(4)]
        oT = sb("oT1", [128, 16, 512], BF16)
        oTk = [Tk("o1T%d" % j) for j in range(16)]

        def rope(col0, T, dst, scale):
            for cidx in range(2):
                pst = project(col0 + cidx * 128, T)
                op("act", lambda e, cidx=cidx: e.activation(out=raw[cidx].t[:, :T], in_=pst.t[:, :T], func=AF.Identity, scale=scale),
                   r=[pst.k], w=[raw[cidx].k])
            x1, x2 = raw
            op("dve", lambda e: e.tensor_tensor(ta.t[:, :T], x1.t[:, :T], cosb.t[:, :T], ALU.mult), r=[x1.k, cosb.k], w=[ta.k])
            op("pool", lambda e: e.tensor_tensor(tb.t[:, :T], x2.t[:, :T], sinb.t[:, :T], ALU.mult), r=[x2.k, sinb.k], w=[tb.k])
            op("dve", lambda e: e.tensor_tensor(dst[0].t[:, :T], ta.t[:, :T], tb.t[:, :T], ALU.subtract), r=[ta.k, tb.k], w=[dst[0].k])
            op("dve", lambda e: e.tensor_tensor(ta.t[:, :T], x1.t[:, :T], sinb.t[:, :T], ALU.mult), r=[x1.k, sinb.k], w=[ta.k])
            op("pool", lambda e: e.tensor_tensor(tb.t[:, :T], x2.t[:, :T], cosb.t[:, :T], ALU.mult), r=[x2.k, cosb.k], w=[tb.k])
            op("dve", lambda e: e.tensor_tensor(dst[1].t[:, :T], ta.t[:, :T], tb.t[:, :T], ALU.add), r=[ta.k, tb.k], w=[dst[1].k])

        class S:
            pass
        s = S()
        s.qk = sb("r_qk", [128, 128])
        s.ss = sb("r_ss", [128, 4])
        s.ps = [banks[2], banks[3]]
        s.pi = 0
        s.xtok = sb("r_xtok", [128, 512])
        s.kd = sb("r_kd", [128, 256])
        s.t512b = sb("r_t512b", [128, 512])
        s.otok = sb("r_otok", [128, 512])
        fring = [0]

        def fullbank():
            b = banks[4 + fring[0] % 3]
            fring[0] += 1
            return b

        def pst_of():
            j = s.pi % 8
            s.pi += 1
            b = s.ps[j % 2]
            r = j // 2
            return b.t[:, r * 128:(r + 1) * 128], b.k

        def ret_unit(h, q_aps, k_aps, v_aps, sg_aps, rk, S_ap, S_k, out_aps, out_ks, eg_col, ed_col, elast_f):
            pq, kq = pst_of()
            op("pe", lambda e: e.matmul(pq, k_aps[0], q_aps[0], start=True, stop=False), r=rk, w=[kq])
            op("pe", lambda e: e.matmul(pq, k_aps[1], q_aps[1], start=False, stop=True), r=rk, w=[kq])
            op("dve", lambda e: e.tensor_tensor(s.qk.t[:], pq, retc.t[:, h, 0:128], ALU.mult), r=[kq, retc.k], w=[s.qk.k])
            px = fullbank()
            for cidx in range(4):
                op("pe", lambda e, cidx=cidx: e.matmul(px.t[:, cidx * 128:(cidx + 1) * 128], v_aps[cidx], ident, start=True, stop=True),
                   r=rk + [CK], w=[px.k])
            op("act", lambda e: e.activation(out=s.xtok.t[:], in_=px.t[:], func=AF.Copy), r=[px.k], w=[s.xtok.k])
            pk = fullbank()
            for cidx in range(2):
                op("pe", lambda e, cidx=cidx: e.matmul(pk.t[:, cidx * 128:(cidx + 1) * 128], k_aps[cidx], ident, start=True, stop=True),
                   r=rk + [CK], w=[pk.k])
            op("dve", lambda e: e.tensor_scalar(s.kd.t[:], pk.t[:, 0:256], ed_col, None, ALU.mult), r=[pk.k, retc.k], w=[s.kd.k])
            po = fullbank()
            op("pe", lambda e: e.matmul(po.t[:], s.qk.t[:], s.xtok.t[:], start=True, stop=True), r=[s.qk.k, s.xtok.k], w=[po.k])
            pqs = fullbank()
            op("pe", lambda e: e.matmul(pqs.t[:], q_aps[0], S_ap[:, 0, :], start=True, stop=False), r=rk + [S_k], w=[pqs.k])
            op("pe", lambda e: e.matmul(pqs.t[:], q_aps[1], S_ap[:, 1, :], start=False, stop=True), r=rk + [S_k], w=[pqs.k])
            op("dve", lambda e: e.tensor_scalar(s.t512b.t[:], pqs.t[:], eg_col, None, ALU.mult), r=[pqs.k, retc.k], w=[s.t512b.k])
            op("dve", lambda e: e.tensor_tensor(s.otok.t[:], po.t[:], s.t512b.t[:], ALU.add), r=[po.k, s.t512b.k], w=[s.otok.k])
            op("dve", lambda e: e.reduce_sum(s.ss.t[:, 0:1], s.otok.t[:], AX.X), r=[s.otok.k], w=[s.ss.k])
            op("dve", lambda e: e.tensor_scalar(s.ss.t[:, 0:1], s.ss.t[:, 0:1], -1.0 / 512, None, ALU.mult), r=[s.ss.k], w=[s.ss.k])
            op("dve", lambda e: e.tensor_scalar(s.otok.t[:], s.otok.t[:], s.ss.t[:, 0:1], None, ALU.add), r=[s.otok.k, s.ss.k], w=[s.otok.k])
            op("act", lambda e: e.activation(out=s.t512b.t[:], in_=s.otok.t[:], func=AF.Square), r=[s.otok.k], w=[s.t512b.k])
            op("dve", lambda e: e.reduce_sum(s.ss.t[:, 1:2], s.t512b.t[:], AX.X), r=[s.t512b.k], w=[s.ss.k])
            op("act", lambda e: e.activation(out=s.ss.t[:, 2:3], in_=s.ss.t[:, 1:2], func=AF.Sqrt, bias=EPS, scale=1.0 / 512), r=[s.ss.k], w=[s.ss.k])
            op("dve", lambda e: e.reciprocal(s.ss.t[:, 2:3], s.ss.t[:, 2:3]), r=[s.ss.k], w=[s.ss.k])
            op("dve", lambda e: e.scalar_tensor_tensor(s.otok.t[:], s.otok.t[:], s.ss.t[:, 2:3], rngb.t[:, h, :], ALU.mult, ALU.mult),
               r=[s.otok.k, s.ss.k, rngb.k], w=[s.otok.k])
            pt = fullbank()
            for cidx in range(4):
                op("pe", lambda e, cidx=cidx: e.matmul(pt.t[:, cidx * 128:(cidx + 1) * 128], s.otok.t[:, cidx * 128:(cidx + 1) * 128], ident,
                                                       start=True, stop=True), r=[s.otok.k, CK], w=[pt.k])
            for cidx in range(4):
                op("dve", lambda e, cidx=cidx: e.tensor_tensor(out_aps[cidx], pt.t[:, cidx * 128:(cidx + 1) * 128], sg_aps[cidx], ALU.mult),
                   r=[pt.k] + rk, w=[out_ks[cidx]])
            for cidx in range(2):
                pS = fullbank()
                op("pe", lambda e, cidx=cidx: e.matmul(pS.t[:], s.kd.t[:, cidx * 128:(cidx + 1) * 128], s.xtok.t[:], start=True, stop=True),
                   r=[s.kd.k, s.xtok.k], w=[pS.k])
                op("dve", lambda e, cidx=cidx: e.tensor_scalar(s.t512b.t[:], S_ap[:, cidx, :], elast_f, None, ALU.mult), r=[S_k, s.t512b.k], w=[s.t512b.k])
                op("dve", lambda e, cidx=cidx: e.tensor_tensor(S_ap[:, cidx, :], pS.t[:], s.t512b.t[:], ALU.add), r=[pS.k, s.t512b.k], w=[S_k])

        def out_proj(g, T, t0):
            sq = self.nsq
            for m in range(8):
                pst = proj_ps()
                for half in range(2):
                    buf = load_w(m * 128, w_out, row0=half * 1024)
                    for k in range(8):
                        kk = half * 8 + k
                        op("pe", lambda e, k=k, kk=kk: e.matmul(pst.t[:, :T], buf.t[:, k, :], oT.t[:, kk, :T],
                                                                start=(kk == 0), stop=(kk == 15)), r=[buf.k, oTk[kk]], w=[pst.k])
                if g < 4:
                    op("dve", lambda e, m=m: e.scalar_tensor_tensor(
                        xT.t[:, m, t0:t0 + T], pst.t[:, :T], modT.t[:, 1, 16 + m, 0:1], xT.t[:, m, t0:t0 + T], ALU.mult, ALU.add),
                        r=[pst.k, modT.k, xk[m][g]], w=[xk[m][g]])
                else:
                    op("dve", lambda e, m=m: e.tensor_tensor(sq.t[:, :T], pst.t[:, :T], modT.t[:, 1, 16 + m, 1:17], ALU.mult),
                       r=[pst.k, modT.k], w=[sq.k])
                    op("dve", lambda e, m=m: e.tensor_tensor(xT.t[:, m, t0:t0 + T], xT.t[:, m, t0:t0 + T], sq.t[:, :T], ALU.add),
                       r=[sq.k, xk[m][g]], w=[xk[m][g]])

        def head_inputs(h, T):
            rope(h * 256, T, qT, 1.0)
            rope(1024 + h * 256, T, kT, 1.0 / 16)
            for cidx in range(4):
                pst = project(2048 + h * 512 + cidx * 128, T)
                op("act", lambda e, cidx=cidx: e.activation(out=vT[cidx].t[:, :T], in_=pst.t[:, :T], func=AF.Copy), r=[pst.k], w=[vT[cidx].k])
                pst = project(4096 + h * 512 + cidx * 128, T)
                op("act", lambda e, cidx=cidx: e.activation(out=sgT[cidx].t[:, :T], in_=pst.t[:, :T], func=AF.Silu), r=[pst.k], w=[sgT[cidx].k])

        allk = [t.k for t in qT + kT + vT + sgT]
        for g in range(min(KG, 4)):
            t0, T = self.groups[g]
            self.norm_mod_layer(1, 0, g, hm)
            dma("sp", lcs, lambda e: e.dma_start(out=cosb.t[:, :T], in_=D["cosT"][:, t0:t0 + T]), w=[cosb.k])
            dma("sp", lcs, lambda e: e.dma_start(out=sinb.t[:, :T], in_=D["sinT"][:, t0:t0 + T]), w=[sinb.k])
            for h in range(4):
                head_inputs(h, T)
                for cc in range(T // 128):
                    cs = slice(cc * 128, (cc + 1) * 128)
                    ret_unit(h, [qT[i].t[:, cs] for i in range(2)], [kT[i].t[:, cs] for i in range(2)],
                             [vT[i].t[:, cs] for i in range(4)], [sgT[i].t[:, cs] for i in range(4)], allk,
                             Sr.t[:, h, :, :], Srk[h], [oT.t[:, 4 * h + i, cs] for i in range(4)], [oTk[4 * h + i] for i in range(4)],
                             retc.t[:, h, 128:129], retc.t[:, h, 129:130], gam[h] ** 128)
            out_proj(g, T, t0)
        op_l = P.lane()
        dma("sp", op_l, lambda e: e.dma_start(out=D["ret_p"].rearrange("h (c p) e -> p h c e", p=128), in_=Sr.t[:]), r=Srk)

        if KG >= 5:
            g, (t0, T) = 4, self.groups[4]
            self.norm_mod_layer(1, 0, 4, hm)
            dma("sp", lcs, lambda e: e.dma_start(out=cosb.t[:, :T], in_=D["cosT"][:, t0:t0 + T]), w=[cosb.k])
            dma("sp", lcs, lambda e: e.dma_start(out=sinb.t[:, :T], in_=D["sinT"][:, t0:t0 + T]), w=[sinb.k])
            pads = [sb("rpad%d" % i, [128, 128]) for i in range(12)]
            for p_ in pads:
                op("pool", lambda e, p_=p_: e.memset(p_.t[:], 0.0), w=[p_.k])
            otmp = sb("rotmp", [128, 4, 128])
            otk = [Tk() for _ in range(4)]
            NSB = 2
            Sts = [sb("Srs%d" % i, [128, 2, 512]) for i in range(NSB)]
            lsi = [P.lane() for _ in range(NSB)]
            lso = [P.lane() for _ in range(NSB)]
            cnt = 0
            srcs = qT + kT + vT + sgT
            for h in range(4):
                head_inputs(h, T)
                for smp in range(KSMP):
                    i = cnt % NSB
                    cnt += 1
                    St = Sts[i]
                    dma("sp", lsi[i], lambda e: e.dma_start(out=St.t[:], in_=D["st_ret"][smp, h, :, :].rearrange("(c p) e -> p c e", p=128)), w=[St.k])
                    for j in range(12):
                        op("pool", lambda e, j=j: e.tensor_copy(pads[j].t[:, 0:1], srcs[j].t[:, smp:smp + 1]), r=[srcs[j].k], w=[pads[j].k])
                    ret_unit(h, [pads[0].t[:], pads[1].t[:]], [pads[2].t[:], pads[3].t[:]], [pads[4 + i_].t[:] for i_ in range(4)],
                             [pads[8 + i_].t[:] for i_ in range(4)], [p_.k for p_ in pads], St.t[:], St.k,
                             [otmp.t[:, i_, :] for i_ in range(4)], otk, retc.t[:, h, 130:131], retc.t[:, h, 131:132], gam[h])
                    dma("sp", lso[i], lambda e: e.dma_start(out=D["ret_s"][smp, h, :, :].rearrange("(c p) e -> p c e", p=128), in_=St.t[:]), r=[St.k])
                    for i_ in range(4):
                        op("pool", lambda e, i_=i_: e.tensor_copy(oT.t[:, 4 * h + i_, smp:smp + 1], otmp.t[:, i_, 0:1]), r=[otk[i_]], w=[oTk[4 * h + i_]])
            out_proj(4, T, t0)
        P.barrier()
        es.close()


    def peer(self, l, D):
        nc, P = self.nc, self.P
        op, dma = P.op, P.dma
        c = self.c
        ident, ones, iota, CK = c["ident"], c["ones"], c["iota"], c["CK"]
        banks = self.banks
        xT, xk, modT, modA = self.xT, self.xk, self.modT, self.modA
        es = contextlib.ExitStack()
        sb = lambda name, shape, dt=F32: self.sb(name, shape, dt, es=es)
        KPG = int(os.environ.get('KPG', '17'))
        KPI = int(os.environ.get('KPI', '64'))
        nm = "p%d_" % l

        lp = P.lane()
        keysT = sb(nm + "keysT", [128, 2, 128])
        dma("sp", lp, lambda e: e.dma_start(out=keysT.t[:], in_=D["keysT"][l].rearrange("p d n -> d p n")), w=[keysT.k])
        hp = sb(nm + "hp", [128, 8, 128], BF16)
        NWB = 4
        wbuf = [sb(nm + "wq%d" % i, [128, 8, 128], BF16) for i in range(NWB)]
        wl = [P.lane() for _ in range(NWB)]
        wi = [0]
        ub = [sb(nm + "ub%d" % i, [128, 8, 256], BF16) for i in range(2)]
        vb = [sb(nm + "vb%d" % i, [128, 2, 1024], BF16) for i in range(2)]
        ul = [P.lane() for _ in range(2)]
        vl = [P.lane() for _ in range(2)]
        qc = [sb(nm + "qc%d" % i, [128, 128]) for i in range(2)]
        sc = [sb(nm + "sc%d" % i, [128, 128]) for i in range(2)]
        work = sb(nm + "work", [128, 128])
        sv = sb(nm + "sv", [128, 16, 16])
        si = sb(nm + "si", [128, 16, 16], U32)
        sif = sb(nm + "sif", [128, 16, 16])
        cand = sb(nm + "cand", [128, 256])
        cwork = sb(nm + "cwork", [128, 256])
        cv = sb(nm + "cv", [128, 8, 16])
        ci = sb(nm + "ci", [128, 8, 16], U32)
        au = sb(nm + "au", [128, 8, 16], U32)
        bu = sb(nm + "bu", [128, 8, 16], U32)
        af = sb(nm + "af", [128, 8, 16])
        bf = sb(nm + "bf", [128, 8, 16])
        eq = sb(nm + "eq", [128, 16, 16])
        iidx = sb(nm + "iidx", [128, 128])
        jidx = sb(nm + "jidx", [128, 128])
        gate = sb(nm + "gate", [128, 8, 16])
        zz = sb(nm + "zz", [128, 8])
        iT = sb(nm + "iT", [128, 128])
        jT = sb(nm + "jT", [128, 128])
        gT = sb(nm + "gT", [128, 128])
        OHj = sb(nm + "OHj", [128, 16, 128], BF16)
        OHi = sb(nm + "OHi", [128, 16, 128], BF16)
        Wall = sb(nm + "Wall", [128, 128, 128], BF16)
        zs = [sb(nm + "zs%d" % i, [128, 128]) for i in range(2)]
        Ab = [sb(nm + "A%d" % i, [128, 128], BF16) for i in range(2)]
        ysb = sb(nm + "ysb", [128, 1024])
        stmp = sb(nm + "stmp", [128, 128])
        pring = [0]

        def proj_ps():
            b = banks[pring[0] % 2]
            pring[0] += 1
            return b

        groups = [(i * 128, 128) for i in range(16)] + [(2048, 16)]
        ui = [0]
        for gi, (t0, T) in enumerate(groups[:KPG]):
            g5 = 4 if t0 >= 2048 else t0 // 512
            sh0 = 24
            if g5 < 4:
                self.norm_mod(g5, hp, lambda k: modA.t[:, l, 1, k, 0:1], lambda k: modT.t[:, l, sh0 + k, 0:1], t0, T)
            else:
                self.norm_mod(g5, hp, None, None, t0, T,
                              samp_A=lambda k: modA.t[:, l, 1, k, 1:17], samp_B=lambda k: modT.t[:, l, sh0 + k, 1:17])
            for c16 in range(16):
                i_ = wi[0] % NWB
                wi[0] += 1
                wq = wbuf[i_]
                dma("pool", wl[i_], lambda e: e.dma_start(
                    out=wq.t[:], in_=D["w_q"][l, :, c16 * 128:(c16 + 1) * 128].rearrange("(k p) c -> p k c", p=128)), w=[wq.k])
                pq = proj_ps()
                for k in range(8):
                    op("pe", lambda e, k=k: e.matmul(pq.t[:, :T], wq.t[:, k, :], hp.t[:, k, :T], start=(k == 0), stop=(k == 7)),
                       r=[wq.k, hp.k], w=[pq.k])
                q_ = qc[c16 % 2]
                op("act", lambda e: e.activation(out=q_.t[:, :T], in_=pq.t[:, :T], func=AF.Copy), r=[pq.k], w=[q_.k])
                pss = banks[2 + c16 % 2]
                op("pe", lambda e: e.matmul(pss.t[:T, 0:128], q_.t[:, :T], keysT.t[:, c16 % 2, :], start=True, stop=True),
                   r=[q_.k, keysT.k], w=[pss.k])
                s_ = sc[c16 % 2]
                op("dve", lambda e: e.tensor_copy(s_.t[:T, :], pss.t[:T, 0:128]), r=[pss.k], w=[s_.k])
                op("dve", lambda e: e.max(out=sv.t[:T, c16, 0:8], in_=s_.t[:T, :]), r=[s_.k], w=[sv.k])
                op("dve", lambda e: e.match_replace(out=work.t[:T, :], in_to_replace=sv.t[:T, c16, 0:8], in_values=s_.t[:T, :], imm_value=-1e30),
                   r=[s_.k, sv.k], w=[work.k])
                op("dve", lambda e: e.max(out=sv.t[:T, c16, 8:16], in_=work.t[:T, :]), r=[work.k, sv.k], w=[sv.k])
                op("dve", lambda e: e.max_index(out=si.t[:T, c16, 0:8], in_max=sv.t[:T, c16, 0:8], in_values=s_.t[:T, :]), r=[s_.k, sv.k], w=[si.k])
                op("dve", lambda e: e.max_index(out=si.t[:T, c16, 8:16], in_max=sv.t[:T, c16, 8:16], in_values=s_.t[:T, :]), r=[s_.k, sv.k, si.k], w=[si.k])
            op("dve", lambda e: e.tensor_copy(sif.t[:T], si.t[:T]), r=[si.k], w=[sif.k])
            for h in range(8):
                op("dve", lambda e: e.tensor_tensor(cand.t[:T, :].rearrange("p (a b) -> p a b", a=16),
                                                    sv.t[:T, 2 * h, :].unsqueeze(2).broadcast_to([T, 16, 16]),
                                                    sv.t[:T, 2 * h + 1, :].unsqueeze(1).broadcast_to([T, 16, 16]), ALU.add), r=[sv.k], w=[cand.k])
                op("dve", lambda e: e.max(out=cv.t[:T, h, 0:8], in_=cand.t[:T, :]), r=[cand.k], w=[cv.k])
                op("dve", lambda e: e.match_replace(out=cwork.t[:T, :], in_to_replace=cv.t[:T, h, 0:8], in_values=cand.t[:T, :], imm_value=-1e30),
                   r=[cand.k, cv.k], w=[cwork.k])
                op("dve", lambda e: e.max(out=cv.t[:T, h, 8:16], in_=cwork.t[:T, :]), r=[cwork.k, cv.k], w=[cv.k])
                op("dve", lambda e: e.max_index(out=ci.t[:T, h, 0:8], in_max=cv.t[:T, h, 0:8], in_values=cand.t[:T, :]), r=[cand.k, cv.k], w=[ci.k])
                op("dve", lambda e: e.max_index(out=ci.t[:T, h, 8:16], in_max=cv.t[:T, h, 8:16], in_values=cand.t[:T, :]), r=[cand.k, cv.k, ci.k], w=[ci.k])
            op("dve", lambda e: e.tensor_single_scalar(au.t[:T], ci.t[:T], 4, ALU.logical_shift_right), r=[ci.k], w=[au.k])
            op("dve", lambda e: e.tensor_single_scalar(bu.t[:T], ci.t[:T], 15, ALU.bitwise_and), r=[ci.k], w=[bu.k])
            op("dve", lambda e: e.tensor_copy(af.t[:T], au.t[:T]), r=[au.k], w=[af.k])
            op("dve", lambda e: e.tensor_copy(bf.t[:T], bu.t[:T]), r=[bu.k], w=[bf.k])
            for h in range(8):
                for (xf, col, dst) in ((af, 2 * h, iidx), (bf, 2 * h + 1, jidx)):
                    op("dve", lambda e: e.tensor_tensor(eq.t[:T], xf.t[:T, h, :].unsqueeze(2).broadcast_to([T, 16, 16]),
                                                        iota[:T, 0:16].unsqueeze(1).broadcast_to([T, 16, 16]), ALU.is_equal),
                       r=[xf.k, CK], w=[eq.k])
                    op("dve", lambda e: e.tensor_tensor(eq.t[:T], eq.t[:T], sif.t[:T, col, :].unsqueeze(1).broadcast_to([T, 16, 16]), ALU.mult),
                       r=[eq.k, sif.k], w=[eq.k])
                    op("dve", lambda e: e.reduce_sum(dst.t[:T, h * 16:(h + 1) * 16], eq.t[:T], AX.X), r=[eq.k], w=[dst.k])
            op("dve", lambda e: e.tensor_tensor(gate.t[:T], cv.t[:T], cv.t[:T, :, 0:1].broadcast_to([T, 8, 16]), ALU.subtract), r=[cv.k], w=[gate.k])
            op("act", lambda e: e.activation(out=gate.t[:T], in_=gate.t[:T], func=AF.Exp), r=[gate.k], w=[gate.k])
            op("dve", lambda e: e.reduce_sum(zz.t[:T, :], gate.t[:T], AX.X), r=[gate.k], w=[zz.k])
            op("dve", lambda e: e.reciprocal(zz.t[:T, :], zz.t[:T, :]), r=[zz.k], w=[zz.k])
            op("dve", lambda e: e.tensor_tensor(gate.t[:T], gate.t[:T], zz.t[:T, :].unsqueeze(2).broadcast_to([T, 8, 16]), ALU.mult),
               r=[gate.k, zz.k], w=[gate.k])
            for src_ap, src_k, dstT in ((iidx.t[:T, :], iidx.k, iT), (jidx.t[:T, :], jidx.k, jT),
                                        (gate.t[:T].rearrange("p h k -> p (h k)"), gate.k, gT)):
                pt = banks[6]
                op("pe", lambda e: e.matmul(pt.t[:, :T], src_ap, ident[:T, :T], start=True, stop=True), r=[src_k, CK], w=[pt.k])
                op("dve", lambda e: e.tensor_copy(dstT.t[:, :T], pt.t[:, :T]), r=[pt.k], w=[dstT.k])
            for sub in range((T + 15) // 16):
                ts = sub * 16
                n = min(16, T - ts)
                op("dve", lambda e: e.tensor_tensor(OHj.t[:, :n, :], iota.unsqueeze(1).broadcast_to([128, n, 128]),
                                                    jT.t[:, ts:ts + n].unsqueeze(2).broadcast_to([128, n, 128]), ALU.is_equal),
                   r=[jT.k, CK], w=[OHj.k])
                op("dve", lambda e: e.tensor_tensor(OHi.t[:, :n, :], iota.unsqueeze(1).broadcast_to([128, n, 128]),
                                                     iT.t[:, ts:ts + n].unsqueeze(2).broadcast_to([128, n, 128]), ALU.is_equal),
                   r=[iT.k, CK], w=[OHi.k])
                op("pool", lambda e: e.tensor_tensor(OHi.t[:, :n, :], OHi.t[:, :n, :],
                                                     gT.t[:, ts:ts + n].unsqueeze(2).broadcast_to([128, n, 128]), ALU.mult),
                   r=[OHi.k, gT.k], w=[OHi.k])
                for q4 in range((n + 3) // 4):
                    pw = banks[2 + q4 % 2]
                    n4 = min(4, n - q4 * 4)
                    for t in range(n4):
                        tt = q4 * 4 + t
                        op("pe", lambda e, t=t, tt=tt: e.matmul(pw.t[:, t * 128:(t + 1) * 128], OHj.t[:, tt, :], OHi.t[:, tt, :], start=True, stop=True),
                           r=[OHj.k, OHi.k], w=[pw.k])
                    tg = ts + q4 * 4
                    eng = ("act", "dve")[q4 % 2]
                    if eng == "act":
                        op("act", lambda e: e.activation(out=Wall.t[:, tg:tg + n4, :].rearrange("p t i -> p (t i)"), in_=pw.t[:, :n4 * 128], func=AF.Copy),
                           r=[pw.k], w=[Wall.k])
                    else:
                        op("dve", lambda e: e.tensor_copy(Wall.t[:, tg:tg + n4, :].rearrange("p t i -> p (t i)"), pw.t[:, :n4 * 128]), r=[pw.k], w=[Wall.k])
            py = [banks[4], banks[5]]
            for i2 in range(KPI):
                b_ = ui[0] % 2
                ui[0] += 1
                u_, v_ = ub[b_], vb[b_]
                dma("pool", ul[b_], lambda e: e.dma_start(
                    out=u_.t[:], in_=D["uT"][l, :, i2 * 256:(i2 + 1) * 256].rearrange("(k p) e -> p k e", p=128)), w=[u_.k])
                dma("pool", vl[b_], lambda e: e.dma_start(
                    out=v_.t[:], in_=D["pv"][l, i2 * 256:(i2 + 1) * 256, :].rearrange("(ii p) d -> p ii d", p=128)), w=[v_.k])
                for ii in range(2):
                    i = i2 * 2 + ii
                    pz = proj_ps()
                    for k in range(8):
                        op("pe", lambda e, k=k: e.matmul(pz.t[:, :T], u_.t[:, k, ii * 128:(ii + 1) * 128], hp.t[:, k, :T], start=(k == 0), stop=(k == 7)),
                           r=[u_.k, hp.k], w=[pz.k])
                    z_ = zs[i % 2]
                    a_ = Ab[i % 2]
                    op("act", lambda e: e.activation(out=z_.t[:, :T], in_=pz.t[:, :T], func=AF.Gelu), r=[pz.k], w=[z_.k])
                    op("dve", lambda e: e.tensor_tensor(a_.t[:, :T], z_.t[:, :T], Wall.t[:, :T, i], ALU.mult), r=[z_.k, Wall.k], w=[a_.k])
                    for half in range(2):
                        op("pe", lambda e, half=half: e.matmul(py[half].t[:T, :], a_.t[:, :T], v_.t[:, ii, half * 512:(half + 1) * 512],
                                                               start=(i == 0), stop=(i == 2 * KPI - 1)), r=[a_.k, v_.k], w=[py[half].k])
            for half in range(2):
                op("act", lambda e, half=half: e.activation(out=ysb.t[:T, half * 512:(half + 1) * 512], in_=py[half].t[:T, :], func=AF.Copy),
                   r=[py[half].k], w=[ysb.k])
            for m in range(8):
                pt = banks[6 + m % 2]
                op("pe", lambda e, m=m: e.matmul(pt.t[:, :T], ysb.t[:T, m * 128:(m + 1) * 128], ident[:T, :T], start=True, stop=True),
                   r=[ysb.k, CK], w=[pt.k])
                if g5 < 4:
                    op("dve", lambda e, m=m: e.scalar_tensor_tensor(
                        xT.t[:, m, t0:t0 + T], pt.t[:, :T], modT.t[:, l, 40 + m, 0:1], xT.t[:, m, t0:t0 + T], ALU.mult, ALU.add),
                        r=[pt.k, modT.k, xk[m][g5]], w=[xk[m][g5]])
                else:
                    op("dve", lambda e, m=m: e.tensor_tensor(stmp.t[:, :T], pt.t[:, :T], modT.t[:, l, 40 + m, 1:17], ALU.mult),
                       r=[pt.k, modT.k], w=[stmp.k])
                    op("dve", lambda e, m=m: e.tensor_tensor(xT.t[:, m, t0:t0 + T], xT.t[:, m, t0:t0 + T], stmp.t[:, :T], ALU.add),
                       r=[stmp.k, xk[m][g5]], w=[xk[m][g5]])
        P.barrier()
        es.close()


def _consts():
    c = np.zeros((128, 6, 128), np.float32)
    p = np.arange(128)[:, None]
    f = np.arange(128)[None, :]
    c[:, 0] = (p == f)
    c[:, 1] = (p <= f)
    c[:, 2] = 1.0
    c[:, 3] = np.where(f >= p, BIG, 0.0)
    c[:, 4] = np.where(f < p, -BIG, 0.0)
    c[:, 5] = f
    return c


def fm(v, k):
    return np.ascontiguousarray(np.asarray(v, np.float32).reshape(k, 128).T)


def make_in_maps(inp, ncores=8):
    f32 = np.float32
    shared = {}
    shared["ada_w"] = np.ascontiguousarray(inp["ada_w"], f32)
    shared["ada_b"] = np.ascontiguousarray(np.stack([fm(inp["ada_b"][l], 48) for l in range(2)], 1))
    shared["n1g"] = np.ascontiguousarray(np.stack([fm(inp["norm1_g"][l], 8) for l in range(2)], 1))
    shared["n2g"] = np.ascontiguousarray(np.stack([fm(inp["norm2_g"][l], 8) for l in range(2)], 1))
    shared["fg"] = fm(inp["final_g"], 8)
    shared["consts"] = _consts()
    shared["ab_w_in"] = np.ascontiguousarray(inp["ab_w_in"][0], f32)
    cw = inp["ab_conv_w"][0]
    shared["cw"] = np.ascontiguousarray(cw.reshape(4, 36, 128).transpose(2, 1, 0))
    shared["cb"] = fm(inp["ab_conv_b"][0], 36)
    auxp = np.zeros((1, 5, 32), f32)
    auxp[0, 0, 8:16] = inp["gdn_dt_bias"][0]
    auxp[0, 0, 16:32] = inp["ssm_dt_bias"][0]
    auxp[0, 1, 8:16] = inp["gdn_a_log"][0]
    auxp[0, 1, 16:32] = inp["ssm_a_log"][0]
    auxp[0, 2, 16:32] = inp["ssm_d"][0]
    shared["auxp"] = auxp
    shared["gng"] = np.ascontiguousarray(inp["gdn_norm_g"][0].reshape(128, 1), f32)
    shared["sng"] = fm(inp["ssm_norm_g"][0], 8)
    shared["ab_w_out"] = np.ascontiguousarray(inp["ab_w_out"][0], f32)
    shared["ret_w_in"] = np.ascontiguousarray(inp["ret_w_in"][0], f32)
    shared["ret_w_out"] = np.ascontiguousarray(inp["ret_w_out"][0], f32)
    shared["w_q"] = np.ascontiguousarray(inp["peer_w_q"], f32)
    shared["keysT"] = np.ascontiguousarray(np.asarray(inp["peer_keys"], f32).transpose(0, 1, 3, 2))
    shared["uT"] = np.ascontiguousarray(np.asarray(inp["peer_u"], f32).transpose(0, 2, 1))
    shared["pv"] = np.ascontiguousarray(inp["peer_v"], f32)
    shared["rng"] = np.ascontiguousarray(inp["ret_norm_g"], f32).reshape(1, 4, 512)
    retc = np.zeros((128, 4, 132), np.float64)
    ii = np.arange(128)
    for h in range(4):
        gm = 1.0 - 2.0 ** (-5.0 - h)
        d = ii[None, :] - ii[:, None]
        retc[:, h, 0:128] = np.where(d >= 0, gm ** np.maximum(d, 0), 0.0)
        retc[:, h, 128] = gm ** (ii + 1)
        retc[:, h, 129] = gm ** (127 - ii)
        retc[:, h, 130] = gm
        retc[:, h, 131] = 1.0
    shared["retc"] = retc.astype(f32)
    inv = 10000.0 ** (-np.arange(0, 256, 2, dtype=np.float64) / 256)
    pos = np.concatenate([np.arange(2048, dtype=np.float64), np.full(16, 16384.0)])
    ang = (pos[None, :].astype(f32) * inv[:, None].astype(f32)).astype(f32)
    shared["cosT"] = np.cos(ang.astype(np.float64)).astype(f32)
    shared["sinT"] = np.sin(ang.astype(np.float64)).astype(f32)
    maps = []
    for b in range(ncores):
        m = dict(shared)
        x_all = np.concatenate([inp["x_prompt"][b], inp["x_sample"][16 * b:16 * b + 16, 0]], 0)
        m["xT"] = np.ascontiguousarray(x_all.reshape(NT, 8, 128).transpose(2, 1, 0))
        c_all = np.concatenate([inp["c_prompt"][b:b + 1], inp["c_sample"][16 * b:16 * b + 16]], 0)
        m["cT"] = np.ascontiguousarray(c_all.reshape(17, 8, 128).transpose(2, 1, 0))
        sc = inp["state_conv"][0, 16 * b:16 * b + 16]
        m["st_conv"] = np.ascontiguousarray(sc.reshape(16, 3, 36, 128).transpose(3, 2, 0, 1))
        m["st_gdn"] = np.ascontiguousarray(inp["state_gdn"][0, 16 * b:16 * b + 16])
        m["st_ssm"] = np.ascontiguousarray(inp["state_ssm"][0, 16 * b:16 * b + 16])
        m["st_ret"] = np.ascontiguousarray(inp["state_ret"][0, 16 * b:16 * b + 16])
        maps.append(m)
    return maps


_CACHE = {}


def run(inp, ncores=8, dbg=(), stages=("l0", "p0", "l1", "p1")):
    bld = Builder(dbg=dbg, stages=stages)
    bld.build()
    maps = make_in_maps(inp, ncores)
    maps = [{k: v for k, v in m.items() if k in bld.in_names} for m in maps]
    res = run_bass_kernel_spmd(bld.nc, maps, core_ids=list(range(ncores)))
    return res.results


def kernel(**inputs):
    inp = {k: np.asarray(v) for k, v in inputs.items()}
    n = 8
    bld = Builder(dbg=("xb0", "xa1"))
    bld.build()
    maps = make_in_maps(inp, n)
    maps = [{k: v for k, v in m.items() if k in bld.in_names} for m in maps]
    res = run_bass_kernel_spmd(bld.nc, maps, core_ids=list(range(n))).results
    f32 = np.float32
    y_prompt = np.zeros((8, 2048, 1024), f32)
    y_sample = np.zeros((128, 1, 1024), f32)
    conv_p = np.zeros((1, 8, 3, 4608), f32)
    gdn_p = np.zeros((1, 8, 8, 128, 128), f32)
    ssm_p = np.zeros((1, 8, 16, 128, 64), f32)
    ret_p = np.zeros((1, 8, 4, 256, 512), f32)
    conv_s = np.zeros((1, 128, 3, 4608), f32)
    gdn_s = np.zeros((1, 128, 8, 128, 128), f32)
    ssm_s = np.zeros((1, 128, 16, 128, 64), f32)
    ret_s = np.zeros((1, 128, 4, 256, 512), f32)
    for b in range(n):
        r = res[b]
        y = np.asarray(r["yT"]).transpose(2, 1, 0).reshape(NT, 1024)
        y_prompt[b] = y[:2048]
        y_sample[16 * b:16 * b + 16, 0] = y[2048:]
        conv_p[0, b] = np.asarray(r["conv_p"]).transpose(2, 1, 0).reshape(3, 4608)
        gdn_p[0, b] = r["gdn_p"]
        ssm_p[0, b] = r["ssm_p"]
        ret_p[0, b] = r["ret_p"]
        conv_s[0, 16 * b:16 * b + 16] = np.asarray(r["conv_s"]).transpose(2, 3, 1, 0).reshape(16, 3, 4608)
        gdn_s[0, 16 * b:16 * b + 16] = r["gdn_s"]
        ssm_s[0, 16 * b:16 * b + 16] = r["ssm_s"]
        ret_s[0, 16 * b:16 * b + 16] = r["ret_s"]
    return (y_prompt, y_sample, conv_p, gdn_p, ssm_p, ret_p, conv_s, gdn_s, ssm_s, ret_s)
```

```python
import contextlib
import os
import numpy as np
import concourse.bass as bass
import concourse.mybir as mybir
from concourse.bass_utils import run_bass_kernel_spmd

F32 = mybir.dt.float32
BF16 = mybir.dt.bfloat16
I32 = mybir.dt.int32
U32 = mybir.dt.uint32
AF = mybir.ActivationFunctionType
ALU = mybir.AluOpType
AX = mybir.AxisListType

NT = 2064
NP = 2048
NS = 16
EPS = 1e-6
BIG = 30000.0


class Tk:
    __slots__ = ("name", "w", "rs")

    def __init__(self, name=""):
        self.name = name
        self.w = []
        self.rs = []


class Lane:
    __slots__ = ("sem", "cum", "key")

    def __init__(self, sem, key):
        self.sem = sem
        self.cum = 0
        self.key = key


class _Rec:
    __slots__ = ("call",)

    def __init__(self):
        self.call = None

    def __getattr__(self, name):
        def f(*a, **kw):
            self.call = (name, a, kw)
            return self
        return f


def _eager(fn):
    r = _Rec()
    fn(r)
    name, a, kw = r.call
    return lambda e: getattr(e, name)(*a, **kw)


class Prog:
    ENGS = ("pe", "dve", "act", "pool", "sp")

    def __init__(self, nc):
        self.nc = nc
        self.ops = {e: [] for e in self.ENGS}
        self.cnt = {e: 0 for e in self.ENGS}
        self.sems = {}
        self.waited = {e: {} for e in self.ENGS}
        self._ctx = []
        for e in ("pe", "dve", "act", "pool"):
            self.sems[e] = self._sem("p_" + e)
        self.lanes = []

    def _sem(self, name):
        g = self.nc.semaphore(name)
        s = g.__enter__()
        self._ctx.append(g)
        return s

    def lane(self):
        key = "L%d" % len(self.lanes)
        self.sems[key] = self._sem(key)
        ln = Lane(self.sems[key], key)
        self.lanes.append(ln)
        return ln

    def _need(self, eng, deps):
        wd = self.waited[eng]
        best = {}
        for k, v in deps:
            if eng == "pe" and k == "pe":
                continue
            if wd.get(k, 0) >= v:
                continue
            if best.get(k, 0) < v:
                best[k] = v
        for k, v in best.items():
            wd[k] = v
            sem = self.sems[k]
            self.ops[eng].append(lambda e, sem=sem, v=v: e.wait_ge(sem, v))

    @staticmethod
    def _deps_for(r, w):
        deps = []
        for t in r:
            deps.extend(t.w)
        for t in w:
            deps.extend(t.w)
            deps.extend(t.rs)
        return deps

    def op(self, eng, fn, r=(), w=()):
        fn = _eager(fn)
        self._need(eng, self._deps_for(r, w))
        self.cnt[eng] += 1
        n = self.cnt[eng]
        sem = self.sems[eng]
        self.ops[eng].append(lambda e, fn=fn, sem=sem: fn(e).then_inc(sem, 1))
        tag = (eng, n)
        for t in r:
            t.rs.append(tag)
        for t in w:
            t.w = [tag]
            t.rs = []

    def dma(self, q, lane, fn, r=(), w=(), group=False):
        fn = _eager(fn)
        deps = self._deps_for(r, w)
        if group:
            deps = [d for d in deps if d[0] != lane.key]
        elif lane.cum > 0:
            deps.append((lane.key, lane.cum))
        self._need(q, deps)
        lane.cum += 16
        sem = lane.sem
        self.ops[q].append(lambda e, fn=fn, sem=sem: fn(e).then_inc(sem, 16))
        tag = (lane.key, lane.cum)
        for t in r:
            t.rs.append(tag)
        for t in w:
            if group:
                t.w = [x for x in t.w if x[0] != lane.key] + [tag]
            else:
                t.w = [tag]
            t.rs = []

    def group_finish(self, lane, tks):
        tag = (lane.key, lane.cum)
        for t in tks:
            t.w = [tag if x[0] == lane.key else x for x in t.w]
            t.rs = [tag if x[0] == lane.key else x for x in t.rs]

    def wait_all(self, eng):
        deps = [(e, self.cnt[e]) for e in ("pe", "dve", "act", "pool") if self.cnt[e] > 0]
        deps += [(ln.key, ln.cum) for ln in self.lanes if ln.cum > 0]
        self._need(eng, deps)

    def barrier(self):
        for e in self.ENGS:
            self.wait_all(e)

    def emit(self):
        with self.nc.Block() as block:
            def mk(name):
                lst = self.ops[name]

                def body(e):
                    for f in lst:
                        f(e)
                return body
            block.tensor(mk("pe"))
            block.vector(mk("dve"))
            block.scalar(mk("act"))
            block.gpsimd(mk("pool"))
            block.sync(mk("sp"))

    def close(self):
        for g in reversed(self._ctx):
            g.__exit__(None, None, None)
        self._ctx = []


class B:
    __slots__ = ("t", "k")

    def __init__(self, t, name=""):
        self.t = t
        self.k = Tk(name)


def interleave(gens, width):
    gens = list(gens)
    active = []
    while gens or active:
        while gens and len(active) < width:
            active.append(gens.pop(0))
        nxt = []
        for g in active:
            try:
                next(g)
                nxt.append(g)
            except StopIteration:
                pass
        active = nxt


class Builder:
    def __init__(self, dbg=(), stages=("l0", "p0", "l1", "p1")):
        self.nc = bass.Bass("TRN2", target_bir_lowering=False)
        self.P = Prog(self.nc)
        self.es = contextlib.ExitStack()
        self.dbg = set(dbg)
        self.stages = stages
        self.in_names = []
        self.out_names = []
        self.rr = 0

    def din(self, name, shape, dt=F32):
        self.in_names.append(name)
        return self.nc.dram_tensor(name, list(shape), dt, kind="ExternalInput").ap()

    def dout(self, name, shape, dt=F32):
        self.out_names.append(name)
        return self.nc.dram_tensor(name, list(shape), dt, kind="ExternalOutput").ap()

    def dscratch(self, name, shape, dt=F32):
        return self.nc.dram_tensor(name, list(shape), dt, kind="Internal").ap()

    def sb(self, name, shape, dt=F32, es=None):
        t = (es or self.es).enter_context(self.nc.sbuf_tensor(name, list(shape), dt))
        return B(t, name)

    def psb(self, name, shape, dt=F32):
        t = self.es.enter_context(self.nc.psum_tensor(name, list(shape), dt))
        return B(t, name)

    def ew(self):
        self.rr += 1
        return ("dve", "pool")[self.rr % 2]

    def build(self):
        nc, P = self.nc, self.P
        op, dma = P.op, P.dma
        xT_d = self.din("xT", [128, 8, NT])
        cT_d = self.din("cT", [128, 8, 17])
        ada_w = self.din("ada_w", [2, 1024, 6144])
        ada_b = self.din("ada_b", [128, 2, 48])
        n1g = self.din("n1g", [128, 2, 8])
        n2g = self.din("n2g", [128, 2, 8])
        fg = self.din("fg", [128, 8])
        consts_d = self.din("consts", [128, 6, 128])
        w_in = self.din("ab_w_in", [1024, 6688])
        cw_d = self.din("cw", [128, 36, 4])
        cb_d = self.din("cb", [128, 36])
        auxp_d = self.din("auxp", [1, 5, 32])
        gng_d = self.din("gng", [128, 1])
        sng_d = self.din("sng", [128, 8])
        w_out = self.din("ab_w_out", [2048, 1024])
        D = dict(w_in=w_in, w_out=w_out, cw=cw_d, cb=cb_d, auxp=auxp_d, gng=gng_d, sng=sng_d)
        D["st_conv"] = self.din("st_conv", [128, 36, 16, 3])
        D["st_gdn"] = self.din("st_gdn", [16, 8, 128, 128])
        D["st_ssm"] = self.din("st_ssm", [16, 16, 128, 64])
        D["ret_w_in"] = self.din("ret_w_in", [1024, 6144])
        D["ret_w_out"] = self.din("ret_w_out", [2048, 1024])
        D["rng"] = self.din("rng", [1, 4, 512])
        D["retc"] = self.din("retc", [128, 4, 132])
        D["cosT"] = self.din("cosT", [128, NT])
        D["sinT"] = self.din("sinT", [128, NT])
        D["st_ret"] = self.din("st_ret", [16, 4, 256, 512])
        D["w_q"] = self.din("w_q", [2, 1024, 2048])
        D["keysT"] = self.din("keysT", [2, 2, 128, 128])
        D["uT"] = self.din("uT", [2, 1024, 16384])
        D["pv"] = self.din("pv", [2, 16384, 1024])
        yT_d = self.dout("yT", [128, 8, NT])
        D["ret_p"] = self.dout("ret_p", [4, 256, 512])
        D["ret_s"] = self.dout("ret_s", [16, 4, 256, 512])
        D["conv_p"] = self.dout("conv_p", [128, 36, 3])
        D["gdn_p"] = self.dout("gdn_p", [8, 128, 128])
        D["ssm_p"] = self.dout("ssm_p", [16, 128, 64])
        D["conv_s"] = self.dout("conv_s", [128, 36, 16, 3])
        D["gdn_s"] = self.dout("gdn_s", [16, 8, 128, 128])
        D["ssm_s"] = self.dout("ssm_s", [16, 16, 128, 64])
        dbg_d = {}
        for nm in self.dbg:
            dbg_d[nm] = self.dout("dbg_" + nm, [128, 8, NT])

        xT = self.sb("xT_sb", [128, 8, NT])
        xk = [[Tk("x%d_%d" % (m, g)) for g in range(5)] for m in range(8)]
        cT = self.sb("cT_sb", [128, 8, 17])
        modT = self.sb("modT", [128, 2, 48, 17])
        consts = self.sb("consts_sb", [128, 6, 128])
        n1 = self.sb("n1_sb", [128, 2, 8])
        n2 = self.sb("n2_sb", [128, 2, 8])
        fgs = self.sb("fg_sb", [128, 8])
        adab = self.sb("adab_sb", [128, 2, 48])
        ident = consts.t[:, 0, :]
        triU = consts.t[:, 1, :]
        ones = consts.t[:, 2, :]
        maskA = consts.t[:, 3, :]
        maskB = consts.t[:, 4, :]
        iota = consts.t[:, 5, :]
        CK = consts.k

        banks = [self.psb("ps%d" % i, [128, 512]) for i in range(8)]
        self.banks = banks

        lc = P.lane()
        first = [True]

        def cload(dst, src_ap, q="sp"):
            dma(q, lc, lambda e: e.dma_start(out=dst.t[:], in_=src_ap), w=[dst.k], group=not first[0])
            first[0] = False

        cload(consts, consts_d[:, :, :])
        cload(cT, cT_d[:, :, :])
        cload(n1, n1g[:, :, :])
        cload(n2, n2g[:, :, :])
        cload(fgs, fg[:, :])
        cload(adab, ada_b[:, :, :])
        P.group_finish(lc, [consts.k, cT.k, n1.k, n2.k, fgs.k, adab.k])
        lx = P.lane()
        for m in range(8):
            dma("sp", lx, lambda e, m=m: e.dma_start(out=xT.t[:, m, :], in_=xT_d[:, m, :]),
                w=xk[m], group=(m > 0))
        P.group_finish(lx, [t for row in xk for t in row])

        op("act", lambda e: e.activation(out=cT.t[:], in_=cT.t[:], func=AF.Silu), r=[cT.k], w=[cT.k])
        with contextlib.ExitStack() as es:
            NAB = 4
            abuf = [self.sb("adaw%d" % i, [128, 8, 128], es=es) for i in range(NAB)]
            la = [P.lane() for _ in range(NAB)]
            it = 0
            for l in range(2):
                for half in range(2):
                    pst = banks[half]
                    for mm in range(24):
                        bi = it % NAB
                        it += 1
                        buf = abuf[bi]
                        col0 = (half * 24 + mm) * 128
                        dma("sp", la[bi], lambda e, l=l, col0=col0, buf=buf: e.dma_start(
                            out=buf.t[:], in_=ada_w[l, :, col0:col0 + 128].rearrange("(k p) c -> p k c", p=128)),
                            w=[buf.k])
                        for kk in range(8):
                            op("pe", lambda e, mm=mm, kk=kk, buf=buf, pst=pst: e.matmul(
                                pst.t[:, mm * 17:(mm + 1) * 17], buf.t[:, kk, :], cT.t[:, kk, :],
                                start=(kk == 0), stop=(kk == 7)), r=[buf.k, cT.k], w=[pst.k])
                    op("dve", lambda e, l=l, half=half, pst=pst: e.tensor_tensor(
                        modT.t[:, l, half * 24:(half + 1) * 24, :],
                        pst.t[:, 0:408].rearrange("p (m j) -> p m j", j=17),
                        adab.t[:, l, half * 24:(half + 1) * 24].unsqueeze(2).broadcast_to([128, 24, 17]), ALU.add),
                        r=[pst.k, adab.k], w=[modT.k])
            P.barrier()

        modA = self.sb("modA", [128, 2, 2, 8, 17])
        for l in range(2):
            for wh in range(2):
                sc0 = 8 + 24 * wh
                gsrc = (n1, n2)[wh]
                op("dve", lambda e, l=l, wh=wh, sc0=sc0: e.tensor_scalar(
                    modA.t[:, l, wh, :, :], modT.t[:, l, sc0:sc0 + 8, :], 1.0, None, ALU.add),
                    r=[modT.k], w=[modA.k])
                op("dve", lambda e, l=l, wh=wh, gsrc=gsrc: e.tensor_tensor(
                    modA.t[:, l, wh, :, :], modA.t[:, l, wh, :, :],
                    gsrc.t[:, l, :].unsqueeze(2).broadcast_to([128, 8, 17]), ALU.mult),
                    r=[modA.k, gsrc.k], w=[modA.k])

        self.groups = [(g * 512, 512) for g in range(4)] + [(2048, 16)]

        nsq = self.sb("nsq", [128, 512])
        nrs = self.sb("nrs", [128, 512])
        ntmp = self.sb("ntmp", [128, 512])

        def norm_mod(g, hm, A_ap_fn, B_ap_fn, t0, T, samp_A=None, samp_B=None, out_dt_tile=None):
            pst = banks[7]
            for k in range(8):
                op("act", lambda e, k=k: e.activation(out=nsq.t[:, :T], in_=xT.t[:, k, t0:t0 + T], func=AF.Square),
                   r=[xk[k][g]], w=[nsq.k])
                op("pe", lambda e, k=k: e.matmul(pst.t[:, :T], ones, nsq.t[:, :T], start=(k == 0), stop=(k == 7)),
                   r=[nsq.k, CK], w=[pst.k])
            op("act", lambda e: e.activation(out=nrs.t[:, :T], in_=pst.t[:, :T], func=AF.Sqrt, bias=EPS, scale=1.0 / 1024),
               r=[pst.k], w=[nrs.k])
            op("dve", lambda e: e.reciprocal(nrs.t[:, :T], nrs.t[:, :T]), r=[nrs.k], w=[nrs.k])
            for k in range(8):
                if samp_A is None:
                    op("dve", lambda e, k=k: e.scalar_tensor_tensor(
                        ntmp.t[:, :T], xT.t[:, k, t0:t0 + T], A_ap_fn(k), nrs.t[:, :T], ALU.mult, ALU.mult),
                        r=[xk[k][g], nrs.k, modA.k, fgs.k], w=[ntmp.k])
                    if B_ap_fn is None:
                        op("act", lambda e, k=k: e.activation(out=hm.t[:, k, :T], in_=ntmp.t[:, :T], func=AF.Copy),
                           r=[ntmp.k], w=[hm.k])
                    else:
                        op("act", lambda e, k=k: e.activation(out=hm.t[:, k, :T], in_=ntmp.t[:, :T], func=AF.Identity,
                                                             bias=B_ap_fn(k), scale=1.0),
                           r=[ntmp.k, modT.k], w=[hm.k])
                else:
                    op("dve", lambda e, k=k: e.tensor_tensor(ntmp.t[:, :T], xT.t[:, k, t0:t0 + T], nrs.t[:, :T], ALU.mult),
                       r=[xk[k][g], nrs.k], w=[ntmp.k])
                    op("dve", lambda e, k=k: e.tensor_tensor(ntmp.t[:, :T], ntmp.t[:, :T], samp_A(k), ALU.mult),
                       r=[ntmp.k, modA.k], w=[ntmp.k])
                    op("dve", lambda e, k=k: e.tensor_tensor(hm.t[:, k, :T], ntmp.t[:, :T], samp_B(k), ALU.add),
                       r=[ntmp.k, modT.k], w=[hm.k])

        def norm_mod_layer(l, wh, g, hm):
            t0, T = self.groups[g]
            sh0 = 24 * wh
            if g < 4:
                norm_mod(g, hm, lambda k: modA.t[:, l, wh, k, 0:1], lambda k: modT.t[:, l, sh0 + k, 0:1], t0, T)
            else:
                norm_mod(g, hm, None, None, t0, T,
                         samp_A=lambda k: modA.t[:, l, wh, k, 1:17], samp_B=lambda k: modT.t[:, l, sh0 + k, 1:17])

        self.norm_mod = norm_mod
        self.nsq, self.nrs = nsq, nrs
        self.norm_mod_layer = norm_mod_layer
        self.xT, self.xk, self.modT, self.modA = xT, xk, modT, modA
        self.c = dict(ident=ident, triU=triU, ones=ones, maskA=maskA, maskB=maskB, iota=iota, CK=CK)

        def dump(nm):
            if nm in self.dbg:
                P.barrier()
                ld = P.lane()
                dma("sp", ld, lambda e: e.dma_start(out=dbg_d[nm][:, :, :], in_=xT.t[:]),
                    r=[t for row in xk for t in row])
                P.barrier()

        if "l0" in self.stages:
            self.layer0(D)
        dump("xa0")
        if "p0" in self.stages:
            self.peer(0, D)
        dump("xb0")
        if "l1" in self.stages:
            self.layer1(D)
        dump("xa1")
        if "p1" in self.stages:
            self.peer(1, D)
        dump("xb1")

        yb = self.sb("ybuf_fin", [128, 8, 512])
        ly = P.lane()
        for g in range(5):
            t0, T = self.groups[g]
            norm_mod(g, yb, lambda k: fgs.t[:, k:k + 1], None, t0, T)
            dma("sp", ly, lambda e, t0=t0, T=T: e.dma_start(out=yT_d[:, :, t0:t0 + T], in_=yb.t[:, :, :T]), r=[yb.k])
        P.wait_all("sp")
        P.emit()
        P.close()
        self.es.close()

    def layer0(self, D):
        nc, P = self.nc, self.P
        op, dma = P.op, P.dma
        c = self.c
        ident, triU, ones, maskA, maskB, CK = c["ident"], c["triU"], c["ones"], c["maskA"], c["maskB"], c["CK"]
        banks = self.banks
        xT, xk, modT, modA = self.xT, self.xk, self.modT, self.modA
        w_in, w_out = D["w_in"], D["w_out"]
        es = contextlib.ExitStack()
        sb = lambda name, shape, dt=F32: self.sb(name, shape, dt, es=es)
        KG = int(os.environ.get('KG', '5'))
        KH = int(os.environ.get('KH', '8'))
        KSS = int(os.environ.get('KSS', '1'))
        KSMP = int(os.environ.get('KSMP', '16'))

        lp = P.lane()
        cw = sb("cw_sb", [128, 36, 4])
        cb = sb("cb_sb", [128, 36])
        auxp = sb("auxp_sb", [128, 5, 32])
        gng = sb("gng_sb", [128, 1])
        sng = sb("sng_sb", [128, 8])
        dma("sp", lp, lambda e: e.dma_start(out=cw.t[:], in_=D["cw"][:, :, :]), w=[cw.k])
        dma("sp", lp, lambda e: e.dma_start(out=cb.t[:], in_=D["cb"][:, :]), w=[cb.k], group=True)
        dma("sp", lp, lambda e: e.dma_start(out=auxp.t[:], in_=D["auxp"][0:1, :, :].broadcast_to([128, 5, 32])), w=[auxp.k], group=True)
        dma("sp", lp, lambda e: e.dma_start(out=gng.t[:], in_=D["gng"][:, :]), w=[gng.k], group=True)
        dma("sp", lp, lambda e: e.dma_start(out=sng.t[:], in_=D["sng"][:, :]), w=[sng.k], group=True)
        P.group_finish(lp, [cw.k, cb.k, auxp.k, gng.k, sng.k])
        op("act", lambda e: e.activation(out=auxp.t[:, 1, 8:32], in_=auxp.t[:, 1, 8:32], func=AF.Exp), r=[auxp.k], w=[auxp.k])
        op("dve", lambda e: e.tensor_scalar(auxp.t[:, 1, 8:32], auxp.t[:, 1, 8:32], -1.0, None, ALU.mult), r=[auxp.k], w=[auxp.k])

        Sg = sb("Sg", [128, 8, 128])
        Sgk = [Tk("Sg%d" % h) for h in range(8)]
        Ss = sb("Ss", [128, 16, 64])
        Ssk = [Tk("Ss%d" % g) for g in range(2)]
        halo = sb("halo", [128, 36, 3])
        halok = [Tk("halo%d" % i) for i in range(36)]
        op("pool", lambda e: e.memset(Sg.t[:], 0.0), w=Sgk)
        op("pool", lambda e: e.memset(Ss.t[:], 0.0), w=Ssk)
        op("pool", lambda e: e.memset(halo.t[:], 0.0), w=halok)

        hm = sb("hm", [128, 8, 512], BF16)
        NWB = 4
        wbuf = [sb("wb%d" % i, [128, 8, 128], BF16) for i in range(NWB)]
        wl = [P.lane() for _ in range(NWB)]
        wi = [0]

        def load_w(col0, src, ncol=128, row0=0):
            i = wi[0] % NWB
            wi[0] += 1
            buf = wbuf[i]
            dma("pool", wl[i], lambda e: e.dma_start(
                out=buf.t[:, :, :ncol], in_=src[row0:row0 + 1024, col0:col0 + ncol].rearrange("(k p) c -> p k c", p=128)), w=[buf.k])
            return buf

        pring = [0]

        def proj_ps():
            b = banks[pring[0] % 2]
            pring[0] += 1
            return b

        def project(col0, T, ncol=128):
            buf = load_w(col0, w_in, ncol=ncol)
            pst = proj_ps()
            for k in range(8):
                op("pe", lambda e, k=k: e.matmul(pst.t[:ncol, :T], buf.t[:, k, :ncol], hm.t[:, k, :T],
                                                 start=(k == 0), stop=(k == 7)), r=[buf.k, hm.k], w=[pst.k])
            return pst

        NPRE = 2
        prehk = [Tk("preh%d" % i) for i in range(NPRE)]
        prei = [0]
        acc_t = [sb("cacc%d" % i, [128, 512]) for i in range(2)]
        acci = [0]
        pres = [sb("pres%d" % i, [128, 16, 4]) for i in range(2)]
        preshk = [Tk() for i in range(2)]
        presi = [0]
        lcs_in = [P.lane() for _ in range(2)]
        lcs_out = P.lane()

        def conv_chunk(ch, T, dst_ap, dst_k, col0):
            pst = project(col0, T)
            p = pre[prei[0] % NPRE]
            ph = prehk[prei[0] % NPRE]
            prei[0] += 1
            op("act", lambda e: e.activation(out=p.t[:, 3:3 + T], in_=pst.t[:, :T], func=AF.Copy), r=[pst.k], w=[p.k])
            op("pool", lambda e: e.tensor_copy(p.t[:, 0:3], halo.t[:, ch, :]), r=[halok[ch]], w=[ph])
            a = acc_t[acci[0] % 2]
            acci[0] += 1
            op("dve", lambda e: e.tensor_scalar(a.t[:, :T], p.t[:, 0:T], cw.t[:, ch, 0:1], cb.t[:, ch:ch + 1], ALU.mult, ALU.add),
               r=[p.k, ph, cw.k, cb.k], w=[a.k])
            for i in range(1, 4):
                op("dve", lambda e, i=i: e.scalar_tensor_tensor(a.t[:, :T], p.t[:, i:i + T], cw.t[:, ch, i:i + 1], a.t[:, :T],
                                                                ALU.mult, ALU.add), r=[p.k, ph, a.k], w=[a.k])
            op("pool", lambda e: e.tensor_copy(halo.t[:, ch, :], p.t[:, T:T + 3]), r=[p.k, ph], w=[halok[ch]])
            op("act", lambda e: e.activation(out=dst_ap, in_=a.t[:, :T], func=AF.Silu), r=[a.k], w=[dst_k])

        def conv_chunk_s(ch, dst_ap, dst_k, col0):
            pst = project(col0, 16)
            j = presi[0] % 2
            presi[0] += 1
            p, ph = pres[j], preshk[j]
            dma("sp", lcs_in[j], lambda e: e.dma_start(out=p.t[:, :, 0:3], in_=D["st_conv"][:, ch, :, :]), w=[ph])
            op("act", lambda e: e.activation(out=p.t[:, :, 3], in_=pst.t[:, :16], func=AF.Copy), r=[pst.k], w=[p.k])
            a = acc_t[acci[0] % 2]
            acci[0] += 1
            op("dve", lambda e: e.tensor_scalar(a.t[:, :16], p.t[:, :, 0], cw.t[:, ch, 0:1], cb.t[:, ch:ch + 1], ALU.mult, ALU.add),
               r=[p.k, ph, cw.k, cb.k], w=[a.k])
            for i in range(1, 4):
                op("dve", lambda e, i=i: e.scalar_tensor_tensor(a.t[:, :16], p.t[:, :, i], cw.t[:, ch, i:i + 1], a.t[:, :16],
                                                                ALU.mult, ALU.add), r=[p.k, ph, a.k], w=[a.k])
            dma("sp", lcs_out, lambda e: e.dma_start(out=D["conv_s"][:, ch, :, :], in_=p.t[:, :, 1:4]), r=[p.k, ph])
            op("act", lambda e: e.activation(out=dst_ap, in_=a.t[:, :16], func=AF.Silu), r=[a.k], w=[dst_k])

        qT = sb("qT", [128, 512])
        kT = sb("kT", [128, 512])
        vT = sb("vT", [128, 512])
        sgT = sb("sgT", [128, 512])
        sq = self.nsq
        rs = self.nrs
        oT = sb("oT", [128, 16, 512], BF16)
        oTk = [Tk("oT%d" % j) for j in range(16)]
        auxT = sb("auxT", [32, 512])
        aux = [sb("aux%d" % cc, [128, 32]) for cc in range(4)]
        beta = [sb("beta%d" % cc, [128, 8]) for cc in range(4)]
        gg = [sb("gg%d" % cc, [128, 24]) for cc in range(4)]
        dtt = [sb("dt%d" % cc, [128, 16]) for cc in range(4)]
        gc = [sb("gc%d" % cc, [128, 24]) for cc in range(4)]
        ngc = [sb("ngc%d" % cc, [128, 24]) for cc in range(4)]
        egc = [sb("egc%d" % cc, [128, 24]) for cc in range(4)]
        elast = [sb("elast%d" % cc, [128, 24]) for cc in range(4)]
        edl = [sb("edl%d" % cc, [128, 24]) for cc in range(4)]
        begc = [sb("begc%d" % cc, [128, 8]) for cc in range(4)]
        dte = [sb("dte%d" % cc, [128, 16]) for cc in range(4)]
        sptmp = sb("sptmp", [128, 32])

        def aux_compute(cc, src_ap, src_k, sample=False):
            pa = banks[7]
            op("pe", lambda e: e.matmul(pa.t[:, 0:32], src_ap, ident[:32, :32], start=True, stop=True), r=[src_k, CK], w=[pa.k])
            op("dve", lambda e: e.tensor_copy(aux[cc].t[:], pa.t[:, 0:32]), r=[pa.k], w=[aux[cc].k])
            op("act", lambda e: e.activation(out=beta[cc].t[:], in_=aux[cc].t[:, 0:8], func=AF.Sigmoid), r=[aux[cc].k], w=[beta[cc].k])
            op("dve", lambda e: e.tensor_tensor(sptmp.t[:, 8:32], aux[cc].t[:, 8:32], auxp.t[:, 0, 8:32], ALU.add),
               r=[aux[cc].k, auxp.k], w=[sptmp.k])
            op("act", lambda e: e.activation(out=sptmp.t[:, 8:32], in_=sptmp.t[:, 8:32], func=AF.Exp), r=[sptmp.k], w=[sptmp.k])
            op("act", lambda e: e.activation(out=sptmp.t[:, 8:32], in_=sptmp.t[:, 8:32], func=AF.Ln, bias=1.0), r=[sptmp.k], w=[sptmp.k])
            op("dve", lambda e: e.tensor_copy(dtt[cc].t[:], sptmp.t[:, 16:32]), r=[sptmp.k], w=[dtt[cc].k])
            op("dve", lambda e: e.tensor_tensor(gg[cc].t[:], sptmp.t[:, 8:32], auxp.t[:, 1, 8:32], ALU.mult),
               r=[sptmp.k, auxp.k], w=[gg[cc].k])
            if sample:
                op("dve", lambda e: e.tensor_scalar(gg[cc].t[:], gg[cc].t[:], ident[:, 0:1], None, ALU.mult), r=[gg[cc].k, CK], w=[gg[cc].k])
            op("pe", lambda e: e.matmul(pa.t[:, 32:56], triU, gg[cc].t[:], start=True, stop=True), r=[gg[cc].k, CK], w=[pa.k])
            op("pe", lambda e: e.matmul(pa.t[:, 64:88], ones, gg[cc].t[:], start=True, stop=True), r=[gg[cc].k, CK], w=[pa.k])
            op("dve", lambda e: e.tensor_copy(gc[cc].t[:], pa.t[:, 32:56]), r=[pa.k], w=[gc[cc].k])
            op("dve", lambda e: e.tensor_scalar(ngc[cc].t[:], pa.t[:, 32:56], -1.0, None, ALU.mult), r=[pa.k], w=[ngc[cc].k])
            op("act", lambda e: e.activation(out=egc[cc].t[:], in_=pa.t[:, 32:56], func=AF.Exp), r=[pa.k], w=[egc[cc].k])
            op("act", lambda e: e.activation(out=elast[cc].t[:], in_=pa.t[:, 64:88], func=AF.Exp), r=[pa.k], w=[elast[cc].k])
            op("dve", lambda e: e.tensor_tensor(edl[cc].t[:], pa.t[:, 64:88], gc[cc].t[:], ALU.subtract), r=[pa.k, gc[cc].k], w=[edl[cc].k])
            op("act", lambda e: e.activation(out=edl[cc].t[:], in_=edl[cc].t[:], func=AF.Exp), r=[edl[cc].k], w=[edl[cc].k])
            op("dve", lambda e: e.tensor_tensor(begc[cc].t[:], beta[cc].t[:], egc[cc].t[:, 0:8], ALU.mult),
               r=[beta[cc].k, egc[cc].k], w=[begc[cc].k])
            op("dve", lambda e: e.tensor_tensor(dte[cc].t[:], dtt[cc].t[:], edl[cc].t[:, 8:24], ALU.mult),
               r=[dtt[cc].k, edl[cc].k], w=[dte[cc].k])

        def l2norm(buf, T, scale):
            pst = banks[7]
            op("act", lambda e: e.activation(out=sq.t[:, :T], in_=buf.t[:, :T], func=AF.Square), r=[buf.k], w=[sq.k])
            op("pe", lambda e: e.matmul(pst.t[:, :T], ones, sq.t[:, :T], start=True, stop=True), r=[sq.k, CK], w=[pst.k])
            op("act", lambda e: e.activation(out=rs.t[:, :T], in_=pst.t[:, :T], func=AF.Sqrt, bias=EPS, scale=1.0), r=[pst.k], w=[rs.k])
            op("dve", lambda e: e.reciprocal(rs.t[:, :T], rs.t[:, :T]), r=[rs.k], w=[rs.k])
            op("dve", lambda e: e.scalar_tensor_tensor(buf.t[:, :T], buf.t[:, :T], scale, rs.t[:, :T], ALU.mult, ALU.mult),
               r=[buf.k, rs.k], w=[buf.k])

        class S:
            pass
        s = S()
        for nm in ("tmp", "dec", "decT", "A", "Bm", "A2", "B2", "Pm", "P2", "kbg", "kd", "vb", "nwT", "u", "qk", "t1", "o", "on",
                   "GT", "Btok", "MT", "Bd"):
            setattr(s, nm, sb("u_%s" % nm, [128, 128]))
        s.ss = sb("u_ss", [128, 2])
        s.ps = [banks[2], banks[3]]
        s.pi = 0
        s.xtok = sb("u_xtok", [128, 512])
        s.t512 = sb("u_t512", [128, 512])
        s.t512b = sb("u_t512b", [128, 512])
        s.otok = sb("u_otok", [128, 512])
        fring = [0]

        def fullbank():
            b = banks[4 + fring[0] % 3]
            fring[0] += 1
            return b

        def pst_of(s):
            j = s.pi % 8
            s.pi += 1
            b = s.ps[j % 2]
            r = j // 2
            return b.t[:, r * 128:(r + 1) * 128], b.k

        def gdn_unit(h, cc, q_ap, k_ap, v_ap, sg_ap, rk, S_ap, S_k, out_ap, out_k, single=False):
            g_b = gg[cc].t[:, h:h + 1].broadcast_to([128, 128])
            p2, k2 = pst_of(s)
            op("pe", lambda e: e.matmul(p2, ident, maskB, start=True, stop=False), r=[CK], w=[k2])
            op("pe", lambda e: e.matmul(p2, g_b, triU, start=False, stop=True), r=[gg[cc].k, CK], w=[k2])
            op("act", lambda e: e.activation(out=s.decT.t[:], in_=p2, func=AF.Exp, bias=ngc[cc].t[:, h:h + 1], scale=1.0),
               r=[k2, ngc[cc].k], w=[s.decT.k])
            if not single:
                p1, k1 = pst_of(s)
                op("pe", lambda e: e.matmul(p1, ident, maskA, start=True, stop=False), r=[CK], w=[k1])
                op("pe", lambda e: e.matmul(p1, g_b, triU, start=False, stop=True), r=[gg[cc].k, CK], w=[k1])
                op("act", lambda e: e.activation(out=s.dec.t[:], in_=p1, func=AF.Exp, bias=gc[cc].t[:, h:h + 1], scale=-1.0),
                   r=[k1, gc[cc].k], w=[s.dec.k])
                p3, k3 = pst_of(s)
                op("pe", lambda e: e.matmul(p3, k_ap, k_ap, start=True, stop=True), r=rk, w=[k3])
                op("dve", lambda e: e.scalar_tensor_tensor(s.A.t[:], p3, beta[cc].t[:, h:h + 1], s.dec.t[:], ALU.mult, ALU.mult),
                   r=[k3, beta[cc].k, s.dec.k], w=[s.A.k])
                p4, k4 = pst_of(s)
                op("pe", lambda e: e.matmul(p4, s.A.t[:], ident, start=True, stop=True), r=[s.A.k, CK], w=[k4])
                op("act", lambda e: e.activation(out=s.Bm.t[:], in_=p4, func=AF.Copy), r=[k4], w=[s.Bm.k])
                op("dve", lambda e: e.tensor_tensor(s.Pm.t[:], ident, s.Bm.t[:], ALU.subtract), r=[s.Bm.k, CK], w=[s.Pm.k])
                Ac, Bc, An, Bn = s.A, s.Bm, s.A2, s.B2
                Pc, Pn = s.Pm, s.P2
                for lvl in range(6):
                    pa_, ka = pst_of(s)
                    op("pe", lambda e, Ac=Ac, Bc=Bc, pa_=pa_: e.matmul(pa_, Bc.t[:], Ac.t[:], start=True, stop=True),
                       r=[Ac.k, Bc.k], w=[ka])
                    op("act", lambda e, An=An, pa_=pa_: e.activation(out=An.t[:], in_=pa_, func=AF.Copy), r=[ka], w=[An.k])
                    if lvl < 5:
                        pb, kb = pst_of(s)
                        op("pe", lambda e, Ac=Ac, Bc=Bc, pb=pb: e.matmul(pb, Ac.t[:], Bc.t[:], start=True, stop=True),
                           r=[Ac.k, Bc.k], w=[kb])
                        op("dve", lambda e, Bn=Bn, pb=pb: e.tensor_copy(Bn.t[:], pb), r=[kb], w=[Bn.k])
                    pc, kc = pst_of(s)
                    op("pe", lambda e, Pc=Pc, pc=pc: e.matmul(pc, ident, Pc.t[:], start=True, stop=False), r=[Pc.k, CK], w=[kc])
                    op("pe", lambda e, Pc=Pc, An=An, pc=pc: e.matmul(pc, An.t[:], Pc.t[:], start=False, stop=True),
                       r=[Pc.k, An.k], w=[kc])
                    op("dve", lambda e, Pn=Pn, pc=pc: e.tensor_copy(Pn.t[:], pc), r=[kc], w=[Pn.k])
                    Ac, An = An, Ac
                    Bc, Bn = Bn, Bc
                    Pc, Pn = Pn, Pc
                TT_ap, TT_k = Pc.t[:], Pc.k
            else:
                TT_ap, TT_k = ident, CK
            p5, k5 = pst_of(s)
            op("pe", lambda e: e.matmul(p5, k_ap, ident, start=True, stop=True), r=rk + [CK], w=[k5])
            op("dve", lambda e: e.tensor_scalar(s.kbg.t[:], p5, begc[cc].t[:, h:h + 1], None, ALU.mult), r=[k5, begc[cc].k], w=[s.kbg.k])
            op("dve", lambda e: e.tensor_scalar(s.kd.t[:], p5, edl[cc].t[:, h:h + 1], None, ALU.mult), r=[k5, edl[cc].k], w=[s.kd.k])
            p6, k6 = pst_of(s)
            op("pe", lambda e: e.matmul(p6, v_ap, ident, start=True, stop=True), r=rk + [CK], w=[k6])
            op("dve", lambda e: e.tensor_scalar(s.vb.t[:], p6, beta[cc].t[:, h:h + 1], None, ALU.mult), r=[k6, beta[cc].k], w=[s.vb.k])
            p7, k7 = pst_of(s)
            op("pe", lambda e: e.matmul(p7, s.kbg.t[:], TT_ap, start=True, stop=True), r=[s.kbg.k, TT_k], w=[k7])
            op("act", lambda e: e.activation(out=s.nwT.t[:], in_=p7, func=AF.Identity, scale=-1.0), r=[k7], w=[s.nwT.k])
            p8, k8 = pst_of(s)
            op("pe", lambda e: e.matmul(p8, TT_ap, s.vb.t[:], start=True, stop=False), r=[TT_k, s.vb.k], w=[k8])
            op("pe", lambda e: e.matmul(p8, s.nwT.t[:], S_ap, start=False, stop=True), r=[s.nwT.k, S_k], w=[k8])
            op("act", lambda e: e.activation(out=s.u.t[:], in_=p8, func=AF.Copy), r=[k8], w=[s.u.k])
            p9, k9 = pst_of(s)
            op("pe", lambda e: e.matmul(p9, k_ap, q_ap, start=True, stop=True), r=rk, w=[k9])
            op("dve", lambda e: e.tensor_tensor(s.qk.t[:], p9, s.decT.t[:], ALU.mult), r=[k9, s.decT.k], w=[s.qk.k])
            p10, k10 = pst_of(s)
            op("pe", lambda e: e.matmul(p10, q_ap, S_ap, start=True, stop=True), r=rk + [S_k], w=[k10])
            op("dve", lambda e: e.tensor_scalar(s.t1.t[:], p10, egc[cc].t[:, h:h + 1], None, ALU.mult), r=[k10, egc[cc].k], w=[s.t1.k])
            p11, k11 = pst_of(s)
            op("pe", lambda e: e.matmul(p11, s.qk.t[:], s.u.t[:], start=True, stop=True), r=[s.qk.k, s.u.k], w=[k11])
            op("dve", lambda e: e.tensor_tensor(s.o.t[:], p11, s.t1.t[:], ALU.add), r=[k11, s.t1.k], w=[s.o.k])
            p12, k12 = pst_of(s)
            op("pe", lambda e: e.matmul(p12, s.kd.t[:], s.u.t[:], start=True, stop=True), r=[s.kd.k, s.u.k], w=[k12])
            op("dve", lambda e: e.tensor_scalar(s.tmp.t[:], S_ap, elast[cc].t[:, h:h + 1], None, ALU.mult),
               r=[elast[cc].k, S_k], w=[s.tmp.k])
            op("dve", lambda e: e.tensor_tensor(S_ap, p12, s.tmp.t[:], ALU.add), r=[k12, s.tmp.k], w=[S_k])
            op("act", lambda e: e.activation(out=s.on.t[:], in_=s.o.t[:], func=AF.Square), r=[s.o.k], w=[s.on.k])
            op("dve", lambda e: e.reduce_sum(s.ss.t[:, 0:1], s.on.t[:], AX.X), r=[s.on.k], w=[s.ss.k])
            op("act", lambda e: e.activation(out=s.ss.t[:, 1:2], in_=s.ss.t[:, 0:1], func=AF.Sqrt, bias=EPS, scale=1.0 / 128), r=[s.ss.k], w=[s.ss.k])
            op("dve", lambda e: e.reciprocal(s.ss.t[:, 1:2], s.ss.t[:, 1:2]), r=[s.ss.k], w=[s.ss.k])
            op("dve", lambda e: e.tensor_scalar(s.on.t[:], s.o.t[:], s.ss.t[:, 1:2], None, ALU.mult), r=[s.o.k, s.ss.k], w=[s.on.k])
            p13, k13 = pst_of(s)
            op("pe", lambda e: e.matmul(p13, s.on.t[:], ident, start=True, stop=True), r=[s.on.k, CK], w=[k13])
            op("dve", lambda e: e.scalar_tensor_tensor(out_ap, p13, gng.t[:, 0:1], sg_ap, ALU.mult, ALU.mult),
               r=[k13, gng.k] + rk, w=[out_k])


        def ssd_unit(grp, cc, B_ap, C_ap, xs_aps, sz_aps, rk, S_ap, S_k, y_aps, y_ks):
            h0 = 8 * grp
            pg, kg_ = pst_of(s)
            op("pe", lambda e: e.matmul(pg, B_ap, C_ap, start=True, stop=True), r=rk, w=[kg_])
            op("act", lambda e: e.activation(out=s.GT.t[:], in_=pg, func=AF.Copy), r=[kg_], w=[s.GT.k])
            px = fullbank()
            for j in range(4):
                op("pe", lambda e, j=j: e.matmul(px.t[:, j * 128:(j + 1) * 128], xs_aps[j], ident, start=True, stop=True),
                   r=rk + [CK], w=[px.k])
            op("act", lambda e: e.activation(out=s.xtok.t[:], in_=px.t[:], func=AF.Copy), r=[px.k], w=[s.xtok.k])
            pb_, kb_ = pst_of(s)
            op("pe", lambda e: e.matmul(pb_, B_ap, ident, start=True, stop=True), r=rk + [CK], w=[kb_])
            op("dve", lambda e: e.tensor_copy(s.Btok.t[:], pb_), r=[kb_], w=[s.Btok.k])
            pcs = fullbank()
            op("pe", lambda e: e.matmul(pcs.t[:], C_ap, S_ap.rearrange("p h q -> p (h q)"), start=True, stop=True), r=rk + [S_k], w=[pcs.k])
            op("dve", lambda e: e.tensor_tensor(s.t512.t[:].rearrange("p (h q) -> p h q", q=64),
                                                pcs.t[:].rearrange("p (h q) -> p h q", q=64),
                                                egc[cc].t[:, 8 + h0:16 + h0].unsqueeze(2).broadcast_to([128, 8, 64]), ALU.mult),
               r=[pcs.k, egc[cc].k], w=[s.t512.k])
            po = fullbank()
            pS = fullbank()
            for hh in range(8):
                col = 8 + h0 + hh
                g_b = gg[cc].t[:, col:col + 1].broadcast_to([128, 128])
                pd, kd_ = pst_of(s)
                op("pe", lambda e: e.matmul(pd, ident, maskB, start=True, stop=False), r=[CK], w=[kd_])
                op("pe", lambda e: e.matmul(pd, g_b, triU, start=False, stop=True), r=[gg[cc].k, CK], w=[kd_])
                op("act", lambda e: e.activation(out=s.decT.t[:], in_=pd, func=AF.Exp, bias=ngc[cc].t[:, col:col + 1], scale=1.0),
                   r=[kd_, ngc[cc].k], w=[s.decT.k])
                op("dve", lambda e: e.scalar_tensor_tensor(s.MT.t[:], s.GT.t[:], dtt[cc].t[:, h0 + hh:h0 + hh + 1], s.decT.t[:],
                                                           ALU.mult, ALU.mult), r=[s.GT.k, dtt[cc].k, s.decT.k], w=[s.MT.k])
                op("pe", lambda e: e.matmul(po.t[:, hh * 64:(hh + 1) * 64], s.MT.t[:], s.xtok.t[:, hh * 64:(hh + 1) * 64],
                                            start=True, stop=True), r=[s.MT.k, s.xtok.k], w=[po.k])
                op("dve", lambda e: e.tensor_scalar(s.Bd.t[:], s.Btok.t[:], dte[cc].t[:, h0 + hh:h0 + hh + 1], None, ALU.mult),
                   r=[s.Btok.k, dte[cc].k], w=[s.Bd.k])
                op("pe", lambda e: e.matmul(pS.t[:, hh * 64:(hh + 1) * 64], s.Bd.t[:], s.xtok.t[:, hh * 64:(hh + 1) * 64],
                                            start=True, stop=True), r=[s.Bd.k, s.xtok.k], w=[pS.k])
            op("dve", lambda e: e.tensor_tensor(s.otok.t[:], po.t[:], s.t512.t[:], ALU.add), r=[po.k, s.t512.k], w=[s.otok.k])
            op("dve", lambda e: e.tensor_tensor(s.t512b.t[:].rearrange("p (h q) -> p h q", q=64),
                                                s.xtok.t[:].rearrange("p (h q) -> p h q", q=64),
                                                auxp.t[:, 2, 16 + h0:24 + h0].unsqueeze(2).broadcast_to([128, 8, 64]), ALU.mult),
               r=[s.xtok.k, auxp.k], w=[s.t512b.k])
            op("dve", lambda e: e.tensor_tensor(s.otok.t[:], s.otok.t[:], s.t512b.t[:], ALU.add), r=[s.otok.k, s.t512b.k], w=[s.otok.k])
            op("dve", lambda e: e.tensor_tensor(s.t512.t[:].rearrange("p (h q) -> p h q", q=64), S_ap,
                                                elast[cc].t[:, 8 + h0:16 + h0].unsqueeze(2).broadcast_to([128, 8, 64]), ALU.mult),
               r=[S_k, elast[cc].k, s.t512.k], w=[s.t512.k])
            op("dve", lambda e: e.tensor_tensor(S_ap, pS.t[:].rearrange("p (h q) -> p h q", q=64),
                                                s.t512.t[:].rearrange("p (h q) -> p h q", q=64), ALU.add),
               r=[pS.k, s.t512.k], w=[S_k])
            pt = fullbank()
            for j in range(4):
                op("pe", lambda e, j=j: e.matmul(pt.t[:, j * 128:(j + 1) * 128], s.otok.t[:, j * 128:(j + 1) * 128], ident,
                                                 start=True, stop=True), r=[s.otok.k, CK], w=[pt.k])
            for j in range(4):
                op("dve", lambda e, j=j: e.tensor_tensor(y_aps[j], pt.t[:, j * 128:(j + 1) * 128], sz_aps[j], ALU.mult),
                   r=[pt.k] + rk, w=[y_ks[j]])

        def ssd_norm_out(T):
            pst = banks[7]
            for j in range(8):
                op("act", lambda e, j=j: e.activation(out=sq.t[:, :T], in_=oT.t[:, 8 + j, :T], func=AF.Square), r=[oTk[8 + j]], w=[sq.k])
                op("pe", lambda e, j=j: e.matmul(pst.t[:, :T], ones, sq.t[:, :T], start=(j == 0), stop=(j == 7)), r=[sq.k, CK], w=[pst.k])
            op("act", lambda e: e.activation(out=rs.t[:, :T], in_=pst.t[:, :T], func=AF.Sqrt, bias=EPS, scale=1.0 / 1024), r=[pst.k], w=[rs.k])
            op("dve", lambda e: e.reciprocal(rs.t[:, :T], rs.t[:, :T]), r=[rs.k], w=[rs.k])
            for j in range(8):
                op("dve", lambda e, j=j: e.scalar_tensor_tensor(oT.t[:, 8 + j, :T], oT.t[:, 8 + j, :T], sng.t[:, j:j + 1], rs.t[:, :T],
                                                                ALU.mult, ALU.mult), r=[oTk[8 + j], sng.k, rs.k], w=[oTk[8 + j]])

        def out_proj(g, T, t0):
            for m in range(8):
                pst = proj_ps()
                for half in range(2):
                    buf = load_w(m * 128, w_out, row0=half * 1024)
                    for k in range(8):
                        kk = half * 8 + k
                        op("pe", lambda e, k=k, kk=kk: e.matmul(pst.t[:, :T], buf.t[:, k, :], oT.t[:, kk, :T],
                                                                start=(kk == 0), stop=(kk == 15)), r=[buf.k, oTk[kk]], w=[pst.k])
                if g < 4:
                    op("dve", lambda e, m=m: e.scalar_tensor_tensor(
                        xT.t[:, m, t0:t0 + T], pst.t[:, :T], modT.t[:, 0, 16 + m, 0:1], xT.t[:, m, t0:t0 + T], ALU.mult, ALU.add),
                        r=[pst.k, modT.k, xk[m][g]], w=[xk[m][g]])
                else:
                    op("dve", lambda e, m=m: e.tensor_tensor(sq.t[:, :T], pst.t[:, :T], modT.t[:, 0, 16 + m, 1:17], ALU.mult),
                       r=[pst.k, modT.k], w=[sq.k])
                    op("dve", lambda e, m=m: e.tensor_tensor(xT.t[:, m, t0:t0 + T], xT.t[:, m, t0:t0 + T], sq.t[:, :T], ALU.add),
                       r=[sq.k, xk[m][g]], w=[xk[m][g]])

        es_p = contextlib.ExitStack()
        sbp = lambda name, shape, dt=F32: self.sb(name, shape, dt, es=es_p)
        pre = [sbp("pre%d" % i, [128, 515]) for i in range(NPRE)]
        BT = [sbp("BT%d" % i, [128, 512]) for i in range(2)]
        CT = [sbp("CT%d" % i, [128, 512]) for i in range(2)]
        xsT = [sbp("xsT%d" % i, [128, 512]) for i in range(4)]
        szT = [sbp("szT%d" % i, [128, 512]) for i in range(4)]
        for g in range(min(KG, 4)):
            t0, T = self.groups[g]
            nch = T // 128
            self.norm_mod_layer(0, 0, g, hm)
            pst = project(6656, T, ncol=32)
            op("act", lambda e: e.activation(out=auxT.t[:, :T], in_=pst.t[:32, :T], func=AF.Copy), r=[pst.k], w=[auxT.k])
            for cc in range(nch):
                aux_compute(cc, auxT.t[:, cc * 128:(cc + 1) * 128], auxT.k)
            for h in range(KH):
                conv_chunk(h, T, qT.t[:, :T], qT.k, h * 128)
                conv_chunk(8 + h, T, kT.t[:, :T], kT.k, 1024 + h * 128)
                conv_chunk(16 + h, T, vT.t[:, :T], vT.k, 2048 + h * 128)
                pst = project(4608 + h * 128, T)
                op("act", lambda e: e.activation(out=sgT.t[:, :T], in_=pst.t[:, :T], func=AF.Silu), r=[pst.k], w=[sgT.k])
                l2norm(qT, T, 128.0 ** -0.5)
                l2norm(kT, T, 1.0)
                for cc in range(nch):
                    cs = slice(cc * 128, (cc + 1) * 128)
                    gdn_unit(h, cc, qT.t[:, cs], kT.t[:, cs], vT.t[:, cs], sgT.t[:, cs], [qT.k, kT.k, vT.k, sgT.k],
                             Sg.t[:, h, :], Sgk[h], oT.t[:, h, cs], oTk[h])
            for j in range(KH, 8):
                op("pool", lambda e, j=j: e.memset(oT.t[:, j, :T], 0.0), w=[oTk[j]])
            if KSS:
                for grp in range(2):
                    conv_chunk(32 + grp, T, BT[grp].t[:, :T], BT[grp].k, 4096 + grp * 128)
                    conv_chunk(34 + grp, T, CT[grp].t[:, :T], CT[grp].k, 4352 + grp * 128)
                    for j in range(4):
                        jj = 4 * grp + j
                        conv_chunk(24 + jj, T, xsT[j].t[:, :T], xsT[j].k, 3072 + jj * 128)
                        pst = project(5632 + jj * 128, T)
                        op("act", lambda e, j=j: e.activation(out=szT[j].t[:, :T], in_=pst.t[:, :T], func=AF.Silu), r=[pst.k], w=[szT[j].k])
                    rk = [BT[grp].k, CT[grp].k] + [xsT[j].k for j in range(4)] + [szT[j].k for j in range(4)]
                    for cc in range(nch):
                        cs = slice(cc * 128, (cc + 1) * 128)
                        ssd_unit(grp, cc, BT[grp].t[:, cs], CT[grp].t[:, cs], [xsT[j].t[:, cs] for j in range(4)],
                                 [szT[j].t[:, cs] for j in range(4)], rk, Ss.t[:, 8 * grp:8 * grp + 8, :], Ssk[grp],
                                 [oT.t[:, 8 + 4 * grp + j, cs] for j in range(4)], [oTk[8 + 4 * grp + j] for j in range(4)])
                ssd_norm_out(T)
            else:
                for j in range(8, 16):
                    op("pool", lambda e, j=j: e.memset(oT.t[:, j, :T], 0.0), w=[oTk[j]])
            out_proj(g, T, t0)

        op_l = P.lane()
        dma("sp", op_l, lambda e: e.dma_start(out=D["gdn_p"].rearrange("h k v -> k h v"), in_=Sg.t[:]), r=Sgk)
        dma("sp", op_l, lambda e: e.dma_start(out=D["ssm_p"].rearrange("h n p -> n h p"), in_=Ss.t[:]), r=Ssk, group=True)
        dma("sp", op_l, lambda e: e.dma_start(out=D["conv_p"][:, :, :], in_=halo.t[:]), r=halok, group=True)
        P.group_finish(op_l, Sgk + Ssk + halok)

        P.barrier()
        es_p.close()
        if KG >= 5:
            g, (t0, T) = 4, self.groups[4]
            self.norm_mod_layer(0, 0, 4, hm)
            pst = project(6656, T, ncol=32)
            op("act", lambda e: e.activation(out=auxT.t[:, :T], in_=pst.t[:32, :T], func=AF.Copy), r=[pst.k], w=[auxT.k])
            qs = sb("qs", [128, 8, 16])
            ks_ = sb("ks", [128, 8, 16])
            vs = sb("vs", [128, 8, 16])
            sgs = sb("sgs", [128, 8, 16])
            Bs = sb("Bs", [128, 2, 16])
            Cs = sb("Cs", [128, 2, 16])
            xss = sb("xss", [128, 8, 16])
            szs = sb("szs", [128, 8, 16])
            for h in range(8):
                conv_chunk_s(h, qT.t[:, :16], qT.k, h * 128)
                conv_chunk_s(8 + h, kT.t[:, :16], kT.k, 1024 + h * 128)
                conv_chunk_s(16 + h, vs.t[:, h, :], vs.k, 2048 + h * 128)
                pst = project(4608 + h * 128, 16)
                op("act", lambda e, h=h: e.activation(out=sgs.t[:, h, :], in_=pst.t[:, :16], func=AF.Silu), r=[pst.k], w=[sgs.k])
                l2norm(qT, 16, 128.0 ** -0.5)
                l2norm(kT, 16, 1.0)
                op("pool", lambda e, h=h: e.tensor_copy(qs.t[:, h, :], qT.t[:, :16]), r=[qT.k], w=[qs.k])
                op("pool", lambda e, h=h: e.tensor_copy(ks_.t[:, h, :], kT.t[:, :16]), r=[kT.k], w=[ks_.k])
            for grp in range(2):
                conv_chunk_s(32 + grp, Bs.t[:, grp, :], Bs.k, 4096 + grp * 128)
                conv_chunk_s(34 + grp, Cs.t[:, grp, :], Cs.k, 4352 + grp * 128)
            for jj in range(8):
                conv_chunk_s(24 + jj, xss.t[:, jj, :], xss.k, 3072 + jj * 128)
                pst = project(5632 + jj * 128, 16)
                op("act", lambda e, jj=jj: e.activation(out=szs.t[:, jj, :], in_=pst.t[:, :16], func=AF.Silu), r=[pst.k], w=[szs.k])
            pads = {}
            for nm in ["q", "k", "v", "sg", "B", "C"] + ["xs%d" % j for j in range(4)] + ["sz%d" % j for j in range(4)]:
                pads[nm] = sb("pad_" + nm, [128, 128])
                op("pool", lambda e, nm=nm: e.memset(pads[nm].t[:], 0.0), w=[pads[nm].k])
            auxpad = sb("auxpad", [32, 128])
            op("pool", lambda e: e.memset(auxpad.t[:], 0.0), w=[auxpad.k])
            otmp = sb("otmp", [128, 128])
            ytmp = sb("ytmp", [128, 4, 128])
            ytk = [Tk() for _ in range(4)]
            NSB = 2
            Sgs = [sb("Sgs%d" % i, [128, 128]) for i in range(NSB)]
            Sss = [sb("Sss%d" % i, [128, 8, 64]) for i in range(NSB)]
            lsi = [P.lane() for _ in range(NSB)]
            lso = [P.lane() for _ in range(NSB)]
            lsi2 = [P.lane() for _ in range(NSB)]
            lso2 = [P.lane() for _ in range(NSB)]
            cnt = 0
            for smp in range(KSMP):
                op("pool", lambda e, smp=smp: e.tensor_copy(auxpad.t[:, 0:1], auxT.t[:, smp:smp + 1]), r=[auxT.k], w=[auxpad.k])
                aux_compute(0, auxpad.t[:], auxpad.k, sample=True)
                for h in range(KH):
                    i = cnt % NSB
                    cnt += 1
                    St = Sgs[i]
                    dma("sp", lsi[i], lambda e: e.dma_start(out=St.t[:], in_=D["st_gdn"][smp, h, :, :]), w=[St.k])
                    for nm, src in (("q", qs), ("k", ks_), ("v", vs), ("sg", sgs)):
                        op("pool", lambda e, nm=nm, src=src: e.tensor_copy(pads[nm].t[:, 0:1], src.t[:, h, smp:smp + 1]),
                           r=[src.k], w=[pads[nm].k])
                    gdn_unit(h, 0, pads["q"].t[:], pads["k"].t[:], pads["v"].t[:], pads["sg"].t[:],
                             [pads["q"].k, pads["k"].k, pads["v"].k, pads["sg"].k], St.t[:], St.k, otmp.t[:], otmp.k, single=True)
                    dma("sp", lso[i], lambda e: e.dma_start(out=D["gdn_s"][smp, h, :, :], in_=St.t[:]), r=[St.k])
                    op("pool", lambda e, h=h, smp=smp: e.tensor_copy(oT.t[:, h, smp:smp + 1], otmp.t[:, 0:1]), r=[otmp.k], w=[oTk[h]])
                if KSS:
                    for grp in range(2):
                        i = cnt % NSB
                        cnt += 1
                        St = Sss[i]
                        dma("sp", lsi2[i], lambda e: e.dma_start(
                            out=St.t[:], in_=D["st_ssm"][smp, 8 * grp:8 * grp + 8, :, :].rearrange("h n p -> n h p")), w=[St.k])
                        op("pool", lambda e: e.tensor_copy(pads["B"].t[:, 0:1], Bs.t[:, grp, smp:smp + 1]), r=[Bs.k], w=[pads["B"].k])
                        op("pool", lambda e: e.tensor_copy(pads["C"].t[:, 0:1], Cs.t[:, grp, smp:smp + 1]), r=[Cs.k], w=[pads["C"].k])
                        for j in range(4):
                            op("pool", lambda e, j=j: e.tensor_copy(pads["xs%d" % j].t[:, 0:1], xss.t[:, 4 * grp + j, smp:smp + 1]),
                               r=[xss.k], w=[pads["xs%d" % j].k])
                            op("pool", lambda e, j=j: e.tensor_copy(pads["sz%d" % j].t[:, 0:1], szs.t[:, 4 * grp + j, smp:smp + 1]),
                               r=[szs.k], w=[pads["sz%d" % j].k])
                        rk = [pads[n].k for n in ["B", "C"] + ["xs%d" % j for j in range(4)] + ["sz%d" % j for j in range(4)]]
                        ssd_unit(grp, 0, pads["B"].t[:], pads["C"].t[:], [pads["xs%d" % j].t[:] for j in range(4)],
                                 [pads["sz%d" % j].t[:] for j in range(4)], rk, St.t[:], St.k,
                                 [ytmp.t[:, j, :] for j in range(4)], ytk)
                        dma("sp", lso2[i], lambda e: e.dma_start(
                            out=D["ssm_s"][smp, 8 * grp:8 * grp + 8, :, :].rearrange("h n p -> n h p"), in_=St.t[:]), r=[St.k])
                        for j in range(4):
                            op("pool", lambda e, j=j: e.tensor_copy(oT.t[:, 8 + 4 * grp + j, smp:smp + 1], ytmp.t[:, j, 0:1]),
                               r=[ytk[j]], w=[oTk[8 + 4 * grp + j]])
            for j in range(KH, 8):
                op("pool", lambda e, j=j: e.memset(oT.t[:, j, :T], 0.0), w=[oTk[j]])
            if KSS:
                ssd_norm_out(T)
            else:
                for j in range(8, 16):
                    op("pool", lambda e, j=j: e.memset(oT.t[:, j, :T], 0.0), w=[oTk[j]])
            out_proj(4, T, t0)
        P.barrier()
        es.close()


    def layer1(self, D):
        nc, P = self.nc, self.P
        op, dma = P.op, P.dma
        c = self.c
        ident, ones, CK = c["ident"], c["ones"], c["CK"]
        banks = self.banks
        xT, xk, modT = self.xT, self.xk, self.modT
        w_in, w_out = D["ret_w_in"], D["ret_w_out"]
        es = contextlib.ExitStack()
        sb = lambda name, shape, dt=F32: self.sb(name, shape, dt, es=es)
        KG = int(os.environ.get('KG', '5'))
        KSMP = int(os.environ.get('KSMP', '16'))
        gam = [1.0 - 2.0 ** (-5.0 - h) for h in range(4)]

        lp = P.lane()
        retc = sb("retc_sb", [128, 4, 132])
        rngb = sb("rngb", [128, 4, 512])
        dma("sp", lp, lambda e: e.dma_start(out=retc.t[:], in_=D["retc"][:, :, :]), w=[retc.k])
        dma("sp", lp, lambda e: e.dma_start(out=rngb.t[:], in_=D["rng"][0:1, :, :].broadcast_to([128, 4, 512])), w=[rngb.k], group=True)
        P.group_finish(lp, [retc.k, rngb.k])
        Sr = sb("Sr", [128, 4, 2, 512])
        Srk = [Tk("Sr%d" % h) for h in range(4)]
        op("pool", lambda e: e.memset(Sr.t[:], 0.0), w=Srk)

        hm = sb("hm1", [128, 8, 512], BF16)
        NWB = 4
        wbuf = [sb("w1b%d" % i, [128, 8, 128], BF16) for i in range(NWB)]
        wl = [P.lane() for _ in range(NWB)]
        wi = [0]

        def load_w(col0, src, row0=0):
            i = wi[0] % NWB
            wi[0] += 1
            buf = wbuf[i]
            dma("pool", wl[i], lambda e: e.dma_start(
                out=buf.t[:], in_=src[row0:row0 + 1024, col0:col0 + 128].rearrange("(k p) c -> p k c", p=128)), w=[buf.k])
            return buf

        pring = [0]

        def proj_ps():
            b = banks[pring[0] % 2]
            pring[0] += 1
            return b

        def project(col0, T):
            buf = load_w(col0, w_in)
            pst = proj_ps()
            for k in range(8):
                op("pe", lambda e, k=k: e.matmul(pst.t[:, :T], buf.t[:, k, :], hm.t[:, k, :T], start=(k == 0), stop=(k == 7)),
                   r=[buf.k, hm.k], w=[pst.k])
            return pst

        cosb = sb("cosb", [128, 512])
        sinb = sb("sinb", [128, 512])
        lcs = P.lane()
        raw = [sb("rraw%d" % i, [128, 512]) for i in range(2)]
        ta = sb("rta", [128, 512])
        tb = sb("rtb", [128, 512])
        qT = [sb("rq%d" % i, [128, 512]) for i in range(2)]
        kT = [sb("rk%d" % i, [128, 512]) for i in range(2)]
        vT = [sb("rv%d" % i, [128, 512]) for i in range(4)]
        sgT = [sb("rsg%d" % i, [128, 512]) for i in range(4)]
        oT = sb("oT1", [128, 16, 512], BF16)
        oTk = [Tk("o1T%d" % j) for j in range(16)]

        def rope(col0, T, dst, scale):
            for cidx in range(2):
                pst = project(col0 + cidx * 128, T)
                op("act", lambda e, cidx=cidx: e.activation(out=raw[cidx].t[:, :T], in_=pst.t[:, :T], func=AF.Identity, scale=scale),
                   r=[pst.k], w=[raw[cidx].k])
            x1, x2 = raw
            op("dve", lambda e: e.tensor_tensor(ta.t[:, :T], x1.t[:, :T], cosb.t[:, :T], ALU.mult), r=[x1.k, cosb.k], w=[ta.k])
            op("pool", lambda e: e.tensor_tensor(tb.t[:, :T], x2.t[:, :T], sinb.t[:, :T], ALU.mult), r=[x2.k, sinb.k], w=[tb.k])
            op("dve", lambda e: e.tensor_tensor(dst[0].t[:, :T], ta.t[:, :T], tb.t[:, :T], ALU.subtract), r=[ta.k, tb.k], w=[dst[0].k])
            op("dve", lambda e: e.tensor_tensor(ta.t[:, :T], x1.t[:, :T], sinb.t[:, :T], ALU.mult), r=[x1.k, sinb.k], w=[ta.k])
            op("pool", lambda e: e.tensor_tensor(tb.t[:, :T], x2.t[:, :T], cosb.t[:, :T], ALU.mult), r=[x2.k, cosb.k], w=[tb.k])
            op("dve", lambda e: e.tensor_tensor(dst[1].t[:, :T], ta.t[:, :T], tb.t[:, :T], ALU.add), r=[ta.k, tb.k], w=[dst[1].k])

        class S:
            pass
        s = S()
        s.qk = sb("r_qk", [128, 128])
        s.ss = sb("r_ss", [128, 4])
        s.ps = [banks[2], banks[3]]
        s.pi = 0
        s.xtok = sb("r_xtok", [128, 512])
        s.kd = sb("r_kd", [128, 256])
        s.t512b = sb("r_t512b", [128, 512])
        s.otok = sb("r_otok", [128, 512])
        fring = [0]

        def fullbank():
            b = banks[4 + fring[0] % 3]
            fring[0] += 1
            return b

        def pst_of():
            j = s.pi % 8
            s.pi += 1
            b = s.ps[j % 2]
            r = j // 2
            return b.t[:, r * 128:(r + 1) * 128], b.k

        def ret_unit(h, q_aps, k_aps, v_aps, sg_aps, rk, S_ap, S_k, out_aps, out_ks, eg_col, ed_col, elast_f):
            pq, kq = pst_of()
            op("pe", lambda e: e.matmul(pq, k_aps[0], q_aps[0], start=True, stop=False), r=rk, w=[kq])
            op("pe", lambda e: e.matmul(pq, k_aps[1], q_aps[1], start=False, stop=True), r=rk, w=[kq])
            op("dve", lambda e: e.tensor_tensor(s.qk.t[:], pq, retc.t[:, h, 0:128], ALU.mult), r=[kq, retc.k], w=[s.qk.k])
            px = fullbank()
            for cidx in range(4):
                op("pe", lambda e, cidx=cidx: e.matmul(px.t[:, cidx * 128:(cidx + 1) * 128], v_aps[cidx], ident, start=True, stop=True),
                   r=rk + [CK], w=[px.k])
            op("act", lambda e: e.activation(out=s.xtok.t[:], in_=px.t[:], func=AF.Copy), r=[px.k], w=[s.xtok.k])
            pk = fullbank()
            for cidx in range(2):
                op("pe", lambda e, cidx=cidx: e.matmul(pk.t[:, cidx * 128:(cidx + 1) * 128], k_aps[cidx], ident, start=True, stop=True),
                   r=rk + [CK], w=[pk.k])
            op("dve", lambda e: e.tensor_scalar(s.kd.t[:], pk.t[:, 0:256], ed_col, None, ALU.mult), r=[pk.k, retc.k], w=[s.kd.k])
            po = fullbank()
            op("pe", lambda e: e.matmul(po.t[:], s.qk.t[:], s.xtok.t[:], start=True, stop=True), r=[s.qk.k, s.xtok.k], w=[po.k])
            pqs = fullbank()
            op("pe", lambda e: e.matmul(pqs.t[:], q_aps[0], S_ap[:, 0, :], start=True, stop=False), r=rk + [S_k], w=[pqs.k])
            op("pe", lambda e: e.matmul(pqs.t[:], q_aps[1], S_ap[:, 1, :], start=False, stop=True), r=rk + [S_k], w=[pqs.k])
            op("dve", lambda e: e.tensor_scalar(s.t512b.t[:], pqs.t[:], eg_col, None, ALU.mult), r=[pqs.k, retc.k], w=[s.t512b.k])
            op("dve", lambda e: e.tensor_tensor(s.otok.t[:], po.t[:], s.t512b.t[:], ALU.add), r=[po.k, s.t512b.k], w=[s.otok.k])
            op("dve", lambda e: e.reduce_sum(s.ss.t[:, 0:1], s.otok.t[:], AX.X), r=[s.otok.k], w=[s.ss.k])
            op("dve", lambda e: e.tensor_scalar(s.ss.t[:, 0:1], s.ss.t[:, 0:1], -1.0 / 512, None, ALU.mult), r=[s.ss.k], w=[s.ss.k])
            op("dve", lambda e: e.tensor_scalar(s.otok.t[:], s.otok.t[:], s.ss.t[:, 0:1], None, ALU.add), r=[s.otok.k, s.ss.k], w=[s.otok.k])
            op("act", lambda e: e.activation(out=s.t512b.t[:], in_=s.otok.t[:], func=AF.Square), r=[s.otok.k], w=[s.t512b.k])
            op("dve", lambda e: e.reduce_sum(s.ss.t[:, 1:2], s.t512b.t[:], AX.X), r=[s.t512b.k], w=[s.ss.k])
            op("act", lambda e: e.activation(out=s.ss.t[:, 2:3], in_=s.ss.t[:, 1:2], func=AF.Sqrt, bias=EPS, scale=1.0 / 512), r=[s.ss.k], w=[s.ss.k])
            op("dve", lambda e: e.reciprocal(s.ss.t[:, 2:3], s.ss.t[:, 2:3]), r=[s.ss.k], w=[s.ss.k])
            op("dve", lambda e: e.scalar_tensor_tensor(s.otok.t[:], s.otok.t[:], s.ss.t[:, 2:3], rngb.t[:, h, :], ALU.mult, ALU.mult),
               r=[s.otok.k, s.ss.k, rngb.k], w=[s.otok.k])
            pt = fullbank()
            for cidx in range(4):
                op("pe", lambda e, cidx=cidx: e.matmul(pt.t[:, cidx * 128:(cidx + 1) * 128], s.otok.t[:, cidx * 128:(cidx + 1) * 128], ident,
                                                       start=True, stop=True), r=[s.otok.k, CK], w=[pt.k])
            for cidx in range(4):
                op("dve", lambda e, cidx=cidx: e.tensor_tensor(out_aps[cidx], pt.t[:, cidx * 128:(cidx + 1) * 128], sg_aps[cidx], ALU.mult),
                   r=[pt.k] + rk, w=[out_ks[cidx]])
            for cidx in range(2):
                pS = fullbank()
                op("pe", lambda e, cidx=cidx: e.matmul(pS.t[:], s.kd.t[:, cidx * 128:(cidx + 1) * 128], s.xtok.t[:], start=True, stop=True),
                   r=[s.kd.k, s.xtok.k], w=[pS.k])
                op("dve", lambda e, cidx=cidx: e.tensor_scalar(s.t512b.t[:], S_ap[:, cidx, :], elast_f, None, ALU.mult), r=[S_k, s.t512b.k], w=[s.t512b.k])
                op("dve", lambda e, cidx=cidx: e.tensor_tensor(S_ap[:, cidx, :], pS.t[:], s.t512b.t[:], ALU.add), r=[pS.k, s.t512b.k], w=[S_k])

        def out_proj(g, T, t0):
            sq = self.nsq
            for m in range(8):
                pst = proj_ps()
                for half in range(2):
                    buf = load_w(m * 128, w_out, row0=half * 1024)
                    for k in range(8):
                        kk = half * 8 + k
                        op("pe", lambda e, k=k, kk=kk: e.matmul(pst.t[:, :T], buf.t[:, k, :], oT.t[:, kk, :T],
                                                                start=(kk == 0), stop=(kk == 15)), r=[buf.k, oTk[kk]], w=[pst.k])
                if g < 4:
                    op("dve", lambda e, m=m: e.scalar_tensor_tensor(
                        xT.t[:, m, t0:t0 + T], pst.t[:, :T], modT.t[:, 1, 16 + m, 0:1], xT.t[:, m, t0:t0 + T], ALU.mult, ALU.add),
                        r=[pst.k, modT.k, xk[m][g]], w=[xk[m][g]])
                else:
                    op("dve", lambda e, m=m: e.tensor_tensor(sq.t[:, :T], pst.t[:, :T], modT.t[:, 1, 16 + m, 1:17], ALU.mult),
                       r=[pst.k, modT.k], w=[sq.k])
                    op("dve", lambda e, m=m: e.tensor_tensor(xT.t[:, m, t0:t0 + T], xT.t[:, m, t0:t0 + T], sq.t[:, :T], ALU.add),
                       r=[sq.k, xk[m][g]], w=[xk[m][g]])

        def head_inputs(h, T):
            rope(h * 256, T, qT, 1.0)
            rope(1024 + h * 256, T, kT, 1.0 / 16)
            for cidx in range(4):
                pst = project(2048 + h * 512 + cidx * 128, T)
                op("act", lambda e, cidx=cidx: e.activation(out=vT[cidx].t[:, :T], in_=pst.t[:, :T], func=AF.Copy), r=[pst.k], w=[vT[cidx].k])
                pst = project(4096 + h * 512 + cidx * 128, T)
                op("act", lambda e, cidx=cidx: e.activation(out=sgT[cidx].t[:, :T], in_=pst.t[:, :T], func=AF.Silu), r=[pst.k], w=[sgT[cidx].k])

        allk = [t.k for t in qT + kT + vT + sgT]
        for g in range(min(KG, 4)):
            t0, T = self.groups[g]
            self.norm_mod_layer(1, 0, g, hm)
            dma("sp", lcs, lambda e: e.dma_start(out=cosb.t[:, :T], in_=D["cosT"][:, t0:t0 + T]), w=[cosb.k])
            dma("sp", lcs, lambda e: e.dma_start(out=sinb.t[:, :T], in_=D["sinT"][:, t0:t0 + T]), w=[sinb.k])
            for h in range(4):
                head_inputs(h, T)
                for cc in range(T // 128):
                    cs = slice(cc * 128, (cc + 1) * 128)
                    ret_unit(h, [qT[i].t[:, cs] for i in range(2)], [kT[i].t[:, cs] for i in range(2)],
                             [vT[i].t[:, cs] for i in range(4)], [sgT[i].t[:, cs] for i in range(4)], allk,
                             Sr.t[:, h, :, :], Srk[h], [oT.t[:, 4 * h + i, cs] for i in range(4)], [oTk[4 * h + i] for i in range(4)],
                             retc.t[:, h, 128:129], retc.t[:, h, 129:130], gam[h] ** 128)
            out_proj(g, T, t0)
        op_l = P.lane()
        dma("sp", op_l, lambda e: e.dma_start(out=D["ret_p"].rearrange("h (c p) e -> p h c e", p=128), in_=Sr.t[:]), r=Srk)

        if KG >= 5:
            g, (t0, T) = 4, self.groups[4]
            self.norm_mod_layer(1, 0, 4, hm)
            dma("sp", lcs, lambda e: e.dma_start(out=cosb.t[:, :T], in_=D["cosT"][:, t0:t0 + T]), w=[cosb.k])
            dma("sp", lcs, lambda e: e.dma_start(out=sinb.t[:, :T], in_=D["sinT"][:, t0:t0 + T]), w=[sinb.k])
            pads = [sb("rpad%d" % i, [128, 128]) for i in range(12)]
            for p_ in pads:
                op("pool", lambda e, p_=p_: e.memset(p_.t[:], 0.0), w=[p_.k])
            otmp = sb("rotmp", [128, 4, 128])
            otk = [Tk() for _ in range(4)]
            NSB = 2
            Sts = [sb("Srs%d" % i, [128, 2, 512]) for i in range(NSB)]
            lsi = [P.lane() for _ in range(NSB)]
            lso = [P.lane() for _ in range(NSB)]
            cnt = 0
            srcs = qT + kT + vT + sgT
            for h in range(4):
                head_inputs(h, T)
                for smp in range(KSMP):
                    i = cnt % NSB
                    cnt += 1
                    St = Sts[i]
                    dma("sp", lsi[i], lambda e: e.dma_start(out=St.t[:], in_=D["st_ret"][smp, h, :, :].rearrange("(c p) e -> p c e", p=128)), w=[St.k])
                    for j in range(12):
                        op("pool", lambda e, j=j: e.tensor_copy(pads[j].t[:, 0:1], srcs[j].t[:, smp:smp + 1]), r=[srcs[j].k], w=[pads[j].k])
                    ret_unit(h, [pads[0].t[:], pads[1].t[:]], [pads[2].t[:], pads[3].t[:]], [pads[4 + i_].t[:] for i_ in range(4)],
                             [pads[8 + i_].t[:] for i_ in range(4)], [p_.k for p_ in pads], St.t[:], St.k,
                             [otmp.t[:, i_, :] for i_ in range(4)], otk, retc.t[:, h, 130:131], retc.t[:, h, 131:132], gam[h])
                    dma("sp", lso[i], lambda e: e.dma_start(out=D["ret_s"][smp, h, :, :].rearrange("(c p) e -> p c e", p=128), in_=St.t[:]), r=[St.k])
                    for i_ in range(4):
                        op("pool", lambda e, i_=i_: e.tensor_copy(oT.t[:, 4 * h + i_, smp:smp + 1], otmp.t[:, i_, 0:1]), r=[otk[i_]], w=[oTk[4 * h + i_]])
            out_proj(4, T, t0)
        P.barrier()
        es.close()


    def peer(self, l, D):
        nc, P = self.nc, self.P
        op, dma = P.op, P.dma
        c = self.c
        ident, ones, iota, CK = c["ident"], c["ones"], c["iota"], c["CK"]
        banks = self.banks
        xT, xk, modT, modA = self.xT, self.xk, self.modT, self.modA
        es = contextlib.ExitStack()
        sb = lambda name, shape, dt=F32: self.sb(name, shape, dt, es=es)
        KPG = int(os.environ.get('KPG', '9'))
        KPI = int(os.environ.get('KPI', '64'))
        nm = "p%d_" % l

        lp = P.lane()
        keysT = sb(nm + "keysT", [128, 2, 128])
        dma("sp", lp, lambda e: e.dma_start(out=keysT.t[:], in_=D["keysT"][l].rearrange("p d n -> d p n")), w=[keysT.k])
        hp = sb(nm + "hp", [128, 8, 256], BF16)
        NWB = 4
        wbuf = [sb(nm + "wq%d" % i, [128, 8, 128], BF16) for i in range(NWB)]
        wl = [P.lane() for _ in range(NWB)]
        wi = [0]
        ub = [sb(nm + "ub%d" % i, [128, 8, 256], BF16) for i in range(2)]
        vb = [sb(nm + "vb%d" % i, [128, 2, 1024], BF16) for i in range(2)]
        ul = [P.lane() for _ in range(2)]
        vl = [P.lane() for _ in range(2)]
        qc = [sb(nm + "qc%d" % i, [128, 128]) for i in range(2)]
        sc = [sb(nm + "sc%d" % i, [128, 128]) for i in range(2)]
        work = sb(nm + "work", [128, 128])
        sv = sb(nm + "sv", [128, 16, 16])
        si = sb(nm + "si", [128, 16, 16], U32)
        sif = sb(nm + "sif", [128, 16, 16])
        cand = sb(nm + "cand", [128, 256])
        cwork = sb(nm + "cwork", [128, 256])
        cv = sb(nm + "cv", [128, 8, 16])
        ci = sb(nm + "ci", [128, 8, 16], U32)
        au = sb(nm + "au", [128, 8, 16], U32)
        bu = sb(nm + "bu", [128, 8, 16], U32)
        af = sb(nm + "af", [128, 8, 16])
        bf = sb(nm + "bf", [128, 8, 16])
        eq = sb(nm + "eq", [128, 16, 16])
        iidx = sb(nm + "iidx", [128, 128])
        jidx = sb(nm + "jidx", [128, 128])
        gate = sb(nm + "gate", [128, 8, 16])
        zz = sb(nm + "zz", [128, 8])
        iT = sb(nm + "iT", [128, 128])
        jT = sb(nm + "jT", [128, 128])
        gT = sb(nm + "gT", [128, 128])
        OHj = sb(nm + "OHj", [128, 16, 128], BF16)
        OHi = sb(nm + "OHi", [128, 16, 128], BF16)
        Wall = sb(nm + "Wall", [128, 256, 128], BF16)
        zs = [sb(nm + "zs%d" % i, [128, 256]) for i in range(2)]
        Ab = [sb(nm + "A%d" % i, [128, 256], BF16) for i in range(2)]
        ysb = sb(nm + "ysb", [128, 1024])
        stmp = sb(nm + "stmp", [128, 128])
        pring = [0]

        def proj_ps():
            b = banks[pring[0] % 2]
            pring[0] += 1
            return b

        groups = [(i * 256, 256) for i in range(8)] + [(2048, 16)]
        ui = [0]
        for gi, (t0, T) in enumerate(groups[:KPG]):
            g5 = 4 if t0 >= 2048 else t0 // 512
            sh0 = 24
            if g5 < 4:
                self.norm_mod(g5, hp, lambda k: modA.t[:, l, 1, k, 0:1], lambda k: modT.t[:, l, sh0 + k, 0:1], t0, T)
            else:
                self.norm_mod(g5, hp, None, None, t0, T,
                              samp_A=lambda k: modA.t[:, l, 1, k, 1:17], samp_B=lambda k: modT.t[:, l, sh0 + k, 1:17])
            def route(tt0, T):
                for c16 in range(16):
                    i_ = wi[0] % NWB
                    wi[0] += 1
                    wq = wbuf[i_]
                    dma("pool", wl[i_], lambda e: e.dma_start(
                        out=wq.t[:], in_=D["w_q"][l, :, c16 * 128:(c16 + 1) * 128].rearrange("(k p) c -> p k c", p=128)), w=[wq.k])
                    pq = proj_ps()
                    for k in range(8):
                        op("pe", lambda e, k=k: e.matmul(pq.t[:, :T], wq.t[:, k, :], hp.t[:, k, tt0:tt0 + T], start=(k == 0), stop=(k == 7)),
                           r=[wq.k, hp.k], w=[pq.k])
                    q_ = qc[c16 % 2]
                    op("act", lambda e: e.activation(out=q_.t[:, :T], in_=pq.t[:, :T], func=AF.Copy), r=[pq.k], w=[q_.k])
                    pss = banks[2 + c16 % 2]
                    op("pe", lambda e: e.matmul(pss.t[:T, 0:128], q_.t[:, :T], keysT.t[:, c16 % 2, :], start=True, stop=True),
                       r=[q_.k, keysT.k], w=[pss.k])
                    s_ = sc[c16 % 2]
                    op("dve", lambda e: e.tensor_copy(s_.t[:T, :], pss.t[:T, 0:128]), r=[pss.k], w=[s_.k])
                    op("dve", lambda e: e.max(out=sv.t[:T, c16, 0:8], in_=s_.t[:T, :]), r=[s_.k], w=[sv.k])
                    op("dve", lambda e: e.match_replace(out=work.t[:T, :], in_to_replace=sv.t[:T, c16, 0:8], in_values=s_.t[:T, :], imm_value=-1e30),
                       r=[s_.k, sv.k], w=[work.k])
                    op("dve", lambda e: e.max(out=sv.t[:T, c16, 8:16], in_=work.t[:T, :]), r=[work.k, sv.k], w=[sv.k])
                    op("dve", lambda e: e.max_index(out=si.t[:T, c16, 0:8], in_max=sv.t[:T, c16, 0:8], in_values=s_.t[:T, :]), r=[s_.k, sv.k], w=[si.k])
                    op("dve", lambda e: e.max_index(out=si.t[:T, c16, 8:16], in_max=sv.t[:T, c16, 8:16], in_values=s_.t[:T, :]), r=[s_.k, sv.k, si.k], w=[si.k])
                op("dve", lambda e: e.tensor_copy(sif.t[:T], si.t[:T]), r=[si.k], w=[sif.k])
                for h in range(8):
                    op("dve", lambda e: e.tensor_tensor(cand.t[:T, :].rearrange("p (a b) -> p a b", a=16),
                                                        sv.t[:T, 2 * h, :].unsqueeze(2).broadcast_to([T, 16, 16]),
                                                        sv.t[:T, 2 * h + 1, :].unsqueeze(1).broadcast_to([T, 16, 16]), ALU.add), r=[sv.k], w=[cand.k])
                    op("dve", lambda e: e.max(out=cv.t[:T, h, 0:8], in_=cand.t[:T, :]), r=[cand.k], w=[cv.k])
                    op("dve", lambda e: e.match_replace(out=cwork.t[:T, :], in_to_replace=cv.t[:T, h, 0:8], in_values=cand.t[:T, :], imm_value=-1e30),
                       r=[cand.k, cv.k], w=[cwork.k])
                    op("dve", lambda e: e.max(out=cv.t[:T, h, 8:16], in_=cwork.t[:T, :]), r=[cwork.k, cv.k], w=[cv.k])
                    op("dve", lambda e: e.max_index(out=ci.t[:T, h, 0:8], in_max=cv.t[:T, h, 0:8], in_values=cand.t[:T, :]), r=[cand.k, cv.k], w=[ci.k])
                    op("dve", lambda e: e.max_index(out=ci.t[:T, h, 8:16], in_max=cv.t[:T, h, 8:16], in_values=cand.t[:T, :]), r=[cand.k, cv.k, ci.k], w=[ci.k])
                op("dve", lambda e: e.tensor_single_scalar(au.t[:T], ci.t[:T], 4, ALU.logical_shift_right), r=[ci.k], w=[au.k])
                op("dve", lambda e: e.tensor_single_scalar(bu.t[:T], ci.t[:T], 15, ALU.bitwise_and), r=[ci.k], w=[bu.k])
                op("dve", lambda e: e.tensor_copy(af.t[:T], au.t[:T]), r=[au.k], w=[af.k])
                op("dve", lambda e: e.tensor_copy(bf.t[:T], bu.t[:T]), r=[bu.k], w=[bf.k])
                for h in range(8):
                    for (xf, col, dst) in ((af, 2 * h, iidx), (bf, 2 * h + 1, jidx)):
                        op("dve", lambda e: e.tensor_tensor(eq.t[:T], xf.t[:T, h, :].unsqueeze(2).broadcast_to([T, 16, 16]),
                                                            iota[:T, 0:16].unsqueeze(1).broadcast_to([T, 16, 16]), ALU.is_equal),
                           r=[xf.k, CK], w=[eq.k])
                        op("dve", lambda e: e.tensor_tensor(eq.t[:T], eq.t[:T], sif.t[:T, col, :].unsqueeze(1).broadcast_to([T, 16, 16]), ALU.mult),
                           r=[eq.k, sif.k], w=[eq.k])
                        op("dve", lambda e: e.reduce_sum(dst.t[:T, h * 16:(h + 1) * 16], eq.t[:T], AX.X), r=[eq.k], w=[dst.k])
                op("dve", lambda e: e.tensor_tensor(gate.t[:T], cv.t[:T], cv.t[:T, :, 0:1].broadcast_to([T, 8, 16]), ALU.subtract), r=[cv.k], w=[gate.k])
                op("act", lambda e: e.activation(out=gate.t[:T], in_=gate.t[:T], func=AF.Exp), r=[gate.k], w=[gate.k])
                op("dve", lambda e: e.reduce_sum(zz.t[:T, :], gate.t[:T], AX.X), r=[gate.k], w=[zz.k])
                op("dve", lambda e: e.reciprocal(zz.t[:T, :], zz.t[:T, :]), r=[zz.k], w=[zz.k])
                op("dve", lambda e: e.tensor_tensor(gate.t[:T], gate.t[:T], zz.t[:T, :].unsqueeze(2).broadcast_to([T, 8, 16]), ALU.mult),
                   r=[gate.k, zz.k], w=[gate.k])
                for src_ap, src_k, dstT in ((iidx.t[:T, :], iidx.k, iT), (jidx.t[:T, :], jidx.k, jT),
                                            (gate.t[:T].rearrange("p h k -> p (h k)"), gate.k, gT)):
                    pt = banks[3]
                    op("pe", lambda e: e.matmul(pt.t[:, :T], src_ap, ident[:T, :T], start=True, stop=True), r=[src_k, CK], w=[pt.k])
                    op("dve", lambda e: e.tensor_copy(dstT.t[:, :T], pt.t[:, :T]), r=[pt.k], w=[dstT.k])
                for sub in range((T + 15) // 16):
                    ts = sub * 16
                    n = min(16, T - ts)
                    op("dve", lambda e: e.tensor_tensor(OHj.t[:, :n, :], iota.unsqueeze(1).broadcast_to([128, n, 128]),
                                                        jT.t[:, ts:ts + n].unsqueeze(2).broadcast_to([128, n, 128]), ALU.is_equal),
                       r=[jT.k, CK], w=[OHj.k])
                    op("dve", lambda e: e.tensor_tensor(OHi.t[:, :n, :], iota.unsqueeze(1).broadcast_to([128, n, 128]),
                                                         iT.t[:, ts:ts + n].unsqueeze(2).broadcast_to([128, n, 128]), ALU.is_equal),
                       r=[iT.k, CK], w=[OHi.k])
                    op("pool", lambda e: e.tensor_tensor(OHi.t[:, :n, :], OHi.t[:, :n, :],
                                                         gT.t[:, ts:ts + n].unsqueeze(2).broadcast_to([128, n, 128]), ALU.mult),
                       r=[OHi.k, gT.k], w=[OHi.k])
                    for q4 in range((n + 3) // 4):
                        pw = banks[2 + q4 % 2]
                        n4 = min(4, n - q4 * 4)
                        for t in range(n4):
                            tt = q4 * 4 + t
                            op("pe", lambda e, t=t, tt=tt: e.matmul(pw.t[:, t * 128:(t + 1) * 128], OHj.t[:, tt, :], OHi.t[:, tt, :], start=True, stop=True),
                               r=[OHj.k, OHi.k], w=[pw.k])
                        tg = tt0 + ts + q4 * 4
                        eng = ("act", "dve")[q4 % 2]
                        if eng == "act":
                            op("act", lambda e: e.activation(out=Wall.t[:, tg:tg + n4, :].rearrange("p t i -> p (t i)"), in_=pw.t[:, :n4 * 128], func=AF.Copy),
                               r=[pw.k], w=[Wall.k])
                        else:
                            op("dve", lambda e: e.tensor_copy(Wall.t[:, tg:tg + n4, :].rearrange("p t i -> p (t i)"), pw.t[:, :n4 * 128]), r=[pw.k], w=[Wall.k])

            tiles = [(tt0, min(128, T - tt0)) for tt0 in range(0, T, 128)]
            for tt0, nt in tiles:
                route(tt0, nt)
            py = [banks[4], banks[5], banks[6], banks[7]]
            for i2 in range(KPI):
                b_ = ui[0] % 2
                ui[0] += 1
                u_, v_ = ub[b_], vb[b_]
                dma("pool", ul[b_], lambda e: e.dma_start(
                    out=u_.t[:], in_=D["uT"][l, :, i2 * 256:(i2 + 1) * 256].rearrange("(k p) e -> p k e", p=128)), w=[u_.k])
                dma("pool", vl[b_], lambda e: e.dma_start(
                    out=v_.t[:], in_=D["pv"][l, i2 * 256:(i2 + 1) * 256, :].rearrange("(ii p) d -> p ii d", p=128)), w=[v_.k])
                for ii in range(2):
                    i = i2 * 2 + ii
                    pz = proj_ps()
                    for k in range(8):
                        op("pe", lambda e, k=k: e.matmul(pz.t[:, :T], u_.t[:, k, ii * 128:(ii + 1) * 128], hp.t[:, k, :T], start=(k == 0), stop=(k == 7)),
                           r=[u_.k, hp.k], w=[pz.k])
                    z_ = zs[i % 2]
                    a_ = Ab[i % 2]
                    op("act", lambda e: e.activation(out=z_.t[:, :T], in_=pz.t[:, :T], func=AF.Gelu), r=[pz.k], w=[z_.k])
                    op("dve", lambda e: e.tensor_tensor(a_.t[:, :T], z_.t[:, :T], Wall.t[:, :T, i], ALU.mult), r=[z_.k, Wall.k], w=[a_.k])
                    for ti, (tt0, nt) in enumerate(tiles):
                        for half in range(2):
                            pyb = py[ti * 2 + half]
                            op("pe", lambda e, half=half: e.matmul(pyb.t[:nt, :], a_.t[:, tt0:tt0 + nt], v_.t[:, ii, half * 512:(half + 1) * 512],
                                                                   start=(i == 0), stop=(i == 2 * KPI - 1)), r=[a_.k, v_.k], w=[pyb.k])
            for ti, (tt0, nt) in enumerate(tiles):
                for half in range(2):
                    pyb = py[ti * 2 + half]
                    op("act", lambda e, half=half: e.activation(out=ysb.t[:nt, half * 512:(half + 1) * 512], in_=pyb.t[:nt, :], func=AF.Copy),
                       r=[pyb.k], w=[ysb.k])
                c0 = t0 + tt0
                for m in range(8):
                    pt = banks[2 + m % 2]
                    op("pe", lambda e, m=m: e.matmul(pt.t[:, :nt], ysb.t[:nt, m * 128:(m + 1) * 128], ident[:nt, :nt], start=True, stop=True),
                       r=[ysb.k, CK], w=[pt.k])
                    if g5 < 4:
                        op("dve", lambda e, m=m: e.scalar_tensor_tensor(
                            xT.t[:, m, c0:c0 + nt], pt.t[:, :nt], modT.t[:, l, 40 + m, 0:1], xT.t[:, m, c0:c0 + nt], ALU.mult, ALU.add),
                            r=[pt.k, modT.k, xk[m][g5]], w=[xk[m][g5]])
                    else:
                        op("dve", lambda e, m=m: e.tensor_tensor(stmp.t[:, :nt], pt.t[:, :nt], modT.t[:, l, 40 + m, 1:17], ALU.mult),
                           r=[pt.k, modT.k], w=[stmp.k])
                        op("dve", lambda e, m=m: e.tensor_tensor(xT.t[:, m, c0:c0 + nt], xT.t[:, m, c0:c0 + nt], stmp.t[:, :nt], ALU.add),
                           r=[stmp.k, xk[m][g5]], w=[xk[m][g5]])
        P.barrier()
        es.close()


def _consts():
    c = np.zeros((128, 6, 128), np.float32)
    p = np.arange(128)[:, None]
    f = np.arange(128)[None, :]
    c[:, 0] = (p == f)
    c[:, 1] = (p <= f)
    c[:, 2] = 1.0
    c[:, 3] = np.where(f >= p, BIG, 0.0)
    c[:, 4] = np.where(f < p, -BIG, 0.0)
    c[:, 5] = f
    return c


def fm(v, k):
    return np.ascontiguousarray(np.asarray(v, np.float32).reshape(k, 128).T)


def make_in_maps(inp, ncores=8):
    f32 = np.float32
    shared = {}
    shared["ada_w"] = np.ascontiguousarray(inp["ada_w"], f32)
    shared["ada_b"] = np.ascontiguousarray(np.stack([fm(inp["ada_b"][l], 48) for l in range(2)], 1))
    shared["n1g"] = np.ascontiguousarray(np.stack([fm(inp["norm1_g"][l], 8) for l in range(2)], 1))
    shared["n2g"] = np.ascontiguousarray(np.stack([fm(inp["norm2_g"][l], 8) for l in range(2)], 1))
    shared["fg"] = fm(inp["final_g"], 8)
    shared["consts"] = _consts()
    shared["ab_w_in"] = np.ascontiguousarray(inp["ab_w_in"][0], f32)
    cw = inp["ab_conv_w"][0]
    shared["cw"] = np.ascontiguousarray(cw.reshape(4, 36, 128).transpose(2, 1, 0))
    shared["cb"] = fm(inp["ab_conv_b"][0], 36)
    auxp = np.zeros((1, 5, 32), f32)
    auxp[0, 0, 8:16] = inp["gdn_dt_bias"][0]
    auxp[0, 0, 16:32] = inp["ssm_dt_bias"][0]
    auxp[0, 1, 8:16] = inp["gdn_a_log"][0]
    auxp[0, 1, 16:32] = inp["ssm_a_log"][0]
    auxp[0, 2, 16:32] = inp["ssm_d"][0]
    shared["auxp"] = auxp
    shared["gng"] = np.ascontiguousarray(inp["gdn_norm_g"][0].reshape(128, 1), f32)
    shared["sng"] = fm(inp["ssm_norm_g"][0], 8)
    shared["ab_w_out"] = np.ascontiguousarray(inp["ab_w_out"][0], f32)
    shared["ret_w_in"] = np.ascontiguousarray(inp["ret_w_in"][0], f32)
    shared["ret_w_out"] = np.ascontiguousarray(inp["ret_w_out"][0], f32)
    shared["w_q"] = np.ascontiguousarray(inp["peer_w_q"], f32)
    shared["keysT"] = np.ascontiguousarray(np.asarray(inp["peer_keys"], f32).transpose(0, 1, 3, 2))
    shared["uT"] = np.ascontiguousarray(np.asarray(inp["peer_u"], f32).transpose(0, 2, 1))
    shared["pv"] = np.ascontiguousarray(inp["peer_v"], f32)
    shared["rng"] = np.ascontiguousarray(inp["ret_norm_g"], f32).reshape(1, 4, 512)
    retc = np.zeros((128, 4, 132), np.float64)
    ii = np.arange(128)
    for h in range(4):
        gm = 1.0 - 2.0 ** (-5.0 - h)
        d = ii[None, :] - ii[:, None]
        retc[:, h, 0:128] = np.where(d >= 0, gm ** np.maximum(d, 0), 0.0)
        retc[:, h, 128] = gm ** (ii + 1)
        retc[:, h, 129] = gm ** (127 - ii)
        retc[:, h, 130] = gm
        retc[:, h, 131] = 1.0
    shared["retc"] = retc.astype(f32)
    inv = 10000.0 ** (-np.arange(0, 256, 2, dtype=np.float64) / 256)
    pos = np.concatenate([np.arange(2048, dtype=np.float64), np.full(16, 16384.0)])
    ang = (pos[None, :].astype(f32) * inv[:, None].astype(f32)).astype(f32)
    shared["cosT"] = np.cos(ang.astype(np.float64)).astype(f32)
    shared["sinT"] = np.sin(ang.astype(np.float64)).astype(f32)
    maps = []
    for b in range(ncores):
        m = dict(shared)
        x_all = np.concatenate([inp["x_prompt"][b], inp["x_sample"][16 * b:16 * b + 16, 0]], 0)
        m["xT"] = np.ascontiguousarray(x_all.reshape(NT, 8, 128).transpose(2, 1, 0))
        c_all = np.concatenate([inp["c_prompt"][b:b + 1], inp["c_sample"][16 * b:16 * b + 16]], 0)
        m["cT"] = np.ascontiguousarray(c_all.reshape(17, 8, 128).transpose(2, 1, 0))
        sc = inp["state_conv"][0, 16 * b:16 * b + 16]
        m["st_conv"] = np.ascontiguousarray(sc.reshape(16, 3, 36, 128).transpose(3, 2, 0, 1))
        m["st_gdn"] = np.ascontiguousarray(inp["state_gdn"][0, 16 * b:16 * b + 16])
        m["st_ssm"] = np.ascontiguousarray(inp["state_ssm"][0, 16 * b:16 * b + 16])
        m["st_ret"] = np.ascontiguousarray(inp["state_ret"][0, 16 * b:16 * b + 16])
        maps.append(m)
    return maps


_CACHE = {}


def run(inp, ncores=8, dbg=(), stages=("l0", "p0", "l1", "p1")):
    bld = Builder(dbg=dbg, stages=stages)
    bld.build()
    maps = make_in_maps(inp, ncores)
    maps = [{k: v for k, v in m.items() if k in bld.in_names} for m in maps]
    res = run_bass_kernel_spmd(bld.nc, maps, core_ids=list(range(ncores)))
    return res.results


def kernel(**inputs):
    inp = {k: np.asarray(v) for k, v in inputs.items()}
    n = 8
    bld = Builder(dbg=("xb0", "xa1"))
    bld.build()
    maps = make_in_maps(inp, n)
    maps = [{k: v for k, v in m.items() if k in bld.in_names} for m in maps]
    res = run_bass_kernel_spmd(bld.nc, maps, core_ids=list(range(n))).results
    f32 = np.float32
    y_prompt = np.zeros((8, 2048, 1024), f32)
    y_sample = np.zeros((128, 1, 1024), f32)
    conv_p = np.zeros((1, 8, 3, 4608), f32)
    gdn_p = np.zeros((1, 8, 8, 128, 128), f32)
    ssm_p = np.zeros((1, 8, 16, 128, 64), f32)
    ret_p = np.zeros((1, 8, 4, 256, 512), f32)
    conv_s = np.zeros((1, 128, 3, 4608), f32)
    gdn_s = np.zeros((1, 128, 8, 128, 128), f32)
    ssm_s = np.zeros((1, 128, 16, 128, 64), f32)
    ret_s = np.zeros((1, 128, 4, 256, 512), f32)
    for b in range(n):
        r = res[b]
        y = np.asarray(r["yT"]).transpose(2, 1, 0).reshape(NT, 1024)
        y_prompt[b] = y[:2048]
        y_sample[16 * b:16 * b + 16, 0] = y[2048:]
        conv_p[0, b] = np.asarray(r["conv_p"]).transpose(2, 1, 0).reshape(3, 4608)
        gdn_p[0, b] = r["gdn_p"]
        ssm_p[0, b] = r["ssm_p"]
        ret_p[0, b] = r["ret_p"]
        conv_s[0, 16 * b:16 * b + 16] = np.asarray(r["conv_s"]).transpose(2, 3, 1, 0).reshape(16, 3, 4608)
        gdn_s[0, 16 * b:16 * b + 16] = r["gdn_s"]
        ssm_s[0, 16 * b:16 * b + 16] = r["ssm_s"]
        ret_s[0, 16 * b:16 * b + 16] = r["ret_s"]
    return (y_prompt, y_sample, conv_p, gdn_p, ssm_p, ret_p, conv_s, gdn_s, ssm_s, ret_s)
```

```python
import contextlib
import os
import numpy as np
import concourse.bass as bass
import concourse.mybir as mybir
from concourse.bass_utils import run_bass_kernel_spmd

F32 = mybir.dt.float32
BF16 = mybir.dt.bfloat16
I32 = mybir.dt.int32
U32 = mybir.dt.uint32
AF = mybir.ActivationFunctionType
ALU = mybir.AluOpType
AX = mybir.AxisListType

NT = 2064
NP = 2048
NS = 16
EPS = 1e-6
BIG = 30000.0


class Tk:
    __slots__ = ("name", "w", "rs")

    def __init__(self, name=""):
        self.name = name
        self.w = []
        self.rs = []


class Lane:
    __slots__ = ("sem", "cum", "key")

    def __init__(self, sem, key):
        self.sem = sem
        self.cum = 0
        self.key = key


class _Rec:
    __slots__ = ("call",)

    def __init__(self):
        self.call = None

    def __getattr__(self, name):
        def f(*a, **kw):
            self.call = (name, a, kw)
            return self
        return f


def _eager(fn):
    r = _Rec()
    fn(r)
    name, a, kw = r.call
    return lambda e: getattr(e, name)(*a, **kw)


class Prog:
    ENGS = ("pe", "dve", "act", "pool", "sp")

    def __init__(self, nc):
        self.nc = nc
        self.ops = {e: [] for e in self.ENGS}
        self.cnt = {e: 0 for e in self.ENGS}
        self.sems = {}
        self.waited = {e: {} for e in self.ENGS}
        self.cur = None
        self._ctx = []
        for e in ("pe", "dve", "act", "pool"):
            self.sems[e] = self._sem("p_" + e)
        self.lanes = []

    def _sem(self, name):
        g = self.nc.semaphore(name)
        s = g.__enter__()
        self._ctx.append(g)
        return s

    def lane(self):
        key = "L%d" % len(self.lanes)
        self.sems[key] = self._sem(key)
        ln = Lane(self.sems[key], key)
        self.lanes.append(ln)
        return ln

    def _need(self, eng, deps):
        wd = self.waited[eng]
        best = {}
        for k, v in deps:
            if eng == "pe" and k == "pe":
                continue
            if wd.get(k, 0) >= v:
                continue
            if best.get(k, 0) < v:
                best[k] = v
        for k, v in best.items():
            wd[k] = v
            sem = self.sems[k]
            self.ops[eng].append(lambda e, sem=sem, v=v: e.wait_ge(sem, v))

    @staticmethod
    def _deps_for(r, w):
        deps = []
        for t in r:
            deps.extend(t.w)
        for t in w:
            deps.extend(t.w)
            deps.extend(t.rs)
        return deps

    def op(self, eng, fn, r=(), w=()):
        fn = _eager(fn)
        if self.cur is not None:
            self.cur.append(("op", (eng, fn, tuple(r), tuple(w))))
            return
        self._op(eng, fn, r, w)

    def _op(self, eng, fn, r=(), w=()):
        self._need(eng, self._deps_for(r, w))
        self.cnt[eng] += 1
        n = self.cnt[eng]
        sem = self.sems[eng]
        self.ops[eng].append(lambda e, fn=fn, sem=sem: fn(e).then_inc(sem, 1))
        tag = (eng, n)
        for t in r:
            t.rs.append(tag)
        for t in w:
            t.w = [tag]
            t.rs = []

    def dma(self, q, lane, fn, r=(), w=(), group=False):
        fn = _eager(fn)
        if self.cur is not None:
            self.cur.append(("dma", (q, lane, fn, tuple(r), tuple(w), group)))
            return
        self._dma(q, lane, fn, r, w, group)

    @contextlib.contextmanager
    def record(self):
        outer = self.cur
        th = []
        self.cur = th
        try:
            yield th
        finally:
            self.cur = outer

    @contextlib.contextmanager
    def atomic(self):
        if self.cur is None:
            yield
            return
        outer = self.cur
        blk = []
        self.cur = blk
        try:
            yield
        finally:
            self.cur = outer
            outer.append(("blk", blk))

    def _dispatch(self, item):
        kind, a = item
        if kind == "op":
            self._op(*a)
        elif kind == "dma":
            self._dma(*a)
        else:
            for it in a:
                self._dispatch(it)

    def merge(self, threads):
        assert self.cur is None
        idx = [0] * len(threads)
        live = True
        while live:
            live = False
            for i, th in enumerate(threads):
                if idx[i] < len(th):
                    self._dispatch(th[idx[i]])
                    idx[i] += 1
                    live = True

    def _dma(self, q, lane, fn, r=(), w=(), group=False):
        deps = self._deps_for(r, w)
        if group:
            deps = [d for d in deps if d[0] != lane.key]
        elif lane.cum > 0:
            deps.append((lane.key, lane.cum))
        self._need(q, deps)
        lane.cum += 16
        sem = lane.sem
        self.ops[q].append(lambda e, fn=fn, sem=sem: fn(e).then_inc(sem, 16))
        tag = (lane.key, lane.cum)
        for t in r:
            t.rs.append(tag)
        for t in w:
            if group:
                t.w = [x for x in t.w if x[0] != lane.key] + [tag]
            else:
                t.w = [tag]
            t.rs = []

    def group_finish(self, lane, tks):
        tag = (lane.key, lane.cum)
        for t in tks:
            t.w = [tag if x[0] == lane.key else x for x in t.w]
            t.rs = [tag if x[0] == lane.key else x for x in t.rs]

    def wait_all(self, eng):
        deps = [(e, self.cnt[e]) for e in ("pe", "dve", "act", "pool") if self.cnt[e] > 0]
        deps += [(ln.key, ln.cum) for ln in self.lanes if ln.cum > 0]
        self._need(eng, deps)

    def barrier(self):
        for e in self.ENGS:
            self.wait_all(e)

    def emit(self):
        with self.nc.Block() as block:
            def mk(name):
                lst = self.ops[name]

                def body(e):
                    for f in lst:
                        f(e)
                return body
            block.tensor(mk("pe"))
            block.vector(mk("dve"))
            block.scalar(mk("act"))
            block.gpsimd(mk("pool"))
            block.sync(mk("sp"))

    def close(self):
        for g in reversed(self._ctx):
            g.__exit__(None, None, None)
        self._ctx = []


class B:
    __slots__ = ("t", "k")

    def __init__(self, t, name=""):
        self.t = t
        self.k = Tk(name)


def interleave(gens, width):
    gens = list(gens)
    active = []
    while gens or active:
        while gens and len(active) < width:
            active.append(gens.pop(0))
        nxt = []
        for g in active:
            try:
                next(g)
                nxt.append(g)
            except StopIteration:
                pass
        active = nxt


class Builder:
    def __init__(self, dbg=(), stages=("l0", "p0", "l1", "p1")):
        self.nc = bass.Bass("TRN2", target_bir_lowering=False)
        self.P = Prog(self.nc)
        self.es = contextlib.ExitStack()
        self.dbg = set(dbg)
        self.stages = stages
        self.in_names = []
        self.out_names = []
        self.rr = 0

    def din(self, name, shape, dt=F32):
        self.in_names.append(name)
        return self.nc.dram_tensor(name, list(shape), dt, kind="ExternalInput").ap()

    def dout(self, name, shape, dt=F32):
        self.out_names.append(name)
        return self.nc.dram_tensor(name, list(shape), dt, kind="ExternalOutput").ap()

    def dscratch(self, name, shape, dt=F32):
        return self.nc.dram_tensor(name, list(shape), dt, kind="Internal").ap()

    def sb(self, name, shape, dt=F32, es=None):
        t = (es or self.es).enter_context(self.nc.sbuf_tensor(name, list(shape), dt))
        return B(t, name)

    def psb(self, name, shape, dt=F32):
        t = self.es.enter_context(self.nc.psum_tensor(name, list(shape), dt))
        return B(t, name)

    def ew(self):
        self.rr += 1
        return ("dve", "pool")[self.rr % 2]

    def build(self):
        nc, P = self.nc, self.P
        op, dma = P.op, P.dma
        xT_d = self.din("xT", [128, 8, NT])
        cT_d = self.din("cT", [128, 8, 17])
        ada_w = self.din("ada_w", [2, 1024, 6144])
        ada_b = self.din("ada_b", [128, 2, 48])
        n1g = self.din("n1g", [128, 2, 8])
        n2g = self.din("n2g", [128, 2, 8])
        fg = self.din("fg", [128, 8])
        consts_d = self.din("consts", [128, 6, 128])
        w_in = self.din("ab_w_in", [1024, 6688])
        cw_d = self.din("cw", [128, 36, 4])
        cb_d = self.din("cb", [128, 36])
        auxp_d = self.din("auxp", [1, 5, 32])
        gng_d = self.din("gng", [128, 1])
        sng_d = self.din("sng", [128, 8])
        w_out = self.din("ab_w_out", [2048, 1024])
        D = dict(w_in=w_in, w_out=w_out, cw=cw_d, cb=cb_d, auxp=auxp_d, gng=gng_d, sng=sng_d)
        D["st_conv"] = self.din("st_conv", [128, 36, 16, 3])
        D["st_gdn"] = self.din("st_gdn", [16, 8, 128, 128])
        D["st_ssm"] = self.din("st_ssm", [16, 16, 128, 64])
        D["ret_w_in"] = self.din("ret_w_in", [1024, 6144])
        D["ret_w_out"] = self.din("ret_w_out", [2048, 1024])
        D["rng"] = self.din("rng", [1, 4, 512])
        D["retc"] = self.din("retc", [128, 4, 132])
        D["cosT"] = self.din("cosT", [128, NT])
        D["sinT"] = self.din("sinT", [128, NT])
        D["st_ret"] = self.din("st_ret", [16, 4, 256, 512])
        D["w_q"] = self.din("w_q", [2, 1024, 2048])
        D["keysT"] = self.din("keysT", [2, 2, 128, 128])
        D["uT"] = self.din("uT", [2, 1024, 16384])
        D["pv"] = self.din("pv", [2, 16384, 1024])
        yT_d = self.dout("yT", [128, 8, NT])
        D["ret_p"] = self.dout("ret_p", [4, 256, 512])
        D["ret_s"] = self.dout("ret_s", [16, 4, 256, 512])
        D["conv_p"] = self.dout("conv_p", [128, 36, 3])
        D["gdn_p"] = self.dout("gdn_p", [8, 128, 128])
        D["ssm_p"] = self.dout("ssm_p", [16, 128, 64])
        D["conv_s"] = self.dout("conv_s", [128, 36, 16, 3])
        D["gdn_s"] = self.dout("gdn_s", [16, 8, 128, 128])
        D["ssm_s"] = self.dout("ssm_s", [16, 16, 128, 64])
        dbg_d = {}
        for nm in self.dbg:
            dbg_d[nm] = self.dout("dbg_" + nm, [128, 8, NT])

        xT = self.sb("xT_sb", [128, 8, NT])
        xk = [[Tk("x%d_%d" % (m, g)) for g in range(5)] for m in range(8)]
        cT = self.sb("cT_sb", [128, 8, 17])
        modT = self.sb("modT", [128, 2, 48, 17])
        consts = self.sb("consts_sb", [128, 6, 128])
        n1 = self.sb("n1_sb", [128, 2, 8])
        n2 = self.sb("n2_sb", [128, 2, 8])
        fgs = self.sb("fg_sb", [128, 8])
        adab = self.sb("adab_sb", [128, 2, 48])
        ident = consts.t[:, 0, :]
        triU = consts.t[:, 1, :]
        ones = consts.t[:, 2, :]
        maskA = consts.t[:, 3, :]
        maskB = consts.t[:, 4, :]
        iota = consts.t[:, 5, :]
        CK = consts.k

        banks = [self.psb("ps%d" % i, [128, 512]) for i in range(8)]
        self.banks = banks

        lc = P.lane()
        first = [True]

        def cload(dst, src_ap, q="sp"):
            dma(q, lc, lambda e: e.dma_start(out=dst.t[:], in_=src_ap), w=[dst.k], group=not first[0])
            first[0] = False

        cload(consts, consts_d[:, :, :])
        cload(cT, cT_d[:, :, :])
        cload(n1, n1g[:, :, :])
        cload(n2, n2g[:, :, :])
        cload(fgs, fg[:, :])
        cload(adab, ada_b[:, :, :])
        P.group_finish(lc, [consts.k, cT.k, n1.k, n2.k, fgs.k, adab.k])
        lx = P.lane()
        for m in range(8):
            dma("sp", lx, lambda e, m=m: e.dma_start(out=xT.t[:, m, :], in_=xT_d[:, m, :]),
                w=xk[m], group=(m > 0))
        P.group_finish(lx, [t for row in xk for t in row])

        op("act", lambda e: e.activation(out=cT.t[:], in_=cT.t[:], func=AF.Silu), r=[cT.k], w=[cT.k])
        with contextlib.ExitStack() as es:
            NAB = 4
            abuf = [self.sb("adaw%d" % i, [128, 8, 128], es=es) for i in range(NAB)]
            la = [P.lane() for _ in range(NAB)]
            it = 0
            for l in range(2):
                for half in range(2):
                    pst = banks[half]
                    for mm in range(24):
                        bi = it % NAB
                        it += 1
                        buf = abuf[bi]
                        col0 = (half * 24 + mm) * 128
                        dma("sp", la[bi], lambda e, l=l, col0=col0, buf=buf: e.dma_start(
                            out=buf.t[:], in_=ada_w[l, :, col0:col0 + 128].rearrange("(k p) c -> p k c", p=128)),
                            w=[buf.k])
                        for kk in range(8):
                            op("pe", lambda e, mm=mm, kk=kk, buf=buf, pst=pst: e.matmul(
                                pst.t[:, mm * 17:(mm + 1) * 17], buf.t[:, kk, :], cT.t[:, kk, :],
                                start=(kk == 0), stop=(kk == 7)), r=[buf.k, cT.k], w=[pst.k])
                    op("dve", lambda e, l=l, half=half, pst=pst: e.tensor_tensor(
                        modT.t[:, l, half * 24:(half + 1) * 24, :],
                        pst.t[:, 0:408].rearrange("p (m j) -> p m j", j=17),
                        adab.t[:, l, half * 24:(half + 1) * 24].unsqueeze(2).broadcast_to([128, 24, 17]), ALU.add),
                        r=[pst.k, adab.k], w=[modT.k])
            P.barrier()

        modA = self.sb("modA", [128, 2, 2, 8, 17])
        for l in range(2):
            for wh in range(2):
                sc0 = 8 + 24 * wh
                gsrc = (n1, n2)[wh]
                op("dve", lambda e, l=l, wh=wh, sc0=sc0: e.tensor_scalar(
                    modA.t[:, l, wh, :, :], modT.t[:, l, sc0:sc0 + 8, :], 1.0, None, ALU.add),
                    r=[modT.k], w=[modA.k])
                op("dve", lambda e, l=l, wh=wh, gsrc=gsrc: e.tensor_tensor(
                    modA.t[:, l, wh, :, :], modA.t[:, l, wh, :, :],
                    gsrc.t[:, l, :].unsqueeze(2).broadcast_to([128, 8, 17]), ALU.mult),
                    r=[modA.k, gsrc.k], w=[modA.k])

        self.groups = [(g * 512, 512) for g in range(4)] + [(2048, 16)]

        nsq = self.sb("nsq", [128, 512])
        nrs = self.sb("nrs", [128, 512])
        ntmp = self.sb("ntmp", [128, 512])

        def norm_mod(g, hm, A_ap_fn, B_ap_fn, t0, T, samp_A=None, samp_B=None, out_dt_tile=None):
            pst = banks[7]
            for k in range(8):
                op("act", lambda e, k=k: e.activation(out=nsq.t[:, :T], in_=xT.t[:, k, t0:t0 + T], func=AF.Square),
                   r=[xk[k][g]], w=[nsq.k])
                op("pe", lambda e, k=k: e.matmul(pst.t[:, :T], ones, nsq.t[:, :T], start=(k == 0), stop=(k == 7)),
                   r=[nsq.k, CK], w=[pst.k])
            op("act", lambda e: e.activation(out=nrs.t[:, :T], in_=pst.t[:, :T], func=AF.Sqrt, bias=EPS, scale=1.0 / 1024),
               r=[pst.k], w=[nrs.k])
            op("dve", lambda e: e.reciprocal(nrs.t[:, :T], nrs.t[:, :T]), r=[nrs.k], w=[nrs.k])
            for k in range(8):
                if samp_A is None:
                    op("dve", lambda e, k=k: e.scalar_tensor_tensor(
                        ntmp.t[:, :T], xT.t[:, k, t0:t0 + T], A_ap_fn(k), nrs.t[:, :T], ALU.mult, ALU.mult),
                        r=[xk[k][g], nrs.k, modA.k, fgs.k], w=[ntmp.k])
                    if B_ap_fn is None:
                        op("act", lambda e, k=k: e.activation(out=hm.t[:, k, :T], in_=ntmp.t[:, :T], func=AF.Copy),
                           r=[ntmp.k], w=[hm.k])
                    else:
                        op("act", lambda e, k=k: e.activation(out=hm.t[:, k, :T], in_=ntmp.t[:, :T], func=AF.Identity,
                                                             bias=B_ap_fn(k), scale=1.0),
                           r=[ntmp.k, modT.k], w=[hm.k])
                else:
                    op("dve", lambda e, k=k: e.tensor_tensor(ntmp.t[:, :T], xT.t[:, k, t0:t0 + T], nrs.t[:, :T], ALU.mult),
                       r=[xk[k][g], nrs.k], w=[ntmp.k])
                    op("dve", lambda e, k=k: e.tensor_tensor(ntmp.t[:, :T], ntmp.t[:, :T], samp_A(k), ALU.mult),
                       r=[ntmp.k, modA.k], w=[ntmp.k])
                    op("dve", lambda e, k=k: e.tensor_tensor(hm.t[:, k, :T], ntmp.t[:, :T], samp_B(k), ALU.add),
                       r=[ntmp.k, modT.k], w=[hm.k])

        def norm_mod_layer(l, wh, g, hm):
            t0, T = self.groups[g]
            sh0 = 24 * wh
            if g < 4:
                norm_mod(g, hm, lambda k: modA.t[:, l, wh, k, 0:1], lambda k: modT.t[:, l, sh0 + k, 0:1], t0, T)
            else:
                norm_mod(g, hm, None, None, t0, T,
                         samp_A=lambda k: modA.t[:, l, wh, k, 1:17], samp_B=lambda k: modT.t[:, l, sh0 + k, 1:17])

        self.norm_mod = norm_mod
        self.nsq, self.nrs = nsq, nrs
        self.norm_mod_layer = norm_mod_layer
        self.xT, self.xk, self.modT, self.modA = xT, xk, modT, modA
        self.c = dict(ident=ident, triU=triU, ones=ones, maskA=maskA, maskB=maskB, iota=iota, CK=CK)

        def dump(nm):
            if nm in self.dbg:
                P.barrier()
                ld = P.lane()
                dma("sp", ld, lambda e: e.dma_start(out=dbg_d[nm][:, :, :], in_=xT.t[:]),
                    r=[t for row in xk for t in row])
                P.barrier()

        if "l0" in self.stages:
            self.layer0(D)
        dump("xa0")
        if "p0" in self.stages:
            self.peer(0, D)
        dump("xb0")
        if "l1" in self.stages:
            self.layer1(D)
        dump("xa1")
        if "p1" in self.stages:
            self.peer(1, D)
        dump("xb1")

        yb = self.sb("ybuf_fin", [128, 8, 512])
        ly = P.lane()
        for g in range(5):
            t0, T = self.groups[g]
            norm_mod(g, yb, lambda k: fgs.t[:, k:k + 1], None, t0, T)
            dma("sp", ly, lambda e, t0=t0, T=T: e.dma_start(out=yT_d[:, :, t0:t0 + T], in_=yb.t[:, :, :T]), r=[yb.k])
        P.wait_all("sp")
        P.emit()
        P.close()
        self.es.close()

    def layer0(self, D):
        nc, P = self.nc, self.P
        op, dma = P.op, P.dma
        c = self.c
        ident, triU, ones, maskA, maskB, CK = c["ident"], c["triU"], c["ones"], c["maskA"], c["maskB"], c["CK"]
        banks = self.banks
        xT, xk, modT, modA = self.xT, self.xk, self.modT, self.modA
        w_in, w_out = D["w_in"], D["w_out"]
        es = contextlib.ExitStack()
        sb = lambda name, shape, dt=F32: self.sb(name, shape, dt, es=es)
        KG = int(os.environ.get('KG', '5'))
        KH = int(os.environ.get('KH', '8'))
        KSS = int(os.environ.get('KSS', '1'))
        KSMP = int(os.environ.get('KSMP', '16'))

        lp = P.lane()
        cw = sb("cw_sb", [128, 36, 4])
        cb = sb("cb_sb", [128, 36])
        auxp = sb("auxp_sb", [128, 5, 32])
        gng = sb("gng_sb", [128, 1])
        sng = sb("sng_sb", [128, 8])
        dma("sp", lp, lambda e: e.dma_start(out=cw.t[:], in_=D["cw"][:, :, :]), w=[cw.k])
        dma("sp", lp, lambda e: e.dma_start(out=cb.t[:], in_=D["cb"][:, :]), w=[cb.k], group=True)
        dma("sp", lp, lambda e: e.dma_start(out=auxp.t[:], in_=D["auxp"][0:1, :, :].broadcast_to([128, 5, 32])), w=[auxp.k], group=True)
        dma("sp", lp, lambda e: e.dma_start(out=gng.t[:], in_=D["gng"][:, :]), w=[gng.k], group=True)
        dma("sp", lp, lambda e: e.dma_start(out=sng.t[:], in_=D["sng"][:, :]), w=[sng.k], group=True)
        P.group_finish(lp, [cw.k, cb.k, auxp.k, gng.k, sng.k])
        op("act", lambda e: e.activation(out=auxp.t[:, 1, 8:32], in_=auxp.t[:, 1, 8:32], func=AF.Exp), r=[auxp.k], w=[auxp.k])
        op("dve", lambda e: e.tensor_scalar(auxp.t[:, 1, 8:32], auxp.t[:, 1, 8:32], -1.0, None, ALU.mult), r=[auxp.k], w=[auxp.k])

        Sg = sb("Sg", [128, 8, 128])
        Sgk = [Tk("Sg%d" % h) for h in range(8)]
        Ss = sb("Ss", [128, 16, 64])
        Ssk = [Tk("Ss%d" % g) for g in range(2)]
        halo = sb("halo", [128, 36, 3])
        halok = [Tk("halo%d" % i) for i in range(36)]
        op("pool", lambda e: e.memset(Sg.t[:], 0.0), w=Sgk)
        op("pool", lambda e: e.memset(Ss.t[:], 0.0), w=Ssk)
        op("pool", lambda e: e.memset(halo.t[:], 0.0), w=halok)

        hm = sb("hm", [128, 8, 512], BF16)
        NWB = 3
        wbuf = [sb("wb%d" % i, [128, 8, 128], BF16) for i in range(NWB)]
        wl = [P.lane() for _ in range(NWB)]
        wi = [0]

        def load_w(col0, src, ncol=128, row0=0):
            i = wi[0] % NWB
            wi[0] += 1
            buf = wbuf[i]
            dma("pool", wl[i], lambda e: e.dma_start(
                out=buf.t[:, :, :ncol], in_=src[row0:row0 + 1024, col0:col0 + ncol].rearrange("(k p) c -> p k c", p=128)), w=[buf.k])
            return buf

        pring = [0]

        def proj_ps():
            b = banks[pring[0] % 2]
            pring[0] += 1
            return b

        def project(col0, T, ncol=128):
            buf = load_w(col0, w_in, ncol=ncol)
            pst = proj_ps()
            with P.atomic():
                for k in range(8):
                    op("pe", lambda e, k=k: e.matmul(pst.t[:ncol, :T], buf.t[:, k, :ncol], hm.t[:, k, :T],
                                                     start=(k == 0), stop=(k == 7)), r=[buf.k, hm.k], w=[pst.k])
            return pst

        NPRE = 2
        prehk = [Tk("preh%d" % i) for i in range(NPRE)]
        prei = [0]
        acc_t = [sb("cacc%d" % i, [128, 512]) for i in range(2)]
        acci = [0]
        pres = [sb("pres%d" % i, [128, 16, 4]) for i in range(2)]
        preshk = [Tk() for i in range(2)]
        presi = [0]
        lcs_in = [P.lane() for _ in range(2)]
        lcs_out = P.lane()

        def conv_chunk(ch, T, dst_ap, dst_k, col0):
            pst = project(col0, T)
            p = pre[prei[0] % NPRE]
            ph = prehk[prei[0] % NPRE]
            prei[0] += 1
            op("act", lambda e: e.activation(out=p.t[:, 3:3 + T], in_=pst.t[:, :T], func=AF.Copy), r=[pst.k], w=[p.k])
            op("pool", lambda e: e.tensor_copy(p.t[:, 0:3], halo.t[:, ch, :]), r=[halok[ch]], w=[ph])
            a = acc_t[acci[0] % 2]
            acci[0] += 1
            op("dve", lambda e: e.tensor_scalar(a.t[:, :T], p.t[:, 0:T], cw.t[:, ch, 0:1], cb.t[:, ch:ch + 1], ALU.mult, ALU.add),
               r=[p.k, ph, cw.k, cb.k], w=[a.k])
            for i in range(1, 4):
                op("dve", lambda e, i=i: e.scalar_tensor_tensor(a.t[:, :T], p.t[:, i:i + T], cw.t[:, ch, i:i + 1], a.t[:, :T],
                                                                ALU.mult, ALU.add), r=[p.k, ph, a.k], w=[a.k])
            op("pool", lambda e: e.tensor_copy(halo.t[:, ch, :], p.t[:, T:T + 3]), r=[p.k, ph], w=[halok[ch]])
            op("act", lambda e: e.activation(out=dst_ap, in_=a.t[:, :T], func=AF.Silu), r=[a.k], w=[dst_k])

        def conv_chunk_s(ch, dst_ap, dst_k, col0):
            pst = project(col0, 16)
            j = presi[0] % 2
            presi[0] += 1
            p, ph = pres[j], preshk[j]
            dma("sp", lcs_in[j], lambda e: e.dma_start(out=p.t[:, :, 0:3], in_=D["st_conv"][:, ch, :, :]), w=[ph])
            op("act", lambda e: e.activation(out=p.t[:, :, 3], in_=pst.t[:, :16], func=AF.Copy), r=[pst.k], w=[p.k])
            a = acc_t[acci[0] % 2]
            acci[0] += 1
            op("dve", lambda e: e.tensor_scalar(a.t[:, :16], p.t[:, :, 0], cw.t[:, ch, 0:1], cb.t[:, ch:ch + 1], ALU.mult, ALU.add),
               r=[p.k, ph, cw.k, cb.k], w=[a.k])
            for i in range(1, 4):
                op("dve", lambda e, i=i: e.scalar_tensor_tensor(a.t[:, :16], p.t[:, :, i], cw.t[:, ch, i:i + 1], a.t[:, :16],
                                                                ALU.mult, ALU.add), r=[p.k, ph, a.k], w=[a.k])
            dma("sp", lcs_out, lambda e: e.dma_start(out=D["conv_s"][:, ch, :, :], in_=p.t[:, :, 1:4]), r=[p.k, ph])
            op("act", lambda e: e.activation(out=dst_ap, in_=a.t[:, :16], func=AF.Silu), r=[a.k], w=[dst_k])

        qT = sb("qT", [128, 512])
        kT = sb("kT", [128, 512])
        vT = sb("vT", [128, 512])
        sgT = sb("sgT", [128, 512])
        sq = self.nsq
        rs = self.nrs
        oT = sb("oT", [128, 16, 512], BF16)
        oTk = [Tk("oT%d" % j) for j in range(16)]
        auxT = sb("auxT", [32, 512])
        aux = [sb("aux%d" % cc, [128, 32]) for cc in range(4)]
        beta = [sb("beta%d" % cc, [128, 8]) for cc in range(4)]
        gg = [sb("gg%d" % cc, [128, 24]) for cc in range(4)]
        dtt = [sb("dt%d" % cc, [128, 16]) for cc in range(4)]
        gc = [sb("gc%d" % cc, [128, 24]) for cc in range(4)]
        ngc = [sb("ngc%d" % cc, [128, 24]) for cc in range(4)]
        egc = [sb("egc%d" % cc, [128, 24]) for cc in range(4)]
        elast = [sb("elast%d" % cc, [128, 24]) for cc in range(4)]
        edl = [sb("edl%d" % cc, [128, 24]) for cc in range(4)]
        begc = [sb("begc%d" % cc, [128, 8]) for cc in range(4)]
        dte = [sb("dte%d" % cc, [128, 16]) for cc in range(4)]
        sptmp = sb("sptmp", [128, 32])

        def aux_compute(cc, src_ap, src_k, sample=False):
            pa = banks[7]
            op("pe", lambda e: e.matmul(pa.t[:, 0:32], src_ap, ident[:32, :32], start=True, stop=True), r=[src_k, CK], w=[pa.k])
            op("dve", lambda e: e.tensor_copy(aux[cc].t[:], pa.t[:, 0:32]), r=[pa.k], w=[aux[cc].k])
            op("act", lambda e: e.activation(out=beta[cc].t[:], in_=aux[cc].t[:, 0:8], func=AF.Sigmoid), r=[aux[cc].k], w=[beta[cc].k])
            op("dve", lambda e: e.tensor_tensor(sptmp.t[:, 8:32], aux[cc].t[:, 8:32], auxp.t[:, 0, 8:32], ALU.add),
               r=[aux[cc].k, auxp.k], w=[sptmp.k])
            op("act", lambda e: e.activation(out=sptmp.t[:, 8:32], in_=sptmp.t[:, 8:32], func=AF.Exp), r=[sptmp.k], w=[sptmp.k])
            op("act", lambda e: e.activation(out=sptmp.t[:, 8:32], in_=sptmp.t[:, 8:32], func=AF.Ln, bias=1.0), r=[sptmp.k], w=[sptmp.k])
            op("dve", lambda e: e.tensor_copy(dtt[cc].t[:], sptmp.t[:, 16:32]), r=[sptmp.k], w=[dtt[cc].k])
            op("dve", lambda e: e.tensor_tensor(gg[cc].t[:], sptmp.t[:, 8:32], auxp.t[:, 1, 8:32], ALU.mult),
               r=[sptmp.k, auxp.k], w=[gg[cc].k])
            if sample:
                op("dve", lambda e: e.tensor_scalar(gg[cc].t[:], gg[cc].t[:], ident[:, 0:1], None, ALU.mult), r=[gg[cc].k, CK], w=[gg[cc].k])
            op("pe", lambda e: e.matmul(pa.t[:, 32:56], triU, gg[cc].t[:], start=True, stop=True), r=[gg[cc].k, CK], w=[pa.k])
            op("pe", lambda e: e.matmul(pa.t[:, 64:88], ones, gg[cc].t[:], start=True, stop=True), r=[gg[cc].k, CK], w=[pa.k])
            op("dve", lambda e: e.tensor_copy(gc[cc].t[:], pa.t[:, 32:56]), r=[pa.k], w=[gc[cc].k])
            op("dve", lambda e: e.tensor_scalar(ngc[cc].t[:], pa.t[:, 32:56], -1.0, None, ALU.mult), r=[pa.k], w=[ngc[cc].k])
            op("act", lambda e: e.activation(out=egc[cc].t[:], in_=pa.t[:, 32:56], func=AF.Exp), r=[pa.k], w=[egc[cc].k])
            op("act", lambda e: e.activation(out=elast[cc].t[:], in_=pa.t[:, 64:88], func=AF.Exp), r=[pa.k], w=[elast[cc].k])
            op("dve", lambda e: e.tensor_tensor(edl[cc].t[:], pa.t[:, 64:88], gc[cc].t[:], ALU.subtract), r=[pa.k, gc[cc].k], w=[edl[cc].k])
            op("act", lambda e: e.activation(out=edl[cc].t[:], in_=edl[cc].t[:], func=AF.Exp), r=[edl[cc].k], w=[edl[cc].k])
            op("dve", lambda e: e.tensor_tensor(begc[cc].t[:], beta[cc].t[:], egc[cc].t[:, 0:8], ALU.mult),
               r=[beta[cc].k, egc[cc].k], w=[begc[cc].k])
            op("dve", lambda e: e.tensor_tensor(dte[cc].t[:], dtt[cc].t[:], edl[cc].t[:, 8:24], ALU.mult),
               r=[dtt[cc].k, edl[cc].k], w=[dte[cc].k])

        def l2norm(buf, T, scale):
            pst = banks[7]
            op("act", lambda e: e.activation(out=sq.t[:, :T], in_=buf.t[:, :T], func=AF.Square), r=[buf.k], w=[sq.k])
            op("pe", lambda e: e.matmul(pst.t[:, :T], ones, sq.t[:, :T], start=True, stop=True), r=[sq.k, CK], w=[pst.k])
            op("act", lambda e: e.activation(out=rs.t[:, :T], in_=pst.t[:, :T], func=AF.Sqrt, bias=EPS, scale=1.0), r=[pst.k], w=[rs.k])
            op("dve", lambda e: e.reciprocal(rs.t[:, :T], rs.t[:, :T]), r=[rs.k], w=[rs.k])
            op("dve", lambda e: e.scalar_tensor_tensor(buf.t[:, :T], buf.t[:, :T], scale, rs.t[:, :T], ALU.mult, ALU.mult),
               r=[buf.k, rs.k], w=[buf.k])

        class S:
            pass
        s = S()
        for nm in ("tmp", "dec", "decT", "A", "Bm", "A2", "B2", "Pm", "P2", "kbg", "kd", "vb", "nwT", "u", "qk", "t1", "o", "on",
                   "GT", "Btok", "MT", "Bd"):
            setattr(s, nm, sb("u_%s" % nm, [128, 128]))
        s.ss = sb("u_ss", [128, 2])
        s.ps = [banks[2], banks[3]]
        s.pi = 0
        s2 = S()
        for nm in ("tmp", "dec", "decT", "A", "Bm", "A2", "B2", "Pm", "P2", "kbg", "kd", "vb", "nwT", "u", "qk", "t1", "o", "on"):
            setattr(s2, nm, sb("v_%s" % nm, [128, 128]))
        s2.ss = sb("v_ss", [128, 2])
        s2.ps = [banks[4], banks[5]]
        s2.pi = 0
        scr = [s, s2]
        qkv = [(qT, kT, vT, sgT), tuple(sb("t2_%s" % nm, [128, 512]) for nm in ("q", "k", "v", "sg"))]
        s.xtok = sb("u_xtok", [128, 512])
        s.t512 = sb("u_t512", [128, 512])
        s.t512b = sb("u_t512b", [128, 512])
        s.otok = sb("u_otok", [128, 512])
        fring = [0]

        def fullbank():
            b = banks[4 + fring[0] % 3]
            fring[0] += 1
            return b

        def pst_of(s):
            j = s.pi % 8
            s.pi += 1
            b = s.ps[j % 2]
            r = j // 2
            return b.t[:, r * 128:(r + 1) * 128], b.k

        def gdn_unit(s, h, cc, q_ap, k_ap, v_ap, sg_ap, rk, S_ap, S_k, out_ap, out_k, single=False):
            g_b = gg[cc].t[:, h:h + 1].broadcast_to([128, 128])
            p2, k2 = pst_of(s)
            with P.atomic():
                op("pe", lambda e: e.matmul(p2, ident, maskB, start=True, stop=False), r=[CK], w=[k2])
                op("pe", lambda e: e.matmul(p2, g_b, triU, start=False, stop=True), r=[gg[cc].k, CK], w=[k2])
            op("act", lambda e: e.activation(out=s.decT.t[:], in_=p2, func=AF.Exp, bias=ngc[cc].t[:, h:h + 1], scale=1.0),
               r=[k2, ngc[cc].k], w=[s.decT.k])
            if not single:
                p1, k1 = pst_of(s)
                with P.atomic():
                    op("pe", lambda e: e.matmul(p1, ident, maskA, start=True, stop=False), r=[CK], w=[k1])
                    op("pe", lambda e: e.matmul(p1, g_b, triU, start=False, stop=True), r=[gg[cc].k, CK], w=[k1])
                op("act", lambda e: e.activation(out=s.dec.t[:], in_=p1, func=AF.Exp, bias=gc[cc].t[:, h:h + 1], scale=-1.0),
                   r=[k1, gc[cc].k], w=[s.dec.k])
                p3, k3 = pst_of(s)
                op("pe", lambda e: e.matmul(p3, k_ap, k_ap, start=True, stop=True), r=rk, w=[k3])
                op("dve", lambda e: e.scalar_tensor_tensor(s.A.t[:], p3, beta[cc].t[:, h:h + 1], s.dec.t[:], ALU.mult, ALU.mult),
                   r=[k3, beta[cc].k, s.dec.k], w=[s.A.k])
                p4, k4 = pst_of(s)
                op("pe", lambda e: e.matmul(p4, s.A.t[:], ident, start=True, stop=True), r=[s.A.k, CK], w=[k4])
                op("act", lambda e: e.activation(out=s.Bm.t[:], in_=p4, func=AF.Copy), r=[k4], w=[s.Bm.k])
                op("dve", lambda e: e.tensor_tensor(s.Pm.t[:], ident, s.Bm.t[:], ALU.subtract), r=[s.Bm.k, CK], w=[s.Pm.k])
                Ac, Bc, An, Bn = s.A, s.Bm, s.A2, s.B2
                Pc, Pn = s.Pm, s.P2
                for lvl in range(6):
                    pa_, ka = pst_of(s)
                    op("pe", lambda e, Ac=Ac, Bc=Bc, pa_=pa_: e.matmul(pa_, Bc.t[:], Ac.t[:], start=True, stop=True),
                       r=[Ac.k, Bc.k], w=[ka])
                    op("act", lambda e, An=An, pa_=pa_: e.activation(out=An.t[:], in_=pa_, func=AF.Copy), r=[ka], w=[An.k])
                    if lvl < 5:
                        pb, kb = pst_of(s)
                        op("pe", lambda e, Ac=Ac, Bc=Bc, pb=pb: e.matmul(pb, Ac.t[:], Bc.t[:], start=True, stop=True),
                           r=[Ac.k, Bc.k], w=[kb])
                        op("dve", lambda e, Bn=Bn, pb=pb: e.tensor_copy(Bn.t[:], pb), r=[kb], w=[Bn.k])
                    pc, kc = pst_of(s)
                    with P.atomic():
                        op("pe", lambda e, Pc=Pc, pc=pc: e.matmul(pc, ident, Pc.t[:], start=True, stop=False), r=[Pc.k, CK], w=[kc])
                        op("pe", lambda e, Pc=Pc, An=An, pc=pc: e.matmul(pc, An.t[:], Pc.t[:], start=False, stop=True),
                           r=[Pc.k, An.k], w=[kc])
                    op("dve", lambda e, Pn=Pn, pc=pc: e.tensor_copy(Pn.t[:], pc), r=[kc], w=[Pn.k])
                    Ac, An = An, Ac
                    Bc, Bn = Bn, Bc
                    Pc, Pn = Pn, Pc
                TT_ap, TT_k = Pc.t[:], Pc.k
            else:
                TT_ap, TT_k = ident, CK
            p5, k5 = pst_of(s)
            op("pe", lambda e: e.matmul(p5, k_ap, ident, start=True, stop=True), r=rk + [CK], w=[k5])
            op("dve", lambda e: e.tensor_scalar(s.kbg.t[:], p5, begc[cc].t[:, h:h + 1], None, ALU.mult), r=[k5, begc[cc].k], w=[s.kbg.k])
            op("dve", lambda e: e.tensor_scalar(s.kd.t[:], p5, edl[cc].t[:, h:h + 1], None, ALU.mult), r=[k5, edl[cc].k], w=[s.kd.k])
            p6, k6 = pst_of(s)
            op("pe", lambda e: e.matmul(p6, v_ap, ident, start=True, stop=True), r=rk + [CK], w=[k6])
            op("dve", lambda e: e.tensor_scalar(s.vb.t[:], p6, beta[cc].t[:, h:h + 1], None, ALU.mult), r=[k6, beta[cc].k], w=[s.vb.k])
            p7, k7 = pst_of(s)
            op("pe", lambda e: e.matmul(p7, s.kbg.t[:], TT_ap, start=True, stop=True), r=[s.kbg.k, TT_k], w=[k7])
            op("act", lambda e: e.activation(out=s.nwT.t[:], in_=p7, func=AF.Identity, scale=-1.0), r=[k7], w=[s.nwT.k])
            p8, k8 = pst_of(s)
            with P.atomic():
                op("pe", lambda e: e.matmul(p8, TT_ap, s.vb.t[:], start=True, stop=False), r=[TT_k, s.vb.k], w=[k8])
                op("pe", lambda e: e.matmul(p8, s.nwT.t[:], S_ap, start=False, stop=True), r=[s.nwT.k, S_k], w=[k8])
            op("act", lambda e: e.activation(out=s.u.t[:], in_=p8, func=AF.Copy), r=[k8], w=[s.u.k])
            p9, k9 = pst_of(s)
            op("pe", lambda e: e.matmul(p9, k_ap, q_ap, start=True, stop=True), r=rk, w=[k9])
            op("dve", lambda e: e.tensor_tensor(s.qk.t[:], p9, s.decT.t[:], ALU.mult), r=[k9, s.decT.k], w=[s.qk.k])
            p10, k10 = pst_of(s)
            op("pe", lambda e: e.matmul(p10, q_ap, S_ap, start=True, stop=True), r=rk + [S_k], w=[k10])
            op("dve", lambda e: e.tensor_scalar(s.t1.t[:], p10, egc[cc].t[:, h:h + 1], None, ALU.mult), r=[k10, egc[cc].k], w=[s.t1.k])
            p11, k11 = pst_of(s)
            op("pe", lambda e: e.matmul(p11, s.qk.t[:], s.u.t[:], start=True, stop=True), r=[s.qk.k, s.u.k], w=[k11])
            op("dve", lambda e: e.tensor_tensor(s.o.t[:], p11, s.t1.t[:], ALU.add), r=[k11, s.t1.k], w=[s.o.k])
            p12, k12 = pst_of(s)
            op("pe", lambda e: e.matmul(p12, s.kd.t[:], s.u.t[:], start=True, stop=True), r=[s.kd.k, s.u.k], w=[k12])
            op("dve", lambda e: e.tensor_scalar(s.tmp.t[:], S_ap, elast[cc].t[:, h:h + 1], None, ALU.mult),
               r=[elast[cc].k, S_k], w=[s.tmp.k])
            op("dve", lambda e: e.tensor_tensor(S_ap, p12, s.tmp.t[:], ALU.add), r=[k12, s.tmp.k], w=[S_k])
            op("act", lambda e: e.activation(out=s.on.t[:], in_=s.o.t[:], func=AF.Square), r=[s.o.k], w=[s.on.k])
            op("dve", lambda e: e.reduce_sum(s.ss.t[:, 0:1], s.on.t[:], AX.X), r=[s.on.k], w=[s.ss.k])
            op("act", lambda e: e.activation(out=s.ss.t[:, 1:2], in_=s.ss.t[:, 0:1], func=AF.Sqrt, bias=EPS, scale=1.0 / 128), r=[s.ss.k], w=[s.ss.k])
            op("dve", lambda e: e.reciprocal(s.ss.t[:, 1:2], s.ss.t[:, 1:2]), r=[s.ss.k], w=[s.ss.k])
            op("dve", lambda e: e.tensor_scalar(s.on.t[:], s.o.t[:], s.ss.t[:, 1:2], None, ALU.mult), r=[s.o.k, s.ss.k], w=[s.on.k])
            p13, k13 = pst_of(s)
            op("pe", lambda e: e.matmul(p13, s.on.t[:], ident, start=True, stop=True), r=[s.on.k, CK], w=[k13])
            op("dve", lambda e: e.scalar_tensor_tensor(out_ap, p13, gng.t[:, 0:1], sg_ap, ALU.mult, ALU.mult),
               r=[k13, gng.k] + rk, w=[out_k])


        def ssd_unit(grp, cc, B_ap, C_ap, xs_aps, sz_aps, rk, S_ap, S_k, y_aps, y_ks):
            h0 = 8 * grp
            pg, kg_ = pst_of(s)
            op("pe", lambda e: e.matmul(pg, B_ap, C_ap, start=True, stop=True), r=rk, w=[kg_])
            op("act", lambda e: e.activation(out=s.GT.t[:], in_=pg, func=AF.Copy), r=[kg_], w=[s.GT.k])
            px = fullbank()
            for j in range(4):
                op("pe", lambda e, j=j: e.matmul(px.t[:, j * 128:(j + 1) * 128], xs_aps[j], ident, start=True, stop=True),
                   r=rk + [CK], w=[px.k])
            op("act", lambda e: e.activation(out=s.xtok.t[:], in_=px.t[:], func=AF.Copy), r=[px.k], w=[s.xtok.k])
            pb_, kb_ = pst_of(s)
            op("pe", lambda e: e.matmul(pb_, B_ap, ident, start=True, stop=True), r=rk + [CK], w=[kb_])
            op("dve", lambda e: e.tensor_copy(s.Btok.t[:], pb_), r=[kb_], w=[s.Btok.k])
            pcs = fullbank()
            op("pe", lambda e: e.matmul(pcs.t[:], C_ap, S_ap.rearrange("p h q -> p (h q)"), start=True, stop=True), r=rk + [S_k], w=[pcs.k])
            op("dve", lambda e: e.tensor_tensor(s.t512.t[:].rearrange("p (h q) -> p h q", q=64),
                                                pcs.t[:].rearrange("p (h q) -> p h q", q=64),
                                                egc[cc].t[:, 8 + h0:16 + h0].unsqueeze(2).broadcast_to([128, 8, 64]), ALU.mult),
               r=[pcs.k, egc[cc].k], w=[s.t512.k])
            po = fullbank()
            pS = fullbank()
            for hh in range(8):
                col = 8 + h0 + hh
                g_b = gg[cc].t[:, col:col + 1].broadcast_to([128, 128])
                pd, kd_ = pst_of(s)
                op("pe", lambda e: e.matmul(pd, ident, maskB, start=True, stop=False), r=[CK], w=[kd_])
                op("pe", lambda e: e.matmul(pd, g_b, triU, start=False, stop=True), r=[gg[cc].k, CK], w=[kd_])
                op("act", lambda e: e.activation(out=s.decT.t[:], in_=pd, func=AF.Exp, bias=ngc[cc].t[:, col:col + 1], scale=1.0),
                   r=[kd_, ngc[cc].k], w=[s.decT.k])
                op("dve", lambda e: e.scalar_tensor_tensor(s.MT.t[:], s.GT.t[:], dtt[cc].t[:, h0 + hh:h0 + hh + 1], s.decT.t[:],
                                                           ALU.mult, ALU.mult), r=[s.GT.k, dtt[cc].k, s.decT.k], w=[s.MT.k])
                op("pe", lambda e: e.matmul(po.t[:, hh * 64:(hh + 1) * 64], s.MT.t[:], s.xtok.t[:, hh * 64:(hh + 1) * 64],
                                            start=True, stop=True), r=[s.MT.k, s.xtok.k], w=[po.k])
                op("dve", lambda e: e.tensor_scalar(s.Bd.t[:], s.Btok.t[:], dte[cc].t[:, h0 + hh:h0 + hh + 1], None, ALU.mult),
                   r=[s.Btok.k, dte[cc].k], w=[s.Bd.k])
                op("pe", lambda e: e.matmul(pS.t[:, hh * 64:(hh + 1) * 64], s.Bd.t[:], s.xtok.t[:, hh * 64:(hh + 1) * 64],
                                            start=True, stop=True), r=[s.Bd.k, s.xtok.k], w=[pS.k])
            op("dve", lambda e: e.tensor_tensor(s.otok.t[:], po.t[:], s.t512.t[:], ALU.add), r=[po.k, s.t512.k], w=[s.otok.k])
            op("dve", lambda e: e.tensor_tensor(s.t512b.t[:].rearrange("p (h q) -> p h q", q=64),
                                                s.xtok.t[:].rearrange("p (h q) -> p h q", q=64),
                                                auxp.t[:, 2, 16 + h0:24 + h0].unsqueeze(2).broadcast_to([128, 8, 64]), ALU.mult),
               r=[s.xtok.k, auxp.k], w=[s.t512b.k])
            op("dve", lambda e: e.tensor_tensor(s.otok.t[:], s.otok.t[:], s.t512b.t[:], ALU.add), r=[s.otok.k, s.t512b.k], w=[s.otok.k])
            op("dve", lambda e: e.tensor_tensor(s.t512.t[:].rearrange("p (h q) -> p h q", q=64), S_ap,
                                                elast[cc].t[:, 8 + h0:16 + h0].unsqueeze(2).broadcast_to([128, 8, 64]), ALU.mult),
               r=[S_k, elast[cc].k, s.t512.k], w=[s.t512.k])
            op("dve", lambda e: e.tensor_tensor(S_ap, pS.t[:].rearrange("p (h q) -> p h q", q=64),
                                                s.t512.t[:].rearrange("p (h q) -> p h q", q=64), ALU.add),
               r=[pS.k, s.t512.k], w=[S_k])
            pt = fullbank()
            for j in range(4):
                op("pe", lambda e, j=j: e.matmul(pt.t[:, j * 128:(j + 1) * 128], s.otok.t[:, j * 128:(j + 1) * 128], ident,
                                                 start=True, stop=True), r=[s.otok.k, CK], w=[pt.k])
            for j in range(4):
                op("dve", lambda e, j=j: e.tensor_tensor(y_aps[j], pt.t[:, j * 128:(j + 1) * 128], sz_aps[j], ALU.mult),
                   r=[pt.k] + rk, w=[y_ks[j]])

        def ssd_norm_out(T):
            pst = banks[7]
            for j in range(8):
                op("act", lambda e, j=j: e.activation(out=sq.t[:, :T], in_=oT.t[:, 8 + j, :T], func=AF.Square), r=[oTk[8 + j]], w=[sq.k])
                op("pe", lambda e, j=j: e.matmul(pst.t[:, :T], ones, sq.t[:, :T], start=(j == 0), stop=(j == 7)), r=[sq.k, CK], w=[pst.k])
            op("act", lambda e: e.activation(out=rs.t[:, :T], in_=pst.t[:, :T], func=AF.Sqrt, bias=EPS, scale=1.0 / 1024), r=[pst.k], w=[rs.k])
            op("dve", lambda e: e.reciprocal(rs.t[:, :T], rs.t[:, :T]), r=[rs.k], w=[rs.k])
            for j in range(8):
                op("dve", lambda e, j=j: e.scalar_tensor_tensor(oT.t[:, 8 + j, :T], oT.t[:, 8 + j, :T], sng.t[:, j:j + 1], rs.t[:, :T],
                                                                ALU.mult, ALU.mult), r=[oTk[8 + j], sng.k, rs.k], w=[oTk[8 + j]])

        def out_proj(g, T, t0):
            for m in range(8):
                pst = proj_ps()
                for half in range(2):
                    buf = load_w(m * 128, w_out, row0=half * 1024)
                    for k in range(8):
                        kk = half * 8 + k
                        op("pe", lambda e, k=k, kk=kk: e.matmul(pst.t[:, :T], buf.t[:, k, :], oT.t[:, kk, :T],
                                                                start=(kk == 0), stop=(kk == 15)), r=[buf.k, oTk[kk]], w=[pst.k])
                if g < 4:
                    op("dve", lambda e, m=m: e.scalar_tensor_tensor(
                        xT.t[:, m, t0:t0 + T], pst.t[:, :T], modT.t[:, 0, 16 + m, 0:1], xT.t[:, m, t0:t0 + T], ALU.mult, ALU.add),
                        r=[pst.k, modT.k, xk[m][g]], w=[xk[m][g]])
                else:
                    op("dve", lambda e, m=m: e.tensor_tensor(sq.t[:, :T], pst.t[:, :T], modT.t[:, 0, 16 + m, 1:17], ALU.mult),
                       r=[pst.k, modT.k], w=[sq.k])
                    op("dve", lambda e, m=m: e.tensor_tensor(xT.t[:, m, t0:t0 + T], xT.t[:, m, t0:t0 + T], sq.t[:, :T], ALU.add),
                       r=[sq.k, xk[m][g]], w=[xk[m][g]])

        es_p = contextlib.ExitStack()
        sbp = lambda name, shape, dt=F32: self.sb(name, shape, dt, es=es_p)
        pre = [sbp("pre%d" % i, [128, 515]) for i in range(NPRE)]
        BT = [sbp("BT%d" % i, [128, 512]) for i in range(2)]
        CT = [sbp("CT%d" % i, [128, 512]) for i in range(2)]
        xsT = [sbp("xsT%d" % i, [128, 512]) for i in range(4)]
        szT = [sbp("szT%d" % i, [128, 512]) for i in range(4)]
        for g in range(min(KG, 4)):
            t0, T = self.groups[g]
            nch = T // 128
            self.norm_mod_layer(0, 0, g, hm)
            pst = project(6656, T, ncol=32)
            op("act", lambda e: e.activation(out=auxT.t[:, :T], in_=pst.t[:32, :T], func=AF.Copy), r=[pst.k], w=[auxT.k])
            for cc in range(nch):
                aux_compute(cc, auxT.t[:, cc * 128:(cc + 1) * 128], auxT.k)
            def head_thread(h, ti):
                q_, k_, v_, sg_ = qkv[ti]
                with P.atomic():
                    conv_chunk(h, T, q_.t[:, :T], q_.k, h * 128)
                    conv_chunk(8 + h, T, k_.t[:, :T], k_.k, 1024 + h * 128)
                    conv_chunk(16 + h, T, v_.t[:, :T], v_.k, 2048 + h * 128)
                    pst = project(4608 + h * 128, T)
                    op("act", lambda e: e.activation(out=sg_.t[:, :T], in_=pst.t[:, :T], func=AF.Silu), r=[pst.k], w=[sg_.k])
                    l2norm(q_, T, 128.0 ** -0.5)
                    l2norm(k_, T, 1.0)
                for cc in range(nch):
                    cs = slice(cc * 128, (cc + 1) * 128)
                    gdn_unit(scr[ti], h, cc, q_.t[:, cs], k_.t[:, cs], v_.t[:, cs], sg_.t[:, cs], [q_.k, k_.k, v_.k, sg_.k],
                             Sg.t[:, h, :], Sgk[h], oT.t[:, h, cs], oTk[h])
            WP = int(os.environ.get('KTHP', '2'))
            for h0 in range(0, KH, WP):
                ths = []
                for ti in range(min(WP, KH - h0)):
                    with P.record() as th:
                        head_thread(h0 + ti, ti)
                    ths.append(th)
                P.merge(ths)
            for j in range(KH, 8):
                op("pool", lambda e, j=j: e.memset(oT.t[:, j, :T], 0.0), w=[oTk[j]])
            if KSS:
                for grp in range(2):
                    conv_chunk(32 + grp, T, BT[grp].t[:, :T], BT[grp].k, 4096 + grp * 128)
                    conv_chunk(34 + grp, T, CT[grp].t[:, :T], CT[grp].k, 4352 + grp * 128)
                    for j in range(4):
                        jj = 4 * grp + j
                        conv_chunk(24 + jj, T, xsT[j].t[:, :T], xsT[j].k, 3072 + jj * 128)
                        pst = project(5632 + jj * 128, T)
                        op("act", lambda e, j=j: e.activation(out=szT[j].t[:, :T], in_=pst.t[:, :T], func=AF.Silu), r=[pst.k], w=[szT[j].k])
                    rk = [BT[grp].k, CT[grp].k] + [xsT[j].k for j in range(4)] + [szT[j].k for j in range(4)]
                    for cc in range(nch):
                        cs = slice(cc * 128, (cc + 1) * 128)
                        ssd_unit(grp, cc, BT[grp].t[:, cs], CT[grp].t[:, cs], [xsT[j].t[:, cs] for j in range(4)],
                                 [szT[j].t[:, cs] for j in range(4)], rk, Ss.t[:, 8 * grp:8 * grp + 8, :], Ssk[grp],
                                 [oT.t[:, 8 + 4 * grp + j, cs] for j in range(4)], [oTk[8 + 4 * grp + j] for j in range(4)])
                ssd_norm_out(T)
            else:
                for j in range(8, 16):
                    op("pool", lambda e, j=j: e.memset(oT.t[:, j, :T], 0.0), w=[oTk[j]])
            out_proj(g, T, t0)

        op_l = P.lane()
        dma("sp", op_l, lambda e: e.dma_start(out=D["gdn_p"].rearrange("h k v -> k h v"), in_=Sg.t[:]), r=Sgk)
        dma("sp", op_l, lambda e: e.dma_start(out=D["ssm_p"].rearrange("h n p -> n h p"), in_=Ss.t[:]), r=Ssk, group=True)
        dma("sp", op_l, lambda e: e.dma_start(out=D["conv_p"][:, :, :], in_=halo.t[:]), r=halok, group=True)
        P.group_finish(op_l, Sgk + Ssk + halok)

        P.barrier()
        es_p.close()
        if KG >= 5:
            g, (t0, T) = 4, self.groups[4]
            self.norm_mod_layer(0, 0, 4, hm)
            pst = project(6656, T, ncol=32)
            op("act", lambda e: e.activation(out=auxT.t[:, :T], in_=pst.t[:32, :T], func=AF.Copy), r=[pst.k], w=[auxT.k])
            qs = sb("qs", [128, 8, 16])
            ks_ = sb("ks", [128, 8, 16])
            vs = sb("vs", [128, 8, 16])
            sgs = sb("sgs", [128, 8, 16])
            Bs = sb("Bs", [128, 2, 16])
            Cs = sb("Cs", [128, 2, 16])
            xss = sb("xss", [128, 8, 16])
            szs = sb("szs", [128, 8, 16])
            for h in range(8):
                conv_chunk_s(h, qT.t[:, :16], qT.k, h * 128)
                conv_chunk_s(8 + h, kT.t[:, :16], kT.k, 1024 + h * 128)
                conv_chunk_s(16 + h, vs.t[:, h, :], vs.k, 2048 + h * 128)
                pst = project(4608 + h * 128, 16)
                op("act", lambda e, h=h: e.activation(out=sgs.t[:, h, :], in_=pst.t[:, :16], func=AF.Silu), r=[pst.k], w=[sgs.k])
                l2norm(qT, 16, 128.0 ** -0.5)
                l2norm(kT, 16, 1.0)
                op("pool", lambda e, h=h: e.tensor_copy(qs.t[:, h, :], qT.t[:, :16]), r=[qT.k], w=[qs.k])
                op("pool", lambda e, h=h: e.tensor_copy(ks_.t[:, h, :], kT.t[:, :16]), r=[kT.k], w=[ks_.k])
            for grp in range(2):
                conv_chunk_s(32 + grp, Bs.t[:, grp, :], Bs.k, 4096 + grp * 128)
                conv_chunk_s(34 + grp, Cs.t[:, grp, :], Cs.k, 4352 + grp * 128)
            for jj in range(8):
                conv_chunk_s(24 + jj, xss.t[:, jj, :], xss.k, 3072 + jj * 128)
                pst = project(5632 + jj * 128, 16)
                op("act", lambda e, jj=jj: e.activation(out=szs.t[:, jj, :], in_=pst.t[:, :16], func=AF.Silu), r=[pst.k], w=[szs.k])
            pads = {}
            for nm in ["q", "k", "v", "sg", "B", "C"] + ["xs%d" % j for j in range(4)] + ["sz%d" % j for j in range(4)]:
                pads[nm] = sb("pad_" + nm, [128, 128])
                op("pool", lambda e, nm=nm: e.memset(pads[nm].t[:], 0.0), w=[pads[nm].k])
            auxpad = sb("auxpad", [32, 128])
            op("pool", lambda e: e.memset(auxpad.t[:], 0.0), w=[auxpad.k])
            otmp = sb("otmp", [128, 128])
            otmps = [otmp, sb("otmp2", [128, 128])]
            pads2 = {}
            for nm in ["q", "k", "v", "sg"]:
                pads2[nm] = sb("pad2_" + nm, [128, 128])
                op("pool", lambda e, nm=nm: e.memset(pads2[nm].t[:], 0.0), w=[pads2[nm].k])
            padsets = [pads, pads2]
            ytmp = sb("ytmp", [128, 4, 128])
            ytk = [Tk() for _ in range(4)]
            NSB = 2
            Sgs = [sb("Sgs%d" % i, [128, 128]) for i in range(NSB)]
            Sss = [sb("Sss%d" % i, [128, 8, 64]) for i in range(NSB)]
            lsi = [P.lane() for _ in range(NSB)]
            lso = [P.lane() for _ in range(NSB)]
            lsi2 = [P.lane() for _ in range(NSB)]
            lso2 = [P.lane() for _ in range(NSB)]
            cnt = 0
            for smp in range(KSMP):
                op("pool", lambda e, smp=smp: e.tensor_copy(auxpad.t[:, 0:1], auxT.t[:, smp:smp + 1]), r=[auxT.k], w=[auxpad.k])
                aux_compute(0, auxpad.t[:], auxpad.k, sample=True)
                def smp_thread(h, ti):
                    St = Sgs[ti]
                    pd = padsets[ti]
                    dma("sp", lsi[ti], lambda e: e.dma_start(out=St.t[:], in_=D["st_gdn"][smp, h, :, :]), w=[St.k])
                    for nm, src in (("q", qs), ("k", ks_), ("v", vs), ("sg", sgs)):
                        op("pool", lambda e, nm=nm, src=src: e.tensor_copy(pd[nm].t[:, 0:1], src.t[:, h, smp:smp + 1]),
                           r=[src.k], w=[pd[nm].k])
                    gdn_unit(scr[ti], h, 0, pd["q"].t[:], pd["k"].t[:], pd["v"].t[:], pd["sg"].t[:],
                             [pd["q"].k, pd["k"].k, pd["v"].k, pd["sg"].k], St.t[:], St.k, otmps[ti].t[:], otmps[ti].k, single=True)
                    dma("sp", lso[ti], lambda e: e.dma_start(out=D["gdn_s"][smp, h, :, :], in_=St.t[:]), r=[St.k])
                    op("pool", lambda e: e.tensor_copy(oT.t[:, h, smp:smp + 1], otmps[ti].t[:, 0:1]), r=[otmps[ti].k], w=[oTk[h]])
                WS = int(os.environ.get('KTHS', '1'))
                for h0 in range(0, KH, WS):
                    ths = []
                    for ti in range(min(WS, KH - h0)):
                        with P.record() as th:
                            smp_thread(h0 + ti, ti)
                        ths.append(th)
                    P.merge(ths)
                if KSS:
                    for grp in range(2):
                        i = cnt % NSB
                        cnt += 1
                        St = Sss[i]
                        dma("sp", lsi2[i], lambda e: e.dma_start(
                            out=St.t[:], in_=D["st_ssm"][smp, 8 * grp:8 * grp + 8, :, :].rearrange("h n p -> n h p")), w=[St.k])
                        op("pool", lambda e: e.tensor_copy(pads["B"].t[:, 0:1], Bs.t[:, grp, smp:smp + 1]), r=[Bs.k], w=[pads["B"].k])
                        op("pool", lambda e: e.tensor_copy(pads["C"].t[:, 0:1], Cs.t[:, grp, smp:smp + 1]), r=[Cs.k], w=[pads["C"].k])
                        for j in range(4):
                            op("pool", lambda e, j=j: e.tensor_copy(pads["xs%d" % j].t[:, 0:1], xss.t[:, 4 * grp + j, smp:smp + 1]),
                               r=[xss.k], w=[pads["xs%d" % j].k])
                            op("pool", lambda e, j=j: e.tensor_copy(pads["sz%d" % j].t[:, 0:1], szs.t[:, 4 * grp + j, smp:smp + 1]),
                               r=[szs.k], w=[pads["sz%d" % j].k])
                        rk = [pads[n].k for n in ["B", "C"] + ["xs%d" % j for j in range(4)] + ["sz%d" % j for j in range(4)]]
                        ssd_unit(grp, 0, pads["B"].t[:], pads["C"].t[:], [pads["xs%d" % j].t[:] for j in range(4)],
                                 [pads["sz%d" % j].t[:] for j in range(4)], rk, St.t[:], St.k,
                                 [ytmp.t[:, j, :] for j in range(4)], ytk)
                        dma("sp", lso2[i], lambda e: e.dma_start(
                            out=D["ssm_s"][smp, 8 * grp:8 * grp + 8, :, :].rearrange("h n p -> n h p"), in_=St.t[:]), r=[St.k])
                        for j in range(4):
                            op("pool", lambda e, j=j: e.tensor_copy(oT.t[:, 8 + 4 * grp + j, smp:smp + 1], ytmp.t[:, j, 0:1]),
                               r=[ytk[j]], w=[oTk[8 + 4 * grp + j]])
            for j in range(KH, 8):
                op("pool", lambda e, j=j: e.memset(oT.t[:, j, :T], 0.0), w=[oTk[j]])
            if KSS:
                ssd_norm_out(T)
            else:
                for j in range(8, 16):
                    op("pool", lambda e, j=j: e.memset(oT.t[:, j, :T], 0.0), w=[oTk[j]])
            out_proj(4, T, t0)
        P.barrier()
        es.close()


    def layer1(self, D):
        nc, P = self.nc, self.P
        op, dma = P.op, P.dma
        c = self.c
        ident, ones, CK = c["ident"], c["ones"], c["CK"]
        banks = self.banks
        xT, xk, modT = self.xT, self.xk, self.modT
        w_in, w_out = D["ret_w_in"], D["ret_w_out"]
        es = contextlib.ExitStack()
        sb = lambda name, shape, dt=F32: self.sb(name, shape, dt, es=es)
        KG = int(os.environ.get('KG', '5'))
        KSMP = int(os.environ.get('KSMP', '16'))
        gam = [1.0 - 2.0 ** (-5.0 - h) for h in range(4)]

        lp = P.lane()
        retc = sb("retc_sb", [128, 4, 132])
        rngb = sb("rngb", [128, 4, 512])
        dma("sp", lp, lambda e: e.dma_start(out=retc.t[:], in_=D["retc"][:, :, :]), w=[retc.k])
        dma("sp", lp, lambda e: e.dma_start(out=rngb.t[:], in_=D["rng"][0:1, :, :].broadcast_to([128, 4, 512])), w=[rngb.k], group=True)
        P.group_finish(lp, [retc.k, rngb.k])
        Sr = sb("Sr", [128, 4, 2, 512])
        Srk = [Tk("Sr%d" % h) for h in range(4)]
        op("pool", lambda e: e.memset(Sr.t[:], 0.0), w=Srk)

        hm = sb("hm1", [128, 8, 512], BF16)
        NWB = 4
        wbuf = [sb("w1b%d" % i, [128, 8, 128], BF16) for i in range(NWB)]
        wl = [P.lane() for _ in range(NWB)]
        wi = [0]

        def load_w(col0, src, row0=0):
            i = wi[0] % NWB
            wi[0] += 1
            buf = wbuf[i]
            dma("pool", wl[i], lambda e: e.dma_start(
                out=buf.t[:], in_=src[row0:row0 + 1024, col0:col0 + 128].rearrange("(k p) c -> p k c", p=128)), w=[buf.k])
            return buf

        pring = [0]

        def proj_ps():
            b = banks[pring[0] % 2]
            pring[0] += 1
            return b

        def project(col0, T):
            buf = load_w(col0, w_in)
            pst = proj_ps()
            for k in range(8):
                op("pe", lambda e, k=k: e.matmul(pst.t[:, :T], buf.t[:, k, :], hm.t[:, k, :T], start=(k == 0), stop=(k == 7)),
                   r=[buf.k, hm.k], w=[pst.k])
            return pst

        cosb = sb("cosb", [128, 512])
        sinb = sb("sinb", [128, 512])
        lcs = P.lane()
        raw = [sb("rraw%d" % i, [128, 512]) for i in range(2)]
        ta = sb("rta", [128, 512])
        tb = sb("rtb", [128, 512])
        qT = [sb("rq%d" % i, [128, 512]) for i in range(2)]
        kT = [sb("rk%d" % i, [128, 512]) for i in range(2)]
        vT = [sb("rv%d" % i, [128, 512]) for i in range(4)]
        sgT = [sb("rsg%d" % i, [128, 512]) for i in range(4)]
        oT = sb("oT1", [128, 16, 512], BF16)
        oTk = [Tk("o1T%d" % j) for j in range(16)]

        def rope(col0, T, dst, scale):
            for cidx in range(2):
                pst = project(col0 + cidx * 128, T)
                op("act", lambda e, cidx=cidx: e.activation(out=raw[cidx].t[:, :T], in_=pst.t[:, :T], func=AF.Identity, scale=scale),
                   r=[pst.k], w=[raw[cidx].k])
            x1, x2 = raw
            op("dve", lambda e: e.tensor_tensor(ta.t[:, :T], x1.t[:, :T], cosb.t[:, :T], ALU.mult), r=[x1.k, cosb.k], w=[ta.k])
            op("pool", lambda e: e.tensor_tensor(tb.t[:, :T], x2.t[:, :T], sinb.t[:, :T], ALU.mult), r=[x2.k, sinb.k], w=[tb.k])
            op("dve", lambda e: e.tensor_tensor(dst[0].t[:, :T], ta.t[:, :T], tb.t[:, :T], ALU.subtract), r=[ta.k, tb.k], w=[dst[0].k])
            op("dve", lambda e: e.tensor_tensor(ta.t[:, :T], x1.t[:, :T], sinb.t[:, :T], ALU.mult), r=[x1.k, sinb.k], w=[ta.k])
            op("pool", lambda e: e.tensor_tensor(tb.t[:, :T], x2.t[:, :T], cosb.t[:, :T], ALU.mult), r=[x2.k, cosb.k], w=[tb.k])
            op("dve", lambda e: e.tensor_tensor(dst[1].t[:, :T], ta.t[:, :T], tb.t[:, :T], ALU.add), r=[ta.k, tb.k], w=[dst[1].k])

        class S:
            pass
        s = S()
        s.qk = sb("r_qk", [128, 128])
        s.ss = sb("r_ss", [128, 4])
        s.ps = [banks[2], banks[3]]
        s.pi = 0
        s.xtok = sb("r_xtok", [128, 512])
        s.kd = sb("r_kd", [128, 256])
        s.t512b = sb("r_t512b", [128, 512])
        s.otok = sb("r_otok", [128, 512])
        fring = [0]

        def fullbank():
            b = banks[4 + fring[0] % 3]
            fring[0] += 1
            return b

        def pst_of():
            j = s.pi % 8
            s.pi += 1
            b = s.ps[j % 2]
            r = j // 2
            return b.t[:, r * 128:(r + 1) * 128], b.k

        def ret_unit(h, q_aps, k_aps, v_aps, sg_aps, rk, S_ap, S_k, out_aps, out_ks, eg_col, ed_col, elast_f):
            pq, kq = pst_of()
            op("pe", lambda e: e.matmul(pq, k_aps[0], q_aps[0], start=True, stop=False), r=rk, w=[kq])
            op("pe", lambda e: e.matmul(pq, k_aps[1], q_aps[1], start=False, stop=True), r=rk, w=[kq])
            op("dve", lambda e: e.tensor_tensor(s.qk.t[:], pq, retc.t[:, h, 0:128], ALU.mult), r=[kq, retc.k], w=[s.qk.k])
            px = fullbank()
            for cidx in range(4):
                op("pe", lambda e, cidx=cidx: e.matmul(px.t[:, cidx * 128:(cidx + 1) * 128], v_aps[cidx], ident, start=True, stop=True),
                   r=rk + [CK], w=[px.k])
            op("act", lambda e: e.activation(out=s.xtok.t[:], in_=px.t[:], func=AF.Copy), r=[px.k], w=[s.xtok.k])
            pk = fullbank()
            for cidx in range(2):
                op("pe", lambda e, cidx=cidx: e.matmul(pk.t[:, cidx * 128:(cidx + 1) * 128], k_aps[cidx], ident, start=True, stop=True),
                   r=rk + [CK], w=[pk.k])
            op("dve", lambda e: e.tensor_scalar(s.kd.t[:], pk.t[:, 0:256], ed_col, None, ALU.mult), r=[pk.k, retc.k], w=[s.kd.k])
            po = fullbank()
            op("pe", lambda e: e.matmul(po.t[:], s.qk.t[:], s.xtok.t[:], start=True, stop=True), r=[s.qk.k, s.xtok.k], w=[po.k])
            pqs = fullbank()
            op("pe", lambda e: e.matmul(pqs.t[:], q_aps[0], S_ap[:, 0, :], start=True, stop=False), r=rk + [S_k], w=[pqs.k])
            op("pe", lambda e: e.matmul(pqs.t[:], q_aps[1], S_ap[:, 1, :], start=False, stop=True), r=rk + [S_k], w=[pqs.k])
            op("dve", lambda e: e.tensor_scalar(s.t512b.t[:], pqs.t[:], eg_col, None, ALU.mult), r=[pqs.k, retc.k], w=[s.t512b.k])
            op("dve", lambda e: e.tensor_tensor(s.otok.t[:], po.t[:], s.t512b.t[:], ALU.add), r=[po.k, s.t512b.k], w=[s.otok.k])
            op("dve", lambda e: e.reduce_sum(s.ss.t[:, 0:1], s.otok.t[:], AX.X), r=[s.otok.k], w=[s.ss.k])
            op("dve", lambda e: e.tensor_scalar(s.ss.t[:, 0:1], s.ss.t[:, 0:1], -1.0 / 512, None, ALU.mult), r=[s.ss.k], w=[s.ss.k])
            op("dve", lambda e: e.tensor_scalar(s.otok.t[:], s.otok.t[:], s.ss.t[:, 0:1], None, ALU.add), r=[s.otok.k, s.ss.k], w=[s.otok.k])
            op("act", lambda e: e.activation(out=s.t512b.t[:], in_=s.otok.t[:], func=AF.Square), r=[s.otok.k], w=[s.t512b.k])
            op("dve", lambda e: e.reduce_sum(s.ss.t[:, 1:2], s.t512b.t[:], AX.X), r=[s.t512b.k], w=[s.ss.k])
            op("act", lambda e: e.activation(out=s.ss.t[:, 2:3], in_=s.ss.t[:, 1:2], func=AF.Sqrt, bias=EPS, scale=1.0 / 512), r=[s.ss.k], w=[s.ss.k])
            op("dve", lambda e: e.reciprocal(s.ss.t[:, 2:3], s.ss.t[:, 2:3]), r=[s.ss.k], w=[s.ss.k])
            op("dve", lambda e: e.scalar_tensor_tensor(s.otok.t[:], s.otok.t[:], s.ss.t[:, 2:3], rngb.t[:, h, :], ALU.mult, ALU.mult),
               r=[s.otok.k, s.ss.k, rngb.k], w=[s.otok.k])
            pt = fullbank()
            for cidx in range(4):
                op("pe", lambda e, cidx=cidx: e.matmul(pt.t[:, cidx * 128:(cidx + 1) * 128], s.otok.t[:, cidx * 128:(cidx + 1) * 128], ident,
                                                       start=True, stop=True), r=[s.otok.k, CK], w=[pt.k])
            for cidx in range(4):
                op("dve", lambda e, cidx=cidx: e.tensor_tensor(out_aps[cidx], pt.t[:, cidx * 128:(cidx + 1) * 128], sg_aps[cidx], ALU.mult),
                   r=[pt.k] + rk, w=[out_ks[cidx]])
            for cidx in range(2):
                pS = fullbank()
                op("pe", lambda e, cidx=cidx: e.matmul(pS.t[:], s.kd.t[:, cidx * 128:(cidx + 1) * 128], s.xtok.t[:], start=True, stop=True),
                   r=[s.kd.k, s.xtok.k], w=[pS.k])
                op("dve", lambda e, cidx=cidx: e.tensor_scalar(s.t512b.t[:], S_ap[:, cidx, :], elast_f, None, ALU.mult), r=[S_k, s.t512b.k], w=[s.t512b.k])
                op("dve", lambda e, cidx=cidx: e.tensor_tensor(S_ap[:, cidx, :], pS.t[:], s.t512b.t[:], ALU.add), r=[pS.k, s.t512b.k], w=[S_k])

        def out_proj(g, T, t0):
            sq = self.nsq
            for m in range(8):
                pst = proj_ps()
                for half in range(2):
                    buf = load_w(m * 128, w_out, row0=half * 1024)
                    for k in range(8):
                        kk = half * 8 + k
                        op("pe", lambda e, k=k, kk=kk: e.matmul(pst.t[:, :T], buf.t[:, k, :], oT.t[:, kk, :T],
                                                                start=(kk == 0), stop=(kk == 15)), r=[buf.k, oTk[kk]], w=[pst.k])
                if g < 4:
                    op("dve", lambda e, m=m: e.scalar_tensor_tensor(
                        xT.t[:, m, t0:t0 + T], pst.t[:, :T], modT.t[:, 1, 16 + m, 0:1], xT.t[:, m, t0:t0 + T], ALU.mult, ALU.add),
                        r=[pst.k, modT.k, xk[m][g]], w=[xk[m][g]])
                else:
                    op("dve", lambda e, m=m: e.tensor_tensor(sq.t[:, :T], pst.t[:, :T], modT.t[:, 1, 16 + m, 1:17], ALU.mult),
                       r=[pst.k, modT.k], w=[sq.k])
                    op("dve", lambda e, m=m: e.tensor_tensor(xT.t[:, m, t0:t0 + T], xT.t[:, m, t0:t0 + T], sq.t[:, :T], ALU.add),
                       r=[sq.k, xk[m][g]], w=[xk[m][g]])

        def head_inputs(h, T):
            rope(h * 256, T, qT, 1.0)
            rope(1024 + h * 256, T, kT, 1.0 / 16)
            for cidx in range(4):
                pst = project(2048 + h * 512 + cidx * 128, T)
                op("act", lambda e, cidx=cidx: e.activation(out=vT[cidx].t[:, :T], in_=pst.t[:, :T], func=AF.Copy), r=[pst.k], w=[vT[cidx].k])
                pst = project(4096 + h * 512 + cidx * 128, T)
                op("act", lambda e, cidx=cidx: e.activation(out=sgT[cidx].t[:, :T], in_=pst.t[:, :T], func=AF.Silu), r=[pst.k], w=[sgT[cidx].k])

        allk = [t.k for t in qT + kT + vT + sgT]
        for g in range(min(KG, 4)):
            t0, T = self.groups[g]
            self.norm_mod_layer(1, 0, g, hm)
            dma("sp", lcs, lambda e: e.dma_start(out=cosb.t[:, :T], in_=D["cosT"][:, t0:t0 + T]), w=[cosb.k])
            dma("sp", lcs, lambda e: e.dma_start(out=sinb.t[:, :T], in_=D["sinT"][:, t0:t0 + T]), w=[sinb.k])
            for h in range(4):
                head_inputs(h, T)
                for cc in range(T // 128):
                    cs = slice(cc * 128, (cc + 1) * 128)
                    ret_unit(h, [qT[i].t[:, cs] for i in range(2)], [kT[i].t[:, cs] for i in range(2)],
                             [vT[i].t[:, cs] for i in range(4)], [sgT[i].t[:, cs] for i in range(4)], allk,
                             Sr.t[:, h, :, :], Srk[h], [oT.t[:, 4 * h + i, cs] for i in range(4)], [oTk[4 * h + i] for i in range(4)],
                             retc.t[:, h, 128:129], retc.t[:, h, 129:130], gam[h] ** 128)
            out_proj(g, T, t0)
        op_l = P.lane()
        dma("sp", op_l, lambda e: e.dma_start(out=D["ret_p"].rearrange("h (c p) e -> p h c e", p=128), in_=Sr.t[:]), r=Srk)

        if KG >= 5:
            g, (t0, T) = 4, self.groups[4]
            self.norm_mod_layer(1, 0, 4, hm)
            dma("sp", lcs, lambda e: e.dma_start(out=cosb.t[:, :T], in_=D["cosT"][:, t0:t0 + T]), w=[cosb.k])
            dma("sp", lcs, lambda e: e.dma_start(out=sinb.t[:, :T], in_=D["sinT"][:, t0:t0 + T]), w=[sinb.k])
            pads = [sb("rpad%d" % i, [128, 128]) for i in range(12)]
            for p_ in pads:
                op("pool", lambda e, p_=p_: e.memset(p_.t[:], 0.0), w=[p_.k])
            otmp = sb("rotmp", [128, 4, 128])
            otk = [Tk() for _ in range(4)]
            NSB = 2
            Sts = [sb("Srs%d" % i, [128, 2, 512]) for i in range(NSB)]
            lsi = [P.lane() for _ in range(NSB)]
            lso = [P.lane() for _ in range(NSB)]
            cnt = 0
            srcs = qT + kT + vT + sgT
            for h in range(4):
                head_inputs(h, T)
                for smp in range(KSMP):
                    i = cnt % NSB
                    cnt += 1
                    St = Sts[i]
                    dma("sp", lsi[i], lambda e: e.dma_start(out=St.t[:], in_=D["st_ret"][smp, h, :, :].rearrange("(c p) e -> p c e", p=128)), w=[St.k])
                    for j in range(12):
                        op("pool", lambda e, j=j: e.tensor_copy(pads[j].t[:, 0:1], srcs[j].t[:, smp:smp + 1]), r=[srcs[j].k], w=[pads[j].k])
                    ret_unit(h, [pads[0].t[:], pads[1].t[:]], [pads[2].t[:], pads[3].t[:]], [pads[4 + i_].t[:] for i_ in range(4)],
                             [pads[8 + i_].t[:] for i_ in range(4)], [p_.k for p_ in pads], St.t[:], St.k,
                             [otmp.t[:, i_, :] for i_ in range(4)], otk, retc.t[:, h, 130:131], retc.t[:, h, 131:132], gam[h])
                    dma("sp", lso[i], lambda e: e.dma_start(out=D["ret_s"][smp, h, :, :].rearrange("(c p) e -> p c e", p=128), in_=St.t[:]), r=[St.k])
                    for i_ in range(4):
                        op("pool", lambda e, i_=i_: e.tensor_copy(oT.t[:, 4 * h + i_, smp:smp + 1], otmp.t[:, i_, 0:1]), r=[otk[i_]], w=[oTk[4 * h + i_]])
            out_proj(4, T, t0)
        P.barrier()
        es.close()


    def peer(self, l, D):
        nc, P = self.nc, self.P
        op, dma = P.op, P.dma
        c = self.c
        ident, ones, iota, CK = c["ident"], c["ones"], c["iota"], c["CK"]
        banks = self.banks
        xT, xk, modT, modA = self.xT, self.xk, self.modT, self.modA
        es = contextlib.ExitStack()
        sb = lambda name, shape, dt=F32: self.sb(name, shape, dt, es=es)
        KPG = int(os.environ.get('KPG', '9'))
        KPI = int(os.environ.get('KPI', '64'))
        nm = "p%d_" % l

        lp = P.lane()
        keysT = sb(nm + "keysT", [128, 2, 128])
        dma("sp", lp, lambda e: e.dma_start(out=keysT.t[:], in_=D["keysT"][l].rearrange("p d n -> d p n")), w=[keysT.k])
        hp = sb(nm + "hp", [128, 8, 256], BF16)
        NWB = 4
        wbuf = [sb(nm + "wq%d" % i, [128, 8, 128], BF16) for i in range(NWB)]
        wl = [P.lane() for _ in range(NWB)]
        wi = [0]
        ub = [sb(nm + "ub%d" % i, [128, 8, 256], BF16) for i in range(2)]
        vb = [sb(nm + "vb%d" % i, [128, 2, 1024], BF16) for i in range(2)]
        ul = [P.lane() for _ in range(2)]
        vl = [P.lane() for _ in range(2)]
        qc = [sb(nm + "qc%d" % i, [128, 128]) for i in range(2)]
        sc = [sb(nm + "sc%d" % i, [128, 128]) for i in range(2)]
        work = sb(nm + "work", [128, 128])
        sv = sb(nm + "sv", [128, 16, 16])
        si = sb(nm + "si", [128, 16, 16], U32)
        sif = sb(nm + "sif", [128, 16, 16])
        cand = sb(nm + "cand", [128, 256])
        cwork = sb(nm + "cwork", [128, 256])
        cv = sb(nm + "cv", [128, 8, 16])
        ci = sb(nm + "ci", [128, 8, 16], U32)
        au = sb(nm + "au", [128, 8, 16], U32)
        bu = sb(nm + "bu", [128, 8, 16], U32)
        af = sb(nm + "af", [128, 8, 16])
        bf = sb(nm + "bf", [128, 8, 16])
        eq = sb(nm + "eq", [128, 16, 16])
        iidx = sb(nm + "iidx", [128, 128])
        jidx = sb(nm + "jidx", [128, 128])
        gate = sb(nm + "gate", [128, 8, 16])
        zz = sb(nm + "zz", [128, 8])
        iT = sb(nm + "iT", [128, 128])
        jT = sb(nm + "jT", [128, 128])
        gT = sb(nm + "gT", [128, 128])
        OHj = sb(nm + "OHj", [128, 16, 128], BF16)
        OHi = sb(nm + "OHi", [128, 16, 128], BF16)
        Wall = sb(nm + "Wall", [128, 256, 128], BF16)
        zs = [sb(nm + "zs%d" % i, [128, 256]) for i in range(2)]
        Ab = [sb(nm + "A%d" % i, [128, 256], BF16) for i in range(2)]
        ysb = sb(nm + "ysb", [128, 1024])
        stmp = sb(nm + "stmp", [128, 128])
        pring = [0]

        def proj_ps():
            b = banks[pring[0] % 2]
            pring[0] += 1
            return b

        groups = [(i * 256, 256) for i in range(8)] + [(2048, 16)]
        ui = [0]
        for gi, (t0, T) in enumerate(groups[:KPG]):
            g5 = 4 if t0 >= 2048 else t0 // 512
            sh0 = 24
            if g5 < 4:
                self.norm_mod(g5, hp, lambda k: modA.t[:, l, 1, k, 0:1], lambda k: modT.t[:, l, sh0 + k, 0:1], t0, T)
            else:
                self.norm_mod(g5, hp, None, None, t0, T,
                              samp_A=lambda k: modA.t[:, l, 1, k, 1:17], samp_B=lambda k: modT.t[:, l, sh0 + k, 1:17])
            def route(tt0, T):
                for c16 in range(16):
                    i_ = wi[0] % NWB
                    wi[0] += 1
                    wq = wbuf[i_]
                    dma("pool", wl[i_], lambda e: e.dma_start(
                        out=wq.t[:], in_=D["w_q"][l, :, c16 * 128:(c16 + 1) * 128].rearrange("(k p) c -> p k c", p=128)), w=[wq.k])
                    pq = proj_ps()
                    for k in range(8):
                        op("pe", lambda e, k=k: e.matmul(pq.t[:, :T], wq.t[:, k, :], hp.t[:, k, tt0:tt0 + T], start=(k == 0), stop=(k == 7)),
                           r=[wq.k, hp.k], w=[pq.k])
                    q_ = qc[c16 % 2]
                    op("act", lambda e: e.activation(out=q_.t[:, :T], in_=pq.t[:, :T], func=AF.Copy), r=[pq.k], w=[q_.k])
                    pss = banks[2 + c16 % 2]
                    op("pe", lambda e: e.matmul(pss.t[:T, 0:128], q_.t[:, :T], keysT.t[:, c16 % 2, :], start=True, stop=True),
                       r=[q_.k, keysT.k], w=[pss.k])
                    s_ = sc[c16 % 2]
                    op("dve", lambda e: e.tensor_copy(s_.t[:T, :], pss.t[:T, 0:128]), r=[pss.k], w=[s_.k])
                    op("dve", lambda e: e.max(out=sv.t[:T, c16, 0:8], in_=s_.t[:T, :]), r=[s_.k], w=[sv.k])
                    op("dve", lambda e: e.match_replace(out=work.t[:T, :], in_to_replace=sv.t[:T, c16, 0:8], in_values=s_.t[:T, :], imm_value=-1e30),
                       r=[s_.k, sv.k], w=[work.k])
                    op("dve", lambda e: e.max(out=sv.t[:T, c16, 8:16], in_=work.t[:T, :]), r=[work.k, sv.k], w=[sv.k])
                    op("dve", lambda e: e.max_index(out=si.t[:T, c16, 0:8], in_max=sv.t[:T, c16, 0:8], in_values=s_.t[:T, :]), r=[s_.k, sv.k], w=[si.k])
                    op("dve", lambda e: e.max_index(out=si.t[:T, c16, 8:16], in_max=sv.t[:T, c16, 8:16], in_values=s_.t[:T, :]), r=[s_.k, sv.k, si.k], w=[si.k])
                op("dve", lambda e: e.tensor_copy(sif.t[:T], si.t[:T]), r=[si.k], w=[sif.k])
                for h in range(8):
                    op("dve", lambda e: e.tensor_tensor(cand.t[:T, :].rearrange("p (a b) -> p a b", a=16),
                                                        sv.t[:T, 2 * h, :].unsqueeze(2).broadcast_to([T, 16, 16]),
                                                        sv.t[:T, 2 * h + 1, :].unsqueeze(1).broadcast_to([T, 16, 16]), ALU.add), r=[sv.k], w=[cand.k])
                    op("dve", lambda e: e.max(out=cv.t[:T, h, 0:8], in_=cand.t[:T, :]), r=[cand.k], w=[cv.k])
                    op("dve", lambda e: e.match_replace(out=cwork.t[:T, :], in_to_replace=cv.t[:T, h, 0:8], in_values=cand.t[:T, :], imm_value=-1e30),
                       r=[cand.k, cv.k], w=[cwork.k])
                    op("dve", lambda e: e.max(out=cv.t[:T, h, 8:16], in_=cwork.t[:T, :]), r=[cwork.k, cv.k], w=[cv.k])
                    op("dve", lambda e: e.max_index(out=ci.t[:T, h, 0:8], in_max=cv.t[:T, h, 0:8], in_values=cand.t[:T, :]), r=[cand.k, cv.k], w=[ci.k])
                    op("dve", lambda e: e.max_index(out=ci.t[:T, h, 8:16], in_max=cv.t[:T, h, 8:16], in_values=cand.t[:T, :]), r=[cand.k, cv.k, ci.k], w=[ci.k])
                op("dve", lambda e: e.tensor_single_scalar(au.t[:T], ci.t[:T], 4, ALU.logical_shift_right), r=[ci.k], w=[au.k])
                op("dve", lambda e: e.tensor_single_scalar(bu.t[:T], ci.t[:T], 15, ALU.bitwise_and), r=[ci.k], w=[bu.k])
                op("dve", lambda e: e.tensor_copy(af.t[:T], au.t[:T]), r=[au.k], w=[af.k])
                op("dve", lambda e: e.tensor_copy(bf.t[:T], bu.t[:T]), r=[bu.k], w=[bf.k])
                for h in range(8):
                    for (xf, col, dst) in ((af, 2 * h, iidx), (bf, 2 * h + 1, jidx)):
                        op("dve", lambda e: e.tensor_tensor(eq.t[:T], xf.t[:T, h, :].unsqueeze(2).broadcast_to([T, 16, 16]),
                                                            iota[:T, 0:16].unsqueeze(1).broadcast_to([T, 16, 16]), ALU.is_equal),
                           r=[xf.k, CK], w=[eq.k])
                        op("dve", lambda e: e.tensor_tensor(eq.t[:T], eq.t[:T], sif.t[:T, col, :].unsqueeze(1).broadcast_to([T, 16, 16]), ALU.mult),
                           r=[eq.k, sif.k], w=[eq.k])
                        op("dve", lambda e: e.reduce_sum(dst.t[:T, h * 16:(h + 1) * 16], eq.t[:T], AX.X), r=[eq.k], w=[dst.k])
                op("dve", lambda e: e.tensor_tensor(gate.t[:T], cv.t[:T], cv.t[:T, :, 0:1].broadcast_to([T, 8, 16]), ALU.subtract), r=[cv.k], w=[gate.k])
                op("act", lambda e: e.activation(out=gate.t[:T], in_=gate.t[:T], func=AF.Exp), r=[gate.k], w=[gate.k])
                op("dve", lambda e: e.reduce_sum(zz.t[:T, :], gate.t[:T], AX.X), r=[gate.k], w=[zz.k])
                op("dve", lambda e: e.reciprocal(zz.t[:T, :], zz.t[:T, :]), r=[zz.k], w=[zz.k])
                op("dve", lambda e: e.tensor_tensor(gate.t[:T], gate.t[:T], zz.t[:T, :].unsqueeze(2).broadcast_to([T, 8, 16]), ALU.mult),
                   r=[gate.k, zz.k], w=[gate.k])
                for src_ap, src_k, dstT in ((iidx.t[:T, :], iidx.k, iT), (jidx.t[:T, :], jidx.k, jT),
                                            (gate.t[:T].rearrange("p h k -> p (h k)"), gate.k, gT)):
                    pt = banks[3]
                    op("pe", lambda e: e.matmul(pt.t[:, :T], src_ap, ident[:T, :T], start=True, stop=True), r=[src_k, CK], w=[pt.k])
                    op("dve", lambda e: e.tensor_copy(dstT.t[:, :T], pt.t[:, :T]), r=[pt.k], w=[dstT.k])
                for sub in range((T + 15) // 16):
                    ts = sub * 16
                    n = min(16, T - ts)
                    op("dve", lambda e: e.tensor_tensor(OHj.t[:, :n, :], iota.unsqueeze(1).broadcast_to([128, n, 128]),
                                                        jT.t[:, ts:ts + n].unsqueeze(2).broadcast_to([128, n, 128]), ALU.is_equal),
                       r=[jT.k, CK], w=[OHj.k])
                    op("dve", lambda e: e.tensor_tensor(OHi.t[:, :n, :], iota.unsqueeze(1).broadcast_to([128, n, 128]),
                                                         iT.t[:, ts:ts + n].unsqueeze(2).broadcast_to([128, n, 128]), ALU.is_equal),
                       r=[iT.k, CK], w=[OHi.k])
                    op("pool", lambda e: e.tensor_tensor(OHi.t[:, :n, :], OHi.t[:, :n, :],
                                                         gT.t[:, ts:ts + n].unsqueeze(2).broadcast_to([128, n, 128]), ALU.mult),
                       r=[OHi.k, gT.k], w=[OHi.k])
                    for q4 in range((n + 3) // 4):
                        pw = banks[2 + q4 % 2]
                        n4 = min(4, n - q4 * 4)
                        for t in range(n4):
                            tt = q4 * 4 + t
                            op("pe", lambda e, t=t, tt=tt: e.matmul(pw.t[:, t * 128:(t + 1) * 128], OHj.t[:, tt, :], OHi.t[:, tt, :], start=True, stop=True),
                               r=[OHj.k, OHi.k], w=[pw.k])
                        tg = tt0 + ts + q4 * 4
                        eng = ("act", "dve")[q4 % 2]
                        if eng == "act":
                            op("act", lambda e: e.activation(out=Wall.t[:, tg:tg + n4, :].rearrange("p t i -> p (t i)"), in_=pw.t[:, :n4 * 128], func=AF.Copy),
                               r=[pw.k], w=[Wall.k])
                        else:
                            op("dve", lambda e: e.tensor_copy(Wall.t[:, tg:tg + n4, :].rearrange("p t i -> p (t i)"), pw.t[:, :n4 * 128]), r=[pw.k], w=[Wall.k])

            tiles = [(tt0, min(128, T - tt0)) for tt0 in range(0, T, 128)]
            for tt0, nt in tiles:
                route(tt0, nt)
            py = [banks[4], banks[5], banks[6], banks[7]]
            for i2 in range(KPI):
                b_ = ui[0] % 2
                ui[0] += 1
                u_, v_ = ub[b_], vb[b_]
                dma("pool", ul[b_], lambda e: e.dma_start(
                    out=u_.t[:], in_=D["uT"][l, :, i2 * 256:(i2 + 1) * 256].rearrange("(k p) e -> p k e", p=128)), w=[u_.k])
                dma("pool", vl[b_], lambda e: e.dma_start(
                    out=v_.t[:], in_=D["pv"][l, i2 * 256:(i2 + 1) * 256, :].rearrange("(ii p) d -> p ii d", p=128)), w=[v_.k])
                for ii in range(2):
                    i = i2 * 2 + ii
                    pz = proj_ps()
                    for k in range(8):
                        op("pe", lambda e, k=k: e.matmul(pz.t[:, :T], u_.t[:, k, ii * 128:(ii + 1) * 128], hp.t[:, k, :T], start=(k == 0), stop=(k == 7)),
                           r=[u_.k, hp.k], w=[pz.k])
                    z_ = zs[i % 2]
                    a_ = Ab[i % 2]
                    op("act", lambda e: e.activation(out=z_.t[:, :T], in_=pz.t[:, :T], func=AF.Gelu), r=[pz.k], w=[z_.k])
                    op("dve", lambda e: e.tensor_tensor(a_.t[:, :T], z_.t[:, :T], Wall.t[:, :T, i], ALU.mult), r=[z_.k, Wall.k], w=[a_.k])
                    for ti, (tt0, nt) in enumerate(tiles):
                        for half in range(2):
                            pyb = py[ti * 2 + half]
                            op("pe", lambda e, half=half: e.matmul(pyb.t[:nt, :], a_.t[:, tt0:tt0 + nt], v_.t[:, ii, half * 512:(half + 1) * 512],
                                                                   start=(i == 0), stop=(i == 2 * KPI - 1)), r=[a_.k, v_.k], w=[pyb.k])
            for ti, (tt0, nt) in enumerate(tiles):
                for half in range(2):
                    pyb = py[ti * 2 + half]
                    op("act", lambda e, half=half: e.activation(out=ysb.t[:nt, half * 512:(half + 1) * 512], in_=pyb.t[:nt, :], func=AF.Copy),
                       r=[pyb.k], w=[ysb.k])
                c0 = t0 + tt0
                for m in range(8):
                    pt = banks[2 + m % 2]
                    op("pe", lambda e, m=m: e.matmul(pt.t[:, :nt], ysb.t[:nt, m * 128:(m + 1) * 128], ident[:nt, :nt], start=True, stop=True),
                       r=[ysb.k, CK], w=[pt.k])
                    if g5 < 4:
                        op("dve", lambda e, m=m: e.scalar_tensor_tensor(
                            xT.t[:, m, c0:c0 + nt], pt.t[:, :nt], modT.t[:, l, 40 + m, 0:1], xT.t[:, m, c0:c0 + nt], ALU.mult, ALU.add),
                            r=[pt.k, modT.k, xk[m][g5]], w=[xk[m][g5]])
                    else:
                        op("dve", lambda e, m=m: e.tensor_tensor(stmp.t[:, :nt], pt.t[:, :nt], modT.t[:, l, 40 + m, 1:17], ALU.mult),
                           r=[pt.k, modT.k], w=[stmp.k])
                        op("dve", lambda e, m=m: e.tensor_tensor(xT.t[:, m, c0:c0 + nt], xT.t[:, m, c0:c0 + nt], stmp.t[:, :nt], ALU.add),
                           r=[stmp.k, xk[m][g5]], w=[xk[m][g5]])
        P.barrier()
        es.close()


def _consts():
    c = np.zeros((128, 6, 128), np.float32)
    p = np.arange(128)[:, None]
    f = np.arange(128)[None, :]
    c[:, 0] = (p == f)
    c[:, 1] = (p <= f)
    c[:, 2] = 1.0
    c[:, 3] = np.where(f >= p, BIG, 0.0)
    c[:, 4] = np.where(f < p, -BIG, 0.0)
    c[:, 5] = f
    return c


def fm(v, k):
    return np.ascontiguousarray(np.asarray(v, np.float32).reshape(k, 128).T)


def make_in_maps(inp, ncores=8):
    f32 = np.float32
    shared = {}
    shared["ada_w"] = np.ascontiguousarray(inp["ada_w"], f32)
    shared["ada_b"] = np.ascontiguousarray(np.stack([fm(inp["ada_b"][l], 48) for l in range(2)], 1))
    shared["n1g"] = np.ascontiguousarray(np.stack([fm(inp["norm1_g"][l], 8) for l in range(2)], 1))
    shared["n2g"] = np.ascontiguousarray(np.stack([fm(inp["norm2_g"][l], 8) for l in range(2)], 1))
    shared["fg"] = fm(inp["final_g"], 8)
    shared["consts"] = _consts()
    shared["ab_w_in"] = np.ascontiguousarray(inp["ab_w_in"][0], f32)
    cw = inp["ab_conv_w"][0]
    shared["cw"] = np.ascontiguousarray(cw.reshape(4, 36, 128).transpose(2, 1, 0))
    shared["cb"] = fm(inp["ab_conv_b"][0], 36)
    auxp = np.zeros((1, 5, 32), f32)
    auxp[0, 0, 8:16] = inp["gdn_dt_bias"][0]
    auxp[0, 0, 16:32] = inp["ssm_dt_bias"][0]
    auxp[0, 1, 8:16] = inp["gdn_a_log"][0]
    auxp[0, 1, 16:32] = inp["ssm_a_log"][0]
    auxp[0, 2, 16:32] = inp["ssm_d"][0]
    shared["auxp"] = auxp
    shared["gng"] = np.ascontiguousarray(inp["gdn_norm_g"][0].reshape(128, 1), f32)
    shared["sng"] = fm(inp["ssm_norm_g"][0], 8)
    shared["ab_w_out"] = np.ascontiguousarray(inp["ab_w_out"][0], f32)
    shared["ret_w_in"] = np.ascontiguousarray(inp["ret_w_in"][0], f32)
    shared["ret_w_out"] = np.ascontiguousarray(inp["ret_w_out"][0], f32)
    shared["w_q"] = np.ascontiguousarray(inp["peer_w_q"], f32)
    shared["keysT"] = np.ascontiguousarray(np.asarray(inp["peer_keys"], f32).transpose(0, 1, 3, 2))
    shared["uT"] = np.ascontiguousarray(np.asarray(inp["peer_u"], f32).transpose(0, 2, 1))
    shared["pv"] = np.ascontiguousarray(inp["peer_v"], f32)
    shared["rng"] = np.ascontiguousarray(inp["ret_norm_g"], f32).reshape(1, 4, 512)
    retc = np.zeros((128, 4, 132), np.float64)
    ii = np.arange(128)
    for h in range(4):
        gm = 1.0 - 2.0 ** (-5.0 - h)
        d = ii[None, :] - ii[:, None]
        retc[:, h, 0:128] = np.where(d >= 0, gm ** np.maximum(d, 0), 0.0)
        retc[:, h, 128] = gm ** (ii + 1)
        retc[:, h, 129] = gm ** (127 - ii)
        retc[:, h, 130] = gm
        retc[:, h, 131] = 1.0
    shared["retc"] = retc.astype(f32)
    inv = 10000.0 ** (-np.arange(0, 256, 2, dtype=np.float64) / 256)
    pos = np.concatenate([np.arange(2048, dtype=np.float64), np.full(16, 16384.0)])
    ang = (pos[None, :].astype(f32) * inv[:, None].astype(f32)).astype(f32)
    shared["cosT"] = np.cos(ang.astype(np.float64)).astype(f32)
    shared["sinT"] = np.sin(ang.astype(np.float64)).astype(f32)
    maps = []
    for b in range(ncores):
        m = dict(shared)
        x_all = np.concatenate([inp["x_prompt"][b], inp["x_sample"][16 * b:16 * b + 16, 0]], 0)
        m["xT"] = np.ascontiguousarray(x_all.reshape(NT, 8, 128).transpose(2, 1, 0))
        c_all = np.concatenate([inp["c_prompt"][b:b + 1], inp["c_sample"][16 * b:16 * b + 16]], 0)
        m["cT"] = np.ascontiguousarray(c_all.reshape(17, 8, 128).transpose(2, 1, 0))
        sc = inp["state_conv"][0, 16 * b:16 * b + 16]
        m["st_conv"] = np.ascontiguousarray(sc.reshape(16, 3, 36, 128).transpose(3, 2, 0, 1))
        m["st_gdn"] = np.ascontiguousarray(inp["state_gdn"][0, 16 * b:16 * b + 16])
        m["st_ssm"] = np.ascontiguousarray(inp["state_ssm"][0, 16 * b:16 * b + 16])
        m["st_ret"] = np.ascontiguousarray(inp["state_ret"][0, 16 * b:16 * b + 16])
        maps.append(m)
    return maps


_CACHE = {}


def run(inp, ncores=8, dbg=(), stages=("l0", "p0", "l1", "p1")):
    bld = Builder(dbg=dbg, stages=stages)
    bld.build()
    maps = make_in_maps(inp, ncores)
    maps = [{k: v for k, v in m.items() if k in bld.in_names} for m in maps]
    res = run_bass_kernel_spmd(bld.nc, maps, core_ids=list(range(ncores)))
    return res.results


def kernel(**inputs):
    inp = {k: np.asarray(v) for k, v in inputs.items()}
    n = 8
    bld = Builder(dbg=("xb0", "xa1"))
    bld.build()
    maps = make_in_maps(inp, n)
    maps = [{k: v for k, v in m.items() if k in bld.in_names} for m in maps]
    res = run_bass_kernel_spmd(bld.nc, maps, core_ids=list(range(n))).results
    f32 = np.float32
    y_prompt = np.zeros((8, 2048, 1024), f32)
    y_sample = np.zeros((128, 1, 1024), f32)
    conv_p = np.zeros((1, 8, 3, 4608), f32)
    gdn_p = np.zeros((1, 8, 8, 128, 128), f32)
    ssm_p = np.zeros((1, 8, 16, 128, 64), f32)
    ret_p = np.zeros((1, 8, 4, 256, 512), f32)
    conv_s = np.zeros((1, 128, 3, 4608), f32)
    gdn_s = np.zeros((1, 128, 8, 128, 128), f32)
    ssm_s = np.zeros((1, 128, 16, 128, 64), f32)
    ret_s = np.zeros((1, 128, 4, 256, 512), f32)
    for b in range(n):
        r = res[b]
        y = np.asarray(r["yT"]).transpose(2, 1, 0).reshape(NT, 1024)
        y_prompt[b] = y[:2048]
        y_sample[16 * b:16 * b + 16, 0] = y[2048:]
        conv_p[0, b] = np.asarray(r["conv_p"]).transpose(2, 1, 0).reshape(3, 4608)
        gdn_p[0, b] = r["gdn_p"]
        ssm_p[0, b] = r["ssm_p"]
        ret_p[0, b] = r["ret_p"]
        conv_s[0, 16 * b:16 * b + 16] = np.asarray(r["conv_s"]).transpose(2, 3, 1, 0).reshape(16, 3, 4608)
        gdn_s[0, 16 * b:16 * b + 16] = r["gdn_s"]
        ssm_s[0, 16 * b:16 * b + 16] = r["ssm_s"]
        ret_s[0, 16 * b:16 * b + 16] = r["ret_s"]
    return (y_prompt, y_sample, conv_p, gdn_p, ssm_p, ret_p, conv_s, gdn_s, ssm_s, ret_s)
```

```python
import contextlib
import os
import numpy as np
import concourse.bass as bass
import concourse.mybir as mybir
from concourse.bass_utils import run_bass_kernel_spmd

F32 = mybir.dt.float32
BF16 = mybir.dt.bfloat16
I32 = mybir.dt.int32
U32 = mybir.dt.uint32
AF = mybir.ActivationFunctionType
ALU = mybir.AluOpType
AX = mybir.AxisListType

NT = 2064
NP = 2048
NS = 16
EPS = 1e-6
BIG = 30000.0


class Tk:
    __slots__ = ("name", "w", "rs")

    def __init__(self, name=""):
        self.name = name
        self.w = []
        self.rs = []


class Lane:
    __slots__ = ("sem", "cum", "key")

    def __init__(self, sem, key):
        self.sem = sem
        self.cum = 0
        self.key = key


class _Rec:
    __slots__ = ("call",)

    def __init__(self):
        self.call = None

    def __getattr__(self, name):
        def f(*a, **kw):
            self.call = (name, a, kw)
            return self
        return f


def _eager(fn):
    r = _Rec()
    fn(r)
    name, a, kw = r.call
    return lambda e: getattr(e, name)(*a, **kw)


class Prog:
    ENGS = ("pe", "dve", "act", "pool", "sp")

    def __init__(self, nc):
        self.nc = nc
        self.ops = {e: [] for e in self.ENGS}
        self.cnt = {e: 0 for e in self.ENGS}
        self.sems = {}
        self.waited = {e: {} for e in self.ENGS}
        self.cur = None
        self._ctx = []
        for e in ("pe", "dve", "act", "pool"):
            self.sems[e] = self._sem("p_" + e)
        self.lanes = []

    def _sem(self, name):
        g = self.nc.semaphore(name)
        s = g.__enter__()
        self._ctx.append(g)
        return s

    def lane(self):
        key = "L%d" % len(self.lanes)
        self.sems[key] = self._sem(key)
        ln = Lane(self.sems[key], key)
        self.lanes.append(ln)
        return ln

    def _need(self, eng, deps):
        wd = self.waited[eng]
        best = {}
        for k, v in deps:
            if eng == "pe" and k == "pe":
                continue
            if wd.get(k, 0) >= v:
                continue
            if best.get(k, 0) < v:
                best[k] = v
        for k, v in best.items():
            wd[k] = v
            sem = self.sems[k]
            self.ops[eng].append(lambda e, sem=sem, v=v: e.wait_ge(sem, v))

    @staticmethod
    def _deps_for(r, w):
        deps = []
        for t in r:
            deps.extend(t.w)
        for t in w:
            deps.extend(t.w)
            deps.extend(t.rs)
        return deps

    def op(self, eng, fn, r=(), w=()):
        fn = _eager(fn)
        if self.cur is not None:
            self.cur.append(("op", (eng, fn, tuple(r), tuple(w))))
            return
        self._op(eng, fn, r, w)

    def _op(self, eng, fn, r=(), w=()):
        self._need(eng, self._deps_for(r, w))
        self.cnt[eng] += 1
        n = self.cnt[eng]
        sem = self.sems[eng]
        self.ops[eng].append(lambda e, fn=fn, sem=sem: fn(e).then_inc(sem, 1))
        tag = (eng, n)
        for t in r:
            t.rs.append(tag)
        for t in w:
            t.w = [tag]
            t.rs = []

    def dma(self, q, lane, fn, r=(), w=(), group=False):
        fn = _eager(fn)
        if self.cur is not None:
            self.cur.append(("dma", (q, lane, fn, tuple(r), tuple(w), group)))
            return
        self._dma(q, lane, fn, r, w, group)

    @contextlib.contextmanager
    def record(self):
        outer = self.cur
        th = []
        self.cur = th
        try:
            yield th
        finally:
            self.cur = outer

    @contextlib.contextmanager
    def atomic(self):
        if self.cur is None:
            yield
            return
        outer = self.cur
        blk = []
        self.cur = blk
        try:
            yield
        finally:
            self.cur = outer
            outer.append(("blk", blk))

    def _dispatch(self, item):
        kind, a = item
        if kind == "op":
            self._op(*a)
        elif kind == "dma":
            self._dma(*a)
        else:
            for it in a:
                self._dispatch(it)

    def merge(self, threads):
        assert self.cur is None
        idx = [0] * len(threads)
        live = True
        while live:
            live = False
            for i, th in enumerate(threads):
                if idx[i] < len(th):
                    self._dispatch(th[idx[i]])
                    idx[i] += 1
                    live = True

    def _dma(self, q, lane, fn, r=(), w=(), group=False):
        deps = self._deps_for(r, w)
        if group:
            deps = [d for d in deps if d[0] != lane.key]
        elif lane.cum > 0:
            deps.append((lane.key, lane.cum))
        self._need(q, deps)
        lane.cum += 16
        sem = lane.sem
        self.ops[q].append(lambda e, fn=fn, sem=sem: fn(e).then_inc(sem, 16))
        tag = (lane.key, lane.cum)
        for t in r:
            t.rs.append(tag)
        for t in w:
            if group:
                t.w = [x for x in t.w if x[0] != lane.key] + [tag]
            else:
                t.w = [tag]
            t.rs = []

    def group_finish(self, lane, tks):
        tag = (lane.key, lane.cum)
        for t in tks:
            t.w = [tag if x[0] == lane.key else x for x in t.w]
            t.rs = [tag if x[0] == lane.key else x for x in t.rs]

    def wait_all(self, eng):
        deps = [(e, self.cnt[e]) for e in ("pe", "dve", "act", "pool") if self.cnt[e] > 0]
        deps += [(ln.key, ln.cum) for ln in self.lanes if ln.cum > 0]
        self._need(eng, deps)

    def barrier(self):
        for e in self.ENGS:
            self.wait_all(e)

    def emit(self):
        with self.nc.Block() as block:
            def mk(name):
                lst = self.ops[name]

                def body(e):
                    for f in lst:
                        f(e)
                return body
            block.tensor(mk("pe"))
            block.vector(mk("dve"))
            block.scalar(mk("act"))
            block.gpsimd(mk("pool"))
            block.sync(mk("sp"))

    def close(self):
        for g in reversed(self._ctx):
            g.__exit__(None, None, None)
        self._ctx = []


class B:
    __slots__ = ("t", "k")

    def __init__(self, t, name=""):
        self.t = t
        self.k = Tk(name)


def interleave(gens, width):
    gens = list(gens)
    active = []
    while gens or active:
        while gens and len(active) < width:
            active.append(gens.pop(0))
        nxt = []
        for g in active:
            try:
                next(g)
                nxt.append(g)
            except StopIteration:
                pass
        active = nxt


class Builder:
    def __init__(self, dbg=(), stages=("l0", "p0", "l1", "p1")):
        self.nc = bass.Bass("TRN2", target_bir_lowering=False)
        self.P = Prog(self.nc)
        self.es = contextlib.ExitStack()
        self.dbg = set(dbg)
        self.stages = stages
        self.in_names = []
        self.out_names = []
        self.rr = 0

    def din(self, name, shape, dt=F32):
        self.in_names.append(name)
        return self.nc.dram_tensor(name, list(shape), dt, kind="ExternalInput").ap()

    def dout(self, name, shape, dt=F32):
        self.out_names.append(name)
        return self.nc.dram_tensor(name, list(shape), dt, kind="ExternalOutput").ap()

    def dscratch(self, name, shape, dt=F32):
        return self.nc.dram_tensor(name, list(shape), dt, kind="Internal").ap()

    def sb(self, name, shape, dt=F32, es=None):
        t = (es or self.es).enter_context(self.nc.sbuf_tensor(name, list(shape), dt))
        return B(t, name)

    def psb(self, name, shape, dt=F32):
        t = self.es.enter_context(self.nc.psum_tensor(name, list(shape), dt))
        return B(t, name)

    def ew(self):
        self.rr += 1
        return ("dve", "pool")[self.rr % 2]

    def build(self):
        nc, P = self.nc, self.P
        op, dma = P.op, P.dma
        xT_d = self.din("xT", [128, 8, NT])
        cT_d = self.din("cT", [128, 8, 17])
        ada_w = self.din("ada_w", [2, 1024, 6144])
        ada_b = self.din("ada_b", [128, 2, 48])
        n1g = self.din("n1g", [128, 2, 8])
        n2g = self.din("n2g", [128, 2, 8])
        fg = self.din("fg", [128, 8])
        consts_d = self.din("consts", [128, 6, 128])
        w_in = self.din("ab_w_in", [1024, 6688])
        cw_d = self.din("cw", [128, 36, 4])
        cb_d = self.din("cb", [128, 36])
        auxp_d = self.din("auxp", [1, 5, 32])
        gng_d = self.din("gng", [128, 1])
        sng_d = self.din("sng", [128, 8])
        w_out = self.din("ab_w_out", [2048, 1024])
        D = dict(w_in=w_in, w_out=w_out, cw=cw_d, cb=cb_d, auxp=auxp_d, gng=gng_d, sng=sng_d)
        D["st_conv"] = self.din("st_conv", [128, 36, 16, 3])
        D["st_gdn"] = self.din("st_gdn", [16, 8, 128, 128])
        D["st_ssm"] = self.din("st_ssm", [16, 16, 128, 64])
        D["ret_w_in"] = self.din("ret_w_in", [1024, 6144])
        D["ret_w_out"] = self.din("ret_w_out", [2048, 1024])
        D["rng"] = self.din("rng", [1, 4, 512])
        D["retc"] = self.din("retc", [128, 4, 132])
        D["cosT"] = self.din("cosT", [128, NT])
        D["sinT"] = self.din("sinT", [128, NT])
        D["st_ret"] = self.din("st_ret", [16, 4, 256, 512])
        D["w_q"] = self.din("w_q", [2, 1024, 2048])
        D["keysT"] = self.din("keysT", [2, 2, 128, 128])
        D["uT"] = self.din("uT", [2, 1024, 16384])
        D["pv"] = self.din("pv", [2, 16384, 1024])
        yT_d = self.dout("yT", [128, 8, NT])
        D["ret_p"] = self.dout("ret_p", [4, 256, 512])
        D["ret_s"] = self.dout("ret_s", [16, 4, 256, 512])
        D["conv_p"] = self.dout("conv_p", [128, 36, 3])
        D["gdn_p"] = self.dout("gdn_p", [8, 128, 128])
        D["ssm_p"] = self.dout("ssm_p", [16, 128, 64])
        D["conv_s"] = self.dout("conv_s", [128, 36, 16, 3])
        D["gdn_s"] = self.dout("gdn_s", [16, 8, 128, 128])
        D["ssm_s"] = self.dout("ssm_s", [16, 16, 128, 64])
        dbg_d = {}
        for nm in self.dbg:
            dbg_d[nm] = self.dout("dbg_" + nm, [128, 8, NT])

        xT = self.sb("xT_sb", [128, 8, NT])
        xk = [[Tk("x%d_%d" % (m, g)) for g in range(5)] for m in range(8)]
        cT = self.sb("cT_sb", [128, 8, 17])
        modT = self.sb("modT", [128, 2, 48, 17])
        consts = self.sb("consts_sb", [128, 6, 128])
        n1 = self.sb("n1_sb", [128, 2, 8])
        n2 = self.sb("n2_sb", [128, 2, 8])
        fgs = self.sb("fg_sb", [128, 8])
        adab = self.sb("adab_sb", [128, 2, 48])
        ident = consts.t[:, 0, :]
        triU = consts.t[:, 1, :]
        ones = consts.t[:, 2, :]
        maskA = consts.t[:, 3, :]
        maskB = consts.t[:, 4, :]
        iota = consts.t[:, 5, :]
        CK = consts.k

        banks = [self.psb("ps%d" % i, [128, 512]) for i in range(8)]
        self.banks = banks

        lc = P.lane()
        first = [True]

        def cload(dst, src_ap, q="sp"):
            dma(q, lc, lambda e: e.dma_start(out=dst.t[:], in_=src_ap), w=[dst.k], group=not first[0])
            first[0] = False

        cload(consts, consts_d[:, :, :])
        cload(cT, cT_d[:, :, :])
        cload(n1, n1g[:, :, :])
        cload(n2, n2g[:, :, :])
        cload(fgs, fg[:, :])
        cload(adab, ada_b[:, :, :])
        P.group_finish(lc, [consts.k, cT.k, n1.k, n2.k, fgs.k, adab.k])
        lx = P.lane()
        for m in range(8):
            dma("sp", lx, lambda e, m=m: e.dma_start(out=xT.t[:, m, :], in_=xT_d[:, m, :]),
                w=xk[m], group=(m > 0))
        P.group_finish(lx, [t for row in xk for t in row])

        op("act", lambda e: e.activation(out=cT.t[:], in_=cT.t[:], func=AF.Silu), r=[cT.k], w=[cT.k])
        with contextlib.ExitStack() as es:
            NAB = 4
            abuf = [self.sb("adaw%d" % i, [128, 8, 128], es=es) for i in range(NAB)]
            la = [P.lane() for _ in range(NAB)]
            it = 0
            for l in range(2):
                for half in range(2):
                    pst = banks[half]
                    for mm in range(24):
                        bi = it % NAB
                        it += 1
                        buf = abuf[bi]
                        col0 = (half * 24 + mm) * 128
                        dma("sp", la[bi], lambda e, l=l, col0=col0, buf=buf: e.dma_start(
                            out=buf.t[:], in_=ada_w[l, :, col0:col0 + 128].rearrange("(k p) c -> p k c", p=128)),
                            w=[buf.k])
                        for kk in range(8):
                            op("pe", lambda e, mm=mm, kk=kk, buf=buf, pst=pst: e.matmul(
                                pst.t[:, mm * 17:(mm + 1) * 17], buf.t[:, kk, :], cT.t[:, kk, :],
                                start=(kk == 0), stop=(kk == 7)), r=[buf.k, cT.k], w=[pst.k])
                    op("dve", lambda e, l=l, half=half, pst=pst: e.tensor_tensor(
                        modT.t[:, l, half * 24:(half + 1) * 24, :],
                        pst.t[:, 0:408].rearrange("p (m j) -> p m j", j=17),
                        adab.t[:, l, half * 24:(half + 1) * 24].unsqueeze(2).broadcast_to([128, 24, 17]), ALU.add),
                        r=[pst.k, adab.k], w=[modT.k])
            P.barrier()

        modA = self.sb("modA", [128, 2, 2, 8, 17])
        for l in range(2):
            for wh in range(2):
                sc0 = 8 + 24 * wh
                gsrc = (n1, n2)[wh]
                op("dve", lambda e, l=l, wh=wh, sc0=sc0: e.tensor_scalar(
                    modA.t[:, l, wh, :, :], modT.t[:, l, sc0:sc0 + 8, :], 1.0, None, ALU.add),
                    r=[modT.k], w=[modA.k])
                op("dve", lambda e, l=l, wh=wh, gsrc=gsrc: e.tensor_tensor(
                    modA.t[:, l, wh, :, :], modA.t[:, l, wh, :, :],
                    gsrc.t[:, l, :].unsqueeze(2).broadcast_to([128, 8, 17]), ALU.mult),
                    r=[modA.k, gsrc.k], w=[modA.k])

        self.groups = [(g * 512, 512) for g in range(4)] + [(2048, 16)]

        nsq = self.sb("nsq", [128, 512])
        nrs = self.sb("nrs", [128, 512])
        ntmp = self.sb("ntmp", [128, 512])

        def norm_mod(g, hm, A_ap_fn, B_ap_fn, t0, T, samp_A=None, samp_B=None, out_dt_tile=None):
            pst = banks[7]
            for k in range(8):
                op("act", lambda e, k=k: e.activation(out=nsq.t[:, :T], in_=xT.t[:, k, t0:t0 + T], func=AF.Square),
                   r=[xk[k][g]], w=[nsq.k])
                op("pe", lambda e, k=k: e.matmul(pst.t[:, :T], ones, nsq.t[:, :T], start=(k == 0), stop=(k == 7)),
                   r=[nsq.k, CK], w=[pst.k])
            op("act", lambda e: e.activation(out=nrs.t[:, :T], in_=pst.t[:, :T], func=AF.Sqrt, bias=EPS, scale=1.0 / 1024),
               r=[pst.k], w=[nrs.k])
            op("dve", lambda e: e.reciprocal(nrs.t[:, :T], nrs.t[:, :T]), r=[nrs.k], w=[nrs.k])
            for k in range(8):
                if samp_A is None:
                    op("dve", lambda e, k=k: e.scalar_tensor_tensor(
                        ntmp.t[:, :T], xT.t[:, k, t0:t0 + T], A_ap_fn(k), nrs.t[:, :T], ALU.mult, ALU.mult),
                        r=[xk[k][g], nrs.k, modA.k, fgs.k], w=[ntmp.k])
                    if B_ap_fn is None:
                        op("act", lambda e, k=k: e.activation(out=hm.t[:, k, :T], in_=ntmp.t[:, :T], func=AF.Copy),
                           r=[ntmp.k], w=[hm.k])
                    else:
                        op("act", lambda e, k=k: e.activation(out=hm.t[:, k, :T], in_=ntmp.t[:, :T], func=AF.Identity,
                                                             bias=B_ap_fn(k), scale=1.0),
                           r=[ntmp.k, modT.k], w=[hm.k])
                else:
                    op("dve", lambda e, k=k: e.tensor_tensor(ntmp.t[:, :T], xT.t[:, k, t0:t0 + T], nrs.t[:, :T], ALU.mult),
                       r=[xk[k][g], nrs.k], w=[ntmp.k])
                    op("dve", lambda e, k=k: e.tensor_tensor(ntmp.t[:, :T], ntmp.t[:, :T], samp_A(k), ALU.mult),
                       r=[ntmp.k, modA.k], w=[ntmp.k])
                    op("dve", lambda e, k=k: e.tensor_tensor(hm.t[:, k, :T], ntmp.t[:, :T], samp_B(k), ALU.add),
                       r=[ntmp.k, modT.k], w=[hm.k])

        def norm_mod_layer(l, wh, g, hm):
            t0, T = self.groups[g]
            sh0 = 24 * wh
            if g < 4:
                norm_mod(g, hm, lambda k: modA.t[:, l, wh, k, 0:1], lambda k: modT.t[:, l, sh0 + k, 0:1], t0, T)
            else:
                norm_mod(g, hm, None, None, t0, T,
                         samp_A=lambda k: modA.t[:, l, wh, k, 1:17], samp_B=lambda k: modT.t[:, l, sh0 + k, 1:17])

        self.norm_mod = norm_mod
        self.nsq, self.nrs = nsq, nrs
        self.norm_mod_layer = norm_mod_layer
        self.xT, self.xk, self.modT, self.modA = xT, xk, modT, modA
        self.c = dict(ident=ident, triU=triU, ones=ones, maskA=maskA, maskB=maskB, iota=iota, CK=CK)

        def dump(nm):
            if nm in self.dbg:
                P.barrier()
                ld = P.lane()
                dma("sp", ld, lambda e: e.dma_start(out=dbg_d[nm][:, :, :], in_=xT.t[:]),
                    r=[t for row in xk for t in row])
                P.barrier()

        if "l0" in self.stages:
            self.layer0(D)
        dump("xa0")
        if "p0" in self.stages:
            self.peer(0, D)
        dump("xb0")
        if "l1" in self.stages:
            self.layer1(D)
        dump("xa1")
        if "p1" in self.stages:
            self.peer(1, D)
        dump("xb1")

        yb = self.sb("ybuf_fin", [128, 8, 512])
        ly = P.lane()
        for g in range(5):
            t0, T = self.groups[g]
            norm_mod(g, yb, lambda k: fgs.t[:, k:k + 1], None, t0, T)
            dma("sp", ly, lambda e, t0=t0, T=T: e.dma_start(out=yT_d[:, :, t0:t0 + T], in_=yb.t[:, :, :T]), r=[yb.k])
        P.wait_all("sp")
        P.emit()
        P.close()
        self.es.close()

    def layer0(self, D):
        nc, P = self.nc, self.P
        op, dma = P.op, P.dma
        c = self.c
        ident, triU, ones, maskA, maskB, CK = c["ident"], c["triU"], c["ones"], c["maskA"], c["maskB"], c["CK"]
        banks = self.banks
        xT, xk, modT, modA = self.xT, self.xk, self.modT, self.modA
        w_in, w_out = D["w_in"], D["w_out"]
        es = contextlib.ExitStack()
        sb = lambda name, shape, dt=F32: self.sb(name, shape, dt, es=es)
        KG = int(os.environ.get('KG', '5'))
        KH = int(os.environ.get('KH', '8'))
        KSS = int(os.environ.get('KSS', '1'))
        KSMP = int(os.environ.get('KSMP', '16'))

        lp = P.lane()
        cw = sb("cw_sb", [128, 36, 4])
        cb = sb("cb_sb", [128, 36])
        auxp = sb("auxp_sb", [128, 5, 32])
        gng = sb("gng_sb", [128, 1])
        sng = sb("sng_sb", [128, 8])
        dma("sp", lp, lambda e: e.dma_start(out=cw.t[:], in_=D["cw"][:, :, :]), w=[cw.k])
        dma("sp", lp, lambda e: e.dma_start(out=cb.t[:], in_=D["cb"][:, :]), w=[cb.k], group=True)
        dma("sp", lp, lambda e: e.dma_start(out=auxp.t[:], in_=D["auxp"][0:1, :, :].broadcast_to([128, 5, 32])), w=[auxp.k], group=True)
        dma("sp", lp, lambda e: e.dma_start(out=gng.t[:], in_=D["gng"][:, :]), w=[gng.k], group=True)
        dma("sp", lp, lambda e: e.dma_start(out=sng.t[:], in_=D["sng"][:, :]), w=[sng.k], group=True)
        P.group_finish(lp, [cw.k, cb.k, auxp.k, gng.k, sng.k])
        op("act", lambda e: e.activation(out=auxp.t[:, 1, 8:32], in_=auxp.t[:, 1, 8:32], func=AF.Exp), r=[auxp.k], w=[auxp.k])
        op("dve", lambda e: e.tensor_scalar(auxp.t[:, 1, 8:32], auxp.t[:, 1, 8:32], -1.0, None, ALU.mult), r=[auxp.k], w=[auxp.k])

        Sg = sb("Sg", [128, 8, 128])
        Sgk = [Tk("Sg%d" % h) for h in range(8)]
        Ss = sb("Ss", [128, 16, 64])
        Ssk = [Tk("Ss%d" % g) for g in range(2)]
        halo = sb("halo", [128, 36, 3])
        halok = [Tk("halo%d" % i) for i in range(36)]
        op("pool", lambda e: e.memset(Sg.t[:], 0.0), w=Sgk)
        op("pool", lambda e: e.memset(Ss.t[:], 0.0), w=Ssk)
        op("pool", lambda e: e.memset(halo.t[:], 0.0), w=halok)

        hm = sb("hm", [128, 8, 512], BF16)
        NWB = 3
        wbuf = [sb("wb%d" % i, [128, 8, 128], BF16) for i in range(NWB)]
        wl = [P.lane() for _ in range(NWB)]
        wi = [0]

        def load_w(col0, src, ncol=128, row0=0):
            i = wi[0] % NWB
            wi[0] += 1
            buf = wbuf[i]
            dma("pool", wl[i], lambda e: e.dma_start(
                out=buf.t[:, :, :ncol], in_=src[row0:row0 + 1024, col0:col0 + ncol].rearrange("(k p) c -> p k c", p=128)), w=[buf.k])
            return buf

        pring = [0]

        def proj_ps():
            b = banks[pring[0] % 2]
            pring[0] += 1
            return b

        def project(col0, T, ncol=128):
            buf = load_w(col0, w_in, ncol=ncol)
            pst = proj_ps()
            with P.atomic():
                for k in range(8):
                    op("pe", lambda e, k=k: e.matmul(pst.t[:ncol, :T], buf.t[:, k, :ncol], hm.t[:, k, :T],
                                                     start=(k == 0), stop=(k == 7)), r=[buf.k, hm.k], w=[pst.k])
            return pst

        NPRE = 2
        prehk = [Tk("preh%d" % i) for i in range(NPRE)]
        prei = [0]
        acc_t = [sb("cacc%d" % i, [128, 512]) for i in range(2)]
        acci = [0]
        pres = [sb("pres%d" % i, [128, 16, 4]) for i in range(2)]
        preshk = [Tk() for i in range(2)]
        presi = [0]
        lcs_in = [P.lane() for _ in range(2)]
        lcs_out = P.lane()

        def conv_chunk(ch, T, dst_ap, dst_k, col0):
            pst = project(col0, T)
            p = pre[prei[0] % NPRE]
            ph = prehk[prei[0] % NPRE]
            prei[0] += 1
            op("act", lambda e: e.activation(out=p.t[:, 3:3 + T], in_=pst.t[:, :T], func=AF.Copy), r=[pst.k], w=[p.k])
            op("pool", lambda e: e.tensor_copy(p.t[:, 0:3], halo.t[:, ch, :]), r=[halok[ch]], w=[ph])
            a = acc_t[acci[0] % 2]
            acci[0] += 1
            op("dve", lambda e: e.tensor_scalar(a.t[:, :T], p.t[:, 0:T], cw.t[:, ch, 0:1], cb.t[:, ch:ch + 1], ALU.mult, ALU.add),
               r=[p.k, ph, cw.k, cb.k], w=[a.k])
            for i in range(1, 4):
                op("dve", lambda e, i=i: e.scalar_tensor_tensor(a.t[:, :T], p.t[:, i:i + T], cw.t[:, ch, i:i + 1], a.t[:, :T],
                                                                ALU.mult, ALU.add), r=[p.k, ph, a.k], w=[a.k])
            op("pool", lambda e: e.tensor_copy(halo.t[:, ch, :], p.t[:, T:T + 3]), r=[p.k, ph], w=[halok[ch]])
            op("act", lambda e: e.activation(out=dst_ap, in_=a.t[:, :T], func=AF.Silu), r=[a.k], w=[dst_k])

        def conv_chunk_s(ch, dst_ap, dst_k, col0):
            pst = project(col0, 16)
            j = presi[0] % 2
            presi[0] += 1
            p, ph = pres[j], preshk[j]
            dma("sp", lcs_in[j], lambda e: e.dma_start(out=p.t[:, :, 0:3], in_=D["st_conv"][:, ch, :, :]), w=[ph])
            op("act", lambda e: e.activation(out=p.t[:, :, 3], in_=pst.t[:, :16], func=AF.Copy), r=[pst.k], w=[p.k])
            a = acc_t[acci[0] % 2]
            acci[0] += 1
            op("dve", lambda e: e.tensor_scalar(a.t[:, :16], p.t[:, :, 0], cw.t[:, ch, 0:1], cb.t[:, ch:ch + 1], ALU.mult, ALU.add),
               r=[p.k, ph, cw.k, cb.k], w=[a.k])
            for i in range(1, 4):
                op("dve", lambda e, i=i: e.scalar_tensor_tensor(a.t[:, :16], p.t[:, :, i], cw.t[:, ch, i:i + 1], a.t[:, :16],
                                                                ALU.mult, ALU.add), r=[p.k, ph, a.k], w=[a.k])
            dma("sp", lcs_out, lambda e: e.dma_start(out=D["conv_s"][:, ch, :, :], in_=p.t[:, :, 1:4]), r=[p.k, ph])
            op("act", lambda e: e.activation(out=dst_ap, in_=a.t[:, :16], func=AF.Silu), r=[a.k], w=[dst_k])

        qT = sb("qT", [128, 512])
        kT = sb("kT", [128, 512])
        vT = sb("vT", [128, 512])
        sgT = sb("sgT", [128, 512])
        sq = self.nsq
        rs = self.nrs
        oT = sb("oT", [128, 16, 512], BF16)
        oTk = [Tk("oT%d" % j) for j in range(16)]
        auxT = sb("auxT", [32, 512])
        aux = [sb("aux%d" % cc, [128, 32]) for cc in range(4)]
        beta = [sb("beta%d" % cc, [128, 8]) for cc in range(4)]
        gg = [sb("gg%d" % cc, [128, 24]) for cc in range(4)]
        dtt = [sb("dt%d" % cc, [128, 16]) for cc in range(4)]
        gc = [sb("gc%d" % cc, [128, 24]) for cc in range(4)]
        ngc = [sb("ngc%d" % cc, [128, 24]) for cc in range(4)]
        egc = [sb("egc%d" % cc, [128, 24]) for cc in range(4)]
        elast = [sb("elast%d" % cc, [128, 24]) for cc in range(4)]
        edl = [sb("edl%d" % cc, [128, 24]) for cc in range(4)]
        begc = [sb("begc%d" % cc, [128, 8]) for cc in range(4)]
        dte = [sb("dte%d" % cc, [128, 16]) for cc in range(4)]
        sptmp = sb("sptmp", [128, 32])

        def aux_compute(cc, src_ap, src_k, sample=False):
            pa = banks[7]
            op("pe", lambda e: e.matmul(pa.t[:, 0:32], src_ap, ident[:32, :32], start=True, stop=True), r=[src_k, CK], w=[pa.k])
            op("dve", lambda e: e.tensor_copy(aux[cc].t[:], pa.t[:, 0:32]), r=[pa.k], w=[aux[cc].k])
            op("act", lambda e: e.activation(out=beta[cc].t[:], in_=aux[cc].t[:, 0:8], func=AF.Sigmoid), r=[aux[cc].k], w=[beta[cc].k])
            op("dve", lambda e: e.tensor_tensor(sptmp.t[:, 8:32], aux[cc].t[:, 8:32], auxp.t[:, 0, 8:32], ALU.add),
               r=[aux[cc].k, auxp.k], w=[sptmp.k])
            op("act", lambda e: e.activation(out=sptmp.t[:, 8:32], in_=sptmp.t[:, 8:32], func=AF.Exp), r=[sptmp.k], w=[sptmp.k])
            op("act", lambda e: e.activation(out=sptmp.t[:, 8:32], in_=sptmp.t[:, 8:32], func=AF.Ln, bias=1.0), r=[sptmp.k], w=[sptmp.k])
            op("dve", lambda e: e.tensor_copy(dtt[cc].t[:], sptmp.t[:, 16:32]), r=[sptmp.k], w=[dtt[cc].k])
            op("dve", lambda e: e.tensor_tensor(gg[cc].t[:], sptmp.t[:, 8:32], auxp.t[:, 1, 8:32], ALU.mult),
               r=[sptmp.k, auxp.k], w=[gg[cc].k])
            if sample:
                op("dve", lambda e: e.tensor_scalar(gg[cc].t[:], gg[cc].t[:], ident[:, 0:1], None, ALU.mult), r=[gg[cc].k, CK], w=[gg[cc].k])
            op("pe", lambda e: e.matmul(pa.t[:, 32:56], triU, gg[cc].t[:], start=True, stop=True), r=[gg[cc].k, CK], w=[pa.k])
            op("pe", lambda e: e.matmul(pa.t[:, 64:88], ones, gg[cc].t[:], start=True, stop=True), r=[gg[cc].k, CK], w=[pa.k])
            op("dve", lambda e: e.tensor_copy(gc[cc].t[:], pa.t[:, 32:56]), r=[pa.k], w=[gc[cc].k])
            op("dve", lambda e: e.tensor_scalar(ngc[cc].t[:], pa.t[:, 32:56], -1.0, None, ALU.mult), r=[pa.k], w=[ngc[cc].k])
            op("act", lambda e: e.activation(out=egc[cc].t[:], in_=pa.t[:, 32:56], func=AF.Exp), r=[pa.k], w=[egc[cc].k])
            op("act", lambda e: e.activation(out=elast[cc].t[:], in_=pa.t[:, 64:88], func=AF.Exp), r=[pa.k], w=[elast[cc].k])
            op("dve", lambda e: e.tensor_tensor(edl[cc].t[:], pa.t[:, 64:88], gc[cc].t[:], ALU.subtract), r=[pa.k, gc[cc].k], w=[edl[cc].k])
            op("act", lambda e: e.activation(out=edl[cc].t[:], in_=edl[cc].t[:], func=AF.Exp), r=[edl[cc].k], w=[edl[cc].k])
            op("dve", lambda e: e.tensor_tensor(begc[cc].t[:], beta[cc].t[:], egc[cc].t[:, 0:8], ALU.mult),
               r=[beta[cc].k, egc[cc].k], w=[begc[cc].k])
            op("dve", lambda e: e.tensor_tensor(dte[cc].t[:], dtt[cc].t[:], edl[cc].t[:, 8:24], ALU.mult),
               r=[dtt[cc].k, edl[cc].k], w=[dte[cc].k])

        def l2norm(buf, T, scale):
            pst = banks[7]
            op("act", lambda e: e.activation(out=sq.t[:, :T], in_=buf.t[:, :T], func=AF.Square), r=[buf.k], w=[sq.k])
            op("pe", lambda e: e.matmul(pst.t[:, :T], ones, sq.t[:, :T], start=True, stop=True), r=[sq.k, CK], w=[pst.k])
            op("act", lambda e: e.activation(out=rs.t[:, :T], in_=pst.t[:, :T], func=AF.Sqrt, bias=EPS, scale=1.0), r=[pst.k], w=[rs.k])
            op("dve", lambda e: e.reciprocal(rs.t[:, :T], rs.t[:, :T]), r=[rs.k], w=[rs.k])
            op("dve", lambda e: e.scalar_tensor_tensor(buf.t[:, :T], buf.t[:, :T], scale, rs.t[:, :T], ALU.mult, ALU.mult),
               r=[buf.k, rs.k], w=[buf.k])

        class S:
            pass
        s = S()
        for nm in ("tmp", "dec", "decT", "A", "Bm", "A2", "B2", "Pm", "P2", "kbg", "kd", "vb", "nwT", "u", "qk", "t1", "o", "on",
                   "GT", "Btok", "MT", "Bd"):
            setattr(s, nm, sb("u_%s" % nm, [128, 128]))
        s.ss = sb("u_ss", [128, 2])
        s.ps = [banks[2], banks[3]]
        s.pi = 0
        s2 = S()
        for nm in ("tmp", "dec", "decT", "A", "Bm", "A2", "B2", "Pm", "P2", "kbg", "kd", "vb", "nwT", "u", "qk", "t1", "o", "on"):
            setattr(s2, nm, sb("v_%s" % nm, [128, 128]))
        s2.ss = sb("v_ss", [128, 2])
        s2.ps = [banks[4], banks[5]]
        s2.pi = 0
        scr = [s, s2]
        qkv = [(qT, kT, vT, sgT), tuple(sb("t2_%s" % nm, [128, 512]) for nm in ("q", "k", "v", "sg"))]
        s.xtok = sb("u_xtok", [128, 512])
        s.t512 = sb("u_t512", [128, 512])
        s.t512b = sb("u_t512b", [128, 512])
        s.otok = sb("u_otok", [128, 512])
        fring = [0]

        def fullbank():
            b = banks[4 + fring[0] % 3]
            fring[0] += 1
            return b

        def pst_of(s):
            j = s.pi % 8
            s.pi += 1
            b = s.ps[j % 2]
            r = j // 2
            return b.t[:, r * 128:(r + 1) * 128], b.k

        def gdn_unit(s, h, cc, q_ap, k_ap, v_ap, sg_ap, rk, S_ap, S_k, out_ap, out_k, single=False):
            g_b = gg[cc].t[:, h:h + 1].broadcast_to([128, 128])
            p2, k2 = pst_of(s)
            with P.atomic():
                op("pe", lambda e: e.matmul(p2, ident, maskB, start=True, stop=False), r=[CK], w=[k2])
                op("pe", lambda e: e.matmul(p2, g_b, triU, start=False, stop=True), r=[gg[cc].k, CK], w=[k2])
            op("act", lambda e: e.activation(out=s.decT.t[:], in_=p2, func=AF.Exp, bias=ngc[cc].t[:, h:h + 1], scale=1.0),
               r=[k2, ngc[cc].k], w=[s.decT.k])
            if not single:
                p1, k1 = pst_of(s)
                with P.atomic():
                    op("pe", lambda e: e.matmul(p1, ident, maskA, start=True, stop=False), r=[CK], w=[k1])
                    op("pe", lambda e: e.matmul(p1, g_b, triU, start=False, stop=True), r=[gg[cc].k, CK], w=[k1])
                op("act", lambda e: e.activation(out=s.dec.t[:], in_=p1, func=AF.Exp, bias=gc[cc].t[:, h:h + 1], scale=-1.0),
                   r=[k1, gc[cc].k], w=[s.dec.k])
                p3, k3 = pst_of(s)
                op("pe", lambda e: e.matmul(p3, k_ap, k_ap, start=True, stop=True), r=rk, w=[k3])
                op("dve", lambda e: e.scalar_tensor_tensor(s.A.t[:], p3, beta[cc].t[:, h:h + 1], s.dec.t[:], ALU.mult, ALU.mult),
                   r=[k3, beta[cc].k, s.dec.k], w=[s.A.k])
                p4, k4 = pst_of(s)
                op("pe", lambda e: e.matmul(p4, s.A.t[:], ident, start=True, stop=True), r=[s.A.k, CK], w=[k4])
                op("act", lambda e: e.activation(out=s.Bm.t[:], in_=p4, func=AF.Copy), r=[k4], w=[s.Bm.k])
                op("dve", lambda e: e.tensor_tensor(s.Pm.t[:], ident, s.Bm.t[:], ALU.subtract), r=[s.Bm.k, CK], w=[s.Pm.k])
                Ac, Bc, An, Bn = s.A, s.Bm, s.A2, s.B2
                Pc, Pn = s.Pm, s.P2
                for lvl in range(6):
                    pa_, ka = pst_of(s)
                    op("pe", lambda e, Ac=Ac, Bc=Bc, pa_=pa_: e.matmul(pa_, Bc.t[:], Ac.t[:], start=True, stop=True),
                       r=[Ac.k, Bc.k], w=[ka])
                    op("act", lambda e, An=An, pa_=pa_: e.activation(out=An.t[:], in_=pa_, func=AF.Copy), r=[ka], w=[An.k])
                    if lvl < 5:
                        pb, kb = pst_of(s)
                        op("pe", lambda e, Ac=Ac, Bc=Bc, pb=pb: e.matmul(pb, Ac.t[:], Bc.t[:], start=True, stop=True),
                           r=[Ac.k, Bc.k], w=[kb])
                        op("dve", lambda e, Bn=Bn, pb=pb: e.tensor_copy(Bn.t[:], pb), r=[kb], w=[Bn.k])
                    pc, kc = pst_of(s)
                    with P.atomic():
                        op("pe", lambda e, Pc=Pc, pc=pc: e.matmul(pc, ident, Pc.t[:], start=True, stop=False), r=[Pc.k, CK], w=[kc])
                        op("pe", lambda e, Pc=Pc, An=An, pc=pc: e.matmul(pc, An.t[:], Pc.t[:], start=False, stop=True),
                           r=[Pc.k, An.k], w=[kc])
                    op("dve", lambda e, Pn=Pn, pc=pc: e.tensor_copy(Pn.t[:], pc), r=[kc], w=[Pn.k])
                    Ac, An = An, Ac
                    Bc, Bn = Bn, Bc
                    Pc, Pn = Pn, Pc
                TT_ap, TT_k = Pc.t[:], Pc.k
            else:
                TT_ap, TT_k = ident, CK
            p5, k5 = pst_of(s)
            op("pe", lambda e: e.matmul(p5, k_ap, ident, start=True, stop=True), r=rk + [CK], w=[k5])
            op("dve", lambda e: e.tensor_scalar(s.kbg.t[:], p5, begc[cc].t[:, h:h + 1], None, ALU.mult), r=[k5, begc[cc].k], w=[s.kbg.k])
            op("dve", lambda e: e.tensor_scalar(s.kd.t[:], p5, edl[cc].t[:, h:h + 1], None, ALU.mult), r=[k5, edl[cc].k], w=[s.kd.k])
            p6, k6 = pst_of(s)
            op("pe", lambda e: e.matmul(p6, v_ap, ident, start=True, stop=True), r=rk + [CK], w=[k6])
            op("dve", lambda e: e.tensor_scalar(s.vb.t[:], p6, beta[cc].t[:, h:h + 1], None, ALU.mult), r=[k6, beta[cc].k], w=[s.vb.k])
            p7, k7 = pst_of(s)
            op("pe", lambda e: e.matmul(p7, s.kbg.t[:], TT_ap, start=True, stop=True), r=[s.kbg.k, TT_k], w=[k7])
            op("act", lambda e: e.activation(out=s.nwT.t[:], in_=p7, func=AF.Identity, scale=-1.0), r=[k7], w=[s.nwT.k])
            p8, k8 = pst_of(s)
            with P.atomic():
                op("pe", lambda e: e.matmul(p8, TT_ap, s.vb.t[:], start=True, stop=False), r=[TT_k, s.vb.k], w=[k8])
                op("pe", lambda e: e.matmul(p8, s.nwT.t[:], S_ap, start=False, stop=True), r=[s.nwT.k, S_k], w=[k8])
            op("act", lambda e: e.activation(out=s.u.t[:], in_=p8, func=AF.Copy), r=[k8], w=[s.u.k])
            p9, k9 = pst_of(s)
            op("pe", lambda e: e.matmul(p9, k_ap, q_ap, start=True, stop=True), r=rk, w=[k9])
            op("dve", lambda e: e.tensor_tensor(s.qk.t[:], p9, s.decT.t[:], ALU.mult), r=[k9, s.decT.k], w=[s.qk.k])
            p10, k10 = pst_of(s)
            op("pe", lambda e: e.matmul(p10, q_ap, S_ap, start=True, stop=True), r=rk + [S_k], w=[k10])
            op("dve", lambda e: e.tensor_scalar(s.t1.t[:], p10, egc[cc].t[:, h:h + 1], None, ALU.mult), r=[k10, egc[cc].k], w=[s.t1.k])
            p11, k11 = pst_of(s)
            op("pe", lambda e: e.matmul(p11, s.qk.t[:], s.u.t[:], start=True, stop=True), r=[s.qk.k, s.u.k], w=[k11])
            op("dve", lambda e: e.tensor_tensor(s.o.t[:], p11, s.t1.t[:], ALU.add), r=[k11, s.t1.k], w=[s.o.k])
            p12, k12 = pst_of(s)
            op("pe", lambda e: e.matmul(p12, s.kd.t[:], s.u.t[:], start=True, stop=True), r=[s.kd.k, s.u.k], w=[k12])
            op("dve", lambda e: e.tensor_scalar(s.tmp.t[:], S_ap, elast[cc].t[:, h:h + 1], None, ALU.mult),
               r=[elast[cc].k, S_k], w=[s.tmp.k])
            op("dve", lambda e: e.tensor_tensor(S_ap, p12, s.tmp.t[:], ALU.add), r=[k12, s.tmp.k], w=[S_k])
            op("act", lambda e: e.activation(out=s.on.t[:], in_=s.o.t[:], func=AF.Square), r=[s.o.k], w=[s.on.k])
            op("dve", lambda e: e.reduce_sum(s.ss.t[:, 0:1], s.on.t[:], AX.X), r=[s.on.k], w=[s.ss.k])
            op("act", lambda e: e.activation(out=s.ss.t[:, 1:2], in_=s.ss.t[:, 0:1], func=AF.Sqrt, bias=EPS, scale=1.0 / 128), r=[s.ss.k], w=[s.ss.k])
            op("dve", lambda e: e.reciprocal(s.ss.t[:, 1:2], s.ss.t[:, 1:2]), r=[s.ss.k], w=[s.ss.k])
            op("dve", lambda e: e.tensor_scalar(s.on.t[:], s.o.t[:], s.ss.t[:, 1:2], None, ALU.mult), r=[s.o.k, s.ss.k], w=[s.on.k])
            p13, k13 = pst_of(s)
            op("pe", lambda e: e.matmul(p13, s.on.t[:], ident, start=True, stop=True), r=[s.on.k, CK], w=[k13])
            op("dve", lambda e: e.scalar_tensor_tensor(out_ap, p13, gng.t[:, 0:1], sg_ap, ALU.mult, ALU.mult),
               r=[k13, gng.k] + rk, w=[out_k])


        def ssd_unit(grp, cc, B_ap, C_ap, xs_aps, sz_aps, rk, S_ap, S_k, y_aps, y_ks):
            h0 = 8 * grp
            pg, kg_ = pst_of(s)
            op("pe", lambda e: e.matmul(pg, B_ap, C_ap, start=True, stop=True), r=rk, w=[kg_])
            op("act", lambda e: e.activation(out=s.GT.t[:], in_=pg, func=AF.Copy), r=[kg_], w=[s.GT.k])
            px = fullbank()
            for j in range(4):
                op("pe", lambda e, j=j: e.matmul(px.t[:, j * 128:(j + 1) * 128], xs_aps[j], ident, start=True, stop=True),
                   r=rk + [CK], w=[px.k])
            op("act", lambda e: e.activation(out=s.xtok.t[:], in_=px.t[:], func=AF.Copy), r=[px.k], w=[s.xtok.k])
            pb_, kb_ = pst_of(s)
            op("pe", lambda e: e.matmul(pb_, B_ap, ident, start=True, stop=True), r=rk + [CK], w=[kb_])
            op("dve", lambda e: e.tensor_copy(s.Btok.t[:], pb_), r=[kb_], w=[s.Btok.k])
            pcs = fullbank()
            op("pe", lambda e: e.matmul(pcs.t[:], C_ap, S_ap.rearrange("p h q -> p (h q)"), start=True, stop=True), r=rk + [S_k], w=[pcs.k])
            op("dve", lambda e: e.tensor_tensor(s.t512.t[:].rearrange("p (h q) -> p h q", q=64),
                                                pcs.t[:].rearrange("p (h q) -> p h q", q=64),
                                                egc[cc].t[:, 8 + h0:16 + h0].unsqueeze(2).broadcast_to([128, 8, 64]), ALU.mult),
               r=[pcs.k, egc[cc].k], w=[s.t512.k])
            po = fullbank()
            pS = fullbank()
            for hh in range(8):
                col = 8 + h0 + hh
                g_b = gg[cc].t[:, col:col + 1].broadcast_to([128, 128])
                pd, kd_ = pst_of(s)
                op("pe", lambda e: e.matmul(pd, ident, maskB, start=True, stop=False), r=[CK], w=[kd_])
                op("pe", lambda e: e.matmul(pd, g_b, triU, start=False, stop=True), r=[gg[cc].k, CK], w=[kd_])
                op("act", lambda e: e.activation(out=s.decT.t[:], in_=pd, func=AF.Exp, bias=ngc[cc].t[:, col:col + 1], scale=1.0),
                   r=[kd_, ngc[cc].k], w=[s.decT.k])
                op("dve", lambda e: e.scalar_tensor_tensor(s.MT.t[:], s.GT.t[:], dtt[cc].t[:, h0 + hh:h0 + hh + 1], s.decT.t[:],
                                                           ALU.mult, ALU.mult), r=[s.GT.k, dtt[cc].k, s.decT.k], w=[s.MT.k])
                op("pe", lambda e: e.matmul(po.t[:, hh * 64:(hh + 1) * 64], s.MT.t[:], s.xtok.t[:, hh * 64:(hh + 1) * 64],
                                            start=True, stop=True), r=[s.MT.k, s.xtok.k], w=[po.k])
                op("dve", lambda e: e.tensor_scalar(s.Bd.t[:], s.Btok.t[:], dte[cc].t[:, h0 + hh:h0 + hh + 1], None, ALU.mult),
                   r=[s.Btok.k, dte[cc].k], w=[s.Bd.k])
                op("pe", lambda e: e.matmul(pS.t[:, hh * 64:(hh + 1) * 64], s.Bd.t[:], s.xtok.t[:, hh * 64:(hh + 1) * 64],
                                            start=True, stop=True), r=[s.Bd.k, s.xtok.k], w=[pS.k])
            op("dve", lambda e: e.tensor_tensor(s.otok.t[:], po.t[:], s.t512.t[:], ALU.add), r=[po.k, s.t512.k], w=[s.otok.k])
            op("dve", lambda e: e.tensor_tensor(s.t512b.t[:].rearrange("p (h q) -> p h q", q=64),
                                                s.xtok.t[:].rearrange("p (h q) -> p h q", q=64),
                                                auxp.t[:, 2, 16 + h0:24 + h0].unsqueeze(2).broadcast_to([128, 8, 64]), ALU.mult),
               r=[s.xtok.k, auxp.k], w=[s.t512b.k])
            op("dve", lambda e: e.tensor_tensor(s.otok.t[:], s.otok.t[:], s.t512b.t[:], ALU.add), r=[s.otok.k, s.t512b.k], w=[s.otok.k])
            op("dve", lambda e: e.tensor_tensor(s.t512.t[:].rearrange("p (h q) -> p h q", q=64), S_ap,
                                                elast[cc].t[:, 8 + h0:16 + h0].unsqueeze(2).broadcast_to([128, 8, 64]), ALU.mult),
               r=[S_k, elast[cc].k, s.t512.k], w=[s.t512.k])
            op("dve", lambda e: e.tensor_tensor(S_ap, pS.t[:].rearrange("p (h q) -> p h q", q=64),
                                                s.t512.t[:].rearrange("p (h q) -> p h q", q=64), ALU.add),
               r=[pS.k, s.t512.k], w=[S_k])
            pt = fullbank()
            for j in range(4):
                op("pe", lambda e, j=j: e.matmul(pt.t[:, j * 128:(j + 1) * 128], s.otok.t[:, j * 128:(j + 1) * 128], ident,
                                                 start=True, stop=True), r=[s.otok.k, CK], w=[pt.k])
            for j in range(4):
                op("dve", lambda e, j=j: e.tensor_tensor(y_aps[j], pt.t[:, j * 128:(j + 1) * 128], sz_aps[j], ALU.mult),
                   r=[pt.k] + rk, w=[y_ks[j]])

        def ssd_norm_out(T):
            pst = banks[7]
            for j in range(8):
                op("act", lambda e, j=j: e.activation(out=sq.t[:, :T], in_=oT.t[:, 8 + j, :T], func=AF.Square), r=[oTk[8 + j]], w=[sq.k])
                op("pe", lambda e, j=j: e.matmul(pst.t[:, :T], ones, sq.t[:, :T], start=(j == 0), stop=(j == 7)), r=[sq.k, CK], w=[pst.k])
            op("act", lambda e: e.activation(out=rs.t[:, :T], in_=pst.t[:, :T], func=AF.Sqrt, bias=EPS, scale=1.0 / 1024), r=[pst.k], w=[rs.k])
            op("dve", lambda e: e.reciprocal(rs.t[:, :T], rs.t[:, :T]), r=[rs.k], w=[rs.k])
            for j in range(8):
                op("dve", lambda e, j=j: e.scalar_tensor_tensor(oT.t[:, 8 + j, :T], oT.t[:, 8 + j, :T], sng.t[:, j:j + 1], rs.t[:, :T],
                                                                ALU.mult, ALU.mult), r=[oTk[8 + j], sng.k, rs.k], w=[oTk[8 + j]])

        def out_proj(g, T, t0):
            for m in range(8):
                pst = proj_ps()
                for half in range(2):
                    buf = load_w(m * 128, w_out, row0=half * 1024)
                    for k in range(8):
                        kk = half * 8 + k
                        op("pe", lambda e, k=k, kk=kk: e.matmul(pst.t[:, :T], buf.t[:, k, :], oT.t[:, kk, :T],
                                                                start=(kk == 0), stop=(kk == 15)), r=[buf.k, oTk[kk]], w=[pst.k])
                if g < 4:
                    op("dve", lambda e, m=m: e.scalar_tensor_tensor(
                        xT.t[:, m, t0:t0 + T], pst.t[:, :T], modT.t[:, 0, 16 + m, 0:1], xT.t[:, m, t0:t0 + T], ALU.mult, ALU.add),
                        r=[pst.k, modT.k, xk[m][g]], w=[xk[m][g]])
                else:
                    op("dve", lambda e, m=m: e.tensor_tensor(sq.t[:, :T], pst.t[:, :T], modT.t[:, 0, 16 + m, 1:17], ALU.mult),
                       r=[pst.k, modT.k], w=[sq.k])
                    op("dve", lambda e, m=m: e.tensor_tensor(xT.t[:, m, t0:t0 + T], xT.t[:, m, t0:t0 + T], sq.t[:, :T], ALU.add),
                       r=[sq.k, xk[m][g]], w=[xk[m][g]])

        es_p = contextlib.ExitStack()
        sbp = lambda name, shape, dt=F32: self.sb(name, shape, dt, es=es_p)
        pre = [sbp("pre%d" % i, [128, 515]) for i in range(NPRE)]
        BT = [sbp("BT%d" % i, [128, 512]) for i in range(2)]
        CT = [sbp("CT%d" % i, [128, 512]) for i in range(2)]
        xsT = [sbp("xsT%d" % i, [128, 512]) for i in range(4)]
        szT = [sbp("szT%d" % i, [128, 512]) for i in range(4)]
        for g in range(min(KG, 4)):
            t0, T = self.groups[g]
            nch = T // 128
            self.norm_mod_layer(0, 0, g, hm)
            pst = project(6656, T, ncol=32)
            op("act", lambda e: e.activation(out=auxT.t[:, :T], in_=pst.t[:32, :T], func=AF.Copy), r=[pst.k], w=[auxT.k])
            for cc in range(nch):
                aux_compute(cc, auxT.t[:, cc * 128:(cc + 1) * 128], auxT.k)
            def head_thread(h, ti):
                q_, k_, v_, sg_ = qkv[ti]
                with P.atomic():
                    conv_chunk(h, T, q_.t[:, :T], q_.k, h * 128)
                    conv_chunk(8 + h, T, k_.t[:, :T], k_.k, 1024 + h * 128)
                    conv_chunk(16 + h, T, v_.t[:, :T], v_.k, 2048 + h * 128)
                    pst = project(4608 + h * 128, T)
                    op("act", lambda e: e.activation(out=sg_.t[:, :T], in_=pst.t[:, :T], func=AF.Silu), r=[pst.k], w=[sg_.k])
                    l2norm(q_, T, 128.0 ** -0.5)
                    l2norm(k_, T, 1.0)
                for cc in range(nch):
                    cs = slice(cc * 128, (cc + 1) * 128)
                    gdn_unit(scr[ti], h, cc, q_.t[:, cs], k_.t[:, cs], v_.t[:, cs], sg_.t[:, cs], [q_.k, k_.k, v_.k, sg_.k],
                             Sg.t[:, h, :], Sgk[h], oT.t[:, h, cs], oTk[h])
            WP = int(os.environ.get('KTHP', '2'))
            for h0 in range(0, KH, WP):
                ths = []
                for ti in range(min(WP, KH - h0)):
                    with P.record() as th:
                        head_thread(h0 + ti, ti)
                    ths.append(th)
                P.merge(ths)
            for j in range(KH, 8):
                op("pool", lambda e, j=j: e.memset(oT.t[:, j, :T], 0.0), w=[oTk[j]])
            if KSS:
                for grp in range(2):
                    conv_chunk(32 + grp, T, BT[grp].t[:, :T], BT[grp].k, 4096 + grp * 128)
                    conv_chunk(34 + grp, T, CT[grp].t[:, :T], CT[grp].k, 4352 + grp * 128)
                    for j in range(4):
                        jj = 4 * grp + j
                        conv_chunk(24 + jj, T, xsT[j].t[:, :T], xsT[j].k, 3072 + jj * 128)
                        pst = project(5632 + jj * 128, T)
                        op("act", lambda e, j=j: e.activation(out=szT[j].t[:, :T], in_=pst.t[:, :T], func=AF.Silu), r=[pst.k], w=[szT[j].k])
                    rk = [BT[grp].k, CT[grp].k] + [xsT[j].k for j in range(4)] + [szT[j].k for j in range(4)]
                    for cc in range(nch):
                        cs = slice(cc * 128, (cc + 1) * 128)
                        ssd_unit(grp, cc, BT[grp].t[:, cs], CT[grp].t[:, cs], [xsT[j].t[:, cs] for j in range(4)],
                                 [szT[j].t[:, cs] for j in range(4)], rk, Ss.t[:, 8 * grp:8 * grp + 8, :], Ssk[grp],
                                 [oT.t[:, 8 + 4 * grp + j, cs] for j in range(4)], [oTk[8 + 4 * grp + j] for j in range(4)])
                ssd_norm_out(T)
            else:
                for j in range(8, 16):
                    op("pool", lambda e, j=j: e.memset(oT.t[:, j, :T], 0.0), w=[oTk[j]])
            out_proj(g, T, t0)

        op_l = P.lane()
        dma("sp", op_l, lambda e: e.dma_start(out=D["gdn_p"].rearrange("h k v -> k h v"), in_=Sg.t[:]), r=Sgk)
        dma("sp", op_l, lambda e: e.dma_start(out=D["ssm_p"].rearrange("h n p -> n h p"), in_=Ss.t[:]), r=Ssk, group=True)
        dma("sp", op_l, lambda e: e.dma_start(out=D["conv_p"][:, :, :], in_=halo.t[:]), r=halok, group=True)
        P.group_finish(op_l, Sgk + Ssk + halok)

        P.barrier()
        es_p.close()
        if KG >= 5:
            g, (t0, T) = 4, self.groups[4]
            self.norm_mod_layer(0, 0, 4, hm)
            pst = project(6656, T, ncol=32)
            op("act", lambda e: e.activation(out=auxT.t[:, :T], in_=pst.t[:32, :T], func=AF.Copy), r=[pst.k], w=[auxT.k])
            qs = sb("qs", [128, 8, 16])
            ks_ = sb("ks", [128, 8, 16])
            vs = sb("vs", [128, 8, 16])
            sgs = sb("sgs", [128, 8, 16])
            Bs = sb("Bs", [128, 2, 16])
            Cs = sb("Cs", [128, 2, 16])
            xss = sb("xss", [128, 8, 16])
            szs = sb("szs", [128, 8, 16])
            for h in range(8):
                conv_chunk_s(h, qT.t[:, :16], qT.k, h * 128)
                conv_chunk_s(8 + h, kT.t[:, :16], kT.k, 1024 + h * 128)
                conv_chunk_s(16 + h, vs.t[:, h, :], vs.k, 2048 + h * 128)
                pst = project(4608 + h * 128, 16)
                op("act", lambda e, h=h: e.activation(out=sgs.t[:, h, :], in_=pst.t[:, :16], func=AF.Silu), r=[pst.k], w=[sgs.k])
                l2norm(qT, 16, 128.0 ** -0.5)
                l2norm(kT, 16, 1.0)
                op("pool", lambda e, h=h: e.tensor_copy(qs.t[:, h, :], qT.t[:, :16]), r=[qT.k], w=[qs.k])
                op("pool", lambda e, h=h: e.tensor_copy(ks_.t[:, h, :], kT.t[:, :16]), r=[kT.k], w=[ks_.k])
            for grp in range(2):
                conv_chunk_s(32 + grp, Bs.t[:, grp, :], Bs.k, 4096 + grp * 128)
                conv_chunk_s(34 + grp, Cs.t[:, grp, :], Cs.k, 4352 + grp * 128)
            for jj in range(8):
                conv_chunk_s(24 + jj, xss.t[:, jj, :], xss.k, 3072 + jj * 128)
                pst = project(5632 + jj * 128, 16)
                op("act", lambda e, jj=jj: e.activation(out=szs.t[:, jj, :], in_=pst.t[:, :16], func=AF.Silu), r=[pst.k], w=[szs.k])
            pads = {}
            for nm in ["q", "k", "v", "sg", "B", "C"] + ["xs%d" % j for j in range(4)] + ["sz%d" % j for j in range(4)]:
                pads[nm] = sb("pad_" + nm, [128, 128])
                op("pool", lambda e, nm=nm: e.memset(pads[nm].t[:], 0.0), w=[pads[nm].k])
            auxpad = sb("auxpad", [32, 128])
            op("pool", lambda e: e.memset(auxpad.t[:], 0.0), w=[auxpad.k])
            otmp = sb("otmp", [128, 128])
            otmps = [otmp, sb("otmp2", [128, 128])]
            pads2 = {}
            for nm in ["q", "k", "v", "sg"]:
                pads2[nm] = sb("pad2_" + nm, [128, 128])
                op("pool", lambda e, nm=nm: e.memset(pads2[nm].t[:], 0.0), w=[pads2[nm].k])
            padsets = [pads, pads2]
            ytmp = sb("ytmp", [128, 4, 128])
            ytk = [Tk() for _ in range(4)]
            NSB = 2
            Sgs = [sb("Sgs%d" % i, [128, 128]) for i in range(NSB)]
            Sss = [sb("Sss%d" % i, [128, 8, 64]) for i in range(NSB)]
            lsi = [P.lane() for _ in range(NSB)]
            lso = [P.lane() for _ in range(NSB)]
            lsi2 = [P.lane() for _ in range(NSB)]
            lso2 = [P.lane() for _ in range(NSB)]
            cnt = 0
            for smp in range(KSMP):
                op("pool", lambda e, smp=smp: e.tensor_copy(auxpad.t[:, 0:1], auxT.t[:, smp:smp + 1]), r=[auxT.k], w=[auxpad.k])
                aux_compute(0, auxpad.t[:], auxpad.k, sample=True)
                def smp_thread(h, ti):
                    St = Sgs[ti]
                    pd = padsets[ti]
                    dma("sp", lsi[ti], lambda e: e.dma_start(out=St.t[:], in_=D["st_gdn"][smp, h, :, :]), w=[St.k])
                    for nm, src in (("q", qs), ("k", ks_), ("v", vs), ("sg", sgs)):
                        op("pool", lambda e, nm=nm, src=src: e.tensor_copy(pd[nm].t[:, 0:1], src.t[:, h, smp:smp + 1]),
                           r=[src.k], w=[pd[nm].k])
                    gdn_unit(scr[ti], h, 0, pd["q"].t[:], pd["k"].t[:], pd["v"].t[:], pd["sg"].t[:],
                             [pd["q"].k, pd["k"].k, pd["v"].k, pd["sg"].k], St.t[:], St.k, otmps[ti].t[:], otmps[ti].k, single=True)
                    dma("sp", lso[ti], lambda e: e.dma_start(out=D["gdn_s"][smp, h, :, :], in_=St.t[:]), r=[St.k])
                    op("pool", lambda e: e.tensor_copy(oT.t[:, h, smp:smp + 1], otmps[ti].t[:, 0:1]), r=[otmps[ti].k], w=[oTk[h]])
                WS = int(os.environ.get('KTHS', '1'))
                for h0 in range(0, KH, WS):
                    ths = []
                    for ti in range(min(WS, KH - h0)):
                        with P.record() as th:
                            smp_thread(h0 + ti, ti)
                        ths.append(th)
                    P.merge(ths)
                if KSS:
                    for grp in range(2):
                        i = cnt % NSB
                        cnt += 1
                        St = Sss[i]
                        dma("sp", lsi2[i], lambda e: e.dma_start(
                            out=St.t[:], in_=D["st_ssm"][smp, 8 * grp:8 * grp + 8, :, :].rearrange("h n p -> n h p")), w=[St.k])
                        op("pool", lambda e: e.tensor_copy(pads["B"].t[:, 0:1], Bs.t[:, grp, smp:smp + 1]), r=[Bs.k], w=[pads["B"].k])
                        op("pool", lambda e: e.tensor_copy(pads["C"].t[:, 0:1], Cs.t[:, grp, smp:smp + 1]), r=[Cs.k], w=[pads["C"].k])
                        for j in range(4):
                            op("pool", lambda e, j=j: e.tensor_copy(pads["xs%d" % j].t[:, 0:1], xss.t[:, 4 * grp + j, smp:smp + 1]),
                               r=[xss.k], w=[pads["xs%d" % j].k])
                            op("pool", lambda e, j=j: e.tensor_copy(pads["sz%d" % j].t[:, 0:1], szs.t[:, 4 * grp + j, smp:smp + 1]),
                               r=[szs.k], w=[pads["sz%d" % j].k])
                        rk = [pads[n].k for n in ["B", "C"] + ["xs%d" % j for j in range(4)] + ["sz%d" % j for j in range(4)]]
                        ssd_unit(grp, 0, pads["B"].t[:], pads["C"].t[:], [pads["xs%d" % j].t[:] for j in range(4)],
                                 [pads["sz%d" % j].t[:] for j in range(4)], rk, St.t[:], St.k,
                                 [ytmp.t[:, j, :] for j in range(4)], ytk)
                        dma("sp", lso2[i], lambda e: e.dma_start(
                            out=D["ssm_s"][smp, 8 * grp:8 * grp + 8, :, :].rearrange("h n p -> n h p"), in_=St.t[:]), r=[St.k])
                        for j in range(4):
                            op("pool", lambda e, j=j: e.tensor_copy(oT.t[:, 8 + 4 * grp + j, smp:smp + 1], ytmp.t[:, j, 0:1]),
                               r=[ytk[j]], w=[oTk[8 + 4 * grp + j]])
            for j in range(KH, 8):
                op("pool", lambda e, j=j: e.memset(oT.t[:, j, :T], 0.0), w=[oTk[j]])
            if KSS:
                ssd_norm_out(T)
            else:
                for j in range(8, 16):
                    op("pool", lambda e, j=j: e.memset(oT.t[:, j, :T], 0.0), w=[oTk[j]])
            out_proj(4, T, t0)
        P.barrier()
        es.close()


    def layer1(self, D):
        nc, P = self.nc, self.P
        op, dma = P.op, P.dma
        c = self.c
        ident, ones, CK = c["ident"], c["ones"], c["CK"]
        banks = self.banks
        xT, xk, modT = self.xT, self.xk, self.modT
        w_in, w_out = D["ret_w_in"], D["ret_w_out"]
        es = contextlib.ExitStack()
        sb = lambda name, shape, dt=F32: self.sb(name, shape, dt, es=es)
        KG = int(os.environ.get('KG', '5'))
        KSMP = int(os.environ.get('KSMP', '16'))
        gam = [1.0 - 2.0 ** (-5.0 - h) for h in range(4)]

        lp = P.lane()
        retc = sb("retc_sb", [128, 4, 132])
        rngb = sb("rngb", [128, 4, 512])
        dma("sp", lp, lambda e: e.dma_start(out=retc.t[:], in_=D["retc"][:, :, :]), w=[retc.k])
        dma("sp", lp, lambda e: e.dma_start(out=rngb.t[:], in_=D["rng"][0:1, :, :].broadcast_to([128, 4, 512])), w=[rngb.k], group=True)
        P.group_finish(lp, [retc.k, rngb.k])
        Sr = sb("Sr", [128, 4, 2, 512])
        Srk = [Tk("Sr%d" % h) for h in range(4)]
        op("pool", lambda e: e.memset(Sr.t[:], 0.0), w=Srk)

        hm = sb("hm1", [128, 8, 512], BF16)
        NWB = 4
        wbuf = [sb("w1b%d" % i, [128, 8, 128], BF16) for i in range(NWB)]
        wl = [P.lane() for _ in range(NWB)]
        wi = [0]

        def load_w(col0, src, row0=0):
            i = wi[0] % NWB
            wi[0] += 1
            buf = wbuf[i]
            dma("pool", wl[i], lambda e: e.dma_start(
                out=buf.t[:], in_=src[row0:row0 + 1024, col0:col0 + 128].rearrange("(k p) c -> p k c", p=128)), w=[buf.k])
            return buf

        pring = [0]

        def proj_ps():
            b = banks[pring[0] % 2]
            pring[0] += 1
            return b

        def project(col0, T):
            buf = load_w(col0, w_in)
            pst = proj_ps()
            for k in range(8):
                op("pe", lambda e, k=k: e.matmul(pst.t[:, :T], buf.t[:, k, :], hm.t[:, k, :T], start=(k == 0), stop=(k == 7)),
                   r=[buf.k, hm.k], w=[pst.k])
            return pst

        cosb = sb("cosb", [128, 512])
        sinb = sb("sinb", [128, 512])
        lcs = P.lane()
        raw = [sb("rraw%d" % i, [128, 512]) for i in range(2)]
        ta = sb("rta", [128, 512])
        tb = sb("rtb", [128, 512])
        qT = [sb("rq%d" % i, [128, 512]) for i in range(2)]
        kT = [sb("rk%d" % i, [128, 512]) for i in range(2)]
        vT = [sb("rv%d" % i, [128, 512]) for i in range(4)]
        sgT = [sb("rsg%d" % i, [128, 512]) for i in range(4)]
        oT = sb("oT1", [128, 16, 512], BF16)
        oTk = [Tk("o1T%d" % j) for j in range(16)]

        def rope(col0, T, dst, scale):
            for cidx in range(2):
                pst = project(col0 + cidx * 128, T)
                op("act", lambda e, cidx=cidx: e.activation(out=raw[cidx].t[:, :T], in_=pst.t[:, :T], func=AF.Identity, scale=scale),
                   r=[pst.k], w=[raw[cidx].k])
            x1, x2 = raw
            op("dve", lambda e: e.tensor_tensor(ta.t[:, :T], x1.t[:, :T], cosb.t[:, :T], ALU.mult), r=[x1.k, cosb.k], w=[ta.k])
            op("pool", lambda e: e.tensor_tensor(tb.t[:, :T], x2.t[:, :T], sinb.t[:, :T], ALU.mult), r=[x2.k, sinb.k], w=[tb.k])
            op("dve", lambda e: e.tensor_tensor(dst[0].t[:, :T], ta.t[:, :T], tb.t[:, :T], ALU.subtract), r=[ta.k, tb.k], w=[dst[0].k])
            op("dve", lambda e: e.tensor_tensor(ta.t[:, :T], x1.t[:, :T], sinb.t[:, :T], ALU.mult), r=[x1.k, sinb.k], w=[ta.k])
            op("pool", lambda e: e.tensor_tensor(tb.t[:, :T], x2.t[:, :T], cosb.t[:, :T], ALU.mult), r=[x2.k, cosb.k], w=[tb.k])
            op("dve", lambda e: e.tensor_tensor(dst[1].t[:, :T], ta.t[:, :T], tb.t[:, :T], ALU.add), r=[ta.k, tb.k], w=[dst[1].k])

        class S:
            pass
        s = S()
        s.qk = sb("r_qk", [128, 128])
        s.ss = sb("r_ss", [128, 4])
        s.ps = [banks[2], banks[3]]
        s.pi = 0
        s.xtok = sb("r_xtok", [128, 512])
        s.kd = sb("r_kd", [128, 256])
        s.t512b = sb("r_t512b", [128, 512])
        s.otok = sb("r_otok", [128, 512])
        fring = [0]

        def fullbank():
            b = banks[4 + fring[0] % 3]
            fring[0] += 1
            return b

        def pst_of():
            j = s.pi % 8
            s.pi += 1
            b = s.ps[j % 2]
            r = j // 2
            return b.t[:, r * 128:(r + 1) * 128], b.k

        def ret_unit(h, q_aps, k_aps, v_aps, sg_aps, rk, S_ap, S_k, out_aps, out_ks, eg_col, ed_col, elast_f):
            pq, kq = pst_of()
            op("pe", lambda e: e.matmul(pq, k_aps[0], q_aps[0], start=True, stop=False), r=rk, w=[kq])
            op("pe", lambda e: e.matmul(pq, k_aps[1], q_aps[1], start=False, stop=True), r=rk, w=[kq])
            op("dve", lambda e: e.tensor_tensor(s.qk.t[:], pq, retc.t[:, h, 0:128], ALU.mult), r=[kq, retc.k], w=[s.qk.k])
            px = fullbank()
            for cidx in range(4):
                op("pe", lambda e, cidx=cidx: e.matmul(px.t[:, cidx * 128:(cidx + 1) * 128], v_aps[cidx], ident, start=True, stop=True),
                   r=rk + [CK], w=[px.k])
            op("act", lambda e: e.activation(out=s.xtok.t[:], in_=px.t[:], func=AF.Copy), r=[px.k], w=[s.xtok.k])
            pk = fullbank()
            for cidx in range(2):
                op("pe", lambda e, cidx=cidx: e.matmul(pk.t[:, cidx * 128:(cidx + 1) * 128], k_aps[cidx], ident, start=True, stop=True),
                   r=rk + [CK], w=[pk.k])
            op("dve", lambda e: e.tensor_scalar(s.kd.t[:], pk.t[:, 0:256], ed_col, None, ALU.mult), r=[pk.k, retc.k], w=[s.kd.k])
            po = fullbank()
            op("pe", lambda e: e.matmul(po.t[:], s.qk.t[:], s.xtok.t[:], start=True, stop=True), r=[s.qk.k, s.xtok.k], w=[po.k])
            pqs = fullbank()
            op("pe", lambda e: e.matmul(pqs.t[:], q_aps[0], S_ap[:, 0, :], start=True, stop=False), r=rk + [S_k], w=[pqs.k])
            op("pe", lambda e: e.matmul(pqs.t[:], q_aps[1], S_ap[:, 1, :], start=False, stop=True), r=rk + [S_k], w=[pqs.k])
            op("dve", lambda e: e.tensor_scalar(s.t512b.t[:], pqs.t[:], eg_col, None, ALU.mult), r=[pqs.k, retc.k], w=[s.t512b.k])
            op("dve", lambda e: e.tensor_tensor(s.otok.t[:], po.t[:], s.t512b.t[:], ALU.add), r=[po.k, s.t512b.k], w=[s.otok.k])
            op("dve", lambda e: e.reduce_sum(s.ss.t[:, 0:1], s.otok.t[:], AX.X), r=[s.otok.k], w=[s.ss.k])
            op("dve", lambda e: e.tensor_scalar(s.ss.t[:, 0:1], s.ss.t[:, 0:1], -1.0 / 512, None, ALU.mult), r=[s.ss.k], w=[s.ss.k])
            op("dve", lambda e: e.tensor_scalar(s.otok.t[:], s.otok.t[:], s.ss.t[:, 0:1], None, ALU.add), r=[s.otok.k, s.ss.k], w=[s.otok.k])
            op("act", lambda e: e.activation(out=s.t512b.t[:], in_=s.otok.t[:], func=AF.Square), r=[s.otok.k], w=[s.t512b.k])
            op("dve", lambda e: e.reduce_sum(s.ss.t[:, 1:2], s.t512b.t[:], AX.X), r=[s.t512b.k], w=[s.ss.k])
            op("act", lambda e: e.activation(out=s.ss.t[:, 2:3], in_=s.ss.t[:, 1:2], func=AF.Sqrt, bias=EPS, scale=1.0 / 512), r=[s.ss.k], w=[s.ss.k])
            op("dve", lambda e: e.reciprocal(s.ss.t[:, 2:3], s.ss.t[:, 2:3]), r=[s.ss.k], w=[s.ss.k])
            op("dve", lambda e: e.scalar_tensor_tensor(s.otok.t[:], s.otok.t[:], s.ss.t[:, 2:3], rngb.t[:, h, :], ALU.mult, ALU.mult),
               r=[s.otok.k, s.ss.k, rngb.k], w=[s.otok.k])
            pt = fullbank()
            for cidx in range(4):
                op("pe", lambda e, cidx=cidx: e.matmul(pt.t[:, cidx * 128:(cidx + 1) * 128], s.otok.t[:, cidx * 128:(cidx + 1) * 128], ident,
                                                       start=True, stop=True), r=[s.otok.k, CK], w=[pt.k])
            for cidx in range(4):
                op("dve", lambda e, cidx=cidx: e.tensor_tensor(out_aps[cidx], pt.t[:, cidx * 128:(cidx + 1) * 128], sg_aps[cidx], ALU.mult),
                   r=[pt.k] + rk, w=[out_ks[cidx]])
            for cidx in range(2):
                pS = fullbank()
                op("pe", lambda e, cidx=cidx: e.matmul(pS.t[:], s.kd.t[:, cidx * 128:(cidx + 1) * 128], s.xtok.t[:], start=True, stop=True),
                   r=[s.kd.k, s.xtok.k], w=[pS.k])
                op("dve", lambda e, cidx=cidx: e.tensor_scalar(s.t512b.t[:], S_ap[:, cidx, :], elast_f, None, ALU.mult), r=[S_k, s.t512b.k], w=[s.t512b.k])
                op("dve", lambda e, cidx=cidx: e.tensor_tensor(S_ap[:, cidx, :], pS.t[:], s.t512b.t[:], ALU.add), r=[pS.k, s.t512b.k], w=[S_k])

        def out_proj(g, T, t0):
            sq = self.nsq
            for m in range(8):
                pst = proj_ps()
                for half in range(2):
                    buf = load_w(m * 128, w_out, row0=half * 1024)
                    for k in range(8):
                        kk = half * 8 + k
                        op("pe", lambda e, k=k, kk=kk: e.matmul(pst.t[:, :T], buf.t[:, k, :], oT.t[:, kk, :T],
                                                                start=(kk == 0), stop=(kk == 15)), r=[buf.k, oTk[kk]], w=[pst.k])
                if g < 4:
                    op("dve", lambda e, m=m: e.scalar_tensor_tensor(
                        xT.t[:, m, t0:t0 + T], pst.t[:, :T], modT.t[:, 1, 16 + m, 0:1], xT.t[:, m, t0:t0 + T], ALU.mult, ALU.add),
                        r=[pst.k, modT.k, xk[m][g]], w=[xk[m][g]])
                else:
                    op("dve", lambda e, m=m: e.tensor_tensor(sq.t[:, :T], pst.t[:, :T], modT.t[:, 1, 16 + m, 1:17], ALU.mult),
                       r=[pst.k, modT.k], w=[sq.k])
                    op("dve", lambda e, m=m: e.tensor_tensor(xT.t[:, m, t0:t0 + T], xT.t[:, m, t0:t0 + T], sq.t[:, :T], ALU.add),
                       r=[sq.k, xk[m][g]], w=[xk[m][g]])

        def head_inputs(h, T):
            rope(h * 256, T, qT, 1.0)
            rope(1024 + h * 256, T, kT, 1.0 / 16)
            for cidx in range(4):
                pst = project(2048 + h * 512 + cidx * 128, T)
                op("act", lambda e, cidx=cidx: e.activation(out=vT[cidx].t[:, :T], in_=pst.t[:, :T], func=AF.Copy), r=[pst.k], w=[vT[cidx].k])
                pst = project(4096 + h * 512 + cidx * 128, T)
                op("act", lambda e, cidx=cidx: e.activation(out=sgT[cidx].t[:, :T], in_=pst.t[:, :T], func=AF.Silu), r=[pst.k], w=[sgT[cidx].k])

        allk = [t.k for t in qT + kT + vT + sgT]
        for g in range(min(KG, 4)):
            t0, T = self.groups[g]
            self.norm_mod_layer(1, 0, g, hm)
            dma("sp", lcs, lambda e: e.dma_start(out=cosb.t[:, :T], in_=D["cosT"][:, t0:t0 + T]), w=[cosb.k])
            dma("sp", lcs, lambda e: e.dma_start(out=sinb.t[:, :T], in_=D["sinT"][:, t0:t0 + T]), w=[sinb.k])
            for h in range(4):
                head_inputs(h, T)
                for cc in range(T // 128):
                    cs = slice(cc * 128, (cc + 1) * 128)
                    ret_unit(h, [qT[i].t[:, cs] for i in range(2)], [kT[i].t[:, cs] for i in range(2)],
                             [vT[i].t[:, cs] for i in range(4)], [sgT[i].t[:, cs] for i in range(4)], allk,
                             Sr.t[:, h, :, :], Srk[h], [oT.t[:, 4 * h + i, cs] for i in range(4)], [oTk[4 * h + i] for i in range(4)],
                             retc.t[:, h, 128:129], retc.t[:, h, 129:130], gam[h] ** 128)
            out_proj(g, T, t0)
        op_l = P.lane()
        dma("sp", op_l, lambda e: e.dma_start(out=D["ret_p"].rearrange("h (c p) e -> p h c e", p=128), in_=Sr.t[:]), r=Srk)

        if KG >= 5:
            g, (t0, T) = 4, self.groups[4]
            self.norm_mod_layer(1, 0, 4, hm)
            dma("sp", lcs, lambda e: e.dma_start(out=cosb.t[:, :T], in_=D["cosT"][:, t0:t0 + T]), w=[cosb.k])
            dma("sp", lcs, lambda e: e.dma_start(out=sinb.t[:, :T], in_=D["sinT"][:, t0:t0 + T]), w=[sinb.k])
            pads = [sb("rpad%d" % i, [128, 128]) for i in range(12)]
            for p_ in pads:
                op("pool", lambda e, p_=p_: e.memset(p_.t[:], 0.0), w=[p_.k])
            otmp = sb("rotmp", [128, 4, 128])
            otk = [Tk() for _ in range(4)]
            NSB = 2
            Sts = [sb("Srs%d" % i, [128, 2, 512]) for i in range(NSB)]
            lsi = [P.lane() for _ in range(NSB)]
            lso = [P.lane() for _ in range(NSB)]
            cnt = 0
            srcs = qT + kT + vT + sgT
            for h in range(4):
                head_inputs(h, T)
                for smp in range(KSMP):
                    i = cnt % NSB
                    cnt += 1
                    St = Sts[i]
                    dma("sp", lsi[i], lambda e: e.dma_start(out=St.t[:], in_=D["st_ret"][smp, h, :, :].rearrange("(c p) e -> p c e", p=128)), w=[St.k])
                    for j in range(12):
                        op("pool", lambda e, j=j: e.tensor_copy(pads[j].t[:, 0:1], srcs[j].t[:, smp:smp + 1]), r=[srcs[j].k], w=[pads[j].k])
                    ret_unit(h, [pads[0].t[:], pads[1].t[:]], [pads[2].t[:], pads[3].t[:]], [pads[4 + i_].t[:] for i_ in range(4)],
                             [pads[8 + i_].t[:] for i_ in range(4)], [p_.k for p_ in pads], St.t[:], St.k,
                             [otmp.t[:, i_, :] for i_ in range(4)], otk, retc.t[:, h, 130:131], retc.t[:, h, 131:132], gam[h])
                    dma("sp", lso[i], lambda e: e.dma_start(out=D["ret_s"][smp, h, :, :].rearrange("(c p) e -> p c e", p=128), in_=St.t[:]), r=[St.k])
                    for i_ in range(4):
                        op("pool", lambda e, i_=i_: e.tensor_copy(oT.t[:, 4 * h + i_, smp:smp + 1], otmp.t[:, i_, 0:1]), r=[otk[i_]], w=[oTk[4 * h + i_]])
            out_proj(4, T, t0)
        P.barrier()
        es.close()


    def peer(self, l, D):
        nc, P = self.nc, self.P
        op, dma = P.op, P.dma
        c = self.c
        ident, ones, iota, CK = c["ident"], c["ones"], c["iota"], c["CK"]
        banks = self.banks
        xT, xk, modT, modA = self.xT, self.xk, self.modT, self.modA
        es = contextlib.ExitStack()
        sb = lambda name, shape, dt=F32: self.sb(name, shape, dt, es=es)
        KPG = int(os.environ.get('KPG', '9'))
        KPI = int(os.environ.get('KPI', '64'))
        nm = "p%d_" % l

        lp = P.lane()
        keysT = sb(nm + "keysT", [128, 2, 128])
        dma("sp", lp, lambda e: e.dma_start(out=keysT.t[:], in_=D["keysT"][l].rearrange("p d n -> d p n")), w=[keysT.k])
        hp = sb(nm + "hp", [128, 8, 256], BF16)
        NWB = 4
        wbuf = [sb(nm + "wq%d" % i, [128, 8, 128], BF16) for i in range(NWB)]
        wl = [P.lane() for _ in range(NWB)]
        wi = [0]
        ub = [sb(nm + "ub%d" % i, [128, 8, 256], BF16) for i in range(2)]
        vb = [sb(nm + "vb%d" % i, [128, 2, 1024], BF16) for i in range(2)]
        ul = [P.lane() for _ in range(2)]
        vl = [P.lane() for _ in range(2)]
        qc = [sb(nm + "qc%d" % i, [128, 128]) for i in range(2)]
        sc = [sb(nm + "sc%d" % i, [128, 128]) for i in range(2)]
        work = sb(nm + "work", [128, 128])
        sv = sb(nm + "sv", [128, 16, 16])
        si = sb(nm + "si", [128, 16, 16], U32)
        sif = sb(nm + "sif", [128, 16, 16])
        cand = sb(nm + "cand", [128, 256])
        cwork = sb(nm + "cwork", [128, 256])
        cv = sb(nm + "cv", [128, 8, 16])
        ci = sb(nm + "ci", [128, 8, 16], U32)
        au = sb(nm + "au", [128, 8, 16], U32)
        bu = sb(nm + "bu", [128, 8, 16], U32)
        af = sb(nm + "af", [128, 8, 16])
        bf = sb(nm + "bf", [128, 8, 16])
        eq = sb(nm + "eq", [128, 16, 16])
        iidx = sb(nm + "iidx", [128, 128])
        jidx = sb(nm + "jidx", [128, 128])
        gate = sb(nm + "gate", [128, 8, 16])
        zz = sb(nm + "zz", [128, 8])
        iT = sb(nm + "iT", [128, 128])
        jT = sb(nm + "jT", [128, 128])
        gT = sb(nm + "gT", [128, 128])
        OHj = sb(nm + "OHj", [128, 16, 128], BF16)
        OHi = sb(nm + "OHi", [128, 16, 128], BF16)
        Wall = sb(nm + "Wall", [128, 256, 128], BF16)
        zs = [sb(nm + "zs%d" % i, [128, 256]) for i in range(2)]
        Ab = [sb(nm + "A%d" % i, [128, 256], BF16) for i in range(2)]
        ysb = sb(nm + "ysb", [128, 1024])
        stmp = sb(nm + "stmp", [128, 128])
        pring = [0]

        def proj_ps():
            b = banks[pring[0] % 2]
            pring[0] += 1
            return b

        groups = [(i * 256, 256) for i in range(8)] + [(2048, 16)]
        ui = [0]
        for gi, (t0, T) in enumerate(groups[:KPG]):
            g5 = 4 if t0 >= 2048 else t0 // 512
            sh0 = 24
            if g5 < 4:
                self.norm_mod(g5, hp, lambda k: modA.t[:, l, 1, k, 0:1], lambda k: modT.t[:, l, sh0 + k, 0:1], t0, T)
            else:
                self.norm_mod(g5, hp, None, None, t0, T,
                              samp_A=lambda k: modA.t[:, l, 1, k, 1:17], samp_B=lambda k: modT.t[:, l, sh0 + k, 1:17])
            def route(tt0, T):
                for c16 in range(16):
                    i_ = wi[0] % NWB
                    wi[0] += 1
                    wq = wbuf[i_]
                    dma("pool", wl[i_], lambda e: e.dma_start(
                        out=wq.t[:], in_=D["w_q"][l, :, c16 * 128:(c16 + 1) * 128].rearrange("(k p) c -> p k c", p=128)), w=[wq.k])
                    pq = proj_ps()
                    for k in range(8):
                        op("pe", lambda e, k=k: e.matmul(pq.t[:, :T], wq.t[:, k, :], hp.t[:, k, tt0:tt0 + T], start=(k == 0), stop=(k == 7)),
                           r=[wq.k, hp.k], w=[pq.k])
                    q_ = qc[c16 % 2]
                    op("act", lambda e: e.activation(out=q_.t[:, :T], in_=pq.t[:, :T], func=AF.Copy), r=[pq.k], w=[q_.k])
                    pss = banks[2 + c16 % 2]
                    op("pe", lambda e: e.matmul(pss.t[:T, 0:128], q_.t[:, :T], keysT.t[:, c16 % 2, :], start=True, stop=True),
                       r=[q_.k, keysT.k], w=[pss.k])
                    s_ = sc[c16 % 2]
                    op("dve", lambda e: e.tensor_copy(s_.t[:T, :], pss.t[:T, 0:128]), r=[pss.k], w=[s_.k])
                    op("dve", lambda e: e.max(out=sv.t[:T, c16, 0:8], in_=s_.t[:T, :]), r=[s_.k], w=[sv.k])
                    op("dve", lambda e: e.match_replace(out=work.t[:T, :], in_to_replace=sv.t[:T, c16, 0:8], in_values=s_.t[:T, :], imm_value=-1e30),
                       r=[s_.k, sv.k], w=[work.k])
                    op("dve", lambda e: e.max(out=sv.t[:T, c16, 8:16], in_=work.t[:T, :]), r=[work.k, sv.k], w=[sv.k])
                    op("dve", lambda e: e.max_index(out=si.t[:T, c16, 0:8], in_max=sv.t[:T, c16, 0:8], in_values=s_.t[:T, :]), r=[s_.k, sv.k], w=[si.k])
                    op("dve", lambda e: e.max_index(out=si.t[:T, c16, 8:16], in_max=sv.t[:T, c16, 8:16], in_values=s_.t[:T, :]), r=[s_.k, sv.k, si.k], w=[si.k])
                op("dve", lambda e: e.tensor_copy(sif.t[:T], si.t[:T]), r=[si.k], w=[sif.k])
                for h in range(8):
                    op("dve", lambda e: e.tensor_tensor(cand.t[:T, :].rearrange("p (a b) -> p a b", a=16),
                                                        sv.t[:T, 2 * h, :].unsqueeze(2).broadcast_to([T, 16, 16]),
                                                        sv.t[:T, 2 * h + 1, :].unsqueeze(1).broadcast_to([T, 16, 16]), ALU.add), r=[sv.k], w=[cand.k])
                    op("dve", lambda e: e.max(out=cv.t[:T, h, 0:8], in_=cand.t[:T, :]), r=[cand.k], w=[cv.k])
                    op("dve", lambda e: e.match_replace(out=cwork.t[:T, :], in_to_replace=cv.t[:T, h, 0:8], in_values=cand.t[:T, :], imm_value=-1e30),
                       r=[cand.k, cv.k], w=[cwork.k])
                    op("dve", lambda e: e.max(out=cv.t[:T, h, 8:16], in_=cwork.t[:T, :]), r=[cwork.k, cv.k], w=[cv.k])
                    op("dve", lambda e: e.max_index(out=ci.t[:T, h, 0:8], in_max=cv.t[:T, h, 0:8], in_values=cand.t[:T, :]), r=[cand.k, cv.k], w=[ci.k])
                    op("dve", lambda e: e.max_index(out=ci.t[:T, h, 8:16], in_max=cv.t[:T, h, 8:16], in_values=cand.t[:T, :]), r=[cand.k, cv.k, ci.k], w=[ci.k])
                op("dve", lambda e: e.tensor_single_scalar(au.t[:T], ci.t[:T], 4, ALU.logical_shift_right), r=[ci.k], w=[au.k])
                op("dve", lambda e: e.tensor_single_scalar(bu.t[:T], ci.t[:T], 15, ALU.bitwise_and), r=[ci.k], w=[bu.k])
                op("dve", lambda e: e.tensor_copy(af.t[:T], au.t[:T]), r=[au.k], w=[af.k])
                op("dve", lambda e: e.tensor_copy(bf.t[:T], bu.t[:T]), r=[bu.k], w=[bf.k])
                for h in range(8):
                    for (xf, col, dst) in ((af, 2 * h, iidx), (bf, 2 * h + 1, jidx)):
                        op("dve", lambda e: e.tensor_tensor(eq.t[:T], xf.t[:T, h, :].unsqueeze(2).broadcast_to([T, 16, 16]),
                                                            iota[:T, 0:16].unsqueeze(1).broadcast_to([T, 16, 16]), ALU.is_equal),
                           r=[xf.k, CK], w=[eq.k])
                        op("dve", lambda e: e.tensor_tensor(eq.t[:T], eq.t[:T], sif.t[:T, col, :].unsqueeze(1).broadcast_to([T, 16, 16]), ALU.mult),
                           r=[eq.k, sif.k], w=[eq.k])
                        op("dve", lambda e: e.reduce_sum(dst.t[:T, h * 16:(h + 1) * 16], eq.t[:T], AX.X), r=[eq.k], w=[dst.k])
                op("dve", lambda e: e.tensor_tensor(gate.t[:T], cv.t[:T], cv.t[:T, :, 0:1].broadcast_to([T, 8, 16]), ALU.subtract), r=[cv.k], w=[gate.k])
                op("act", lambda e: e.activation(out=gate.t[:T], in_=gate.t[:T], func=AF.Exp), r=[gate.k], w=[gate.k])
                op("dve", lambda e: e.reduce_sum(zz.t[:T, :], gate.t[:T], AX.X), r=[gate.k], w=[zz.k])
                op("dve", lambda e: e.reciprocal(zz.t[:T, :], zz.t[:T, :]), r=[zz.k], w=[zz.k])
                op("dve", lambda e: e.tensor_tensor(gate.t[:T], gate.t[:T], zz.t[:T, :].unsqueeze(2).broadcast_to([T, 8, 16]), ALU.mult),
                   r=[gate.k, zz.k], w=[gate.k])
                for src_ap, src_k, dstT in ((iidx.t[:T, :], iidx.k, iT), (jidx.t[:T, :], jidx.k, jT),
                                            (gate.t[:T].rearrange("p h k -> p (h k)"), gate.k, gT)):
                    pt = banks[3]
                    op("pe", lambda e: e.matmul(pt.t[:, :T], src_ap, ident[:T, :T], start=True, stop=True), r=[src_k, CK], w=[pt.k])
                    op("dve", lambda e: e.tensor_copy(dstT.t[:, :T], pt.t[:, :T]), r=[pt.k], w=[dstT.k])
                for sub in range((T + 15) // 16):
                    ts = sub * 16
                    n = min(16, T - ts)
                    op("dve", lambda e: e.tensor_tensor(OHj.t[:, :n, :], iota.unsqueeze(1).broadcast_to([128, n, 128]),
                                                        jT.t[:, ts:ts + n].unsqueeze(2).broadcast_to([128, n, 128]), ALU.is_equal),
                       r=[jT.k, CK], w=[OHj.k])
                    op("dve", lambda e: e.tensor_tensor(OHi.t[:, :n, :], iota.unsqueeze(1).broadcast_to([128, n, 128]),
                                                         iT.t[:, ts:ts + n].unsqueeze(2).broadcast_to([128, n, 128]), ALU.is_equal),
                       r=[iT.k, CK], w=[OHi.k])
                    op("pool", lambda e: e.tensor_tensor(OHi.t[:, :n, :], OHi.t[:, :n, :],
                                                         gT.t[:, ts:ts + n].unsqueeze(2).broadcast_to([128, n, 128]), ALU.mult),
                       r=[OHi.k, gT.k], w=[OHi.k])
                    for q4 in range((n + 3) // 4):
                        pw = banks[2 + q4 % 2]
                        n4 = min(4, n - q4 * 4)
                        for t in range(n4):
                            tt = q4 * 4 + t
                            op("pe", lambda e, t=t, tt=tt: e.matmul(pw.t[:, t * 128:(t + 1) * 128], OHj.t[:, tt, :], OHi.t[:, tt, :], start=True, stop=True),
                               r=[OHj.k, OHi.k], w=[pw.k])
                        tg = tt0 + ts + q4 * 4
                        eng = "act"
                        if eng == "act":
                            op("act", lambda e: e.activation(out=Wall.t[:, tg:tg + n4, :].rearrange("p t i -> p (t i)"), in_=pw.t[:, :n4 * 128], func=AF.Copy),
                               r=[pw.k], w=[Wall.k])
                        else:
                            op("dve", lambda e: e.tensor_copy(Wall.t[:, tg:tg + n4, :].rearrange("p t i -> p (t i)"), pw.t[:, :n4 * 128]), r=[pw.k], w=[Wall.k])

            tiles = [(tt0, min(128, T - tt0)) for tt0 in range(0, T, 128)]
            for tt0, nt in tiles:
                route(tt0, nt)
            py = [banks[4], banks[5], banks[6], banks[7]]
            for i2 in range(KPI):
                b_ = ui[0] % 2
                ui[0] += 1
                u_, v_ = ub[b_], vb[b_]
                dma("pool", ul[b_], lambda e: e.dma_start(
                    out=u_.t[:], in_=D["uT"][l, :, i2 * 256:(i2 + 1) * 256].rearrange("(k p) e -> p k e", p=128)), w=[u_.k])
                dma("pool", vl[b_], lambda e: e.dma_start(
                    out=v_.t[:], in_=D["pv"][l, i2 * 256:(i2 + 1) * 256, :].rearrange("(ii p) d -> p ii d", p=128)), w=[v_.k])
                for ii in range(2):
                    i = i2 * 2 + ii
                    pz = proj_ps()
                    for k in range(8):
                        op("pe", lambda e, k=k: e.matmul(pz.t[:, :T], u_.t[:, k, ii * 128:(ii + 1) * 128], hp.t[:, k, :T], start=(k == 0), stop=(k == 7)),
                           r=[u_.k, hp.k], w=[pz.k])
                    z_ = zs[i % 2]
                    a_ = Ab[i % 2]
                    op("act", lambda e: e.activation(out=z_.t[:, :T], in_=pz.t[:, :T], func=AF.Gelu), r=[pz.k], w=[z_.k])
                    op("dve", lambda e: e.tensor_tensor(a_.t[:, :T], z_.t[:, :T], Wall.t[:, :T, i], ALU.mult), r=[z_.k, Wall.k], w=[a_.k])
                    for ti, (tt0, nt) in enumerate(tiles):
                        for half in range(2):
                            pyb = py[ti * 2 + half]
                            op("pe", lambda e, half=half: e.matmul(pyb.t[:nt, :], a_.t[:, tt0:tt0 + nt], v_.t[:, ii, half * 512:(half + 1) * 512],
                                                                   start=(i == 0), stop=(i == 2 * KPI - 1)), r=[a_.k, v_.k], w=[pyb.k])
            for ti, (tt0, nt) in enumerate(tiles):
                for half in range(2):
                    pyb = py[ti * 2 + half]
                    op("act", lambda e, half=half: e.activation(out=ysb.t[:nt, half * 512:(half + 1) * 512], in_=pyb.t[:nt, :], func=AF.Copy),
                       r=[pyb.k], w=[ysb.k])
                c0 = t0 + tt0
                for m in range(8):
                    pt = banks[2 + m % 2]
                    op("pe", lambda e, m=m: e.matmul(pt.t[:, :nt], ysb.t[:nt, m * 128:(m + 1) * 128], ident[:nt, :nt], start=True, stop=True),
                       r=[ysb.k, CK], w=[pt.k])
                    if g5 < 4:
                        op("dve", lambda e, m=m: e.scalar_tensor_tensor(
                            xT.t[:, m, c0:c0 + nt], pt.t[:, :nt], modT.t[:, l, 40 + m, 0:1], xT.t[:, m, c0:c0 + nt], ALU.mult, ALU.add),
                            r=[pt.k, modT.k, xk[m][g5]], w=[xk[m][g5]])
                    else:
                        op("dve", lambda e, m=m: e.tensor_tensor(stmp.t[:, :nt], pt.t[:, :nt], modT.t[:, l, 40 + m, 1:17], ALU.mult),
                           r=[pt.k, modT.k], w=[stmp.k])
                        op("dve", lambda e, m=m: e.tensor_tensor(xT.t[:, m, c0:c0 + nt], xT.t[:, m, c0:c0 + nt], stmp.t[:, :nt], ALU.add),
                           r=[stmp.k, xk[m][g5]], w=[xk[m][g5]])
        P.barrier()
        es.close()


def _consts():
    c = np.zeros((128, 6, 128), np.float32)
    p = np.arange(128)[:, None]
    f = np.arange(128)[None, :]
    c[:, 0] = (p == f)
    c[:, 1] = (p <= f)
    c[:, 2] = 1.0
    c[:, 3] = np.where(f >= p, BIG, 0.0)
    c[:, 4] = np.where(f < p, -BIG, 0.0)
    c[:, 5] = f
    return c


def fm(v, k):
    return np.ascontiguousarray(np.asarray(v, np.float32).reshape(k, 128).T)


def make_in_maps(inp, ncores=8):
    f32 = np.float32
    shared = {}
    shared["ada_w"] = np.ascontiguousarray(inp["ada_w"], f32)
    shared["ada_b"] = np.ascontiguousarray(np.stack([fm(inp["ada_b"][l], 48) for l in range(2)], 1))
    shared["n1g"] = np.ascontiguousarray(np.stack([fm(inp["norm1_g"][l], 8) for l in range(2)], 1))
    shared["n2g"] = np.ascontiguousarray(np.stack([fm(inp["norm2_g"][l], 8) for l in range(2)], 1))
    shared["fg"] = fm(inp["final_g"], 8)
    shared["consts"] = _consts()
    shared["ab_w_in"] = np.ascontiguousarray(inp["ab_w_in"][0], f32)
    cw = inp["ab_conv_w"][0]
    shared["cw"] = np.ascontiguousarray(cw.reshape(4, 36, 128).transpose(2, 1, 0))
    shared["cb"] = fm(inp["ab_conv_b"][0], 36)
    auxp = np.zeros((1, 5, 32), f32)
    auxp[0, 0, 8:16] = inp["gdn_dt_bias"][0]
    auxp[0, 0, 16:32] = inp["ssm_dt_bias"][0]
    auxp[0, 1, 8:16] = inp["gdn_a_log"][0]
    auxp[0, 1, 16:32] = inp["ssm_a_log"][0]
    auxp[0, 2, 16:32] = inp["ssm_d"][0]
    shared["auxp"] = auxp
    shared["gng"] = np.ascontiguousarray(inp["gdn_norm_g"][0].reshape(128, 1), f32)
    shared["sng"] = fm(inp["ssm_norm_g"][0], 8)
    shared["ab_w_out"] = np.ascontiguousarray(inp["ab_w_out"][0], f32)
    shared["ret_w_in"] = np.ascontiguousarray(inp["ret_w_in"][0], f32)
    shared["ret_w_out"] = np.ascontiguousarray(inp["ret_w_out"][0], f32)
    shared["w_q"] = np.ascontiguousarray(inp["peer_w_q"], f32)
    shared["keysT"] = np.ascontiguousarray(np.asarray(inp["peer_keys"], f32).transpose(0, 1, 3, 2))
    shared["uT"] = np.ascontiguousarray(np.asarray(inp["peer_u"], f32).transpose(0, 2, 1))
    shared["pv"] = np.ascontiguousarray(inp["peer_v"], f32)
    shared["rng"] = np.ascontiguousarray(inp["ret_norm_g"], f32).reshape(1, 4, 512)
    retc = np.zeros((128, 4, 132), np.float64)
    ii = np.arange(128)
    for h in range(4):
        gm = 1.0 - 2.0 ** (-5.0 - h)
        d = ii[None, :] - ii[:, None]
        retc[:, h, 0:128] = np.where(d >= 0, gm ** np.maximum(d, 0), 0.0)
        retc[:, h, 128] = gm ** (ii + 1)
        retc[:, h, 129] = gm ** (127 - ii)
        retc[:, h, 130] = gm
        retc[:, h, 131] = 1.0
    shared["retc"] = retc.astype(f32)
    inv = 10000.0 ** (-np.arange(0, 256, 2, dtype=np.float64) / 256)
    pos = np.concatenate([np.arange(2048, dtype=np.float64), np.full(16, 16384.0)])
    ang = (pos[None, :].astype(f32) * inv[:, None].astype(f32)).astype(f32)
    shared["cosT"] = np.cos(ang.astype(np.float64)).astype(f32)
    shared["sinT"] = np.sin(ang.astype(np.float64)).astype(f32)
    maps = []
    for b in range(ncores):
        m = dict(shared)
        x_all = np.concatenate([inp["x_prompt"][b], inp["x_sample"][16 * b:16 * b + 16, 0]], 0)
        m["xT"] = np.ascontiguousarray(x_all.reshape(NT, 8, 128).transpose(2, 1, 0))
        c_all = np.concatenate([inp["c_prompt"][b:b + 1], inp["c_sample"][16 * b:16 * b + 16]], 0)
        m["cT"] = np.ascontiguousarray(c_all.reshape(17, 8, 128).transpose(2, 1, 0))
        sc = inp["state_conv"][0, 16 * b:16 * b + 16]
        m["st_conv"] = np.ascontiguousarray(sc.reshape(16, 3, 36, 128).transpose(3, 2, 0, 1))
        m["st_gdn"] = np.ascontiguousarray(inp["state_gdn"][0, 16 * b:16 * b + 16])
        m["st_ssm"] = np.ascontiguousarray(inp["state_ssm"][0, 16 * b:16 * b + 16])
        m["st_ret"] = np.ascontiguousarray(inp["state_ret"][0, 16 * b:16 * b + 16])
        maps.append(m)
    return maps


_CACHE = {}


def run(inp, ncores=8, dbg=(), stages=("l0", "p0", "l1", "p1")):
    bld = Builder(dbg=dbg, stages=stages)
    bld.build()
    maps = make_in_maps(inp, ncores)
    maps = [{k: v for k, v in m.items() if k in bld.in_names} for m in maps]
    res = run_bass_kernel_spmd(bld.nc, maps, core_ids=list(range(ncores)))
    return res.results


def kernel(**inputs):
    inp = {k: np.asarray(v) for k, v in inputs.items()}
    n = 8
    bld = Builder(dbg=("xb0", "xa1"))
    bld.build()
    maps = make_in_maps(inp, n)
    maps = [{k: v for k, v in m.items() if k in bld.in_names} for m in maps]
    res = run_bass_kernel_spmd(bld.nc, maps, core_ids=list(range(n))).results
    f32 = np.float32
    y_prompt = np.zeros((8, 2048, 1024), f32)
    y_sample = np.zeros((128, 1, 1024), f32)
    conv_p = np.zeros((1, 8, 3, 4608), f32)
    gdn_p = np.zeros((1, 8, 8, 128, 128), f32)
    ssm_p = np.zeros((1, 8, 16, 128, 64), f32)
    ret_p = np.zeros((1, 8, 4, 256, 512), f32)
    conv_s = np.zeros((1, 128, 3, 4608), f32)
    gdn_s = np.zeros((1, 128, 8, 128, 128), f32)
    ssm_s = np.zeros((1, 128, 16, 128, 64), f32)
    ret_s = np.zeros((1, 128, 4, 256, 512), f32)
    for b in range(n):
        r = res[b]
        y = np.asarray(r["yT"]).transpose(2, 1, 0).reshape(NT, 1024)
        y_prompt[b] = y[:2048]
        y_sample[16 * b:16 * b + 16, 0] = y[2048:]
        conv_p[0, b] = np.asarray(r["conv_p"]).transpose(2, 1, 0).reshape(3, 4608)
        gdn_p[0, b] = r["gdn_p"]
        ssm_p[0, b] = r["ssm_p"]
        ret_p[0, b] = r["ret_p"]
        conv_s[0, 16 * b:16 * b + 16] = np.asarray(r["conv_s"]).transpose(2, 3, 1, 0).reshape(16, 3, 4608)
        gdn_s[0, 16 * b:16 * b + 16] = r["gdn_s"]
        ssm_s[0, 16 * b:16 * b + 16] = r["ssm_s"]
        ret_s[0, 16 * b:16 * b + 16] = r["ret_s"]
    return (y_prompt, y_sample, conv_p, gdn_p, ssm_p, ret_p, conv_s, gdn_s, ssm_s, ret_s)
```
